# Optimizing a Trainium2 kernel written in Bass

```python
import jax
import jax.numpy as jnp
from jax import lax
import numpy as np

D_MODEL = 1024
BATCH = 8
SEQ = 2048
DEPTH = 4
DEC_BATCH = 128
DEC_SEQ = 8
PAST_LEN = 2048
PAGE_SIZE = 128

F32 = jnp.float32
N_MIXERS = 3
N_LAYERS_A = (DEPTH + 2) // 3
N_LAYERS_B = (DEPTH + 1) // 3
N_LAYERS_C = DEPTH // 3

DEEPNORM_ALPHA = (2.0 * DEPTH) ** 0.25
DEEPNORM_BETA = (8.0 * DEPTH) ** -0.25
LN_EPS = 1e-5

A_HEAD_DIM = 64
A_HEADS = D_MODEL // A_HEAD_DIM
A_WIDTH = A_HEADS * A_HEAD_DIM
A_LORA_W = 64
A_LORA_A = 64
A_NCOLS = 4 * A_WIDTH + A_LORA_W + A_LORA_A
A_GN_EPS = 64e-5
_A_SPLITS = (A_WIDTH, 2 * A_WIDTH, 3 * A_WIDTH, 4 * A_WIDTH, 4 * A_WIDTH + A_LORA_W)

B_WIDTH = D_MODEL
B_GROUPS = 4
B_GROUP_W = B_WIDTH // B_GROUPS
B_WINDOWS = (2, 4, 8, 16)
B_BUF = max(B_WINDOWS) - 1

C_HEAD_DIM = 64
C_HEADS = 16
C_KV_HEADS = 4
C_HPG = C_HEADS // C_KV_HEADS
C_WIDTH = C_HEADS * C_HEAD_DIM
C_KV_WIDTH = C_KV_HEADS * C_HEAD_DIM
C_CMP_BLOCK = 32
C_SEL_BLOCK = 64
C_TOP_N = 16
C_WINDOW = 512
C_Q_ROWS = 256
C_FORCE = 1e9
C_NCOLS = 2 * C_WIDTH + 6 * C_KV_WIDTH + 3 * C_HEADS

kernel_name = 'hybrid_rwkv7_pool_nsa_deepnorm_step'


def _layer_norm(x, g, b):
    xf = x.astype(F32)
    mu = jnp.mean(xf, axis=-1, keepdims=True)
    var = jnp.mean(jnp.square(xf - mu), axis=-1, keepdims=True)
    return ((xf - mu) * lax.rsqrt(var + LN_EPS) * g.astype(F32) + b.astype(F32)).astype(x.dtype)


def _masked_softmax(s, mask, axis=-1):
    s = jnp.where(mask, s.astype(F32), -jnp.inf)
    m = jnp.max(s, axis=axis, keepdims=True)
    m = jnp.where(jnp.isfinite(m), m, 0.0)
    e = jnp.where(mask, jnp.exp(s - m), 0.0)
    return e / jnp.maximum(jnp.sum(e, axis=axis, keepdims=True), 1e-30)


def _alibi_slopes(n):
    return jnp.power(2.0, -8.0 * (jnp.arange(n, dtype=F32) + 1.0) / n)


def _query_block(batch, t):
    qb = max(1, min(t, C_Q_ROWS // batch))
    while t % qb:
        qb -= 1
    return qb


def _gather_pages(pool, page_table):
    pages = pool[page_table]
    return pages.reshape(page_table.shape[0], page_table.shape[1] * pool.shape[1], pool.shape[2], pool.shape[3])


def _rwkv_mixer(x, S0, p_prev, w_in, mu, w0, w2, a0, a2, k_k, k_a, r_k, lnx_g, lnx_b, w_out):
    B, T, _ = x.shape
    H, N = A_HEADS, A_HEAD_DIM
    p = x @ w_in
    p_shift = jnp.concatenate([p_prev[:, None].astype(p.dtype), p[:, :-1]], axis=1)
    pm = p + (p_shift - p) * mu
    r, k, v, z, lw, la = jnp.split(pm, list(_A_SPLITS), axis=-1)
    heads = lambda t: t.astype(F32).reshape(B, T, H, N)
    w_log = -jax.nn.softplus(-(w0 + jnp.tanh(lw) @ w2).astype(F32)) - 0.5
    decay = heads(jnp.exp(-jnp.exp(w_log)))
    a = heads(jax.nn.sigmoid((a0 + la @ a2).astype(F32)))
    r, k, v = heads(r), heads(k), heads(v)
    kk = k * k_k.astype(F32).reshape(H, N)
    kk = kk * lax.rsqrt(jnp.maximum(jnp.sum(kk * kk, axis=-1, keepdims=True), 1e-24))
    k = k * (1.0 + (a - 1.0) * k_a.astype(F32).reshape(H, N))

    def step(S, inp):
        r_t, w_t, k_t, v_t, kk_t, a_t = inp
        sa = jnp.einsum('bhij,bhj->bhi', S, -kk_t)
        S = (S * w_t[:, :, None, :] + sa[..., None] * (kk_t * a_t)[:, :, None, :]
             + v_t[..., None] * k_t[:, :, None, :])
        return S, jnp.einsum('bhij,bhj->bhi', S, r_t)

    seq = tuple(jnp.swapaxes(t, 0, 1) for t in (r, decay, k, v, kk, a))
    S_final, y = lax.scan(step, S0.astype(F32), seq)
    y = jnp.swapaxes(y, 0, 1)
    mean = jnp.mean(y, axis=-1, keepdims=True)
    var = jnp.mean(jnp.square(y - mean), axis=-1, keepdims=True)
    y = (y - mean) * lax.rsqrt(var + A_GN_EPS)
    y = y * lnx_g.astype(F32).reshape(H, N) + lnx_b.astype(F32).reshape(H, N)
    y = y + jnp.sum(r * k * r_k.astype(F32), axis=-1, keepdims=True) * v
    y = y.reshape(B, T, A_WIDTH).astype(x.dtype)
    out = (y * jax.nn.silu(z)) @ w_out
    return out, S_final, p[:, -1]


def _pool_mixer(x, buf, front_valid, w_in, w_grp, scale, w_out):
    B, T, _ = x.shape
    u, z = jnp.split(x @ w_in, 2, axis=-1)
    ext = jnp.concatenate([buf.astype(u.dtype), u], axis=1)
    valid = jnp.concatenate([jnp.full((B_BUF,), front_valid, dtype=bool),
                             jnp.ones((T,), dtype=bool)]).astype(F32)
    csum = jnp.concatenate([jnp.zeros((B, 1, B_WIDTH), F32),
                            jnp.cumsum(ext.astype(F32) * valid[None, :, None], axis=1)], axis=1)
    ccnt = jnp.concatenate([jnp.zeros((1,), F32), jnp.cumsum(valid)])
    end = B_BUF + 1 + jnp.arange(T)
    pooled = []
    for gi, w in enumerate(B_WINDOWS):
        cs = csum[:, :, gi * B_GROUP_W:(gi + 1) * B_GROUP_W]
        cnt = ccnt[end] - ccnt[end - w]
        pooled.append((cs[:, end] - cs[:, end - w]) / cnt[None, :, None])
    d = jnp.stack(pooled, axis=2) - u.astype(F32).reshape(B, T, B_GROUPS, B_GROUP_W)
    y = jnp.einsum('btgi,gij->btgj', d, w_grp).reshape(B, T, B_WIDTH) * scale
    out = (y.astype(x.dtype) * jax.nn.silu(z)) @ w_out
    return out, ext[:, -B_BUF:]


def _nsa_attend(q, gates, kc, vc, ks, vs, kw_ext, vw_ext, q_off, n_front, cmp_wk, cmp_wv):
    B, T = q.shape[0], q.shape[1]
    Tk = kc.shape[1]
    G, HPG, DH, SB = C_KV_HEADS, C_HPG, C_HEAD_DIM, C_SEL_BLOCK
    scale = DH ** -0.5
    slopes = _alibi_slopes(C_HEADS).reshape(G, HPG)[None, :, :, None, None]
    n_cmp = Tk // C_CMP_BLOCK
    blk = lambda t: t[:, :n_cmp * C_CMP_BLOCK].reshape(B, n_cmp, C_CMP_BLOCK, G, DH)
    k_cmp = jnp.einsum('bnlgd,l->bngd', blk(kc), cmp_wk)
    v_cmp = jnp.einsum('bnlgd,l->bngd', blk(vc), cmp_wv)
    cmp_end = (jnp.arange(n_cmp) + 1) * C_CMP_BLOCK - 1
    n_sel = -(-Tk // SB)
    pad = n_sel * SB - Tk
    selb = lambda t: jnp.pad(t, ((0, 0), (0, pad), (0, 0), (0, 0))).reshape(
        B, n_sel, SB, G, DH).transpose(0, 3, 1, 2, 4)
    k_selb, v_selb = selb(ks), selb(vs)
    ratio = SB // C_CMP_BLOCK
    top_n = min(C_TOP_N, n_sel)
    qb_size = _query_block(B, T)
    nqb = T // qb_size
    q_blocks = q.reshape(B, nqb, qb_size, G, HPG, DH).transpose(1, 0, 3, 4, 2, 5)
    g_blocks = gates.reshape(B, nqb, qb_size, G, HPG, 3).transpose(1, 0, 3, 4, 2, 5)
    starts = jnp.arange(nqb) * qb_size
    bi = jnp.arange(B)[:, None, None, None]
    gi = jnp.arange(G)[None, :, None, None]
    n_win = n_front + qb_size
    blk_id = jnp.arange(n_sel)

    def one_block(inp):
        qb, gb, i0 = inp
        qpos = q_off + i0 + jnp.arange(qb_size)
        dist_c = qpos[:, None] - cmp_end[None, :]
        s_c = jnp.einsum('bgjqd,bngd->bgjqn', qb, k_cmp).astype(F32) * scale - slopes * dist_c
        p_c = _masked_softmax(s_c, dist_c >= 0)
        o_c = jnp.einsum('bgjqn,bngd->bgjqd', p_c, v_cmp)
        imp = jnp.pad(p_c.sum(axis=2), ((0, 0), (0, 0), (0, 0), (0, n_sel * ratio - n_cmp)))
        imp = imp.reshape(B, G, qb_size, n_sel, ratio).sum(-1)
        cur = qpos // SB
        imp = jnp.where(blk_id[None, :] == cur[:, None], C_FORCE, imp)
        imp = jnp.where(blk_id[None, :] <= cur[:, None], imp, -1.0)
        top_v, top_i = lax.top_k(imp, top_n)
        k_g = k_selb[bi, gi, top_i]
        v_g = v_selb[bi, gi, top_i]
        kpos = top_i[..., None] * SB + jnp.arange(SB)
        dist_s = (qpos[:, None, None] - kpos)[:, :, None]
        mask_s = (dist_s >= 0) & (top_v >= 0)[:, :, None, :, :, None]
        s_s = jnp.einsum('bgjqd,bgqnld->bgjqnl', qb, k_g).astype(F32) * scale - slopes[..., None] * dist_s
        p_s = _masked_softmax(s_s, mask_s, axis=(-2, -1))
        o_s = jnp.einsum('bgjqnl,bgqnld->bgjqd', p_s, v_g)
        k_w = lax.dynamic_slice_in_dim(kw_ext, i0, n_win, axis=1)
        v_w = lax.dynamic_slice_in_dim(vw_ext, i0, n_win, axis=1)
        kwpos = q_off - n_front + i0 + jnp.arange(n_win)
        dist_w = qpos[:, None] - kwpos[None, :]
        mask_w = (kwpos[None, :] >= 0) & (dist_w >= 0) & (dist_w < C_WINDOW)
        s_w = jnp.einsum('bgjqd,bkgd->bgjqk', qb, k_w).astype(F32) * scale - slopes * dist_w
        p_w = _masked_softmax(s_w, mask_w)
        o_w = jnp.einsum('bgjqk,bkgd->bgjqd', p_w, v_w)
        gb = gb.astype(F32)
        return gb[..., 0:1] * o_c + gb[..., 1:2] * o_s + gb[..., 2:3] * o_w

    o = lax.map(one_block, (q_blocks, g_blocks, starts))
    return o.transpose(1, 0, 4, 2, 3, 5).reshape(B, T, C_WIDTH)


def _nsa_mixer(x, past, win_k_buf, win_v_buf, q_off, win_keep, w_in, cmp_wk, cmp_wv, w_out):
    B, T, _ = x.shape
    widths = [C_WIDTH] + [C_KV_WIDTH] * 6 + [3 * C_HEADS, C_WIDTH]
    splits = [int(v) for v in np.cumsum(widths)[:-1]]
    q, kc, vc, ks, vs, kw, vw, g, z = jnp.split(x @ w_in, splits, axis=-1)
    kvh = lambda t: t.reshape(B, T, C_KV_HEADS, C_HEAD_DIM)
    kc, vc, ks, vs, kw, vw = kvh(kc), kvh(vc), kvh(ks), kvh(vs), kvh(kw), kvh(vw)
    new_rows = (kc, vc, ks, vs)
    if past is None:
        full = list(new_rows)
    else:
        full = [jnp.concatenate([pst.astype(r.dtype), r], axis=1) for pst, r in zip(past, new_rows)]
    kw_ext = jnp.concatenate([win_k_buf.astype(kw.dtype), kw], axis=1)
    vw_ext = jnp.concatenate([win_v_buf.astype(vw.dtype), vw], axis=1)
    gates = jax.nn.sigmoid(g.astype(F32)).reshape(B, T, C_HEADS, 3)
    o = _nsa_attend(q.reshape(B, T, C_HEADS, C_HEAD_DIM), gates, full[0], full[1], full[2], full[3],
                    kw_ext, vw_ext, q_off, win_k_buf.shape[1], cmp_wk, cmp_wv)
    y = (o.astype(x.dtype) * jax.nn.silu(z)) @ w_out
    return y, new_rows, kw_ext[:, -win_keep:], vw_ext[:, -win_keep:]


def setup_inputs(seed: int = 0) -> dict:
    key = jax.random.key(seed)
    keys = iter(jax.random.split(key, 64))
    nrm = lambda shape, s=1.0: s * jax.random.normal(next(keys), shape, F32)
    uni = lambda shape, lo, hi: jax.random.uniform(next(keys), shape, F32, lo, hi)
    n_pages = PAST_LEN // PAGE_SIZE
    n_pool = (5 * DEC_BATCH * n_pages) // 4
    wb = min(C_WINDOW, PAST_LEN)
    page_table = jax.random.permutation(next(keys), n_pool)[:DEC_BATCH * n_pages].reshape(
        DEC_BATCH, n_pages).astype(jnp.int32)
    kv_pool = (N_LAYERS_C, n_pool, PAGE_SIZE, C_KV_HEADS, C_HEAD_DIM)
    kv_win = (N_LAYERS_C, DEC_BATCH, wb, C_KV_HEADS, C_HEAD_DIM)
    return {
        'x_prompt': nrm((BATCH, SEQ, D_MODEL)),
        'x_sample': nrm((DEC_BATCH, DEC_SEQ, D_MODEL)),
        'state_rwkv_S': nrm((N_LAYERS_A, DEC_BATCH, A_HEADS, A_HEAD_DIM, A_HEAD_DIM), 0.3),
        'state_rwkv_shift': nrm((N_LAYERS_A, DEC_BATCH, A_NCOLS)),
        'state_pool': nrm((N_LAYERS_B, DEC_BATCH, B_BUF, B_WIDTH)),
        'cache_cmp_k': nrm(kv_pool),
        'cache_cmp_v': nrm(kv_pool),
        'cache_sel_k': nrm(kv_pool),
        'cache_sel_v': nrm(kv_pool),
        'state_win_k': nrm(kv_win),
        'state_win_v': nrm(kv_win),
        'page_table': page_table,
        'ln_g': 1.0 + nrm((DEPTH, D_MODEL), 0.05),
        'ln_b': nrm((DEPTH, D_MODEL), 0.02),
        'a_w_in': nrm((N_LAYERS_A, D_MODEL, A_NCOLS), D_MODEL ** -0.5),
        'a_mu': uni((N_LAYERS_A, A_NCOLS), 0.0, 1.0),
        'a_w0': uni((N_LAYERS_A, A_WIDTH), -4.0, 0.0),
        'a_w2': nrm((N_LAYERS_A, A_LORA_W, A_WIDTH), 0.5 * A_LORA_W ** -0.5),
        'a_a0': nrm((N_LAYERS_A, A_WIDTH), 0.5),
        'a_a2': nrm((N_LAYERS_A, A_LORA_A, A_WIDTH), 0.5 * A_LORA_A ** -0.5),
        'a_k_k': 0.85 + nrm((N_LAYERS_A, A_WIDTH), 0.05),
        'a_k_a': 1.0 + nrm((N_LAYERS_A, A_WIDTH), 0.05),
        'a_r_k': nrm((N_LAYERS_A, A_HEADS, A_HEAD_DIM), 0.1),
        'a_lnx_g': 1.0 + nrm((N_LAYERS_A, A_WIDTH), 0.05),
        'a_lnx_b': nrm((N_LAYERS_A, A_WIDTH), 0.02),
        'a_w_out': nrm((N_LAYERS_A, A_WIDTH, D_MODEL), DEEPNORM_BETA * A_WIDTH ** -0.5),
        'b_w_in': nrm((N_LAYERS_B, D_MODEL, 2 * B_WIDTH), D_MODEL ** -0.5),
        'b_w_grp': nrm((N_LAYERS_B, B_GROUPS, B_GROUP_W, B_GROUP_W), B_GROUP_W ** -0.5),
        'b_scale': 1.0 + nrm((N_LAYERS_B, B_WIDTH), 0.1),
        'b_w_out': nrm((N_LAYERS_B, B_WIDTH, D_MODEL), DEEPNORM_BETA * B_WIDTH ** -0.5),
        'c_w_in': nrm((N_LAYERS_C, D_MODEL, C_NCOLS), D_MODEL ** -0.5),
        'c_cmp_wk': (1.0 + nrm((N_LAYERS_C, C_CMP_BLOCK), 0.1)) / C_CMP_BLOCK,
        'c_cmp_wv': (1.0 + nrm((N_LAYERS_C, C_CMP_BLOCK), 0.1)) / C_CMP_BLOCK,
        'c_w_out': nrm((N_LAYERS_C, C_WIDTH, D_MODEL), DEEPNORM_BETA * C_WIDTH ** -0.5),
    }


def reference(x_prompt, x_sample, state_rwkv_S, state_rwkv_shift, state_pool,
              cache_cmp_k, cache_cmp_v, cache_sel_k, cache_sel_v, state_win_k, state_win_v, page_table,
              ln_g, ln_b, a_w_in, a_mu, a_w0, a_w2, a_a0, a_a2, a_k_k, a_k_a, a_r_k, a_lnx_g, a_lnx_b,
              a_w_out, b_w_in, b_w_grp, b_scale, b_w_out, c_w_in, c_cmp_wk, c_cmp_wv, c_w_out):
    Bp, Tp = x_prompt.shape[0], x_prompt.shape[1]
    xp, xs = x_prompt, x_sample
    S_p, S_s, sh_p, sh_s, pl_p, pl_s = [], [], [], [], [], []
    rows_p, rows_s, wk_p, wk_s, wv_p, wv_s = [], [], [], [], [], []
    for layer in range(DEPTH):
        kind, li = layer % N_MIXERS, layer // N_MIXERS
        if kind == 0:
            prm = (a_w_in[li], a_mu[li], a_w0[li], a_w2[li], a_a0[li], a_a2[li], a_k_k[li], a_k_a[li],
                   a_r_k[li], a_lnx_g[li], a_lnx_b[li], a_w_out[li])
            yp, s_new, sh_new = _rwkv_mixer(xp, jnp.zeros((Bp, A_HEADS, A_HEAD_DIM, A_HEAD_DIM), F32),
                                            jnp.zeros((Bp, A_NCOLS), xp.dtype), *prm)
            S_p.append(s_new)
            sh_p.append(sh_new)
            ys, s_new, sh_new = _rwkv_mixer(xs, state_rwkv_S[li], state_rwkv_shift[li], *prm)
            S_s.append(s_new)
            sh_s.append(sh_new)
        elif kind == 1:
            prm = (b_w_in[li], b_w_grp[li], b_scale[li], b_w_out[li])
            yp, buf_new = _pool_mixer(xp, jnp.zeros((Bp, B_BUF, B_WIDTH), xp.dtype), False, *prm)
            pl_p.append(buf_new)
            ys, buf_new = _pool_mixer(xs, state_pool[li], True, *prm)
            pl_s.append(buf_new)
        else:
            prm = (c_w_in[li], c_cmp_wk[li], c_cmp_wv[li], c_w_out[li])
            zero_win = jnp.zeros((Bp, C_WINDOW, C_KV_HEADS, C_HEAD_DIM), xp.dtype)
            yp, r_new, wk_new, wv_new = _nsa_mixer(xp, None, zero_win, zero_win, 0, min(C_WINDOW, Tp), *prm)
            rows_p.append(r_new)
            wk_p.append(wk_new)
            wv_p.append(wv_new)
            past = [_gather_pages(c[li], page_table) for c in (cache_cmp_k, cache_cmp_v, cache_sel_k, cache_sel_v)]
            ys, r_new, wk_new, wv_new = _nsa_mixer(xs, past, state_win_k[li], state_win_v[li], PAST_LEN,
                                                   state_win_k.shape[2], *prm)
            rows_s.append(r_new)
            wk_s.append(wk_new)
            wv_s.append(wv_new)
        xp = _layer_norm(DEEPNORM_ALPHA * xp + yp, ln_g[layer], ln_b[layer])
        xs = _layer_norm(DEEPNORM_ALPHA * xs + ys, ln_g[layer], ln_b[layer])
    rwkv_S_prompt = jnp.stack(S_p)
    rwkv_S_sample = jnp.stack(S_s)
    rwkv_shift_prompt = jnp.stack(sh_p)
    rwkv_shift_sample = jnp.stack(sh_s)
    pool_buf_prompt = jnp.stack(pl_p)
    pool_buf_sample = jnp.stack(pl_s)
    cmp_k_prompt = jnp.stack([r[0] for r in rows_p])
    cmp_k_sample = jnp.stack([r[0] for r in rows_s])
    cmp_v_prompt = jnp.stack([r[1] for r in rows_p])
    cmp_v_sample = jnp.stack([r[1] for r in rows_s])
    sel_k_prompt = jnp.stack([r[2] for r in rows_p])
    sel_k_sample = jnp.stack([r[2] for r in rows_s])
    sel_v_prompt = jnp.stack([r[3] for r in rows_p])
    sel_v_sample = jnp.stack([r[3] for r in rows_s])
    win_k_prompt = jnp.stack(wk_p)
    win_k_sample = jnp.stack(wk_s)
    win_v_prompt = jnp.stack(wv_p)
    win_v_sample = jnp.stack(wv_s)
    return (xp, xs, rwkv_S_prompt, rwkv_S_sample, rwkv_shift_prompt, rwkv_shift_sample,
            pool_buf_prompt, pool_buf_sample, cmp_k_prompt, cmp_k_sample, cmp_v_prompt, cmp_v_sample,
            sel_k_prompt, sel_k_sample, sel_v_prompt, sel_v_sample,
            win_k_prompt, win_k_sample, win_v_prompt, win_v_sample)
```

```python
import contextlib
import numpy as np
import concourse.bass as bass
import concourse.mybir as mybir
from concourse.bass_utils import run_bass_kernel_spmd

F32 = mybir.dt.float32
BF16 = mybir.dt.bfloat16
I32 = mybir.dt.int32
ALU = mybir.AluOpType
AF = mybir.ActivationFunctionType
AX = mybir.AxisListType

ENGS = ("pe", "dve", "act", "pool", "sp")
NDMA = {"sp": 12, "act": 6, "pool": 6}

D = 1024
TP = 2048
NS = 16
TS = 8
DEPTH = 4
ALPHA = (2.0 * DEPTH) ** 0.25
LN_EPS = 1e-5
A_NC = 4224
GN_EPS = 64e-5
C_NC = 3632


def _key(k):
    if isinstance(k, (str, tuple)):
        return k
    t = getattr(k, "tensor", k)
    return getattr(t, "name", str(t))


class Prog:
    def __init__(self, nc):
        self.nc = nc
        self.q = {e: [] for e in ENGS}
        self.cnt = {e: 0 for e in ENGS}
        self.known = {e: {} for e in ENGS}
        self.lastw = {}
        self.readers = {}
        self.dma_rr = {e: 0 for e in NDMA}
        self.dma_cnt = {}
        self.n_inst = 0

    def _deps(self, reads, writes):
        deps = {}

        def add(ev):
            if ev is None:
                return
            s, v = ev
            if deps.get(s, 0) < v:
                deps[s] = v
        for k in reads:
            add(self.lastw.get(k))
        for k in writes:
            add(self.lastw.get(k))
            for ev in self.readers.get(k, ()):
                add(ev)
        return deps

    def _commit(self, ev, reads, writes):
        for k in reads:
            self.readers.setdefault(k, []).append(ev)
        for k in writes:
            self.lastw[k] = ev
            self.readers[k] = []

    def _waits(self, eng, deps):
        waits = []
        kn = self.known[eng]
        for s, v in deps.items():
            if s == "c_pe" and eng == "pe":
                continue
            if kn.get(s, 0) >= v:
                continue
            kn[s] = v
            waits.append((s, v))
        return waits

    def op(self, eng, fn, reads=(), writes=()):
        reads = [_key(k) for k in reads]
        writes = [_key(k) for k in writes]
        writes = writes + [r for r in reads if isinstance(r, str) and r.startswith("psb")]
        waits = self._waits(eng, self._deps(reads, writes))
        self.cnt[eng] += 1
        ev = ("c_" + eng, self.cnt[eng])
        self.q[eng].append(("op", waits, fn, ev))
        self._commit(ev, reads, writes)
        self.n_inst += 1
        return ev

    def dma(self, eng, out, in_, reads=None, writes=None, fn=None, **kw):
        reads = [_key(k) for k in (reads if reads is not None else [in_])]
        writes = [_key(k) for k in (writes if writes is not None else [out])]
        deps = self._deps(reads, writes)
        i = self.dma_rr[eng]
        self.dma_rr[eng] = (i + 1) % NDMA[eng]
        sname = "d_%s%d" % (eng, i)
        n = self.dma_cnt.get(sname, 0)
        if n > 0 and deps.get(sname, 0) < 16 * n:
            deps[sname] = 16 * n
        waits = self._waits(eng, deps)
        self.dma_cnt[sname] = n + 1
        ev = (sname, 16 * (n + 1))
        self.q[eng].append(("dma", waits, (out, in_, kw, fn), ev))
        self._commit(ev, reads, writes)
        self.n_inst += 1
        return ev

    def barrier(self):
        for eng in ENGS:
            deps = {}
            for f in ENGS:
                if f != "sp" and f != eng and self.cnt[f] > 0:
                    deps["c_" + f] = self.cnt[f]
            for s, n in self.dma_cnt.items():
                deps[s] = 16 * n
            waits = self._waits(eng, deps)
            self.q[eng].append(("wait", waits, None, None))

    def emit(self):
        nc = self.nc
        names = ["c_" + e for e in ENGS if e != "sp"]
        for e, n in NDMA.items():
            names += ["d_%s%d" % (e, i) for i in range(n)]
        with contextlib.ExitStack() as st:
            sems = {nm: st.enter_context(nc.semaphore(nm)) for nm in names}
            block = st.enter_context(nc.Block())

            def run(eng):
                def body(e):
                    for kind, waits, payload, ev in self.q[eng]:
                        for s, v in waits:
                            e.wait_ge(sems[s], v)
                        if kind == "op":
                            payload(e).then_inc(sems[ev[0]], 1)
                        elif kind == "dma":
                            out, in_, kw, fn = payload
                            if fn is not None:
                                fn(e).then_inc(sems[ev[0]], 16)
                            else:
                                e.dma_start(out=out, in_=in_, **kw).then_inc(sems[ev[0]], 16)
                    if eng == "sp":
                        for sname, n in self.dma_cnt.items():
                            e.wait_ge(sems[sname], 16 * n)
                        for en in ENGS:
                            if en != "sp" and self.cnt[en] > 0:
                                e.wait_ge(sems["c_" + en], self.cnt[en])
                return body

            block.sync(run("sp"))
            block.tensor(run("pe"))
            block.vector(run("dve"))
            block.scalar(run("act"))
            block.gpsimd(run("pool"))


def _aps(*xs):
    return [x for x in xs if x is not None and not isinstance(x, (int, float))]


class K:
    def __init__(self, P):
        self.P = P

    def mm(self, out, lhsT, rhs, start=True, stop=True):
        self.P.op("pe", lambda e: e.matmul(out, lhsT=lhsT, rhs=rhs, start=start, stop=stop),
                  reads=[lhsT, rhs], writes=[out])

    def tr(self, out, in_, ident):
        self.P.op("pe", lambda e: e.transpose(out, in_, ident), reads=[in_, ident], writes=[out])

    def tt(self, out, a, b, op, eng="dve"):
        self.P.op(eng, lambda e: e.tensor_tensor(out=out, in0=a, in1=b, op=op), reads=[a, b], writes=[out])

    def ts(self, out, a, s1, op0, s2=None, op1=None, eng="dve"):
        if op1 is None:
            fn = lambda e: e.tensor_scalar(out=out, in0=a, scalar1=s1, scalar2=None, op0=op0)
        else:
            fn = lambda e: e.tensor_scalar(out=out, in0=a, scalar1=s1, scalar2=s2, op0=op0, op1=op1)
        self.P.op(eng, fn, reads=_aps(a, s1, s2), writes=[out])

    def stt(self, out, a, s, b, op0, op1, eng="dve"):
        self.P.op(eng, lambda e: e.scalar_tensor_tensor(out=out, in0=a, scalar=s, in1=b, op0=op0, op1=op1),
                  reads=_aps(a, s, b), writes=[out])

    def red(self, out, in_, op=ALU.add, negate=False, axis=AX.X):
        self.P.op("dve", lambda e: e.tensor_reduce(out=out, in_=in_, axis=axis, op=op, negate=negate),
                  reads=[in_], writes=[out])

    def cp(self, out, in_, eng="dve"):
        if eng == "act":
            self.P.op("act", lambda e: e.copy(out, in_), reads=[in_], writes=[out])
        else:
            self.P.op(eng, lambda e: e.tensor_copy(out, in_), reads=[in_], writes=[out])

    def act(self, out, in_, func, bias=None, scale=None, accum=None):
        kw = {}
        if bias is not None:
            kw["bias"] = bias
        if scale is not None:
            kw["scale"] = scale
        if accum is not None:
            kw["accum_out"] = accum
        self.P.op("act", lambda e: e.activation(out=out, in_=in_, func=func, **kw),
                  reads=_aps(in_, bias, scale), writes=_aps(out, accum))

    def recip(self, out, in_):
        self.P.op("dve", lambda e: e.reciprocal(out, in_), reads=[in_], writes=[out])

    def memset(self, ap, v, eng="pool"):
        self.P.op(eng, lambda e: e.memset(ap, v), writes=[ap])

    def dma(self, out, in_, eng="sp", **kw):
        self.P.dma(eng, out, in_, **kw)


def bc(ap, shape):
    return ap.to_broadcast(shape)


class Ctx:
    pass


def build(npool=2560, tp=TP, layers=(0, 1, 2, 3), dbg=False):
    nc = bass.Bass("TRN2", target_bir_lowering=False)
    C = Ctx()
    C.nc = nc
    C.tp = tp
    P = Prog(nc)
    k = K(P)
    C.P, C.k = P, k

    def din(name, shape, dt=F32):
        return nc.dram_tensor(name, list(shape), dt, kind="ExternalInput").ap()

    def dout(name, shape):
        return nc.dram_tensor(name, list(shape), F32, kind="ExternalOutput").ap()

    def dscr(name, shape, dt=F32):
        return nc.dram_tensor(name, list(shape), dt, kind="Internal").ap()

    I = {}
    for nm, shp in [("xp", (tp, D)), ("xs", (128, D)), ("st_S", (2, NS, 16, 64, 64)), ("st_shift", (2, NS, A_NC)),
                    ("st_pool", (NS, 15, D)), ("cmp_k", (npool * 128, 256)), ("cmp_v", (npool * 128, 256)),
                    ("sel_k", (npool * 128, 256)), ("sel_v", (npool * 128, 256)), ("win_k", (NS, 512, 256)),
                    ("win_v", (NS, 512, 256)), ("ln_g", (4, D)), ("ln_b", (4, D)), ("a_w_in", (2, D, A_NC)),
                    ("a_mu", (2, A_NC)), ("a_w0", (2, D)), ("a_w2", (2, 64, D)), ("a_a0", (2, D)), ("a_a2", (2, 64, D)),
                    ("a_k_k", (2, D)), ("a_k_a", (2, D)), ("a_r_k", (2, D)), ("a_lnx_g", (2, D)), ("a_lnx_b", (2, D)),
                    ("a_w_out", (2, D, D)), ("b_w_in", (D, 2 * D)), ("b_w_grp", (4, 256, 256)), ("b_scale", (1, D)),
                    ("b_w_out", (D, D)), ("c_w_in", (D, C_NC)), ("c_cmp_wk", (1, 32)), ("c_cmp_wv", (1, 32)),
                    ("c_w_out", (D, D)), ("identf", (128, 128)), ("cmask", (128, 2048))]:
        I[nm] = din(nm, shp)
    I["ptab"] = din("ptab", (NS, 16), I32)
    for nm, shp in [("n_bc_p", (16, 4, 64, 512)), ("n_bs_p", (16, 4, 128, 512)), ("n_bw_p", (5, 4, 128, 512)),
                    ("n_cb_p", (16, 128, 32)), ("n_ft_p", (16, 128, 32)), ("n_pair_p", (64, 32)), ("n_eexp_p", (32, 2048)),
                    ("n_wbm", (128, 124)), ("n_iota", (128, 1)), ("n_bc_s", (4, 64, 32)), ("n_bs_s", (17, 4, 128, 32)),
                    ("n_bw_s", (5, 4, 128, 32)), ("n_cb_s", (8, 33)), ("n_ft_s", (8, 33)), ("n_pair_s", (64, 33)),
                    ("n_eexp_s", (33, 17 * 128))]:
        I[nm] = din(nm, shp)
    I["selb"] = din("selb", (128, 64 * 128), BF16)
    O = {}
    for nm, shp in [("y_p", (tp, D)), ("y_s", (128, D)), ("S_p", (2, 16, 64, 64)), ("S_s", (2, NS, 16, 64, 64)),
                    ("sh_p", (2, A_NC)), ("sh_s", (2, NS, A_NC)), ("pl_p", (15, D)), ("pl_s", (NS, 15, D)),
                    ("cmpk_p", (tp, 256)), ("cmpk_s", (128, 256)), ("cmpv_p", (tp, 256)), ("cmpv_s", (128, 256)),
                    ("selk_p", (tp, 256)), ("selk_s", (128, 256)), ("selv_p", (tp, 256)), ("selv_s", (128, 256)),
                    ("wink_p", (512, 256)), ("wink_s", (NS, 512, 256)), ("winv_p", (512, 256)), ("winv_s", (NS, 512, 256))]:
        O[nm] = dout(nm, shp)
    if dbg:
        O["dbg_p"] = dout("dbg_p", (tp, D))
        O["dbg_s"] = dout("dbg_s", (128, D))
    C.I, C.O = I, O
    xa_p, xa_s = dscr("xa_p", (tp, D)), dscr("xa_s", (128, D))
    xb_p, xb_s = dscr("xb_p", (tp, D)), dscr("xb_s", (128, D))
    C.wbf = dscr("wbf", (128, 8, A_NC), BF16)
    C.kvs_scr = dscr("kvs_scr", (128, 1536))

    with contextlib.ExitStack() as gst:
        C.identf = gst.enter_context(nc.sbuf_tensor("identf_sb", [128, 128], F32))
        C.identb = gst.enter_context(nc.sbuf_tensor("identb_sb", [128, 128], BF16))
        C.ps = [gst.enter_context(nc.psum_tensor("psb%d" % i, [128, 512], F32)) for i in range(8)]
        k.dma(C.identf[:], I["identf"])
        k.cp(C.identb[:], C.identf[:])
        chain = [(I["xp"], I["xs"]), (xa_p, xa_s), (xb_p, xb_s), (xa_p, xa_s), (O["y_p"], O["y_s"])]
        for L in range(DEPTH):
            if L not in layers:
                continue
            src, dst = chain[L], chain[L + 1]
            if L == max(layers) and dbg:
                dst = (O["dbg_p"], O["dbg_s"])
            P.barrier()
            with contextlib.ExitStack() as lst:
                if L % 3 == 0:
                    rwkv_layer(C, lst, L // 3, L, src, dst)
                elif L % 3 == 1:
                    pool_layer(C, lst, L, src, dst)
                else:
                    nsa_layer(C, lst, L, src, dst)
                P.barrier()
        P.emit()
    return nc


def ln_tail(C, R, npart, L, dst_rows, T1, crow):
    k, I = C.k, C.I
    st = C.lnst
    k.red(st[0:npart, 0:1], R[0:npart, :])
    k.ts(st[0:npart, 1:2], st[0:npart, 0:1], 1.0 / D, ALU.mult)
    k.ts(R[0:npart, :], R[0:npart, :], st[0:npart, 1:2], ALU.subtract)
    k.tt(T1[0:npart, :], R[0:npart, :], R[0:npart, :], ALU.mult)
    k.red(st[0:npart, 2:3], T1[0:npart, :])
    k.act(st[0:npart, 3:4], st[0:npart, 2:3], AF.Sqrt, bias=C.epsln[0:npart, :], scale=1.0 / D)
    k.recip(st[0:npart, 4:5], st[0:npart, 3:4])
    k.ts(R[0:npart, :], R[0:npart, :], st[0:npart, 4:5], ALU.mult)
    k.dma(crow[0][0:npart, :], I["ln_g"][L:L + 1, :].partition_broadcast(npart), eng="act")
    k.tt(R[0:npart, :], R[0:npart, :], crow[0][0:npart, :], ALU.mult)
    k.dma(crow[1][0:npart, :], I["ln_b"][L:L + 1, :].partition_broadcast(npart), eng="act")
    k.tt(R[0:npart, :], R[0:npart, :], crow[1][0:npart, :], ALU.add)
    k.dma(dst_rows, R[0:npart, :])


def rwkv_layer(C, lst, li, L, src, dst):
    nc, P, k, I, O = C.nc, C.P, C.k, C.I, C.O
    tp = C.tp
    sb = lambda n, s, d=F32: lst.enter_context(nc.sbuf_tensor("a%d_" % L + n, list(s), d))
    ps = C.ps
    Wo = sb("Wo", [128, 8, 1024], BF16)
    Wll = sb("Wll", [128, 8, 128], BF16)
    WG = [sb("WG0", [128, 8, 1024], BF16)]
    W2A2 = sb("W2A2", [128, 1024])
    mucol = sb("mucol", [128, 9])
    SEL = sb("SEL", [128, 64, 128], BF16)
    xin2 = sb("xin2", [128, 1024])
    xin = sb("xin", [64, 1024])
    xTd = sb("xTd", [128, 8, 128], BF16)
    xTsd = sb("xTsd", [128, 8, 128], BF16)
    Pt = sb("Pt", [128, 1024])
    PSt = sb("PSt", [128, 1024])
    PM = {g: sb("PM" + g, [128, 1024]) for g in "rkvz"}
    crow = [sb("crow%d" % i, [128, 1024]) for i in range(2)]
    At = sb("At", [128, 1024])
    KP = sb("KP", [128, 1024])
    T1 = sb("T1", [128, 1024])
    T2 = sb("T2", [128, 1024])
    XRf = sb("XRf", [128, 512])
    XRr = sb("XRr", [128, 512])
    XR = {x: [sb("XR%s%d" % (x, j), [128, 512], BF16) for j in range(2)] for x in ("kk", "w", "ka", "k", "r")}
    va = sb("va", [128, 8, 64])
    vs = sb("vs", [128, 8, 64])
    vT = sb("vT", [128, 8, 64])
    lla = sb("lla", [128, 128])
    llb = sb("llb", [128, 128])
    LLt = sb("LLt", [128, 128])
    YT = sb("YT", [128, 8, 64])
    S = sb("S", [128, 8, 64])
    t1 = sb("t1", [128, 8, 64])
    t2 = sb("t2", [128, 8, 64])
    t3 = sb("t3", [128, 8, 64])
    sa = sb("sa", [128, 8])
    st16 = sb("st16", [128, 5, 16])
    bon = sb("bon", [128, 16])
    G = sb("G", [64, 1024], BF16)
    gT = sb("gT", [128, 8, 64], BF16)
    C.lnst = sb("lnst", [128, 8])
    C.epsln = sb("epsln", [128, 1])
    epsgn = sb("epsgn", [128, 1])
    eps24 = sb("eps24", [128, 1])
    k.memset(C.epsln[:], LN_EPS)
    k.memset(epsgn[:], GN_EPS)
    k.memset(eps24[:], 0.0)

    w_in = I["a_w_in"][li].rearrange("(c p) n -> p c n", p=128)
    for j in range(A_NC // 128):
        stg = Pt[:].rearrange("p (c n) -> p c n", c=8) if j % 2 == 0 else PSt[:].rearrange("p (c n) -> p c n", c=8)
        stgb = (T1 if j % 2 == 0 else T2)[:].bitcast(BF16)[:, 0:1024].rearrange("p (c n) -> p c n", c=8)
        k.dma(stg, w_in[:, :, j * 128:(j + 1) * 128], eng="sp" if j % 2 == 0 else "act")
        k.cp(stgb, stg, eng="pool" if j % 2 == 0 else "act")
        k.dma(C.wbf[:, :, j * 128:(j + 1) * 128], stgb, eng="sp")
    w_out = I["a_w_out"][li].rearrange("(c p) n -> p c n", p=128)
    for j in range(8):
        stg = Pt[:].rearrange("p (c n) -> p c n", c=8) if j % 2 == 0 else PSt[:].rearrange("p (c n) -> p c n", c=8)
        k.dma(stg, w_out[:, :, j * 128:(j + 1) * 128], eng="sp" if j % 2 == 0 else "act")
        k.cp(Wo[:, :, j * 128:(j + 1) * 128], stg, eng="pool" if j % 2 == 0 else "act")
    k.dma(Wll[:], C.wbf[:, :, 4096:4224])
    k.dma(W2A2[0:64, :], I["a_w2"][li])
    k.dma(W2A2[64:128, :], I["a_a2"][li])
    k.dma(mucol[:, 0:8], I["a_mu"][li, 2048:3072].rearrange("(c p) -> p c", p=128), allow_slow_non_contiguous=True)
    k.dma(mucol[:, 8:9], I["a_mu"][li, 4096:4224].rearrange("(c p) -> p c", p=128), allow_slow_non_contiguous=True)
    k.dma(SEL[:], I["selb"].rearrange("p (t m) -> p t m", m=128))
    k.memset(S[:], 0.0)
    EPI = sb("EPI", [64, 1024])
    EPN = sb("EPN", [64, 1024])
    EPX = sb("EPX", [64, 1024])
    FMAR = sb("FMAR", [64, 8, 128])
    FMB = sb("FMB", [64, 8, 64])
    FMK = sb("FMK", [64, 8, 64])
    GB = sb("GB", [64, 8, 128])
    GK = sb("GK", [64, 8, 128])
    PQ = [sb("PQ%d" % i, [64, 8, 64]) for i in range(4)]
    Tm = sb("Tm", [64, 8, 64])
    XT = sb("XT", [64, 8, 64])
    UT = sb("UT", [64, 8, 64])
    ST = sb("ST", [64, 16, 64])
    PCc = sb("PCc", [64, 16])
    MK = sb("MK", [64, 320])
    k.dma(MK[:], I["cmask"][0:64, 512:832])
    MASKAR = MK[:, 0:128]
    MASKNT = MK[:, 128:192]
    TRI = MK[:, 192:256]
    IDN = MK[:, 256:320]
    k.memset(ST[:], 0.0)
    pbi = [0]

    def bank():
        pbi[0] = (pbi[0] + 1) % 8
        return ps[pbi[0]]
    import os
    STOP = int(os.environ.get('STOPAT', '99'))
    if STOP <= 1:
        return

    cri = [0]

    def jrow(src_row, npart=128):
        t = crow[cri[0] % 2]
        cri[0] += 1
        k.dma(t[0:npart, :], src_row.partition_broadcast(npart), eng="act")
        return t

    def h4(t):
        return t[:].rearrange("p (a b j) -> p a b j", a=8, b=2)

    def toxr(X, name):
        X4 = h4(X)
        o3 = XRf[:].rearrange("p (a j) -> p a j", a=8)
        k.cp(o3[0:64], X4[0:64, :, 0, :], eng="act")
        k.cp(o3[64:128], X4[64:128, :, 1, :], eng="act")
        k.cp(XR[name][0][:], XRf[:], eng="pool")
        k.tt(XRr[:], XRf[:], XR[name][0][:], ALU.subtract, eng="pool")
        k.cp(XR[name][1][:], XRr[:], eng="pool")

    ntile_p = tp // 64
    import os
    tiles = [("p", n) for n in range(ntile_p)] + ([("s", 0), ("s", 1)] if not os.environ.get("NOSAMPLE") else [])
    wg_i = [0]
    for kind, n in tiles:
        srcx = src[0] if kind == "p" else src[1]
        dstx = dst[0] if kind == "p" else dst[1]
        r0 = n * 64
        if r0 == 0:
            k.memset(xin2[0:1, :], 0.0)
            k.dma(xin2[1:64, :], srcx[0:63, :])
        else:
            k.dma(xin2[0:64, :], srcx[r0 - 1:r0 + 63, :])
        k.dma(xin[:], srcx[r0:r0 + 64, :], eng="act")
        for b in range(2):
            for c in range(4):
                k.tr(ps[b][:, c * 64:(c + 1) * 64], xin[0:64, (4 * b + c) * 128:(4 * b + c + 1) * 128], C.identf[0:64, 0:64])
            for c in range(4):
                k.tr(ps[b][:, 256 + c * 64:256 + (c + 1) * 64], xin2[0:64, (4 * b + c) * 128:(4 * b + c + 1) * 128], C.identf[0:64, 0:64])
            pv = ps[b][:, 0:256].rearrange("p (c t) -> p c t", c=4)
            pw = ps[b][:, 256:512].rearrange("p (c t) -> p c t", c=4)
            k.cp(xTd[:, 4 * b:4 * b + 4, 0:64], pv, eng="act")
            k.cp(xTd[:, 4 * b:4 * b + 4, 64:128], pv, eng="dve")
            k.cp(xTsd[:, 4 * b:4 * b + 4, 0:64], pw, eng="act")
            k.cp(xTsd[:, 4 * b:4 * b + 4, 64:128], pw, eng="dve")
        if STOP <= 2:
            return
        last_rows = []
        if kind == "p" and n == ntile_p - 1:
            last_rows = [(63, O["sh_p"][li])]
        if kind == "s":
            last_rows = [(sl * 8 + 7, O["sh_s"][li, n * 8 + sl]) for sl in range(8)]
        for gi, g in enumerate("rkvz"):
            wg = WG[0]
            wg_i[0] += 1
            k.dma(wg[:], C.wbf[:, :, gi * 1024:(gi + 1) * 1024], eng="sp")
            for hf in range(2):
                cols = slice(hf * 512, (hf + 1) * 512)
                for c in range(8):
                    k.mm(ps[2][:], xTd[:, c, :], wg[:, c, cols], start=(c == 0), stop=(c == 7))
                for c in range(8):
                    k.mm(ps[3][:], xTsd[:, c, :], wg[:, c, cols], start=(c == 0), stop=(c == 7))
                k.cp(Pt[:, cols], ps[2][:], eng="act")
                k.cp(PSt[:, cols], ps[3][:], eng="act")
            if g == "v":
                for c in range(8):
                    for dc in range(8):
                        k.mm(ps[2][:, c * 64:(c + 1) * 64], wg[:, dc, c * 128:(c + 1) * 128], xTd[:, dc, 0:64],
                             start=(dc == 0), stop=(dc == 7))
                for c in range(8):
                    for dc in range(8):
                        k.mm(ps[3][:, c * 64:(c + 1) * 64], wg[:, dc, c * 128:(c + 1) * 128], xTsd[:, dc, 0:64],
                             start=(dc == 0), stop=(dc == 7))
                k.cp(va[:].rearrange("p c t -> p (c t)"), ps[2][:], eng="act")
                k.cp(vs[:].rearrange("p c t -> p (c t)"), ps[3][:], eng="act")
                if kind == "s":
                    for sl in range(8):
                        k.dma(vs[:, :, sl * 8], I["st_shift"][li, n * 8 + sl, 2048:3072].rearrange("(c p) -> p c", p=128), eng="act", allow_slow_non_contiguous=True)
                k.tt(vs[:], vs[:], va[:], ALU.subtract)
                k.tt(vs[:], vs[:], mucol[:, 0:8].unsqueeze(2).to_broadcast([128, 8, 64]), ALU.mult)
                k.tt(vT[:], vs[:], va[:], ALU.add)
            if kind == "s":
                for sl in range(8):
                    for hh in range(2):
                        k.dma(PSt[hh * 64 + sl * 8:hh * 64 + sl * 8 + 1, :],
                              I["st_shift"][li, n * 8 + sl:n * 8 + sl + 1, gi * 1024:(gi + 1) * 1024], eng="act")
            for (row, dap) in last_rows:
                k.dma(dap[gi * 1024:(gi + 1) * 1024].unsqueeze(0), Pt[row:row + 1, :], eng="act")
            mur = jrow(I["a_mu"][li:li + 1, gi * 1024:(gi + 1) * 1024])
            k.tt(PSt[:], PSt[:], Pt[:], ALU.subtract)
            k.tt(PSt[:], PSt[:], mur[:], ALU.mult)
            k.tt(PM[g][:], PSt[:], Pt[:], ALU.add)
        if STOP <= 3:
            return
        for c in range(8):
            k.mm(ps[2][:, 0:128], Wll[:, c, :], xTd[:, c, :], start=(c == 0), stop=(c == 7))
        for c in range(8):
            k.mm(ps[3][:, 0:128], Wll[:, c, :], xTsd[:, c, :], start=(c == 0), stop=(c == 7))
        k.cp(lla[:], ps[2][:, 0:128], eng="act")
        k.cp(llb[:], ps[3][:, 0:128], eng="act")
        if kind == "s":
            for sl in range(8):
                for hh in range(2):
                    k.dma(llb[:, hh * 64 + sl * 8:hh * 64 + sl * 8 + 1],
                          I["st_shift"][li, n * 8 + sl, 4096:4224].rearrange("(c p) -> p c", p=128), eng="act", allow_slow_non_contiguous=True)
        for (row, dap) in last_rows:
            k.dma(dap[4096:4224].rearrange("(c p) -> p c", p=128), lla[:, row:row + 1], eng="act", allow_slow_non_contiguous=True)
        k.tt(llb[:], llb[:], lla[:], ALU.subtract)
        k.stt(LLt[:], llb[:], mucol[:, 8:9], lla[:], ALU.mult, ALU.add)
        k.act(LLt[0:64, :], LLt[0:64, :], AF.Tanh)
        if STOP <= 4:
            return
        for hf in range(2):
            cols = slice(hf * 512, (hf + 1) * 512)
            k.mm(ps[2 + hf][:], LLt[0:64, :], W2A2[0:64, cols])
            k.mm(ps[4 + hf][:], LLt[64:128, :], W2A2[64:128, cols])
        w0r = jrow(I["a_w0"][li:li + 1, :])
        for hf in range(2):
            cols = slice(hf * 512, (hf + 1) * 512)
            k.tt(T1[:, cols], ps[2 + hf][:], w0r[:, cols], ALU.add)
        k.act(T1[:], T1[:], AF.Sigmoid)
        CC = float(np.exp(-0.5))
        if kind == "p":
            for hf in range(2):
                cols = slice(hf * 512, (hf + 1) * 512)
                k.mm(ps[6 + hf][0:64, :], TRI, T1[0:64, cols])
                k.act(EPI[:, cols], ps[6 + hf][0:64, :], AF.Exp, scale=-CC)
                k.act(EPN[:, cols], ps[6 + hf][0:64, :], AF.Exp, scale=CC)
                k.tt(EPX[:, cols], ps[6 + hf][0:64, :], T1[0:64, cols], ALU.subtract)
            k.act(EPX[:], EPX[:], AF.Exp, scale=-CC)
            for h in range(16):
                k.mm(ps[2][0:64, h:h + 1], EPI[:, h * 64:(h + 1) * 64], C.identf[0:64, 63:64])
            k.cp(PCc[:], ps[2][0:64, 0:16], eng="act")
        else:
            k.act(T1[:], T1[:], AF.Exp, scale=-CC)
            toxr(T1, "w")
        a0r = jrow(I["a_a0"][li:li + 1, :])
        for hf in range(2):
            cols = slice(hf * 512, (hf + 1) * 512)
            k.tt(At[:, cols], ps[4 + hf][:], a0r[:, cols], ALU.add)
        k.act(At[:], At[:], AF.Sigmoid)
        kkr = jrow(I["a_k_k"][li:li + 1, :])
        k.tt(T1[:], PM["k"][:], kkr[:], ALU.mult)
        k.tt(T2[:], T1[:], T1[:], ALU.mult)
        k.red(st16[:, 0, :], T2[:].rearrange("p (h j) -> p h j", h=16))
        k.ts(st16[:, 0, :], st16[:, 0, :], 1e-24, ALU.max)
        k.act(st16[:, 1, :], st16[:, 0, :], AF.Sqrt)
        k.recip(st16[:, 2, :], st16[:, 1, :])
        k.tt(T1[:].rearrange("p (h j) -> p h j", h=16), T1[:].rearrange("p (h j) -> p h j", h=16),
             st16[:, 2, :].unsqueeze(2).to_broadcast([128, 16, 64]), ALU.mult)
        if kind == "s":
            toxr(T1, "kk")
        k.tt(T2[:], T1[:], At[:], ALU.mult)
        if kind == "s":
            toxr(T2, "ka")
        kar = jrow(I["a_k_a"][li:li + 1, :])
        k.stt(PSt[:], At[:], -1.0, kar[:], ALU.add, ALU.mult)
        k.stt(KP[:], PSt[:], 1.0, PM["k"][:], ALU.add, ALU.mult)
        if kind == "s":
            toxr(KP, "k")
            toxr(PM["r"], "r")
        rkr = jrow(I["a_r_k"][li:li + 1, :])
        k.tt(Pt[:], PM["r"][:], rkr[:], ALU.mult)
        k.tt(Pt[:], Pt[:], KP[:], ALU.mult)
        k.red(bon[:], Pt[:].rearrange("p (h j) -> p h j", h=16))
        if kind == "p":
            k.stt(EPX[:], T1[0:64, :], -1.0, EPX[:], ALU.mult, ALU.mult)
            k.tt(T2[0:64, :], T2[0:64, :], EPN[:], ALU.mult)
            k.tt(EPN[:], KP[0:64, :], EPN[:], ALU.mult)
            k.tt(EPI[:], PM["r"][0:64, :], EPI[:], ALU.mult)

        if STOP <= 5:
            return
        def step(Sx, tl):
            bks = {}
            for bi, x in enumerate(("kk", "w", "ka", "k", "r")):
                bk = ps[3 + bi] if bi < 5 else None
                k.mm(bk[:], SEL[:, tl, :], XR[x][0][:], start=True, stop=False)
                k.mm(bk[:], SEL[:, tl, :], XR[x][1][:], start=False, stop=True)
                bks[x] = bk[:].rearrange("p (a j) -> p a j", a=8)
            k.tt(t3[:], bks["k"], vT[:, :, tl:tl + 1].to_broadcast([128, 8, 64]), ALU.mult)
            k.tt(t1[:], Sx, bks["kk"], ALU.mult)
            k.red(sa[:], t1[:], negate=True)
            k.tt(Sx, Sx, bks["w"], ALU.mult)
            k.tt(t2[:], bks["ka"], sa[:].unsqueeze(2).to_broadcast([128, 8, 64]), ALU.mult)
            k.tt(t2[:], t2[:], t3[:], ALU.add)
            k.tt(Sx, Sx, t2[:], ALU.add)
            k.tt(t1[:], Sx, bks["r"], ALU.mult)
            k.red(YT[:, :, tl], t1[:])

        if kind == "p":
            i64 = C.identf[0:64, 0:64]
            for hh in range(2):
                H0 = hh * 8
                bA = [bank(), bank()]
                bB, bK = bank(), bank()
                for hl in range(8):
                    hc = slice((H0 + hl) * 64, (H0 + hl + 1) * 64)
                    o = (hl % 4) * 128
                    k.tr(bA[hl // 4][0:64, o:o + 64], EPX[:, hc], i64)
                    k.tr(bA[hl // 4][0:64, o + 64:o + 128], EPI[:, hc], i64)
                    k.tr(bB[0:64, hl * 64:(hl + 1) * 64], T2[0:64, hc], i64)
                    k.tr(bK[0:64, hl * 64:(hl + 1) * 64], EPN[:, hc], i64)
                k.cp(FMAR[:, 0:4, :].rearrange("p a b -> p (a b)"), bA[0][0:64, :], eng="act")
                k.cp(FMAR[:, 4:8, :].rearrange("p a b -> p (a b)"), bA[1][0:64, :], eng="act")
                k.cp(FMB[:].rearrange("p a b -> p (a b)"), bB[0:64, :], eng="dve")
                k.cp(FMK[:].rearrange("p a b -> p (a b)"), bK[0:64, :], eng="dve")
                for (Gd, FMl) in ((GB, FMB), (GK, FMK)):
                    bb = [bank(), bank()]
                    for hl in range(8):
                        o = (hl % 4) * 128
                        k.mm(bb[hl // 4][0:64, o:o + 128], FMl[:, hl, :], FMAR[:, hl, :])
                    for q in range(2):
                        k.tt(Gd[:, 4 * q:4 * q + 4, :], bb[q][0:64, :].rearrange("p (a b) -> p a b", a=4),
                             MASKAR.unsqueeze(1).to_broadcast([64, 4, 128]), ALU.mult)
                bq = bank()
                for hl in range(8):
                    k.mm(bq[0:64, hl * 64:(hl + 1) * 64], FMAR[:, hl, 0:64], FMB[:, hl, :])
                k.tt(PQ[1][:], bq[0:64, :].rearrange("p (a b) -> p a b", a=8), MASKNT.unsqueeze(1).to_broadcast([64, 8, 64]), ALU.mult)
                k.tt(Tm[:], GB[:, :, 0:64], IDN.unsqueeze(1).to_broadcast([64, 8, 64]), ALU.add)
                Pc, Qc = GB[:, :, 0:64], PQ[1]
                for lv in range(5):
                    Pn, Qn = PQ[2 * ((lv + 1) % 2)], PQ[2 * ((lv + 1) % 2) + 1]
                    if lv < 4:
                        bp = bank()
                        for hl in range(8):
                            k.mm(bp[0:64, hl * 64:(hl + 1) * 64], Qc[:, hl, :], Pc[:, hl, :])
                    bq = bank()
                    for hl in range(8):
                        k.mm(bq[0:64, hl * 64:(hl + 1) * 64], Pc[:, hl, :], Qc[:, hl, :])
                    if lv < 4:
                        k.cp(Pn[:].rearrange("p a b -> p (a b)"), bp[0:64, :], eng="act")
                    k.cp(Qn[:].rearrange("p a b -> p (a b)"), bq[0:64, :], eng="act")
                    bt = bank()
                    for hl in range(8):
                        k.mm(bt[0:64, hl * 64:(hl + 1) * 64], Qn[:, hl, :], Tm[:, hl, :])
                    k.tt(Tm[:], Tm[:], bt[0:64, :].rearrange("p (a b) -> p a b", a=8), ALU.add)
                    Pc, Qc = Pn, Qn
                bx = bank()
                for hl in range(8):
                    hc = slice((H0 + hl) * 64, (H0 + hl + 1) * 64)
                    k.mm(bx[0:64, hl * 64:(hl + 1) * 64], FMAR[:, hl, 0:64], ST[:, H0 + hl, :], start=True, stop=False)
                    k.mm(bx[0:64, hl * 64:(hl + 1) * 64], GK[:, hl, 0:64], PM["v"][0:64, hc], start=False, stop=True)
                k.cp(XT[:].rearrange("p a b -> p (a b)"), bx[0:64, :], eng="act")
                bu = bank()
                for hl in range(8):
                    k.mm(bu[0:64, hl * 64:(hl + 1) * 64], Tm[:, hl, :], XT[:, hl, :])
                k.cp(UT[:].rearrange("p a b -> p (a b)"), bu[0:64, :], eng="act")
                by = bank()
                for hl in range(8):
                    hc = slice((H0 + hl) * 64, (H0 + hl + 1) * 64)
                    k.mm(by[0:64, hl * 64:(hl + 1) * 64], FMAR[:, hl, 64:128], ST[:, H0 + hl, :], start=True, stop=False)
                    k.mm(by[0:64, hl * 64:(hl + 1) * 64], GB[:, hl, 64:128], UT[:, hl, :], start=False, stop=False)
                    k.mm(by[0:64, hl * 64:(hl + 1) * 64], GK[:, hl, 64:128], PM["v"][0:64, hc], start=False, stop=True)
                k.cp(KP[0:64, hh * 512:(hh + 1) * 512], by[0:64, :], eng="act")
                bs = bank()
                for hl in range(8):
                    hc = slice((H0 + hl) * 64, (H0 + hl + 1) * 64)
                    k.mm(bs[0:64, hl * 64:(hl + 1) * 64], T2[0:64, hc], UT[:, hl, :], start=True, stop=False)
                    k.mm(bs[0:64, hl * 64:(hl + 1) * 64], EPN[:, hc], PM["v"][0:64, hc], start=False, stop=True)
                k.tt(ST[:, H0:H0 + 8, :], ST[:, H0:H0 + 8, :], bs[0:64, :].rearrange("p (a b) -> p a b", a=8), ALU.add)
                k.tt(ST[:, H0:H0 + 8, :], ST[:, H0:H0 + 8, :], PCc[:, H0:H0 + 8].unsqueeze(2).to_broadcast([64, 8, 64]), ALU.mult)
            if n == ntile_p - 1:
                for q in range(2):
                    bo = bank()
                    for hl in range(8):
                        k.tr(bo[0:64, hl * 64:(hl + 1) * 64], ST[:, q * 8 + hl, :], i64)
                    k.cp(EPI[:, q * 512:(q + 1) * 512], bo[0:64, :], eng="act")
                k.dma(O["S_p"][li].rearrange("h i j -> i h j"), EPI[:].rearrange("p (h j) -> p h j", h=16))
        else:
            for sl in range(8):
                sq = n * 8 + sl
                k.dma(t3[:], I["st_S"][li, sq].rearrange("(a b) i j -> (b i) a j", b=2))
                Sx = KP[:, 0:512].rearrange("p (a j) -> p a j", a=8)
                k.cp(Sx, t3[:], eng="act")
                for t in range(8):
                    step(Sx, sl * 8 + t)
                k.dma(O["S_s"][li, sq].rearrange("(a b) i j -> (b i) a j", b=2), Sx)

        if STOP <= 6:
            return
        Y = T1
        if kind == "p":
            k.cp(Y[0:64, :], KP[0:64, :], eng="pool")
        else:
            for c in range(8):
                k.tr(ps[c // 4][0:64, (c % 4) * 128:(c % 4 + 1) * 128], YT[:, c, :], C.identf[:, :])
            k.cp(Y[0:64, 0:512], ps[0][0:64, :], eng="act")
            k.cp(Y[0:64, 512:1024], ps[1][0:64, :], eng="act")
        Y3 = Y[0:64, :].rearrange("p (h j) -> p h j", h=16)
        k.red(st16[0:64, 0, :], Y3)
        k.ts(st16[0:64, 0, :], st16[0:64, 0, :], 1.0 / 64, ALU.mult)
        k.tt(Y3, Y3, st16[0:64, 0, :].unsqueeze(2).to_broadcast([64, 16, 64]), ALU.subtract)
        k.tt(T2[0:64, :], Y[0:64, :], Y[0:64, :], ALU.mult)
        k.red(st16[0:64, 1, :], T2[0:64, :].rearrange("p (h j) -> p h j", h=16))
        k.act(st16[0:64, 2, :], st16[0:64, 1, :], AF.Sqrt, bias=epsgn[0:64, :], scale=1.0 / 64)
        k.recip(st16[0:64, 3, :], st16[0:64, 2, :])
        k.tt(Y3, Y3, st16[0:64, 3, :].unsqueeze(2).to_broadcast([64, 16, 64]), ALU.mult)
        gr = jrow(I["a_lnx_g"][li:li + 1, :], 64)
        k.tt(Y[0:64, :], Y[0:64, :], gr[0:64, :], ALU.mult)
        br = jrow(I["a_lnx_b"][li:li + 1, :], 64)
        k.tt(Y[0:64, :], Y[0:64, :], br[0:64, :], ALU.add)
        k.tt(T2[0:64, :].rearrange("p (h j) -> p h j", h=16), PM["v"][0:64, :].rearrange("p (h j) -> p h j", h=16),
             bon[0:64, :].unsqueeze(2).to_broadcast([64, 16, 64]), ALU.mult)
        k.tt(Y[0:64, :], Y[0:64, :], T2[0:64, :], ALU.add)
        k.act(T2[0:64, :], PM["z"][0:64, :], AF.Silu)
        k.tt(G[:], Y[0:64, :], T2[0:64, :], ALU.mult)
        psb = ps[2][:].bitcast(BF16)
        for c in range(8):
            k.tr(psb[:, c * 64:(c + 1) * 64], G[:, c * 128:(c + 1) * 128], C.identb[0:64, 0:64])
        k.cp(gT[:].rearrange("p c t -> p (c t)"), psb[:, 0:512], eng="act")
        for hf in range(2):
            cols = slice(hf * 512, (hf + 1) * 512)
            for c in range(8):
                k.mm(ps[hf][0:64, :], gT[:, c, :], Wo[:, c, cols], start=(c == 0), stop=(c == 7))
            k.stt(Y[0:64, cols], xin[:, cols], ALPHA, ps[hf][0:64, :], ALU.mult, ALU.add)
        ln_tail(C, Y, 64, L, dstx[r0:r0 + 64, :], T2, crow)


def pool_layer(C, lst, L, src, dst):
    nc, P, k, I, O = C.nc, C.P, C.k, C.I, C.O
    tp = C.tp
    sb = lambda n, s, d=F32: lst.enter_context(nc.sbuf_tensor("b_" + n, list(s), d))
    ps = C.ps
    Win = sb("Win", [128, 8, 2048], BF16)
    Wg = sb("Wg", [128, 4, 2, 256], BF16)
    Wo = sb("Wo", [128, 8, 1024], BF16)
    scol = sb("scol", [128, 8])
    stg = [sb("stg%d" % i, [128, 8, 128]) for i in range(2)]
    xin = sb("xin", [128, 1024])
    xT = sb("xT", [128, 8, 128], BF16)
    E = sb("E", [128, 8, 16 * 23])
    A = sb("A", [128, 2, 16 * 23])
    B = sb("B", [128, 2, 16 * 23])
    dT = sb("dT", [128, 8, 128], BF16)
    sz = sb("sz", [128, 8, 128])
    gT = sb("gT", [128, 8, 128], BF16)
    R = sb("R", [128, 1024])
    T1 = sb("T1", [128, 1024])
    cinv = sb("cinv", [128, 512])
    crow = [sb("crow%d" % i, [128, 1024]) for i in range(2)]
    C.lnst = sb("lnst", [128, 8])
    C.epsln = sb("epsln", [128, 1])
    k.memset(C.epsln[:], LN_EPS)
    k.dma(cinv[:], I["cmask"][:, 0:512])
    w_in = I["b_w_in"].rearrange("(c p) n -> p c n", p=128)
    for j in range(16):
        k.dma(stg[j % 2][:], w_in[:, :, j * 128:(j + 1) * 128], eng="sp" if j % 2 == 0 else "act")
        k.cp(Win[:, :, j * 128:(j + 1) * 128], stg[j % 2][:], eng="pool" if j % 2 == 0 else "act")
    w_out = I["b_w_out"].rearrange("(c p) n -> p c n", p=128)
    for j in range(8):
        k.dma(stg[j % 2][:], w_out[:, :, j * 128:(j + 1) * 128], eng="sp" if j % 2 == 0 else "act")
        k.cp(Wo[:, :, j * 128:(j + 1) * 128], stg[j % 2][:], eng="pool" if j % 2 == 0 else "act")
    for g in range(4):
        sv = stg[g % 2][:].rearrange("p c n -> p (c n)")[:, 0:512].rearrange("p (c n) -> p c n", c=2)
        k.dma(sv, I["b_w_grp"][g].rearrange("(c p) n -> p c n", p=128), eng="sp")
        k.cp(Wg[:, g, :, :], sv, eng="pool")
    k.dma(scol[:], I["b_scale"][0].rearrange("(c p) -> p c", p=128), allow_slow_non_contiguous=True)
    k.memset(E[:], 0.0)

    ntile_p = tp // 128
    tiles = [("p", n) for n in range(ntile_p)] + [("s", 0)]
    for kind, n in tiles:
        srcx = src[0] if kind == "p" else src[1]
        dstx = dst[0] if kind == "p" else dst[1]
        r0 = n * 128
        nseg, new = (1, 128) if kind == "p" else (16, 8)
        sl = 15 + new
        Ev = E[:, :, 0:nseg * sl].rearrange("p c (s t) -> p c s t", s=nseg)
        k.dma(xin[:], srcx[r0:r0 + 128, :])
        for b in range(2):
            for c in range(4):
                k.tr(ps[b][:, c * 128:(c + 1) * 128], xin[:, (4 * b + c) * 128:(4 * b + c + 1) * 128], C.identf[:, :])
            k.cp(xT[:, 4 * b:4 * b + 4, :].rearrange("p c t -> p (c t)"), ps[b][:], eng="act")
        if kind == "s":
            for hh in range(2):
                k.dma(R[0:120, :], I["st_pool"][hh * 8:(hh + 1) * 8].rearrange("s r d -> (s r) d"))
                for b in range(2):
                    for c in range(4):
                        k.tr(ps[2 + b][:, c * 120:(c + 1) * 120], R[0:120, (4 * b + c) * 128:(4 * b + c + 1) * 128], C.identf[0:120, 0:120])
                    k.cp(Ev[:, 4 * b:4 * b + 4, hh * 8:(hh + 1) * 8, 0:15],
                         ps[2 + b][:, 0:480].rearrange("p (c s r) -> p c s r", c=4, s=8), eng="act")
        for ob in range(4):
            for o4 in range(4):
                oc = ob * 4 + o4
                for dc in range(8):
                    k.mm(ps[4 + ob % 2][:, o4 * 128:(o4 + 1) * 128], Win[:, dc, oc * 128:(oc + 1) * 128], xT[:, dc, :],
                         start=(dc == 0), stop=(dc == 7))
            pv = ps[4 + ob % 2][:].rearrange("p (c s t) -> p c s t", c=4, s=nseg)
            if ob < 2:
                k.cp(Ev[:, ob * 4:ob * 4 + 4, :, 15:sl], pv, eng="act")
            else:
                k.act(sz[:, (ob - 2) * 4:(ob - 2) * 4 + 4, :].rearrange("p c t -> p (c t)"), ps[4 + ob % 2][:], AF.Silu)
        if kind == "p" and n == ntile_p - 1:
            for b in range(2):
                for c in range(4):
                    k.tr(ps[2 + b][0:15, c * 128:(c + 1) * 128], E[:, 4 * b + c, 128:143], C.identf[:, :])
                k.cp(T1[0:15, b * 512:(b + 1) * 512], ps[2 + b][0:15, :], eng="act")
            k.dma(O["pl_p"], T1[0:15, :])
        if kind == "s":
            for hh in range(2):
                for b in range(2):
                    for c in range(4):
                        Ac = A[:, 0, 0:120].rearrange("p (s r) -> p s r", s=8)
                        k.cp(Ac, Ev[:, 4 * b + c, hh * 8:(hh + 1) * 8, 8:23], eng="pool")
                        k.tr(ps[2 + b][0:120, c * 128:(c + 1) * 128], A[:, 0, 0:120], C.identf[:, :])
                    k.cp(T1[0:120, b * 512:(b + 1) * 512], ps[2 + b][0:120, :], eng="act")
                k.dma(O["pl_s"][hh * 8:(hh + 1) * 8].rearrange("s r d -> (s r) d"), T1[0:120, :])
        for g in range(4):
            cur = Ev[:, 2 * g:2 * g + 2]
            bufs = [A[:, :, 0:nseg * sl].rearrange("p c (s t) -> p c s t", s=nseg),
                    B[:, :, 0:nseg * sl].rearrange("p c (s t) -> p c s t", s=nseg)]
            lo = 0
            for si, sh in enumerate((1, 2, 4, 8)[:g + 1]):
                nxt = bufs[si % 2]
                lo2 = lo + sh
                k.tt(nxt[:, :, :, lo2:sl], cur[:, :, :, lo2:sl], cur[:, :, :, lo:sl - sh], ALU.add)
                cur, lo = nxt, lo2
            w = 2 ** (g + 1)
            pooled = bufs[(g + 1) % 2]
            if kind == "p" and n == 0:
                k.tt(pooled[:, :, 0, 15:sl], cur[:, :, 0, 15:sl],
                     cinv[:, g * 128:(g + 1) * 128].unsqueeze(1).to_broadcast([128, 2, 128]), ALU.mult)
                k.tt(dT[:, 2 * g:2 * g + 2, :], pooled[:, :, 0, 15:sl], Ev[:, 2 * g:2 * g + 2, 0, 15:sl], ALU.subtract)
            else:
                k.stt(dT[:, 2 * g:2 * g + 2, :].rearrange("p c (s t) -> p c s t", s=nseg), cur[:, :, :, 15:sl], 1.0 / w,
                      Ev[:, 2 * g:2 * g + 2, :, 15:sl], ALU.mult, ALU.subtract)
        if kind == "p":
            k.cp(A[:, 0, 0:120].rearrange("p (c t) -> p c t", c=8), E[:, :, 128:143], eng="pool")
            k.cp(E[:, :, 0:15], A[:, 0, 0:120].rearrange("p (c t) -> p c t", c=8), eng="pool")
        for jc in range(8):
            g, jl = jc // 2, jc % 2
            for ic in range(2):
                k.mm(ps[6 + jc // 4][:, (jc % 4) * 128:(jc % 4 + 1) * 128], Wg[:, g, ic, jl * 128:(jl + 1) * 128], dT[:, 2 * g + ic, :],
                     start=(ic == 0), stop=(ic == 1))
        for jc in range(8):
            k.stt(gT[:, jc, :], ps[6 + jc // 4][:, (jc % 4) * 128:(jc % 4 + 1) * 128], scol[:, jc:jc + 1], sz[:, jc, :], ALU.mult, ALU.mult)
        for hf in range(2):
            cols = slice(hf * 512, (hf + 1) * 512)
            for c in range(8):
                k.mm(ps[hf][:], gT[:, c, :], Wo[:, c, cols], start=(c == 0), stop=(c == 7))
            k.stt(R[:, cols], xin[:, cols], ALPHA, ps[hf][:], ALU.mult, ALU.add)
        ln_tail(C, R, 128, L, dstx[r0:r0 + 128, :], T1, crow)


NEG = -30000.0
SCL = 0.125


def nsa_layer(C, lst, L, src, dst):
    nc, P, k, I, O = C.nc, C.P, C.k, C.I, C.O
    tp = C.tp
    sb = lambda n, s, d=F32: lst.enter_context(nc.sbuf_tensor("c_" + n, list(s), d))
    ps = C.ps
    Win = sb("Win", [128, 8, C_NC], BF16)
    Wo = sb("Wo", [128, 8, 1024], BF16)
    stg = [sb("stg%d" % i, [128, 8, 128]) for i in range(2)]
    xin = sb("xin", [128, 1024])
    xT = sb("xT", [128, 8, 128], BF16)
    KV = sb("KV", [128, 1536])
    KsT = sb("KsT", [64, 4, 17 * 128], BF16)
    KwT = sb("KwT", [64, 4, 5 * 128], BF16)
    Vs = sb("Vs", [128, 17, 4, 65], BF16)
    Vw = sb("Vw", [128, 5, 4, 65], BF16)
    KcT = sb("KcT", [64, 4, 64], BF16)
    Vc = sb("Vc", [64, 4, 98])
    Wbk = sb("Wbk", [128, 124])
    Wbv = sb("Wbv", [128, 124])
    wcol = sb("wcol", [128, 2])
    QT = sb("QT", [64, 16, 128], BF16)
    GZ = sb("GZ", [128, 1072])
    gates = sb("gates", [128, 48])
    Bt = [sb("Bt%d" % i, [128, 512]) for i in range(4)]
    SBS = sb("SBS", [128, 17 * 128])
    SBW = sb("SBW", [128, 5 * 128])
    SBC = sb("SBC", [64, 128])
    GBUF = [sb("GBUF%d" % i, [128, 1024]) for i in range(2)]
    tmp = sb("tmp", [128, 512])
    ec = sb("ec", [64, 512])
    eb = sb("eb", [128, 512], BF16)
    OB = sb("OB", [128, 4, 98])
    rd = sb("rd", [128, 8])
    imp = sb("imp", [128, 40])
    imp2 = sb("imp2", [128, 40])
    m8 = sb("m8", [128, 16])
    cbt = sb("cbt", [128, 80])
    selT = sb("selT", [40, 128])
    Eexp = sb("Eexp", [40, 17 * 128])
    Oacc = sb("Oacc", [128, 1024])
    Gb = sb("Gb", [128, 1024], BF16)
    gT = sb("gT", [128, 8, 128], BF16)
    T1 = sb("T1", [128, 1024])
    idx = sb("idx", [128, 256], I32)
    idf = GBUF[0][:, 0:256]
    crow = [x[:].rearrange("p c n -> p (c n)") for x in stg]
    C.lnst = sb("lnst", [128, 8])
    C.epsln = sb("epsln", [128, 1])
    k.memset(C.epsln[:], LN_EPS)
    w_in = I["c_w_in"].rearrange("(c p) n -> p c n", p=128)
    nj = (C_NC + 127) // 128
    for j in range(nj):
        wd = min(128, C_NC - j * 128)
        k.dma(stg[j % 2][:, :, 0:wd], w_in[:, :, j * 128:j * 128 + wd], eng="sp" if j % 2 == 0 else "act")
        k.cp(Win[:, :, j * 128:j * 128 + wd], stg[j % 2][:, :, 0:wd], eng="pool" if j % 2 == 0 else "act")
    w_out = I["c_w_out"].rearrange("(c p) n -> p c n", p=128)
    for j in range(8):
        k.dma(stg[j % 2][:], w_out[:, :, j * 128:(j + 1) * 128], eng="sp" if j % 2 == 0 else "act")
        k.cp(Wo[:, :, j * 128:(j + 1) * 128], stg[j % 2][:], eng="pool" if j % 2 == 0 else "act")
    for r in range(4):
        k.dma(wcol[r * 32:(r + 1) * 32, 0:1], I["c_cmp_wk"].rearrange("o l -> l o"), allow_slow_non_contiguous=True)
        k.dma(wcol[r * 32:(r + 1) * 32, 1:2], I["c_cmp_wv"].rearrange("o l -> l o"), allow_slow_non_contiguous=True)
    k.dma(Wbk[:], I["n_wbm"])
    k.cp(Wbv[:], Wbk[:], eng="pool")
    k.ts(Wbk[:], Wbk[:], wcol[:, 0:1], ALU.mult)
    k.ts(Wbv[:], Wbv[:], wcol[:, 1:2], ALU.mult)
    k.memset(Vs[:], 0.0)
    k.memset(Vw[:], 0.0)
    k.memset(KsT[:], 0.0)
    k.memset(KwT[:], 0.0)
    k.memset(Vs[:, :, :, 64:65], 1.0)
    k.memset(Vw[:, :, :, 64:65], 1.0)

    def kv_tile(kt, rows_cmp, rows_sel, rows_win, nrows, do_cmp_block=None, win_slot=None):
        if rows_sel is not None:
            ksr, vsr = rows_sel
            for g in range(4):
                k.tr(ps[0][0:64, g * 128:g * 128 + nrows], ksr[:, g * 64:(g + 1) * 64], C.identf[0:nrows, 0:nrows])
            k.cp(KsT[:, :, kt * 128:kt * 128 + nrows], ps[0][0:64, :].rearrange("p (g t) -> p g t", g=4)[:, :, 0:nrows], eng="act")
            k.cp(Vs[0:nrows, kt, :, 0:64], vsr.rearrange("p (g d) -> p g d", g=4), eng="dve")
        if rows_win is not None:
            kwr, vwr = rows_win
            ws = win_slot
            for g in range(4):
                k.tr(ps[1][0:64, g * 128:g * 128 + nrows], kwr[:, g * 64:(g + 1) * 64], C.identf[0:nrows, 0:nrows])
            k.cp(KwT[:, :, ws * 128:ws * 128 + nrows], ps[1][0:64, :].rearrange("p (g t) -> p g t", g=4)[:, :, 0:nrows], eng="act")
            k.cp(Vw[0:nrows, ws, :, 0:64], vwr.rearrange("p (g d) -> p g d", g=4), eng="dve")
        if rows_cmp is not None:
            kcr, vcr = rows_cmp
            t = do_cmp_block
            for g in range(4):
                k.mm(ps[2][0:64, g * 4:(g + 1) * 4], kcr[:, g * 64:(g + 1) * 64], Wbk[:, 60:64])
            k.cp(KcT[:, :, 4 * t:4 * t + 4], ps[2][0:64, 0:16].rearrange("p (g n) -> p g n", g=4), eng="act")
            k.mm(ps[2][0:64, 128:384], Wbv[:, 60 - 4 * t:124 - 4 * t], vcr)
            k.tt(Vc[:, :, 0:64], Vc[:, :, 0:64], ps[2][0:64, 128:384].rearrange("p (g d) -> p g d", g=4), ALU.add)

    bti = [0]

    def attend(nq, nblk, s_tiles, w_tiles, bc_ap, bs_fn, bw_fn, cb_ap, ft_ap, pair_ap, eexp_cols, load_consts=True, resident=False):
        nc4 = 4 * nq
        if load_consts:
            k.dma(cbt[0:nq, 0:nblk], cb_ap)
            k.dma(cbt[0:nq, 40:40 + nblk], ft_ap)
            for g in range(4):
                k.dma(Vc[:, g, 65:65 + nblk], pair_ap)

        def bias_tile(ap, nk):
            if resident:
                return ap
            t = Bt[bti[0] % 4]
            bti[0] += 1
            k.dma(t[0:nk, 0:nc4], ap, eng="sp")
            return t[0:nk, 0:nc4]

        for g in range(4):
            Qg = QT[:, 4 * g:4 * g + 4, 0:nq]
            Qg2 = ec
            k.mm(ps[3][0:64, 0:nc4], KcT[:, g, :], QTf[:, g, 0:nc4])
            bt = bias_tile(bc_ap(g), 64)
            k.stt(tmp[0:64, 0:nc4], ps[3][0:64, 0:nc4], SCL, bt, ALU.mult, ALU.add)
            k.act(ec[:, 0:nc4], tmp[0:64, 0:nc4], AF.Exp)
            for j in range(4):
                k.mm(ps[4][0:nq, j * 98:j * 98 + 65 + nblk], ec[:, j * nq:(j + 1) * nq], Vc[:, g, 0:65 + nblk])
            k.cp(OB[0:nq, :, 0:65 + nblk], ps[4][0:nq, 0:392].rearrange("p (j c) -> p j c", j=4)[:, :, 0:65 + nblk], eng="act")
            k.ts(rd[0:nq, 0:4], OB[0:nq, :, 64], 1e-30, ALU.max)
            k.recip(rd[0:nq, 0:4], rd[0:nq, 0:4])
            k.ts(imp[0:nq, 0:nblk], OB[0:nq, 0, 65:65 + nblk], rd[0:nq, 0:1], ALU.mult)
            for j in range(1, 4):
                k.stt(imp[0:nq, 0:nblk], OB[0:nq, j, 65:65 + nblk], rd[0:nq, j:j + 1], imp[0:nq, 0:nblk], ALU.mult, ALU.add)
            k.tt(imp[0:nq, 0:nblk], imp[0:nq, 0:nblk], cbt[0:nq, 0:nblk], ALU.mult)
            k.tt(imp[0:nq, 0:nblk], imp[0:nq, 0:nblk], cbt[0:nq, 40:40 + nblk], ALU.add)
            P.op("dve", lambda e: e.max(out=m8[0:nq, 0:8], in_=imp[0:nq, 0:nblk]), reads=[imp], writes=[m8])
            P.op("dve", lambda e: e.match_replace(out=imp2[0:nq, 0:nblk], in_to_replace=m8[0:nq, 0:8], in_values=imp[0:nq, 0:nblk], imm_value=-2.0),
                 reads=[imp, m8], writes=[imp2])
            P.op("dve", lambda e: e.max(out=m8[0:nq, 8:16], in_=imp2[0:nq, 0:nblk]), reads=[imp2], writes=[m8])
            k.ts(m8[0:nq, 15:16], m8[0:nq, 15:16], 0.0, ALU.max)
            k.ts(imp2[0:nq, 0:nblk], imp[0:nq, 0:nblk], m8[0:nq, 15:16], ALU.is_ge)
            k.tr(ps[5][0:nblk, 0:nq], imp2[0:nq, 0:nblk], C.identf[0:nq, 0:nq])
            k.cp(selT[0:nblk, 0:nq], ps[5][0:nblk, 0:nq], eng="act")
            def accum(first, col, Osrc):
                gsl = gates[0:nq, :].rearrange("p (h c) -> p h c", c=3)[:, 4 * g:4 * g + 4, col]
                k.tt(rd[0:nq, 4:8], rd[0:nq, 0:4], gsl, ALU.mult)
                dstv = Oacc[0:nq, g * 256:(g + 1) * 256].rearrange("p (j d) -> p j d", j=4)
                rb = rd[0:nq, 4:8].unsqueeze(2).to_broadcast([nq, 4, 64])
                if first:
                    k.tt(dstv, Osrc, rb, ALU.mult)
                else:
                    k.tt(OB[0:nq, :, 0:64], Osrc, rb, ALU.mult)
                    k.tt(dstv, dstv, OB[0:nq, :, 0:64], ALU.add)
            accum(True, 0, OB[0:nq, :, 0:64])
            for br, tiles, Kt, Vt, bfn in ((1, s_tiles, KsT, Vs, bs_fn), (2, w_tiles, KwT, Vw, bw_fn)):
                for ti, (slot, nk, bidx) in enumerate(tiles):
                    k.mm(ps[3][0:nk, 0:nc4], Kt[:, g, slot * 128:slot * 128 + nk], QTf[:, g, 0:nc4])
                    bt = bias_tile(bfn(bidx, g), nk)
                    k.stt(tmp[0:nk, 0:nc4], ps[3][0:nk, 0:nc4], SCL, bt, ALU.mult, ALU.add)
                    k.act(eb[0:nk, 0:nc4], tmp[0:nk, 0:nc4], AF.Exp)
                    if br == 1:
                        k.mm(ps[5][0:nk, 128:128 + nq], Eexp[0:nblk, eexp_cols(slot)], selT[0:nblk, 0:nq])
                        k.tt(eb[0:nk, 0:nc4].rearrange("p (j q) -> p j q", j=4), eb[0:nk, 0:nc4].rearrange("p (j q) -> p j q", j=4),
                             ps[5][0:nk, 128:128 + nq].unsqueeze(1).to_broadcast([nk, 4, nq]), ALU.mult)
                    for j in range(4):
                        k.mm(ps[(6, 7, 0, 1)[j]][0:nq, 0:65], eb[0:nk, j * nq:(j + 1) * nq], Vt[0:nk, slot, g, :],
                             start=(ti == 0), stop=(ti == len(tiles) - 1))
                for j in range(4):
                    k.cp(OB[0:nq, j, 0:65], ps[(6, 7, 0, 1)[j]][0:nq, 0:65], eng="act")
                k.ts(rd[0:nq, 0:4], OB[0:nq, :, 64], 1e-30, ALU.max)
                k.recip(rd[0:nq, 0:4], rd[0:nq, 0:4])
                accum(False, br, OB[0:nq, :, 0:64])

    QTflat = QT[:].rearrange("p h q -> p (h q)")

    class _QTf:
        nq = 128

        def __getitem__(self, key):
            _, g, _ = key
            n4 = 4 * self.nq
            return QTflat[:, g * n4:(g + 1) * n4]
    QTf = _QTf()

    def qgz(npart, nq_cols):
        for h in range(16):
            for dc in range(8):
                k.mm(ps[7][0:64, (h % 4) * 128:(h % 4) * 128 + nq_cols], Win[:, dc, h * 64:(h + 1) * 64], xT[:, dc, 0:nq_cols],
                     start=(dc == 0), stop=(dc == 7))
            if h % 4 == 3:
                QTf.nq = nq_cols
                qdst = QTflat[:, (h - 3) * nq_cols:(h + 1) * nq_cols].rearrange("p (j q) -> p j q", j=4)
                k.cp(qdst, ps[7][0:64, :].rearrange("p (j q) -> p j q", j=4)[:, :, 0:nq_cols], eng="act")
        for i, (c0, c1) in enumerate(((2560, 3072), (3072, 3584), (3584, 3632))):
            for dc in range(8):
                k.mm(ps[i][0:npart, 0:c1 - c0], xT[:, dc, 0:npart], Win[:, dc, c0:c1], start=(dc == 0), stop=(dc == 7))
            k.cp(GZ[0:npart, c0 - 2560:c1 - 2560], ps[i][0:npart, 0:c1 - c0], eng="act")
        k.act(gates[0:npart, :], GZ[0:npart, 0:48], AF.Sigmoid)

    def finish(npart, dst_rows):
        k.act(T1[0:npart, :], GZ[0:npart, 48:1072], AF.Silu)
        k.tt(Gb[0:npart, :], Oacc[0:npart, :], T1[0:npart, :], ALU.mult)
        psb = ps[2][:].bitcast(BF16)
        for c in range(8):
            k.tr(psb[:, c * 128:c * 128 + npart], Gb[0:npart, c * 128:(c + 1) * 128], C.identb[0:npart, 0:npart])
        k.cp(gT[:, :, 0:npart], psb[:, 0:1024].rearrange("p (c t) -> p c t", c=8)[:, :, 0:npart], eng="act")
        for hf in range(2):
            cols = slice(hf * 512, (hf + 1) * 512)
            for c in range(8):
                k.mm(ps[hf][0:npart, :], gT[:, c, 0:npart], Wo[:, c, cols], start=(c == 0), stop=(c == 7))
            k.stt(T1[0:npart, cols], xin[0:npart, cols], ALPHA, ps[hf][0:npart, :], ALU.mult, ALU.add)
        ln_tail(C, T1, npart, L, dst_rows, Oacc, crow)

    def load_xT(rows_ap, npart):
        k.dma(xin[0:npart, :], rows_ap)
        for b in range(2):
            for c in range(4):
                k.tr(ps[b][:, c * 128:c * 128 + npart], xin[0:npart, (4 * b + c) * 128:(4 * b + c + 1) * 128], C.identf[0:npart, 0:npart])
            k.cp(xT[:, 4 * b:4 * b + 4, 0:npart], ps[b][:].rearrange("p (c t) -> p c t", c=4)[:, :, 0:npart], eng="act")

    def kv_proj(npart):
        for i in range(3):
            for dc in range(8):
                k.mm(ps[3 + i][0:npart, :], xT[:, dc, 0:npart], Win[:, dc, 1024 + i * 512:1024 + (i + 1) * 512], start=(dc == 0), stop=(dc == 7))
            k.cp(KV[0:npart, i * 512:(i + 1) * 512], ps[3 + i][0:npart, :], eng="act")

    k.memset(Vc[:], 0.0)
    k.memset(Vc[:, :, 64:65], 1.0)
    k.memset(KcT[:], 0.0)
    k.dma(Eexp[0:32, 0:2048], I["n_eexp_p"])
    ntile = tp // 128
    for t in range(ntile):
        r0 = t * 128
        load_xT(src[0][r0:r0 + 128, :], 128)
        kv_proj(128)
        for i, nm in enumerate(("cmpk", "cmpv", "selk", "selv")):
            k.dma(O[nm + "_p"][r0:r0 + 128, :], KV[:, i * 256:(i + 1) * 256])
        wr0 = r0 - (tp - min(512, tp))
        if wr0 >= 0:
            k.dma(O["wink_p"][wr0:wr0 + 128, :], KV[:, 1024:1280])
            k.dma(O["winv_p"][wr0:wr0 + 128, :], KV[:, 1280:1536])
        kv_tile(t, (KV[:, 0:256], KV[:, 256:512]), (KV[:, 512:768], KV[:, 768:1024]), (KV[:, 1024:1280], KV[:, 1280:1536]), 128,
                do_cmp_block=t, win_slot=t % 5)
        qgz(128, 128)
        s_tiles = [(kt, 128, t - kt) for kt in range(t + 1)]
        w_tiles = [(kt % 5, 128, t - kt) for kt in range(max(0, t - 4), t + 1)]
        attend(128, 32, s_tiles, w_tiles,
               lambda g: I["n_bc_p"][t, g], lambda d, g: I["n_bs_p"][d, g], lambda d, g: I["n_bw_p"][d, g],
               I["n_cb_p"][t], I["n_ft_p"][t], I["n_pair_p"], lambda slot: slice(slot * 128, (slot + 1) * 128))
        finish(128, dst[0][r0:r0 + 128, :])

    k.dma(idx[:], I["ptab"].rearrange("s n -> (s n)").partition_broadcast(128))
    k.cp(idf, idx[:])
    k.dma(wcol[:, 0:1], I["n_iota"])
    k.ts(idf, idf, 128.0, ALU.mult, wcol[:, 0:1], ALU.add)
    k.cp(idx[:], idf)
    k.dma(Eexp[0:33, 0:17 * 128], I["n_eexp_s"])
    load_xT(src[1][:, :], 128)
    kv_proj(128)
    KVs = C.kvs_scr
    k.dma(KVs, KV[:])
    for i, nm in enumerate(("cmpk", "cmpv", "selk", "selv")):
        k.dma(O[nm + "_s"], KV[:, i * 256:(i + 1) * 256])
    k.dma(SBS[:].rearrange("k (d g c) -> k d g c", d=17, g=4), I["n_bs_s"].rearrange("d g k c -> k d g c"))
    k.dma(SBW[:].rearrange("k (d g c) -> k d g c", d=5, g=4), I["n_bw_s"].rearrange("d g k c -> k d g c"))
    k.dma(SBC[:].rearrange("k (g c) -> k g c", g=4), I["n_bc_s"].rearrange("g k c -> k g c"))
    k.dma(cbt[0:8, 0:33], I["n_cb_s"])
    k.dma(cbt[0:8, 40:73], I["n_ft_s"])
    for g in range(4):
        k.dma(Vc[:, g, 65:98], I["n_pair_s"])
    xTs_all = sb("xTs_all", [128, 8, 128], BF16)
    k.cp(xTs_all[:], xT[:], eng="dve")
    NEW = KV[0:8, :]
    for sq in range(NS):
        k.memset(Vc[:, :, 0:64], 0.0)
        pools = (I["cmp_k"], I["cmp_v"], I["sel_k"], I["sel_v"])

        def gather(pool_ap, slot, dst_tile, sq=sq):
            P.dma("pool", dst_tile, pool_ap, reads=[pool_ap, idx], writes=[dst_tile],
                  fn=lambda e: e.indirect_dma_start(out=dst_tile, out_offset=None, in_=pool_ap,
                                                    in_offset=bass.IndirectOffsetOnAxis(ap=idx[:, sq * 16 + slot:sq * 16 + slot + 1], axis=0)))
        for pg in range(16):
            gb = GBUF[pg % 2]
            for ci in range(4):
                gather(pools[ci], pg, gb[:, ci * 256:(ci + 1) * 256])
            kv_tile(pg, (gb[:, 0:256], gb[:, 256:512]), (gb[:, 512:768], gb[:, 768:1024]), None, 128, do_cmp_block=pg)
        for wt in range(4):
            gb = GBUF[wt % 2]
            k.dma(gb[:, 0:256], I["win_k"][sq, wt * 128:(wt + 1) * 128, :])
            k.dma(gb[:, 256:512], I["win_v"][sq, wt * 128:(wt + 1) * 128, :])
            kv_tile(0, None, None, (gb[:, 0:256], gb[:, 256:512]), 128, win_slot=wt)
        k.dma(NEW, KVs[sq * 8:(sq + 1) * 8, :])
        kv_tile(16, None, (NEW[:, 512:768], NEW[:, 768:1024]), (NEW[:, 1024:1280], NEW[:, 1280:1536]), 8, win_slot=4)
        k.dma(O["wink_s"][sq, 0:504, :], I["win_k"][sq, 8:512, :])
        k.dma(O["winv_s"][sq, 0:504, :], I["win_v"][sq, 8:512, :], eng="act")
        k.dma(O["wink_s"][sq, 504:512, :], NEW[:, 1024:1280])
        k.dma(O["winv_s"][sq, 504:512, :], NEW[:, 1280:1536])
        k.cp(xT[:, :, 0:8], xTs_all[:, :, sq * 8:(sq + 1) * 8], eng="dve")
        k.dma(xin[0:8, :], src[1][sq * 8:(sq + 1) * 8, :])
        qgz(8, 8)
        s_tiles = [(kt, 128, kt) for kt in range(16)] + [(16, 8, 16)]
        w_tiles = [(kt, 128, kt) for kt in range(4)] + [(4, 8, 4)]
        attend(8, 33, s_tiles, w_tiles,
               lambda g: SBC[:, g * 32:(g + 1) * 32],
               lambda d, g: SBS[0:(8 if d == 16 else 128), d * 128 + g * 32:d * 128 + (g + 1) * 32],
               lambda d, g: SBW[0:(8 if d == 4 else 128), d * 128 + g * 32:d * 128 + (g + 1) * 32],
               None, None, None, lambda slot: slice(slot * 128, slot * 128 + (8 if slot == 16 else 128)), load_consts=False, resident=True)
        finish(8, dst[1][sq * 8:(sq + 1) * 8, :])


def _cmask():
    m = np.zeros((128, 2048), np.float32)
    t = np.arange(128)
    for g, w in enumerate((2, 4, 8, 16)):
        m[:, g * 128:(g + 1) * 128] = (1.0 / np.minimum(w, t + 1))[None, :]
    a = np.arange(64)
    su = (a[:, None] < a[None, :]).astype(np.float32)
    ui = (a[:, None] <= a[None, :]).astype(np.float32)
    m[0:64, 512:576] = su
    m[0:64, 576:640] = ui
    m[0:64, 640:704] = su.T
    m[0:64, 704:768] = ui
    m[0:64, 768:832] = np.eye(64, dtype=np.float32)
    return m


def consts():
    import ml_dtypes
    sel = np.zeros((128, 64, 128), np.float32)
    for kk in range(128):
        sel[kk, kk % 64, (kk // 64) * 64:(kk // 64) * 64 + 64] = 1
    return {"identf": np.eye(128, dtype=np.float32), "selb": sel.reshape(128, 64 * 128).astype(ml_dtypes.bfloat16),
            "cmask": _cmask()}


def shard_inputs(inp, c, tp=TP):
    f = lambda a: np.ascontiguousarray(a)
    m = {
        "xp": f(inp["x_prompt"][c, :tp]), "xs": f(inp["x_sample"][16 * c:16 * c + 16].reshape(128, D)),
        "st_S": f(inp["state_rwkv_S"][:, 16 * c:16 * c + 16]), "st_shift": f(inp["state_rwkv_shift"][:, 16 * c:16 * c + 16]),
        "st_pool": f(inp["state_pool"][0, 16 * c:16 * c + 16]),
        "cmp_k": f(inp["cache_cmp_k"][0].reshape(-1, 256)), "cmp_v": f(inp["cache_cmp_v"][0].reshape(-1, 256)),
        "sel_k": f(inp["cache_sel_k"][0].reshape(-1, 256)), "sel_v": f(inp["cache_sel_v"][0].reshape(-1, 256)),
        "win_k": f(inp["state_win_k"][0, 16 * c:16 * c + 16].reshape(16, 512, 256)),
        "win_v": f(inp["state_win_v"][0, 16 * c:16 * c + 16].reshape(16, 512, 256)),
        "ptab": f(inp["page_table"][16 * c:16 * c + 16]).astype(np.int32),
        "a_r_k": f(inp["a_r_k"].reshape(2, D)), "b_w_in": f(inp["b_w_in"][0]), "b_w_grp": f(inp["b_w_grp"][0]),
        "b_scale": f(inp["b_scale"]), "b_w_out": f(inp["b_w_out"][0]), "c_w_in": f(inp["c_w_in"][0]),
        "c_cmp_wk": f(inp["c_cmp_wk"]), "c_cmp_wv": f(inp["c_cmp_wv"]), "c_w_out": f(inp["c_w_out"][0]),
    }
    for nm in ("ln_g", "ln_b", "a_w_in", "a_mu", "a_w0", "a_w2", "a_a0", "a_a2", "a_k_k", "a_k_a", "a_lnx_g", "a_lnx_b", "a_w_out"):
        m[nm] = f(inp[nm])
    m.update(consts())
    m.update(nsa_consts())
    return m


def nsa_consts():
    sl = 2.0 ** (-8.0 * (np.arange(16) + 1) / 16)
    c = {}

    def bias(dist, valid, g):
        K_, nq = dist.shape
        out = np.empty((K_, 4, nq), np.float32)
        for j in range(4):
            out[:, j] = np.where(valid, -sl[4 * g + j] * dist, NEG)
        return out.reshape(K_, 4 * nq)
    q = np.arange(128)[None, :]
    kk = np.arange(128)[:, None]
    n = np.arange(64)[:, None]
    bc = np.zeros((16, 4, 64, 512), np.float32)
    bs = np.zeros((16, 4, 128, 512), np.float32)
    bw = np.zeros((5, 4, 128, 512), np.float32)
    for g in range(4):
        for t in range(16):
            d = 128 * t + q - 32 * n - 31
            bc[t, g] = bias(d, d >= 0, g)
            d = 128 * t + q - kk
            bs[t, g] = bias(d, d >= 0, g)
            if t < 5:
                bw[t, g] = bias(d, (d >= 0) & (d < 512), g)
    c["n_bc_p"], c["n_bs_p"], c["n_bw_p"] = bc, bs, bw
    cb = np.zeros((16, 128, 32), np.float32)
    ft = np.zeros((16, 128, 32), np.float32)
    blk = np.arange(32)[None, :]
    for t in range(16):
        cur = ((128 * t + np.arange(128)) // 64)[:, None]
        cb[t] = (blk < cur)
        ft[t] = np.where(blk == cur, 1e9, np.where(blk > cur, -1.0, 0.0))
    c["n_cb_p"], c["n_ft_p"] = cb, ft
    c["n_pair_p"] = (np.arange(64)[:, None] // 2 == np.arange(32)[None, :]).astype(np.float32)
    c["n_eexp_p"] = (np.arange(2048)[None, :] // 64 == np.arange(32)[:, None]).astype(np.float32)
    wbm = np.zeros((128, 124), np.float32)
    for r in range(128):
        wbm[r, 60 + r // 32] = 1.0
    c["n_wbm"] = wbm
    c["n_iota"] = np.arange(128, dtype=np.float32).reshape(128, 1)
    tq = np.arange(8)[None, :]
    bcs = np.zeros((4, 64, 32), np.float32)
    bss = np.zeros((17, 4, 128, 32), np.float32)
    bws = np.zeros((5, 4, 128, 32), np.float32)
    for g in range(4):
        d = 2048 + tq - 32 * n - 31
        bcs[g] = bias(d, d >= 0, g)
        for kt in range(16):
            d = 2048 + tq - 128 * kt - kk
            bss[kt, g] = bias(d, d >= 0, g)
        d = tq - kk
        newb = bias(d, (d >= 0) & (kk < 8), g)
        bss[16, g] = newb
        for kt in range(4):
            d = 2048 + tq - (1536 + 128 * kt + kk)
            bws[kt, g] = bias(d, (d >= 0) & (d < 512), g)
        bws[4, g] = newb
    c["n_bc_s"], c["n_bs_s"], c["n_bw_s"] = bcs, bss, bws
    cbs = np.ones((8, 33), np.float32)
    cbs[:, 32] = 0
    fts = np.zeros((8, 33), np.float32)
    fts[:, 32] = 1e9
    c["n_cb_s"], c["n_ft_s"] = cbs, fts
    ps_ = np.zeros((64, 33), np.float32)
    ps_[:, :32] = c["n_pair_p"]
    c["n_pair_s"] = ps_
    ee = np.zeros((33, 17 * 128), np.float32)
    ee[:32, :2048] = c["n_eexp_p"]
    ee[32, 2048:] = 1.0
    c["n_eexp_s"] = ee
    return c


_NC_CACHE = {}


def kernel(**inputs):
    n = 8
    npool = inputs["cache_cmp_k"].shape[1]
    key = (npool,)
    if key not in _NC_CACHE:
        _NC_CACHE[key] = build(npool=npool, tp=TP)
    nc = _NC_CACHE[key]
    in_maps = [shard_inputs(inputs, c) for c in range(n)]
    res = run_bass_kernel_spmd(nc, in_maps, core_ids=list(range(n)))
    R = res.results
    cat = lambda nm: np.stack([R[c][nm] for c in range(n)], 0)
    y_p = cat("y_p")
    y_s = np.concatenate([R[c]["y_s"].reshape(16, 8, D) for c in range(n)], 0)
    S_p = np.stack([R[c]["S_p"] for c in range(n)], 1)
    S_s = np.concatenate([R[c]["S_s"] for c in range(n)], 1)
    sh_p = np.stack([R[c]["sh_p"] for c in range(n)], 1)
    sh_s = np.concatenate([R[c]["sh_s"] for c in range(n)], 1)
    pl_p = cat("pl_p")[None]
    pl_s = np.concatenate([R[c]["pl_s"] for c in range(n)], 0)[None]
    outs = [y_p, y_s, S_p, S_s, sh_p, sh_s, pl_p, pl_s]
    for nm in ("cmpk", "cmpv", "selk", "selv"):
        outs.append(cat(nm + "_p").reshape(1, n, TP, 4, 64))
        outs.append(np.concatenate([R[c][nm + "_s"].reshape(16, 8, 4, 64) for c in range(n)], 0)[None])
    for nm in ("wink", "winv"):
        outs.append(cat(nm + "_p").reshape(1, n, 512, 4, 64))
        outs.append(np.concatenate([R[c][nm + "_s"].reshape(16, 512, 4, 64) for c in range(n)], 0)[None])
    return tuple(np.ascontiguousarray(o, dtype=np.float32) for o in outs)
```

```python
import contextlib
import numpy as np
import concourse.bass as bass
import concourse.mybir as mybir
from concourse.bass_utils import run_bass_kernel_spmd

F32 = mybir.dt.float32
BF16 = mybir.dt.bfloat16
I32 = mybir.dt.int32
ALU = mybir.AluOpType
AF = mybir.ActivationFunctionType
AX = mybir.AxisListType

ENGS = ("pe", "dve", "act", "pool", "sp")
NDMA = {"sp": 12, "act": 6, "pool": 6}

D = 1024
TP = 2048
NS = 16
TS = 8
DEPTH = 4
ALPHA = (2.0 * DEPTH) ** 0.25
LN_EPS = 1e-5
A_NC = 4224
GN_EPS = 64e-5
C_NC = 3632


def _key(k):
    if isinstance(k, (str, tuple)):
        return k
    t = getattr(k, "tensor", k)
    return getattr(t, "name", str(t))


class Prog:
    def __init__(self, nc):
        self.nc = nc
        self.q = {e: [] for e in ENGS}
        self.cnt = {e: 0 for e in ENGS}
        self.known = {e: {} for e in ENGS}
        self.lastw = {}
        self.readers = {}
        self.dma_rr = {e: 0 for e in NDMA}
        self.dma_cnt = {}
        self.n_inst = 0

    def _deps(self, reads, writes):
        deps = {}

        def add(ev):
            if ev is None:
                return
            s, v = ev
            if deps.get(s, 0) < v:
                deps[s] = v
        for k in reads:
            add(self.lastw.get(k))
        for k in writes:
            add(self.lastw.get(k))
            for ev in self.readers.get(k, ()):
                add(ev)
        return deps

    def _commit(self, ev, reads, writes):
        for k in reads:
            self.readers.setdefault(k, []).append(ev)
        for k in writes:
            self.lastw[k] = ev
            self.readers[k] = []

    def _waits(self, eng, deps):
        waits = []
        kn = self.known[eng]
        for s, v in deps.items():
            if s == "c_pe" and eng == "pe":
                continue
            if s == "c_" + eng and eng in ("dve", "act") and v < self.cnt[eng]:
                continue
            if kn.get(s, 0) >= v:
                continue
            kn[s] = v
            waits.append((s, v))
        return waits

    def op(self, eng, fn, reads=(), writes=()):
        reads = [_key(k) for k in reads]
        writes = [_key(k) for k in writes]
        writes = writes + [r for r in reads if isinstance(r, str) and r.startswith("psb")]
        waits = self._waits(eng, self._deps(reads, writes))
        self.cnt[eng] += 1
        ev = ("c_" + eng, self.cnt[eng])
        self.q[eng].append(("op", waits, fn, ev))
        self._commit(ev, reads, writes)
        self.n_inst += 1
        return ev

    def dma(self, eng, out, in_, reads=None, writes=None, fn=None, **kw):
        reads = [_key(k) for k in (reads if reads is not None else [in_])]
        writes = [_key(k) for k in (writes if writes is not None else [out])]
        deps = self._deps(reads, writes)
        i = self.dma_rr[eng]
        self.dma_rr[eng] = (i + 1) % NDMA[eng]
        sname = "d_%s%d" % (eng, i)
        n = self.dma_cnt.get(sname, 0)
        if n > 0 and deps.get(sname, 0) < 16 * n:
            deps[sname] = 16 * n
        waits = self._waits(eng, deps)
        self.dma_cnt[sname] = n + 1
        ev = (sname, 16 * (n + 1))
        self.q[eng].append(("dma", waits, (out, in_, kw, fn), ev))
        self._commit(ev, reads, writes)
        self.n_inst += 1
        return ev

    def barrier(self):
        for eng in ENGS:
            deps = {}
            for f in ENGS:
                if f != "sp" and f != eng and self.cnt[f] > 0:
                    deps["c_" + f] = self.cnt[f]
            for s, n in self.dma_cnt.items():
                deps[s] = 16 * n
            waits = self._waits(eng, deps)
            self.q[eng].append(("wait", waits, None, None))

    def emit(self):
        nc = self.nc
        names = ["c_" + e for e in ENGS if e != "sp"]
        for e, n in NDMA.items():
            names += ["d_%s%d" % (e, i) for i in range(n)]
        with contextlib.ExitStack() as st:
            sems = {nm: st.enter_context(nc.semaphore(nm)) for nm in names}
            block = st.enter_context(nc.Block())

            def run(eng):
                def body(e):
                    for kind, waits, payload, ev in self.q[eng]:
                        for s, v in waits:
                            e.wait_ge(sems[s], v)
                        if kind == "op":
                            payload(e).then_inc(sems[ev[0]], 1)
                        elif kind == "dma":
                            out, in_, kw, fn = payload
                            if fn is not None:
                                fn(e).then_inc(sems[ev[0]], 16)
                            else:
                                e.dma_start(out=out, in_=in_, **kw).then_inc(sems[ev[0]], 16)
                    if eng == "sp":
                        for sname, n in self.dma_cnt.items():
                            e.wait_ge(sems[sname], 16 * n)
                        for en in ENGS:
                            if en != "sp" and self.cnt[en] > 0:
                                e.wait_ge(sems["c_" + en], self.cnt[en])
                return body

            block.sync(run("sp"))
            block.tensor(run("pe"))
            block.vector(run("dve"))
            block.scalar(run("act"))
            block.gpsimd(run("pool"))


def _aps(*xs):
    return [x for x in xs if x is not None and not isinstance(x, (int, float))]


class K:
    def __init__(self, P):
        self.P = P

    def mm(self, out, lhsT, rhs, start=True, stop=True):
        self.P.op("pe", lambda e: e.matmul(out, lhsT=lhsT, rhs=rhs, start=start, stop=stop),
                  reads=[lhsT, rhs], writes=[out])

    def tr(self, out, in_, ident):
        self.P.op("pe", lambda e: e.transpose(out, in_, ident), reads=[in_, ident], writes=[out])

    def tt(self, out, a, b, op, eng="dve"):
        self.P.op(eng, lambda e: e.tensor_tensor(out=out, in0=a, in1=b, op=op), reads=[a, b], writes=[out])

    def ts(self, out, a, s1, op0, s2=None, op1=None, eng="dve"):
        if op1 is None:
            fn = lambda e: e.tensor_scalar(out=out, in0=a, scalar1=s1, scalar2=None, op0=op0)
        else:
            fn = lambda e: e.tensor_scalar(out=out, in0=a, scalar1=s1, scalar2=s2, op0=op0, op1=op1)
        self.P.op(eng, fn, reads=_aps(a, s1, s2), writes=[out])

    def stt(self, out, a, s, b, op0, op1, eng="dve"):
        self.P.op(eng, lambda e: e.scalar_tensor_tensor(out=out, in0=a, scalar=s, in1=b, op0=op0, op1=op1),
                  reads=_aps(a, s, b), writes=[out])

    def red(self, out, in_, op=ALU.add, negate=False, axis=AX.X):
        self.P.op("dve", lambda e: e.tensor_reduce(out=out, in_=in_, axis=axis, op=op, negate=negate),
                  reads=[in_], writes=[out])

    def cp(self, out, in_, eng="dve"):
        if eng == "act":
            self.P.op("act", lambda e: e.copy(out, in_), reads=[in_], writes=[out])
        else:
            self.P.op(eng, lambda e: e.tensor_copy(out, in_), reads=[in_], writes=[out])

    def act(self, out, in_, func, bias=None, scale=None, accum=None):
        kw = {}
        if bias is not None:
            kw["bias"] = bias
        if scale is not None:
            kw["scale"] = scale
        if accum is not None:
            kw["accum_out"] = accum
        self.P.op("act", lambda e: e.activation(out=out, in_=in_, func=func, **kw),
                  reads=_aps(in_, bias, scale), writes=_aps(out, accum))

    def recip(self, out, in_):
        self.P.op("dve", lambda e: e.reciprocal(out, in_), reads=[in_], writes=[out])

    def memset(self, ap, v, eng="pool"):
        self.P.op(eng, lambda e: e.memset(ap, v), writes=[ap])

    def dma(self, out, in_, eng="sp", **kw):
        self.P.dma(eng, out, in_, **kw)


def bc(ap, shape):
    return ap.to_broadcast(shape)


class Ctx:
    pass


def build(npool=2560, tp=TP, layers=(0, 1, 2, 3), dbg=False):
    nc = bass.Bass("TRN2", target_bir_lowering=False)
    C = Ctx()
    C.nc = nc
    C.tp = tp
    P = Prog(nc)
    k = K(P)
    C.P, C.k = P, k

    def din(name, shape, dt=F32):
        return nc.dram_tensor(name, list(shape), dt, kind="ExternalInput").ap()

    def dout(name, shape):
        return nc.dram_tensor(name, list(shape), F32, kind="ExternalOutput").ap()

    def dscr(name, shape, dt=F32):
        return nc.dram_tensor(name, list(shape), dt, kind="Internal").ap()

    I = {}
    for nm, shp in [("xp", (tp, D)), ("xs", (128, D)), ("st_S", (2, NS, 16, 64, 64)), ("st_shift", (2, NS, A_NC)),
                    ("st_pool", (NS, 15, D)), ("cmp_k", (npool * 128, 256)), ("cmp_v", (npool * 128, 256)),
                    ("sel_k", (npool * 128, 256)), ("sel_v", (npool * 128, 256)), ("win_k", (NS, 512, 256)),
                    ("win_v", (NS, 512, 256)), ("ln_g", (4, D)), ("ln_b", (4, D)), ("a_w_in", (2, D, A_NC)),
                    ("a_mu", (2, A_NC)), ("a_w0", (2, D)), ("a_w2", (2, 64, D)), ("a_a0", (2, D)), ("a_a2", (2, 64, D)),
                    ("a_k_k", (2, D)), ("a_k_a", (2, D)), ("a_r_k", (2, D)), ("a_lnx_g", (2, D)), ("a_lnx_b", (2, D)),
                    ("a_w_out", (2, D, D)), ("b_w_in", (D, 2 * D)), ("b_w_grp", (4, 256, 256)), ("b_scale", (1, D)),
                    ("b_w_out", (D, D)), ("c_w_in", (D, C_NC)), ("c_cmp_wk", (1, 32)), ("c_cmp_wv", (1, 32)),
                    ("c_w_out", (D, D)), ("identf", (128, 128)), ("cmask", (128, 2048))]:
        I[nm] = din(nm, shp)
    I["ptab"] = din("ptab", (NS, 16), I32)
    for nm, shp in [("n_bc_p", (16, 4, 64, 512)), ("n_bs_p", (16, 4, 128, 512)), ("n_bw_p", (5, 4, 128, 512)),
                    ("n_cb_p", (16, 128, 32)), ("n_ft_p", (16, 128, 32)), ("n_pair_p", (64, 32)), ("n_eexp_p", (32, 2048)),
                    ("n_wbm", (128, 124)), ("n_iota", (128, 1)), ("n_bc_s", (4, 64, 32)), ("n_bs_s", (17, 4, 128, 32)),
                    ("n_bw_s", (5, 4, 128, 32)), ("n_cb_s", (8, 33)), ("n_ft_s", (8, 33)), ("n_pair_s", (64, 33)),
                    ("n_eexp_s", (33, 17 * 128)), ("n_hb", (128, 240))]:
        I[nm] = din(nm, shp)
    I["selb"] = din("selb", (128, 64 * 128), BF16)
    O = {}
    for nm, shp in [("y_p", (tp, D)), ("y_s", (128, D)), ("S_p", (2, 16, 64, 64)), ("S_s", (2, NS, 16, 64, 64)),
                    ("sh_p", (2, A_NC)), ("sh_s", (2, NS, A_NC)), ("pl_p", (15, D)), ("pl_s", (NS, 15, D)),
                    ("cmpk_p", (tp, 256)), ("cmpk_s", (128, 256)), ("cmpv_p", (tp, 256)), ("cmpv_s", (128, 256)),
                    ("selk_p", (tp, 256)), ("selk_s", (128, 256)), ("selv_p", (tp, 256)), ("selv_s", (128, 256)),
                    ("wink_p", (512, 256)), ("wink_s", (NS, 512, 256)), ("winv_p", (512, 256)), ("winv_s", (NS, 512, 256))]:
        O[nm] = dout(nm, shp)
    if dbg:
        O["dbg_p"] = dout("dbg_p", (tp, D))
        O["dbg_s"] = dout("dbg_s", (128, D))
    C.I, C.O = I, O
    xa_p, xa_s = dscr("xa_p", (tp, D)), dscr("xa_s", (128, D))
    xb_p, xb_s = dscr("xb_p", (tp, D)), dscr("xb_s", (128, D))
    C.wbf = dscr("wbf", (128, 8, A_NC), BF16)
    C.kvs_scr = dscr("kvs_scr", (128, 1536))

    with contextlib.ExitStack() as gst:
        C.identf = gst.enter_context(nc.sbuf_tensor("identf_sb", [128, 128], F32))
        C.identb = gst.enter_context(nc.sbuf_tensor("identb_sb", [128, 128], BF16))
        C.ps = [gst.enter_context(nc.psum_tensor("psb%d" % i, [128, 512], F32)) for i in range(8)]
        k.dma(C.identf[:], I["identf"])
        k.cp(C.identb[:], C.identf[:])
        chain = [(I["xp"], I["xs"]), (xa_p, xa_s), (xb_p, xb_s), (xa_p, xa_s), (O["y_p"], O["y_s"])]
        for L in range(DEPTH):
            if L not in layers:
                continue
            src, dst = chain[L], chain[L + 1]
            if L == max(layers) and dbg:
                dst = (O["dbg_p"], O["dbg_s"])
            P.barrier()
            with contextlib.ExitStack() as lst:
                if L % 3 == 0:
                    rwkv_layer(C, lst, L // 3, L, src, dst)
                elif L % 3 == 1:
                    pool_layer(C, lst, L, src, dst)
                else:
                    nsa_layer(C, lst, L, src, dst)
                P.barrier()
        P.emit()
    return nc


def ln_tail(C, R, npart, L, dst_rows, T1, crow):
    k, I = C.k, C.I
    st = C.lnst
    k.red(st[0:npart, 0:1], R[0:npart, :])
    k.ts(st[0:npart, 1:2], st[0:npart, 0:1], 1.0 / D, ALU.mult)
    k.ts(R[0:npart, :], R[0:npart, :], st[0:npart, 1:2], ALU.subtract)
    k.tt(T1[0:npart, :], R[0:npart, :], R[0:npart, :], ALU.mult)
    k.red(st[0:npart, 2:3], T1[0:npart, :])
    k.act(st[0:npart, 3:4], st[0:npart, 2:3], AF.Sqrt, bias=C.epsln[0:npart, :], scale=1.0 / D)
    k.recip(st[0:npart, 4:5], st[0:npart, 3:4])
    k.ts(R[0:npart, :], R[0:npart, :], st[0:npart, 4:5], ALU.mult)
    k.dma(crow[0][0:npart, :], I["ln_g"][L:L + 1, :].partition_broadcast(npart), eng="act")
    k.tt(R[0:npart, :], R[0:npart, :], crow[0][0:npart, :], ALU.mult)
    k.dma(crow[1][0:npart, :], I["ln_b"][L:L + 1, :].partition_broadcast(npart), eng="act")
    k.tt(R[0:npart, :], R[0:npart, :], crow[1][0:npart, :], ALU.add)
    k.dma(dst_rows, R[0:npart, :])


def rwkv_layer(C, lst, li, L, src, dst):
    nc, P, k, I, O = C.nc, C.P, C.k, C.I, C.O
    tp = C.tp
    sb = lambda n, s, d=F32: lst.enter_context(nc.sbuf_tensor("a%d_" % L + n, list(s), d))
    ps = C.ps
    Wo = sb("Wo", [128, 8, 1024], BF16)
    Wll = sb("Wll", [128, 8, 128], BF16)
    WG = [sb("WG0", [128, 8, 1024], BF16)]
    W2A2 = sb("W2A2", [128, 1024])
    mucol = sb("mucol", [128, 9])
    SEL = sb("SEL", [128, 64, 128], BF16)
    xin2 = sb("xin2", [128, 1024])
    xin = sb("xin", [64, 1024])
    xTd = sb("xTd", [128, 8, 128], BF16)
    xTsd = sb("xTsd", [128, 8, 128], BF16)
    Pt = sb("Pt", [128, 1024])
    PSt = sb("PSt", [128, 1024])
    PM = {g: sb("PM" + g, [128, 1024]) for g in "rkvz"}
    crow = [sb("crow%d" % i, [128, 1024]) for i in range(2)]
    At = sb("At", [128, 1024])
    KP = sb("KP", [128, 1024])
    T1 = sb("T1", [128, 1024])
    T2 = sb("T2", [128, 1024])
    XRf = sb("XRf", [128, 512])
    XRr = sb("XRr", [128, 512])
    XR = {x: [sb("XR%s%d" % (x, j), [128, 512], BF16) for j in range(2)] for x in ("kk", "w", "ka", "k", "r")}
    va = sb("va", [128, 8, 64])
    vs = sb("vs", [128, 8, 64])
    vT = sb("vT", [128, 8, 64])
    lla = sb("lla", [128, 128])
    llb = sb("llb", [128, 128])
    LLt = sb("LLt", [128, 128])
    YT = sb("YT", [128, 8, 64])
    S = sb("S", [128, 8, 64])
    t1 = sb("t1", [128, 8, 64])
    t2 = sb("t2", [128, 8, 64])
    t3 = sb("t3", [128, 8, 64])
    sa = sb("sa", [128, 8])
    st16 = sb("st16", [128, 5, 16])
    bon = sb("bon", [128, 16])
    G = sb("G", [64, 1024], BF16)
    gT = sb("gT", [128, 8, 64], BF16)
    C.lnst = sb("lnst", [128, 8])
    C.epsln = sb("epsln", [128, 1])
    epsgn = sb("epsgn", [128, 1])
    eps24 = sb("eps24", [128, 1])
    k.memset(C.epsln[:], LN_EPS)
    k.memset(epsgn[:], GN_EPS)
    k.memset(eps24[:], 0.0)

    w_in = I["a_w_in"][li].rearrange("(c p) n -> p c n", p=128)
    for j in range(A_NC // 128):
        stg = Pt[:].rearrange("p (c n) -> p c n", c=8) if j % 2 == 0 else PSt[:].rearrange("p (c n) -> p c n", c=8)
        stgb = (T1 if j % 2 == 0 else T2)[:].bitcast(BF16)[:, 0:1024].rearrange("p (c n) -> p c n", c=8)
        k.dma(stg, w_in[:, :, j * 128:(j + 1) * 128], eng="sp" if j % 2 == 0 else "act")
        k.cp(stgb, stg, eng="pool" if j % 2 == 0 else "act")
        k.dma(C.wbf[:, :, j * 128:(j + 1) * 128], stgb, eng="sp")
    w_out = I["a_w_out"][li].rearrange("(c p) n -> p c n", p=128)
    for j in range(8):
        stg = Pt[:].rearrange("p (c n) -> p c n", c=8) if j % 2 == 0 else PSt[:].rearrange("p (c n) -> p c n", c=8)
        k.dma(stg, w_out[:, :, j * 128:(j + 1) * 128], eng="sp" if j % 2 == 0 else "act")
        k.cp(Wo[:, :, j * 128:(j + 1) * 128], stg, eng="pool" if j % 2 == 0 else "act")
    k.dma(Wll[:], C.wbf[:, :, 4096:4224])
    k.dma(W2A2[0:64, :], I["a_w2"][li])
    k.dma(W2A2[64:128, :], I["a_a2"][li])
    k.dma(mucol[:, 0:8], I["a_mu"][li, 2048:3072].rearrange("(c p) -> p c", p=128), allow_slow_non_contiguous=True)
    k.dma(mucol[:, 8:9], I["a_mu"][li, 4096:4224].rearrange("(c p) -> p c", p=128), allow_slow_non_contiguous=True)
    k.dma(SEL[:], I["selb"].rearrange("p (t m) -> p t m", m=128))
    k.memset(S[:], 0.0)
    EPI = sb("EPI", [64, 1024])
    EPN = sb("EPN", [64, 1024])
    EPX = sb("EPX", [64, 1024])
    FMAR = sb("FMAR", [64, 8, 128])
    FMB = sb("FMB", [64, 8, 64])
    FMK = sb("FMK", [64, 8, 64])
    GB = sb("GB", [64, 8, 128])
    GK = sb("GK", [64, 8, 128])
    PQ = [sb("PQ%d" % i, [64, 8, 64]) for i in range(4)]
    Tm = sb("Tm", [64, 8, 64])
    XT = sb("XT", [64, 8, 64])
    UT = sb("UT", [64, 8, 64])
    ST = sb("ST", [64, 16, 64])
    PCc = sb("PCc", [64, 16])
    MK = sb("MK", [64, 320])
    k.dma(MK[:], I["cmask"][0:64, 512:832])
    MASKAR = MK[:, 0:128]
    MASKNT = MK[:, 128:192]
    TRI = MK[:, 192:256]
    IDN = MK[:, 256:320]
    k.memset(ST[:], 0.0)
    pbi = [0]

    def bank():
        pbi[0] = (pbi[0] + 1) % 8
        return ps[pbi[0]]
    import os
    STOP = int(os.environ.get('STOPAT', '99'))
    if STOP <= 1:
        return

    cri = [0]

    def jrow(src_row, npart=128):
        t = crow[cri[0] % 2]
        cri[0] += 1
        k.dma(t[0:npart, :], src_row.partition_broadcast(npart), eng="act")
        return t

    def h4(t):
        return t[:].rearrange("p (a b j) -> p a b j", a=8, b=2)

    def toxr(X, name):
        X4 = h4(X)
        o3 = XRf[:].rearrange("p (a j) -> p a j", a=8)
        k.cp(o3[0:64], X4[0:64, :, 0, :], eng="act")
        k.cp(o3[64:128], X4[64:128, :, 1, :], eng="act")
        k.cp(XR[name][0][:], XRf[:], eng="pool")
        k.tt(XRr[:], XRf[:], XR[name][0][:], ALU.subtract, eng="pool")
        k.cp(XR[name][1][:], XRr[:], eng="pool")

    ntile_p = tp // 64
    import os
    tiles = [("p", n) for n in range(ntile_p)] + ([("s", 0), ("s", 1)] if not os.environ.get("NOSAMPLE") else [])
    wg_i = [0]
    for kind, n in tiles:
        srcx = src[0] if kind == "p" else src[1]
        dstx = dst[0] if kind == "p" else dst[1]
        r0 = n * 64
        if r0 == 0:
            k.memset(xin2[0:1, :], 0.0)
            k.dma(xin2[1:64, :], srcx[0:63, :])
        else:
            k.dma(xin2[0:64, :], srcx[r0 - 1:r0 + 63, :])
        k.dma(xin[:], srcx[r0:r0 + 64, :], eng="act")
        for b in range(2):
            for c in range(4):
                k.tr(ps[b][:, c * 64:(c + 1) * 64], xin[0:64, (4 * b + c) * 128:(4 * b + c + 1) * 128], C.identf[0:64, 0:64])
            for c in range(4):
                k.tr(ps[b][:, 256 + c * 64:256 + (c + 1) * 64], xin2[0:64, (4 * b + c) * 128:(4 * b + c + 1) * 128], C.identf[0:64, 0:64])
            pv = ps[b][:, 0:256].rearrange("p (c t) -> p c t", c=4)
            pw = ps[b][:, 256:512].rearrange("p (c t) -> p c t", c=4)
            k.cp(xTd[:, 4 * b:4 * b + 4, 0:64], pv, eng="act")
            k.cp(xTd[:, 4 * b:4 * b + 4, 64:128], pv, eng="dve")
            k.cp(xTsd[:, 4 * b:4 * b + 4, 0:64], pw, eng="act")
            k.cp(xTsd[:, 4 * b:4 * b + 4, 64:128], pw, eng="dve")
        if STOP <= 2:
            return
        last_rows = []
        if kind == "p" and n == ntile_p - 1:
            last_rows = [(63, O["sh_p"][li])]
        if kind == "s":
            last_rows = [(sl * 8 + 7, O["sh_s"][li, n * 8 + sl]) for sl in range(8)]
        for gi, g in enumerate("rkvz"):
            wg = WG[0]
            wg_i[0] += 1
            k.dma(wg[:], C.wbf[:, :, gi * 1024:(gi + 1) * 1024], eng="sp")
            for hf in range(2):
                cols = slice(hf * 512, (hf + 1) * 512)
                for c in range(8):
                    k.mm(ps[2][:], xTd[:, c, :], wg[:, c, cols], start=(c == 0), stop=(c == 7))
                for c in range(8):
                    k.mm(ps[3][:], xTsd[:, c, :], wg[:, c, cols], start=(c == 0), stop=(c == 7))
                k.cp(Pt[:, cols], ps[2][:], eng="act")
                k.cp(PSt[:, cols], ps[3][:], eng="act")
            if g == "v" and kind == "s":
                for c in range(8):
                    for dc in range(8):
                        k.mm(ps[2][:, c * 64:(c + 1) * 64], wg[:, dc, c * 128:(c + 1) * 128], xTd[:, dc, 0:64],
                             start=(dc == 0), stop=(dc == 7))
                for c in range(8):
                    for dc in range(8):
                        k.mm(ps[3][:, c * 64:(c + 1) * 64], wg[:, dc, c * 128:(c + 1) * 128], xTsd[:, dc, 0:64],
                             start=(dc == 0), stop=(dc == 7))
                k.cp(va[:].rearrange("p c t -> p (c t)"), ps[2][:], eng="act")
                k.cp(vs[:].rearrange("p c t -> p (c t)"), ps[3][:], eng="act")
                if kind == "s":
                    for sl in range(8):
                        k.dma(vs[:, :, sl * 8], I["st_shift"][li, n * 8 + sl, 2048:3072].rearrange("(c p) -> p c", p=128), eng="act", allow_slow_non_contiguous=True)
                k.tt(vs[:], vs[:], va[:], ALU.subtract)
                k.tt(vs[:], vs[:], mucol[:, 0:8].unsqueeze(2).to_broadcast([128, 8, 64]), ALU.mult)
                k.tt(vT[:], vs[:], va[:], ALU.add)
            if kind == "s":
                for sl in range(8):
                    for hh in range(2):
                        k.dma(PSt[hh * 64 + sl * 8:hh * 64 + sl * 8 + 1, :],
                              I["st_shift"][li, n * 8 + sl:n * 8 + sl + 1, gi * 1024:(gi + 1) * 1024], eng="act")
            for (row, dap) in last_rows:
                k.dma(dap[gi * 1024:(gi + 1) * 1024].unsqueeze(0), Pt[row:row + 1, :], eng="act")
            mur = jrow(I["a_mu"][li:li + 1, gi * 1024:(gi + 1) * 1024])
            k.tt(PSt[:], PSt[:], Pt[:], ALU.subtract)
            k.tt(PSt[:], PSt[:], mur[:], ALU.mult)
            k.tt(PM[g][:], PSt[:], Pt[:], ALU.add)
        if STOP <= 3:
            return
        for c in range(8):
            k.mm(ps[2][:, 0:128], Wll[:, c, :], xTd[:, c, :], start=(c == 0), stop=(c == 7))
        for c in range(8):
            k.mm(ps[3][:, 0:128], Wll[:, c, :], xTsd[:, c, :], start=(c == 0), stop=(c == 7))
        k.cp(lla[:], ps[2][:, 0:128], eng="act")
        k.cp(llb[:], ps[3][:, 0:128], eng="act")
        if kind == "s":
            for sl in range(8):
                for hh in range(2):
                    k.dma(llb[:, hh * 64 + sl * 8:hh * 64 + sl * 8 + 1],
                          I["st_shift"][li, n * 8 + sl, 4096:4224].rearrange("(c p) -> p c", p=128), eng="act", allow_slow_non_contiguous=True)
        for (row, dap) in last_rows:
            k.dma(dap[4096:4224].rearrange("(c p) -> p c", p=128), lla[:, row:row + 1], eng="act", allow_slow_non_contiguous=True)
        k.tt(llb[:], llb[:], lla[:], ALU.subtract)
        k.stt(LLt[:], llb[:], mucol[:, 8:9], lla[:], ALU.mult, ALU.add)
        k.act(LLt[0:64, :], LLt[0:64, :], AF.Tanh)
        if STOP <= 4:
            return
        for hf in range(2):
            cols = slice(hf * 512, (hf + 1) * 512)
            k.mm(ps[2 + hf][:], LLt[0:64, :], W2A2[0:64, cols])
            k.mm(ps[4 + hf][:], LLt[64:128, :], W2A2[64:128, cols])
        w0r = jrow(I["a_w0"][li:li + 1, :])
        for hf in range(2):
            cols = slice(hf * 512, (hf + 1) * 512)
            k.tt(T1[:, cols], ps[2 + hf][:], w0r[:, cols], ALU.add)
        k.act(T1[:], T1[:], AF.Sigmoid)
        CC = float(np.exp(-0.5))
        if kind == "p":
            for hf in range(2):
                cols = slice(hf * 512, (hf + 1) * 512)
                k.mm(ps[6 + hf][0:64, :], TRI, T1[0:64, cols])
                k.act(EPI[:, cols], ps[6 + hf][0:64, :], AF.Exp, scale=-CC)
                k.act(EPN[:, cols], ps[6 + hf][0:64, :], AF.Exp, scale=CC)
                k.tt(EPX[:, cols], ps[6 + hf][0:64, :], T1[0:64, cols], ALU.subtract)
            k.act(EPX[:], EPX[:], AF.Exp, scale=-CC)
            for h in range(16):
                k.mm(ps[2][0:64, h:h + 1], EPI[:, h * 64:(h + 1) * 64], C.identf[0:64, 63:64])
            k.cp(PCc[:], ps[2][0:64, 0:16], eng="act")
        else:
            k.act(T1[:], T1[:], AF.Exp, scale=-CC)
            toxr(T1, "w")
        a0r = jrow(I["a_a0"][li:li + 1, :])
        for hf in range(2):
            cols = slice(hf * 512, (hf + 1) * 512)
            k.tt(At[:, cols], ps[4 + hf][:], a0r[:, cols], ALU.add)
        k.act(At[:], At[:], AF.Sigmoid)
        kkr = jrow(I["a_k_k"][li:li + 1, :])
        k.tt(T1[:], PM["k"][:], kkr[:], ALU.mult)
        k.tt(T2[:], T1[:], T1[:], ALU.mult)
        k.red(st16[:, 0, :], T2[:].rearrange("p (h j) -> p h j", h=16))
        k.ts(st16[:, 0, :], st16[:, 0, :], 1e-24, ALU.max)
        k.act(st16[:, 1, :], st16[:, 0, :], AF.Sqrt)
        k.recip(st16[:, 2, :], st16[:, 1, :])
        k.tt(T1[:].rearrange("p (h j) -> p h j", h=16), T1[:].rearrange("p (h j) -> p h j", h=16),
             st16[:, 2, :].unsqueeze(2).to_broadcast([128, 16, 64]), ALU.mult)
        if kind == "s":
            toxr(T1, "kk")
        k.tt(T2[:], T1[:], At[:], ALU.mult)
        if kind == "s":
            toxr(T2, "ka")
        kar = jrow(I["a_k_a"][li:li + 1, :])
        k.stt(PSt[:], At[:], -1.0, kar[:], ALU.add, ALU.mult)
        k.stt(KP[:], PSt[:], 1.0, PM["k"][:], ALU.add, ALU.mult)
        if kind == "s":
            toxr(KP, "k")
            toxr(PM["r"], "r")
        rkr = jrow(I["a_r_k"][li:li + 1, :])
        k.tt(Pt[:], PM["r"][:], rkr[:], ALU.mult)
        k.tt(Pt[:], Pt[:], KP[:], ALU.mult)
        k.red(bon[:], Pt[:].rearrange("p (h j) -> p h j", h=16))
        if kind == "p":
            k.stt(EPX[:], T1[0:64, :], -1.0, EPX[:], ALU.mult, ALU.mult)
            k.tt(T2[0:64, :], T2[0:64, :], EPN[:], ALU.mult)
            k.tt(EPN[:], KP[0:64, :], EPN[:], ALU.mult)
            k.tt(EPI[:], PM["r"][0:64, :], EPI[:], ALU.mult)

        if STOP <= 5:
            return
        def step(Sx, tl):
            bks = {}
            for bi, x in enumerate(("kk", "w", "ka", "k", "r")):
                bk = ps[3 + bi] if bi < 5 else None
                k.mm(bk[:], SEL[:, tl, :], XR[x][0][:], start=True, stop=False)
                k.mm(bk[:], SEL[:, tl, :], XR[x][1][:], start=False, stop=True)
                bks[x] = bk[:].rearrange("p (a j) -> p a j", a=8)
            k.tt(t1[:], Sx, bks["kk"], ALU.mult)
            k.tt(t3[:], bks["k"], vT[:, :, tl:tl + 1].to_broadcast([128, 8, 64]), ALU.mult)
            k.red(sa[:], t1[:], negate=True)
            k.tt(Sx, Sx, bks["w"], ALU.mult)
            k.tt(t2[:], bks["ka"], sa[:].unsqueeze(2).to_broadcast([128, 8, 64]), ALU.mult)
            k.tt(Sx, Sx, t3[:], ALU.add)
            k.tt(Sx, Sx, t2[:], ALU.add)
            k.tt(t1[:], Sx, bks["r"], ALU.mult)
            k.red(YT[:, :, tl], t1[:])

        if kind == "p":
            i64 = C.identf[0:64, 0:64]
            for hh in range(2):
                H0 = hh * 8
                bA = [bank(), bank()]
                bB, bK = bank(), bank()
                for hl in range(8):
                    hc = slice((H0 + hl) * 64, (H0 + hl + 1) * 64)
                    o = (hl % 4) * 128
                    k.tr(bA[hl // 4][0:64, o:o + 64], EPX[:, hc], i64)
                    k.tr(bA[hl // 4][0:64, o + 64:o + 128], EPI[:, hc], i64)
                    k.tr(bB[0:64, hl * 64:(hl + 1) * 64], T2[0:64, hc], i64)
                    k.tr(bK[0:64, hl * 64:(hl + 1) * 64], EPN[:, hc], i64)
                k.cp(FMAR[:, 0:4, :].rearrange("p a b -> p (a b)"), bA[0][0:64, :], eng="act")
                k.cp(FMAR[:, 4:8, :].rearrange("p a b -> p (a b)"), bA[1][0:64, :], eng="act")
                k.cp(FMB[:].rearrange("p a b -> p (a b)"), bB[0:64, :], eng="dve")
                k.cp(FMK[:].rearrange("p a b -> p (a b)"), bK[0:64, :], eng="dve")
                for (Gd, FMl) in ((GB, FMB), (GK, FMK)):
                    bb = [bank(), bank()]
                    for hl in range(8):
                        o = (hl % 4) * 128
                        k.mm(bb[hl // 4][0:64, o:o + 128], FMl[:, hl, :], FMAR[:, hl, :])
                    for q in range(2):
                        k.tt(Gd[:, 4 * q:4 * q + 4, :], bb[q][0:64, :].rearrange("p (a b) -> p a b", a=4),
                             MASKAR.unsqueeze(1).to_broadcast([64, 4, 128]), ALU.mult)
                bq = bank()
                for hl in range(8):
                    k.mm(bq[0:64, hl * 64:(hl + 1) * 64], FMAR[:, hl, 0:64], FMB[:, hl, :])
                k.tt(PQ[1][:], bq[0:64, :].rearrange("p (a b) -> p a b", a=8), MASKNT.unsqueeze(1).to_broadcast([64, 8, 64]), ALU.mult)
                k.tt(Tm[:], GB[:, :, 0:64], IDN.unsqueeze(1).to_broadcast([64, 8, 64]), ALU.add)
                Pc, Qc = GB[:, :, 0:64], PQ[1]
                for lv in range(5):
                    Pn, Qn = PQ[2 * ((lv + 1) % 2)], PQ[2 * ((lv + 1) % 2) + 1]
                    if lv < 4:
                        bp = bank()
                        for hl in range(8):
                            k.mm(bp[0:64, hl * 64:(hl + 1) * 64], Qc[:, hl, :], Pc[:, hl, :])
                    bq = bank()
                    for hl in range(8):
                        k.mm(bq[0:64, hl * 64:(hl + 1) * 64], Pc[:, hl, :], Qc[:, hl, :])
                    if lv < 4:
                        k.cp(Pn[:].rearrange("p a b -> p (a b)"), bp[0:64, :], eng="act")
                    k.cp(Qn[:].rearrange("p a b -> p (a b)"), bq[0:64, :], eng="act")
                    bt = bank()
                    for hl in range(8):
                        k.mm(bt[0:64, hl * 64:(hl + 1) * 64], Qn[:, hl, :], Tm[:, hl, :])
                    k.tt(Tm[:], Tm[:], bt[0:64, :].rearrange("p (a b) -> p a b", a=8), ALU.add)
                    Pc, Qc = Pn, Qn
                bx = bank()
                for hl in range(8):
                    hc = slice((H0 + hl) * 64, (H0 + hl + 1) * 64)
                    k.mm(bx[0:64, hl * 64:(hl + 1) * 64], FMAR[:, hl, 0:64], ST[:, H0 + hl, :], start=True, stop=False)
                    k.mm(bx[0:64, hl * 64:(hl + 1) * 64], GK[:, hl, 0:64], PM["v"][0:64, hc], start=False, stop=True)
                k.cp(XT[:].rearrange("p a b -> p (a b)"), bx[0:64, :], eng="act")
                bu = bank()
                for hl in range(8):
                    k.mm(bu[0:64, hl * 64:(hl + 1) * 64], Tm[:, hl, :], XT[:, hl, :])
                k.cp(UT[:].rearrange("p a b -> p (a b)"), bu[0:64, :], eng="act")
                by = bank()
                for hl in range(8):
                    hc = slice((H0 + hl) * 64, (H0 + hl + 1) * 64)
                    k.mm(by[0:64, hl * 64:(hl + 1) * 64], FMAR[:, hl, 64:128], ST[:, H0 + hl, :], start=True, stop=False)
                    k.mm(by[0:64, hl * 64:(hl + 1) * 64], GB[:, hl, 64:128], UT[:, hl, :], start=False, stop=False)
                    k.mm(by[0:64, hl * 64:(hl + 1) * 64], GK[:, hl, 64:128], PM["v"][0:64, hc], start=False, stop=True)
                k.cp(KP[0:64, hh * 512:(hh + 1) * 512], by[0:64, :], eng="act")
                bs = bank()
                for hl in range(8):
                    hc = slice((H0 + hl) * 64, (H0 + hl + 1) * 64)
                    k.mm(bs[0:64, hl * 64:(hl + 1) * 64], T2[0:64, hc], UT[:, hl, :], start=True, stop=False)
                    k.mm(bs[0:64, hl * 64:(hl + 1) * 64], EPN[:, hc], PM["v"][0:64, hc], start=False, stop=True)
                k.tt(ST[:, H0:H0 + 8, :], ST[:, H0:H0 + 8, :], bs[0:64, :].rearrange("p (a b) -> p a b", a=8), ALU.add)
                k.tt(ST[:, H0:H0 + 8, :], ST[:, H0:H0 + 8, :], PCc[:, H0:H0 + 8].unsqueeze(2).to_broadcast([64, 8, 64]), ALU.mult)
            if n == ntile_p - 1:
                for q in range(2):
                    bo = bank()
                    for hl in range(8):
                        k.tr(bo[0:64, hl * 64:(hl + 1) * 64], ST[:, q * 8 + hl, :], i64)
                    k.cp(EPI[:, q * 512:(q + 1) * 512], bo[0:64, :], eng="act")
                k.dma(O["S_p"][li].rearrange("h i j -> i h j"), EPI[:].rearrange("p (h j) -> p h j", h=16))
        else:
            for sl in range(8):
                sq = n * 8 + sl
                k.dma(t3[:], I["st_S"][li, sq].rearrange("(a b) i j -> (b i) a j", b=2))
                Sx = KP[:, 0:512].rearrange("p (a j) -> p a j", a=8)
                k.cp(Sx, t3[:], eng="act")
                for t in range(8):
                    step(Sx, sl * 8 + t)
                k.dma(O["S_s"][li, sq].rearrange("(a b) i j -> (b i) a j", b=2), Sx)

        if STOP <= 6:
            return
        Y = T1
        if kind == "p":
            k.cp(Y[0:64, :], KP[0:64, :], eng="pool")
        else:
            for c in range(8):
                k.tr(ps[c // 4][0:64, (c % 4) * 128:(c % 4 + 1) * 128], YT[:, c, :], C.identf[:, :])
            k.cp(Y[0:64, 0:512], ps[0][0:64, :], eng="act")
            k.cp(Y[0:64, 512:1024], ps[1][0:64, :], eng="act")
        Y3 = Y[0:64, :].rearrange("p (h j) -> p h j", h=16)
        k.red(st16[0:64, 0, :], Y3)
        k.ts(st16[0:64, 0, :], st16[0:64, 0, :], 1.0 / 64, ALU.mult)
        k.tt(Y3, Y3, st16[0:64, 0, :].unsqueeze(2).to_broadcast([64, 16, 64]), ALU.subtract)
        k.tt(T2[0:64, :], Y[0:64, :], Y[0:64, :], ALU.mult)
        k.red(st16[0:64, 1, :], T2[0:64, :].rearrange("p (h j) -> p h j", h=16))
        k.act(st16[0:64, 2, :], st16[0:64, 1, :], AF.Sqrt, bias=epsgn[0:64, :], scale=1.0 / 64)
        k.recip(st16[0:64, 3, :], st16[0:64, 2, :])
        k.tt(Y3, Y3, st16[0:64, 3, :].unsqueeze(2).to_broadcast([64, 16, 64]), ALU.mult)
        gr = jrow(I["a_lnx_g"][li:li + 1, :], 64)
        k.tt(Y[0:64, :], Y[0:64, :], gr[0:64, :], ALU.mult)
        br = jrow(I["a_lnx_b"][li:li + 1, :], 64)
        k.tt(Y[0:64, :], Y[0:64, :], br[0:64, :], ALU.add)
        k.tt(T2[0:64, :].rearrange("p (h j) -> p h j", h=16), PM["v"][0:64, :].rearrange("p (h j) -> p h j", h=16),
             bon[0:64, :].unsqueeze(2).to_broadcast([64, 16, 64]), ALU.mult)
        k.tt(Y[0:64, :], Y[0:64, :], T2[0:64, :], ALU.add)
        k.act(T2[0:64, :], PM["z"][0:64, :], AF.Silu)
        k.tt(G[:], Y[0:64, :], T2[0:64, :], ALU.mult)
        psb = ps[2][:].bitcast(BF16)
        for c in range(8):
            k.tr(psb[:, c * 64:(c + 1) * 64], G[:, c * 128:(c + 1) * 128], C.identb[0:64, 0:64])
        k.cp(gT[:].rearrange("p c t -> p (c t)"), psb[:, 0:512], eng="act")
        for hf in range(2):
            cols = slice(hf * 512, (hf + 1) * 512)
            for c in range(8):
                k.mm(ps[hf][0:64, :], gT[:, c, :], Wo[:, c, cols], start=(c == 0), stop=(c == 7))
            k.stt(Y[0:64, cols], xin[:, cols], ALPHA, ps[hf][0:64, :], ALU.mult, ALU.add)
        ln_tail(C, Y, 64, L, dstx[r0:r0 + 64, :], T2, crow)


def pool_layer(C, lst, L, src, dst):
    nc, P, k, I, O = C.nc, C.P, C.k, C.I, C.O
    tp = C.tp
    sb = lambda n, s, d=F32: lst.enter_context(nc.sbuf_tensor("b_" + n, list(s), d))
    ps = C.ps
    Win = sb("Win", [128, 8, 2048], BF16)
    Wg = sb("Wg", [128, 4, 2, 256], BF16)
    Wo = sb("Wo", [128, 8, 1024], BF16)
    scol = sb("scol", [128, 8])
    stg = [sb("stg%d" % i, [128, 8, 128]) for i in range(2)]
    xin = sb("xin", [128, 1024])
    xT = sb("xT", [128, 8, 128], BF16)
    E = sb("E", [128, 8, 16 * 23])
    A = sb("A", [128, 2, 16 * 23])
    B = sb("B", [128, 2, 16 * 23])
    dT = sb("dT", [128, 8, 128], BF16)
    sz = sb("sz", [128, 8, 128])
    gT = sb("gT", [128, 8, 128], BF16)
    R = sb("R", [128, 1024])
    T1 = sb("T1", [128, 1024])
    cinv = sb("cinv", [128, 512])
    crow = [sb("crow%d" % i, [128, 1024]) for i in range(2)]
    C.lnst = sb("lnst", [128, 8])
    C.epsln = sb("epsln", [128, 1])
    k.memset(C.epsln[:], LN_EPS)
    k.dma(cinv[:], I["cmask"][:, 0:512])
    w_in = I["b_w_in"].rearrange("(c p) n -> p c n", p=128)
    for j in range(16):
        k.dma(stg[j % 2][:], w_in[:, :, j * 128:(j + 1) * 128], eng="sp" if j % 2 == 0 else "act")
        k.cp(Win[:, :, j * 128:(j + 1) * 128], stg[j % 2][:], eng="pool" if j % 2 == 0 else "act")
    w_out = I["b_w_out"].rearrange("(c p) n -> p c n", p=128)
    for j in range(8):
        k.dma(stg[j % 2][:], w_out[:, :, j * 128:(j + 1) * 128], eng="sp" if j % 2 == 0 else "act")
        k.cp(Wo[:, :, j * 128:(j + 1) * 128], stg[j % 2][:], eng="pool" if j % 2 == 0 else "act")
    for g in range(4):
        sv = stg[g % 2][:].rearrange("p c n -> p (c n)")[:, 0:512].rearrange("p (c n) -> p c n", c=2)
        k.dma(sv, I["b_w_grp"][g].rearrange("(c p) n -> p c n", p=128), eng="sp")
        k.cp(Wg[:, g, :, :], sv, eng="pool")
    k.dma(scol[:], I["b_scale"][0].rearrange("(c p) -> p c", p=128), allow_slow_non_contiguous=True)
    k.memset(E[:], 0.0)

    ntile_p = tp // 128
    tiles = [("p", n) for n in range(ntile_p)] + [("s", 0)]
    for kind, n in tiles:
        srcx = src[0] if kind == "p" else src[1]
        dstx = dst[0] if kind == "p" else dst[1]
        r0 = n * 128
        nseg, new = (1, 128) if kind == "p" else (16, 8)
        sl = 15 + new
        Ev = E[:, :, 0:nseg * sl].rearrange("p c (s t) -> p c s t", s=nseg)
        k.dma(xin[:], srcx[r0:r0 + 128, :])
        for b in range(2):
            for c in range(4):
                k.tr(ps[b][:, c * 128:(c + 1) * 128], xin[:, (4 * b + c) * 128:(4 * b + c + 1) * 128], C.identf[:, :])
            k.cp(xT[:, 4 * b:4 * b + 4, :].rearrange("p c t -> p (c t)"), ps[b][:], eng="act")
        if kind == "s":
            for hh in range(2):
                k.dma(R[0:120, :], I["st_pool"][hh * 8:(hh + 1) * 8].rearrange("s r d -> (s r) d"))
                for b in range(2):
                    for c in range(4):
                        k.tr(ps[2 + b][:, c * 120:(c + 1) * 120], R[0:120, (4 * b + c) * 128:(4 * b + c + 1) * 128], C.identf[0:120, 0:120])
                    k.cp(Ev[:, 4 * b:4 * b + 4, hh * 8:(hh + 1) * 8, 0:15],
                         ps[2 + b][:, 0:480].rearrange("p (c s r) -> p c s r", c=4, s=8), eng="act")
        for ob in range(4):
            for o4 in range(4):
                oc = ob * 4 + o4
                for dc in range(8):
                    k.mm(ps[4 + ob % 2][:, o4 * 128:(o4 + 1) * 128], Win[:, dc, oc * 128:(oc + 1) * 128], xT[:, dc, :],
                         start=(dc == 0), stop=(dc == 7))
            pv = ps[4 + ob % 2][:].rearrange("p (c s t) -> p c s t", c=4, s=nseg)
            if ob < 2:
                k.cp(Ev[:, ob * 4:ob * 4 + 4, :, 15:sl], pv, eng="act")
            else:
                k.act(sz[:, (ob - 2) * 4:(ob - 2) * 4 + 4, :].rearrange("p c t -> p (c t)"), ps[4 + ob % 2][:], AF.Silu)
        if kind == "p" and n == ntile_p - 1:
            for b in range(2):
                for c in range(4):
                    k.tr(ps[2 + b][0:15, c * 128:(c + 1) * 128], E[:, 4 * b + c, 128:143], C.identf[:, :])
                k.cp(T1[0:15, b * 512:(b + 1) * 512], ps[2 + b][0:15, :], eng="act")
            k.dma(O["pl_p"], T1[0:15, :])
        if kind == "s":
            for hh in range(2):
                for b in range(2):
                    for c in range(4):
                        Ac = A[:, 0, 0:120].rearrange("p (s r) -> p s r", s=8)
                        k.cp(Ac, Ev[:, 4 * b + c, hh * 8:(hh + 1) * 8, 8:23], eng="pool")
                        k.tr(ps[2 + b][0:120, c * 128:(c + 1) * 128], A[:, 0, 0:120], C.identf[:, :])
                    k.cp(T1[0:120, b * 512:(b + 1) * 512], ps[2 + b][0:120, :], eng="act")
                k.dma(O["pl_s"][hh * 8:(hh + 1) * 8].rearrange("s r d -> (s r) d"), T1[0:120, :])
        for g in range(4):
            cur = Ev[:, 2 * g:2 * g + 2]
            bufs = [A[:, :, 0:nseg * sl].rearrange("p c (s t) -> p c s t", s=nseg),
                    B[:, :, 0:nseg * sl].rearrange("p c (s t) -> p c s t", s=nseg)]
            lo = 0
            for si, sh in enumerate((1, 2, 4, 8)[:g + 1]):
                nxt = bufs[si % 2]
                lo2 = lo + sh
                k.tt(nxt[:, :, :, lo2:sl], cur[:, :, :, lo2:sl], cur[:, :, :, lo:sl - sh], ALU.add)
                cur, lo = nxt, lo2
            w = 2 ** (g + 1)
            pooled = bufs[(g + 1) % 2]
            if kind == "p" and n == 0:
                k.tt(pooled[:, :, 0, 15:sl], cur[:, :, 0, 15:sl],
                     cinv[:, g * 128:(g + 1) * 128].unsqueeze(1).to_broadcast([128, 2, 128]), ALU.mult)
                k.tt(dT[:, 2 * g:2 * g + 2, :], pooled[:, :, 0, 15:sl], Ev[:, 2 * g:2 * g + 2, 0, 15:sl], ALU.subtract)
            else:
                k.stt(dT[:, 2 * g:2 * g + 2, :].rearrange("p c (s t) -> p c s t", s=nseg), cur[:, :, :, 15:sl], 1.0 / w,
                      Ev[:, 2 * g:2 * g + 2, :, 15:sl], ALU.mult, ALU.subtract)
        if kind == "p":
            k.cp(A[:, 0, 0:120].rearrange("p (c t) -> p c t", c=8), E[:, :, 128:143], eng="pool")
            k.cp(E[:, :, 0:15], A[:, 0, 0:120].rearrange("p (c t) -> p c t", c=8), eng="pool")
        for jc in range(8):
            g, jl = jc // 2, jc % 2
            for ic in range(2):
                k.mm(ps[6 + jc // 4][:, (jc % 4) * 128:(jc % 4 + 1) * 128], Wg[:, g, ic, jl * 128:(jl + 1) * 128], dT[:, 2 * g + ic, :],
                     start=(ic == 0), stop=(ic == 1))
        for jc in range(8):
            k.stt(gT[:, jc, :], ps[6 + jc // 4][:, (jc % 4) * 128:(jc % 4 + 1) * 128], scol[:, jc:jc + 1], sz[:, jc, :], ALU.mult, ALU.mult)
        for hf in range(2):
            cols = slice(hf * 512, (hf + 1) * 512)
            for c in range(8):
                k.mm(ps[hf][:], gT[:, c, :], Wo[:, c, cols], start=(c == 0), stop=(c == 7))
            k.stt(R[:, cols], xin[:, cols], ALPHA, ps[hf][:], ALU.mult, ALU.add)
        ln_tail(C, R, 128, L, dstx[r0:r0 + 128, :], T1, crow)


NEG = -30000.0
SCL = 0.125


def nsa_layer(C, lst, L, src, dst):
    nc, P, k, I, O = C.nc, C.P, C.k, C.I, C.O
    tp = C.tp
    sb = lambda n, s, d=F32: lst.enter_context(nc.sbuf_tensor("c_" + n, list(s), d))
    ps = C.ps
    Win = sb("Win", [128, 8, C_NC], BF16)
    Wo = sb("Wo", [128, 8, 1024], BF16)
    stg = [sb("stg%d" % i, [128, 8, 128]) for i in range(2)]
    xin = sb("xin", [128, 1024])
    xT = sb("xT", [128, 8, 128], BF16)
    KV = sb("KV", [128, 1536])
    KsT = sb("KsT", [64, 4, 17 * 128], BF16)
    KwT = sb("KwT", [64, 4, 5 * 128], BF16)
    Vs = sb("Vs", [128, 17, 4, 65], BF16)
    Vw = sb("Vw", [128, 5, 4, 65], BF16)
    KcT = sb("KcT", [64, 4, 64], BF16)
    Vc = sb("Vc", [64, 4, 98])
    Wbk = sb("Wbk", [128, 124])
    Wbv = sb("Wbv", [128, 124])
    wcol = sb("wcol", [128, 2])
    QT = sb("QT", [64, 16, 128], BF16)
    GZ = sb("GZ", [128, 1072])
    gates = sb("gates", [128, 48])
    Bt = [sb("Bt%d" % i, [128, 512]) for i in range(4)]
    SBS = sb("SBS", [128, 17 * 128])
    SBW = sb("SBW", [128, 5 * 128])
    SBC = sb("SBC", [64, 128])
    hbias = sb("hbias", [128, 240])
    k.dma(hbias[:], I["n_hb"])
    GBUF = [sb("GBUF%d" % i, [128, 1024]) for i in range(2)]
    tmp = sb("tmp", [128, 512])
    ec = sb("ec", [64, 512])
    eb = sb("eb", [128, 512], BF16)
    OB = sb("OB", [128, 4, 98])
    rd = sb("rd", [128, 8])
    imp = sb("imp", [128, 40])
    imp2 = sb("imp2", [128, 40])
    m8 = sb("m8", [128, 16])
    cbt = sb("cbt", [128, 80])
    selT = sb("selT", [40, 128])
    Eexp = sb("Eexp", [40, 17 * 128])
    Oacc = sb("Oacc", [128, 1024])
    Gb = sb("Gb", [128, 1024], BF16)
    gT = sb("gT", [128, 8, 128], BF16)
    T1 = sb("T1", [128, 1024])
    idx = sb("idx", [128, 256], I32)
    idf = GBUF[0][:, 0:256]
    crow = [x[:].rearrange("p c n -> p (c n)") for x in stg]
    C.lnst = sb("lnst", [128, 8])
    C.epsln = sb("epsln", [128, 1])
    k.memset(C.epsln[:], LN_EPS)
    w_in = I["c_w_in"].rearrange("(c p) n -> p c n", p=128)
    nj = (C_NC + 127) // 128
    for j in range(nj):
        wd = min(128, C_NC - j * 128)
        k.dma(stg[j % 2][:, :, 0:wd], w_in[:, :, j * 128:j * 128 + wd], eng="sp" if j % 2 == 0 else "act")
        k.cp(Win[:, :, j * 128:j * 128 + wd], stg[j % 2][:, :, 0:wd], eng="pool" if j % 2 == 0 else "act")
    w_out = I["c_w_out"].rearrange("(c p) n -> p c n", p=128)
    for j in range(8):
        k.dma(stg[j % 2][:], w_out[:, :, j * 128:(j + 1) * 128], eng="sp" if j % 2 == 0 else "act")
        k.cp(Wo[:, :, j * 128:(j + 1) * 128], stg[j % 2][:], eng="pool" if j % 2 == 0 else "act")
    for r in range(4):
        k.dma(wcol[r * 32:(r + 1) * 32, 0:1], I["c_cmp_wk"].rearrange("o l -> l o"), allow_slow_non_contiguous=True)
        k.dma(wcol[r * 32:(r + 1) * 32, 1:2], I["c_cmp_wv"].rearrange("o l -> l o"), allow_slow_non_contiguous=True)
    k.dma(Wbk[:], I["n_wbm"])
    k.cp(Wbv[:], Wbk[:], eng="pool")
    k.ts(Wbk[:], Wbk[:], wcol[:, 0:1], ALU.mult)
    k.ts(Wbv[:], Wbv[:], wcol[:, 1:2], ALU.mult)
    k.memset(Vs[:], 0.0)
    k.memset(Vw[:], 0.0)
    k.memset(KsT[:], 0.0)
    k.memset(KwT[:], 0.0)
    k.memset(Vs[:, :, :, 64:65], 1.0)
    k.memset(Vw[:, :, :, 64:65], 1.0)

    def kv_tile(kt, rows_cmp, rows_sel, rows_win, nrows, do_cmp_block=None, win_slot=None):
        if rows_sel is not None:
            ksr, vsr = rows_sel
            for g in range(4):
                k.tr(ps[0][0:64, g * 128:g * 128 + nrows], ksr[:, g * 64:(g + 1) * 64], C.identf[0:nrows, 0:nrows])
            k.cp(KsT[:, :, kt * 128:kt * 128 + nrows], ps[0][0:64, :].rearrange("p (g t) -> p g t", g=4)[:, :, 0:nrows], eng="act")
            k.cp(Vs[0:nrows, kt, :, 0:64], vsr.rearrange("p (g d) -> p g d", g=4), eng="dve")
        if rows_win is not None:
            kwr, vwr = rows_win
            ws = win_slot
            for g in range(4):
                k.tr(ps[1][0:64, g * 128:g * 128 + nrows], kwr[:, g * 64:(g + 1) * 64], C.identf[0:nrows, 0:nrows])
            k.cp(KwT[:, :, ws * 128:ws * 128 + nrows], ps[1][0:64, :].rearrange("p (g t) -> p g t", g=4)[:, :, 0:nrows], eng="act")
            k.cp(Vw[0:nrows, ws, :, 0:64], vwr.rearrange("p (g d) -> p g d", g=4), eng="dve")
        if rows_cmp is not None:
            kcr, vcr = rows_cmp
            t = do_cmp_block
            for g in range(4):
                k.mm(ps[2][0:64, g * 4:(g + 1) * 4], kcr[:, g * 64:(g + 1) * 64], Wbk[:, 60:64])
            k.cp(KcT[:, :, 4 * t:4 * t + 4], ps[2][0:64, 0:16].rearrange("p (g n) -> p g n", g=4), eng="act")
            k.mm(ps[2][0:64, 128:384], Wbv[:, 60 - 4 * t:124 - 4 * t], vcr)
            k.tt(Vc[:, :, 0:64], Vc[:, :, 0:64], ps[2][0:64, 128:384].rearrange("p (g d) -> p g d", g=4), ALU.add)

    bti = [0]

    def attend(nq, nblk, s_tiles, w_tiles, bc_ap, bs_fn, bw_fn, cb_ap, ft_ap, pair_ap, eexp_cols, load_consts=True, resident=False):
        nc4 = 4 * nq
        if load_consts:
            k.dma(cbt[0:nq, 0:nblk], cb_ap)
            k.dma(cbt[0:nq, 40:40 + nblk], ft_ap)
            for g in range(4):
                k.dma(Vc[:, g, 65:65 + nblk], pair_ap)

        def bias_tile(ap, nk):
            if resident:
                return ap
            t = Bt[3]
            k.dma(t[0:nk, 0:nc4], ap, eng="sp")
            return t[0:nk, 0:nc4]
        slopes = [2.0 ** (-8.0 * (h + 1) / 16) for h in range(16)]

        for g in range(4):
            Qg = QT[:, 4 * g:4 * g + 4, 0:nq]
            Qg2 = ec
            k.mm(ps[3][0:64, 0:nc4], KcT[:, g, :], QTf[:, g, 0:nc4])
            bt = bias_tile(bc_ap(g), 64)
            k.stt(tmp[0:64, 0:nc4], ps[3][0:64, 0:nc4], SCL, bt, ALU.mult, ALU.add)
            k.act(ec[:, 0:nc4], tmp[0:64, 0:nc4], AF.Exp)
            for j in range(4):
                k.mm(ps[4][0:nq, j * 98:j * 98 + 65 + nblk], ec[:, j * nq:(j + 1) * nq], Vc[:, g, 0:65 + nblk])
            k.cp(OB[0:nq, :, 0:65 + nblk], ps[4][0:nq, 0:392].rearrange("p (j c) -> p j c", j=4)[:, :, 0:65 + nblk], eng="act")
            k.ts(rd[0:nq, 0:4], OB[0:nq, :, 64], 1e-30, ALU.max)
            k.recip(rd[0:nq, 0:4], rd[0:nq, 0:4])
            k.ts(imp[0:nq, 0:nblk], OB[0:nq, 0, 65:65 + nblk], rd[0:nq, 0:1], ALU.mult)
            for j in range(1, 4):
                k.stt(imp[0:nq, 0:nblk], OB[0:nq, j, 65:65 + nblk], rd[0:nq, j:j + 1], imp[0:nq, 0:nblk], ALU.mult, ALU.add)
            k.tt(imp[0:nq, 0:nblk], imp[0:nq, 0:nblk], cbt[0:nq, 0:nblk], ALU.mult)
            k.tt(imp[0:nq, 0:nblk], imp[0:nq, 0:nblk], cbt[0:nq, 40:40 + nblk], ALU.add)
            P.op("dve", lambda e: e.max(out=m8[0:nq, 0:8], in_=imp[0:nq, 0:nblk]), reads=[imp], writes=[m8])
            P.op("dve", lambda e: e.match_replace(out=imp2[0:nq, 0:nblk], in_to_replace=m8[0:nq, 0:8], in_values=imp[0:nq, 0:nblk], imm_value=-2.0),
                 reads=[imp, m8], writes=[imp2])
            P.op("dve", lambda e: e.max(out=m8[0:nq, 8:16], in_=imp2[0:nq, 0:nblk]), reads=[imp2], writes=[m8])
            k.ts(m8[0:nq, 15:16], m8[0:nq, 15:16], 0.0, ALU.max)
            k.ts(imp2[0:nq, 0:nblk], imp[0:nq, 0:nblk], m8[0:nq, 15:16], ALU.is_ge)
            k.tr(ps[5][0:nblk, 0:nq], imp2[0:nq, 0:nblk], C.identf[0:nq, 0:nq])
            k.cp(selT[0:nblk, 0:nq], ps[5][0:nblk, 0:nq], eng="act")
            def accum(first, col, Osrc):
                gsl = gates[0:nq, :].rearrange("p (h c) -> p h c", c=3)[:, 4 * g:4 * g + 4, col]
                k.tt(rd[0:nq, 4:8], rd[0:nq, 0:4], gsl, ALU.mult)
                dstv = Oacc[0:nq, g * 256:(g + 1) * 256].rearrange("p (j d) -> p j d", j=4)
                rb = rd[0:nq, 4:8].unsqueeze(2).to_broadcast([nq, 4, 64])
                if first:
                    k.tt(dstv, Osrc, rb, ALU.mult)
                else:
                    k.tt(OB[0:nq, :, 0:64], Osrc, rb, ALU.mult)
                    k.tt(dstv, dstv, OB[0:nq, :, 0:64], ALU.add)
            accum(True, 0, OB[0:nq, :, 0:64])
            if not resident:
                k.dma(Bt[0][:], bs_fn(1, g), eng="sp")
                k.dma(Bt[1][:], bs_fn(0, g), eng="sp")
                k.dma(Bt[2][:], bw_fn(4, g), eng="sp")
            for br, tiles, Kt, Vt, bfn in ((1, s_tiles, KsT, Vs, bs_fn), (2, w_tiles, KwT, Vw, bw_fn)):
                for ti, (slot, nk, bidx) in enumerate(tiles):
                    psc = ps[3] if ti % 2 == 0 else ps[2]
                    k.mm(psc[0:nk, 0:nc4], Kt[:, g, slot * 128:slot * 128 + nk], QTf[:, g, 0:nc4])
                    hb = None
                    if resident:
                        bt = bfn(bidx, g)
                    elif bidx == 0:
                        bt = Bt[1][0:nk, 0:nc4]
                    elif br == 2 and bidx == 4:
                        bt = Bt[2][0:nk, 0:nc4]
                    else:
                        bt = Bt[0][0:nk, 0:nc4]
                        if bidx > 1:
                            hb = [-slopes[4 * g + j] * 128.0 * (bidx - 1) for j in range(4)]
                    k.stt(tmp[0:nk, 0:nc4], psc[0:nk, 0:nc4], SCL, bt, ALU.mult, ALU.add)
                    if hb is None:
                        k.act(eb[0:nk, 0:nc4], tmp[0:nk, 0:nc4], AF.Exp)
                    else:
                        for j in range(4):
                            hc = g * 60 + (bidx - 1) * 4 + j
                            k.act(eb[0:nk, j * nq:(j + 1) * nq], tmp[0:nk, j * nq:(j + 1) * nq], AF.Exp, bias=hbias[0:nk, hc:hc + 1])
                    if br == 1:
                        k.mm(ps[5][0:nk, 128:128 + nq], Eexp[0:nblk, eexp_cols(slot)], selT[0:nblk, 0:nq])
                        k.tt(eb[0:nk, 0:nc4].rearrange("p (j q) -> p j q", j=4), eb[0:nk, 0:nc4].rearrange("p (j q) -> p j q", j=4),
                             ps[5][0:nk, 128:128 + nq].unsqueeze(1).to_broadcast([nk, 4, nq]), ALU.mult)
                    for j in range(4):
                        k.mm(ps[(6, 7, 0, 1)[j]][0:nq, 0:65], eb[0:nk, j * nq:(j + 1) * nq], Vt[0:nk, slot, g, :],
                             start=(ti == 0), stop=(ti == len(tiles) - 1))
                for j in range(4):
                    k.cp(OB[0:nq, j, 0:65], ps[(6, 7, 0, 1)[j]][0:nq, 0:65], eng="act")
                k.ts(rd[0:nq, 0:4], OB[0:nq, :, 64], 1e-30, ALU.max)
                k.recip(rd[0:nq, 0:4], rd[0:nq, 0:4])
                accum(False, br, OB[0:nq, :, 0:64])

    QTflat = QT[:].rearrange("p h q -> p (h q)")

    class _QTf:
        nq = 128

        def __getitem__(self, key):
            _, g, _ = key
            n4 = 4 * self.nq
            return QTflat[:, g * n4:(g + 1) * n4]
    QTf = _QTf()

    def qgz(npart, nq_cols):
        for h in range(16):
            for dc in range(8):
                k.mm(ps[7][0:64, (h % 4) * 128:(h % 4) * 128 + nq_cols], Win[:, dc, h * 64:(h + 1) * 64], xT[:, dc, 0:nq_cols],
                     start=(dc == 0), stop=(dc == 7))
            if h % 4 == 3:
                QTf.nq = nq_cols
                qdst = QTflat[:, (h - 3) * nq_cols:(h + 1) * nq_cols].rearrange("p (j q) -> p j q", j=4)
                k.cp(qdst, ps[7][0:64, :].rearrange("p (j q) -> p j q", j=4)[:, :, 0:nq_cols], eng="act")
        for i, (c0, c1) in enumerate(((2560, 3072), (3072, 3584), (3584, 3632))):
            for dc in range(8):
                k.mm(ps[i][0:npart, 0:c1 - c0], xT[:, dc, 0:npart], Win[:, dc, c0:c1], start=(dc == 0), stop=(dc == 7))
            k.cp(GZ[0:npart, c0 - 2560:c1 - 2560], ps[i][0:npart, 0:c1 - c0], eng="act")
        k.act(gates[0:npart, :], GZ[0:npart, 0:48], AF.Sigmoid)

    def finish(npart, dst_rows):
        k.act(T1[0:npart, :], GZ[0:npart, 48:1072], AF.Silu)
        k.tt(Gb[0:npart, :], Oacc[0:npart, :], T1[0:npart, :], ALU.mult)
        psb = ps[2][:].bitcast(BF16)
        for c in range(8):
            k.tr(psb[:, c * 128:c * 128 + npart], Gb[0:npart, c * 128:(c + 1) * 128], C.identb[0:npart, 0:npart])
        k.cp(gT[:, :, 0:npart], psb[:, 0:1024].rearrange("p (c t) -> p c t", c=8)[:, :, 0:npart], eng="act")
        for hf in range(2):
            cols = slice(hf * 512, (hf + 1) * 512)
            for c in range(8):
                k.mm(ps[hf][0:npart, :], gT[:, c, 0:npart], Wo[:, c, cols], start=(c == 0), stop=(c == 7))
            k.stt(T1[0:npart, cols], xin[0:npart, cols], ALPHA, ps[hf][0:npart, :], ALU.mult, ALU.add)
        ln_tail(C, T1, npart, L, dst_rows, Oacc, crow)

    def load_xT(rows_ap, npart):
        k.dma(xin[0:npart, :], rows_ap)
        for b in range(2):
            for c in range(4):
                k.tr(ps[b][:, c * 128:c * 128 + npart], xin[0:npart, (4 * b + c) * 128:(4 * b + c + 1) * 128], C.identf[0:npart, 0:npart])
            k.cp(xT[:, 4 * b:4 * b + 4, 0:npart], ps[b][:].rearrange("p (c t) -> p c t", c=4)[:, :, 0:npart], eng="act")

    def kv_proj(npart):
        for i in range(3):
            for dc in range(8):
                k.mm(ps[3 + i][0:npart, :], xT[:, dc, 0:npart], Win[:, dc, 1024 + i * 512:1024 + (i + 1) * 512], start=(dc == 0), stop=(dc == 7))
            k.cp(KV[0:npart, i * 512:(i + 1) * 512], ps[3 + i][0:npart, :], eng="act")

    k.memset(Vc[:], 0.0)
    k.memset(Vc[:, :, 64:65], 1.0)
    k.memset(KcT[:], 0.0)
    k.dma(Eexp[0:32, 0:2048], I["n_eexp_p"])
    ntile = tp // 128
    for t in range(ntile):
        r0 = t * 128
        load_xT(src[0][r0:r0 + 128, :], 128)
        kv_proj(128)
        for i, nm in enumerate(("cmpk", "cmpv", "selk", "selv")):
            k.dma(O[nm + "_p"][r0:r0 + 128, :], KV[:, i * 256:(i + 1) * 256])
        wr0 = r0 - (tp - min(512, tp))
        if wr0 >= 0:
            k.dma(O["wink_p"][wr0:wr0 + 128, :], KV[:, 1024:1280])
            k.dma(O["winv_p"][wr0:wr0 + 128, :], KV[:, 1280:1536])
        kv_tile(t, (KV[:, 0:256], KV[:, 256:512]), (KV[:, 512:768], KV[:, 768:1024]), (KV[:, 1024:1280], KV[:, 1280:1536]), 128,
                do_cmp_block=t, win_slot=t % 5)
        qgz(128, 128)
        s_tiles = [(kt, 128, t - kt) for kt in range(t + 1)]
        w_tiles = [(kt % 5, 128, t - kt) for kt in range(max(0, t - 4), t + 1)]
        attend(128, 32, s_tiles, w_tiles,
               lambda g: I["n_bc_p"][t, g], lambda d, g: I["n_bs_p"][d, g], lambda d, g: I["n_bw_p"][d, g],
               I["n_cb_p"][t], I["n_ft_p"][t], I["n_pair_p"], lambda slot: slice(slot * 128, (slot + 1) * 128))
        finish(128, dst[0][r0:r0 + 128, :])

    k.dma(idx[:], I["ptab"].rearrange("s n -> (s n)").partition_broadcast(128))
    k.cp(idf, idx[:])
    k.dma(wcol[:, 0:1], I["n_iota"])
    k.ts(idf, idf, 128.0, ALU.mult, wcol[:, 0:1], ALU.add)
    k.cp(idx[:], idf)
    k.dma(Eexp[0:33, 0:17 * 128], I["n_eexp_s"])
    load_xT(src[1][:, :], 128)
    kv_proj(128)
    KVs = C.kvs_scr
    k.dma(KVs, KV[:])
    for i, nm in enumerate(("cmpk", "cmpv", "selk", "selv")):
        k.dma(O[nm + "_s"], KV[:, i * 256:(i + 1) * 256])
    k.dma(SBS[:].rearrange("k (d g c) -> k d g c", d=17, g=4), I["n_bs_s"].rearrange("d g k c -> k d g c"))
    k.dma(SBW[:].rearrange("k (d g c) -> k d g c", d=5, g=4), I["n_bw_s"].rearrange("d g k c -> k d g c"))
    k.dma(SBC[:].rearrange("k (g c) -> k g c", g=4), I["n_bc_s"].rearrange("g k c -> k g c"))
    k.dma(cbt[0:8, 0:33], I["n_cb_s"])
    k.dma(cbt[0:8, 40:73], I["n_ft_s"])
    for g in range(4):
        k.dma(Vc[:, g, 65:98], I["n_pair_s"])
    xTs_all = sb("xTs_all", [128, 8, 128], BF16)
    k.cp(xTs_all[:], xT[:], eng="dve")
    NEW = KV[0:8, :]
    for sq in range(NS):
        k.memset(Vc[:, :, 0:64], 0.0)
        pools = (I["cmp_k"], I["cmp_v"], I["sel_k"], I["sel_v"])

        def gather(pool_ap, slot, dst_tile, sq=sq):
            P.dma("pool", dst_tile, pool_ap, reads=[pool_ap, idx], writes=[dst_tile],
                  fn=lambda e: e.indirect_dma_start(out=dst_tile, out_offset=None, in_=pool_ap,
                                                    in_offset=bass.IndirectOffsetOnAxis(ap=idx[:, sq * 16 + slot:sq * 16 + slot + 1], axis=0)))
        for pg in range(16):
            gb = GBUF[pg % 2]
            for ci in range(4):
                gather(pools[ci], pg, gb[:, ci * 256:(ci + 1) * 256])
            kv_tile(pg, (gb[:, 0:256], gb[:, 256:512]), (gb[:, 512:768], gb[:, 768:1024]), None, 128, do_cmp_block=pg)
        for wt in range(4):
            gb = GBUF[wt % 2]
            k.dma(gb[:, 0:256], I["win_k"][sq, wt * 128:(wt + 1) * 128, :])
            k.dma(gb[:, 256:512], I["win_v"][sq, wt * 128:(wt + 1) * 128, :])
            kv_tile(0, None, None, (gb[:, 0:256], gb[:, 256:512]), 128, win_slot=wt)
        k.dma(NEW, KVs[sq * 8:(sq + 1) * 8, :])
        kv_tile(16, None, (NEW[:, 512:768], NEW[:, 768:1024]), (NEW[:, 1024:1280], NEW[:, 1280:1536]), 8, win_slot=4)
        k.dma(O["wink_s"][sq, 0:504, :], I["win_k"][sq, 8:512, :])
        k.dma(O["winv_s"][sq, 0:504, :], I["win_v"][sq, 8:512, :], eng="act")
        k.dma(O["wink_s"][sq, 504:512, :], NEW[:, 1024:1280])
        k.dma(O["winv_s"][sq, 504:512, :], NEW[:, 1280:1536])
        k.cp(xT[:, :, 0:8], xTs_all[:, :, sq * 8:(sq + 1) * 8], eng="dve")
        k.dma(xin[0:8, :], src[1][sq * 8:(sq + 1) * 8, :])
        qgz(8, 8)
        s_tiles = [(kt, 128, kt) for kt in range(16)] + [(16, 8, 16)]
        w_tiles = [(kt, 128, kt) for kt in range(4)] + [(4, 8, 4)]
        attend(8, 33, s_tiles, w_tiles,
               lambda g: SBC[:, g * 32:(g + 1) * 32],
               lambda d, g: SBS[0:(8 if d == 16 else 128), d * 128 + g * 32:d * 128 + (g + 1) * 32],
               lambda d, g: SBW[0:(8 if d == 4 else 128), d * 128 + g * 32:d * 128 + (g + 1) * 32],
               None, None, None, lambda slot: slice(slot * 128, slot * 128 + (8 if slot == 16 else 128)), load_consts=False, resident=True)
        finish(8, dst[1][sq * 8:(sq + 1) * 8, :])


def _cmask():
    m = np.zeros((128, 2048), np.float32)
    t = np.arange(128)
    for g, w in enumerate((2, 4, 8, 16)):
        m[:, g * 128:(g + 1) * 128] = (1.0 / np.minimum(w, t + 1))[None, :]
    a = np.arange(64)
    su = (a[:, None] < a[None, :]).astype(np.float32)
    ui = (a[:, None] <= a[None, :]).astype(np.float32)
    m[0:64, 512:576] = su
    m[0:64, 576:640] = ui
    m[0:64, 640:704] = su.T
    m[0:64, 704:768] = ui
    m[0:64, 768:832] = np.eye(64, dtype=np.float32)
    return m


def consts():
    import ml_dtypes
    sel = np.zeros((128, 64, 128), np.float32)
    for kk in range(128):
        sel[kk, kk % 64, (kk // 64) * 64:(kk // 64) * 64 + 64] = 1
    return {"identf": np.eye(128, dtype=np.float32), "selb": sel.reshape(128, 64 * 128).astype(ml_dtypes.bfloat16),
            "cmask": _cmask()}


def shard_inputs(inp, c, tp=TP):
    f = lambda a: np.ascontiguousarray(a)
    m = {
        "xp": f(inp["x_prompt"][c, :tp]), "xs": f(inp["x_sample"][16 * c:16 * c + 16].reshape(128, D)),
        "st_S": f(inp["state_rwkv_S"][:, 16 * c:16 * c + 16]), "st_shift": f(inp["state_rwkv_shift"][:, 16 * c:16 * c + 16]),
        "st_pool": f(inp["state_pool"][0, 16 * c:16 * c + 16]),
        "cmp_k": f(inp["cache_cmp_k"][0].reshape(-1, 256)), "cmp_v": f(inp["cache_cmp_v"][0].reshape(-1, 256)),
        "sel_k": f(inp["cache_sel_k"][0].reshape(-1, 256)), "sel_v": f(inp["cache_sel_v"][0].reshape(-1, 256)),
        "win_k": f(inp["state_win_k"][0, 16 * c:16 * c + 16].reshape(16, 512, 256)),
        "win_v": f(inp["state_win_v"][0, 16 * c:16 * c + 16].reshape(16, 512, 256)),
        "ptab": f(inp["page_table"][16 * c:16 * c + 16]).astype(np.int32),
        "a_r_k": f(inp["a_r_k"].reshape(2, D)), "b_w_in": f(inp["b_w_in"][0]), "b_w_grp": f(inp["b_w_grp"][0]),
        "b_scale": f(inp["b_scale"]), "b_w_out": f(inp["b_w_out"][0]), "c_w_in": f(inp["c_w_in"][0]),
        "c_cmp_wk": f(inp["c_cmp_wk"]), "c_cmp_wv": f(inp["c_cmp_wv"]), "c_w_out": f(inp["c_w_out"][0]),
    }
    for nm in ("ln_g", "ln_b", "a_w_in", "a_mu", "a_w0", "a_w2", "a_a0", "a_a2", "a_k_k", "a_k_a", "a_lnx_g", "a_lnx_b", "a_w_out"):
        m[nm] = f(inp[nm])
    m.update(consts())
    m.update(nsa_consts())
    return m


def nsa_consts():
    sl = 2.0 ** (-8.0 * (np.arange(16) + 1) / 16)
    c = {}

    def bias(dist, valid, g):
        K_, nq = dist.shape
        out = np.empty((K_, 4, nq), np.float32)
        for j in range(4):
            out[:, j] = np.where(valid, -sl[4 * g + j] * dist, NEG)
        return out.reshape(K_, 4 * nq)
    q = np.arange(128)[None, :]
    kk = np.arange(128)[:, None]
    n = np.arange(64)[:, None]
    bc = np.zeros((16, 4, 64, 512), np.float32)
    bs = np.zeros((16, 4, 128, 512), np.float32)
    bw = np.zeros((5, 4, 128, 512), np.float32)
    for g in range(4):
        for t in range(16):
            d = 128 * t + q - 32 * n - 31
            bc[t, g] = bias(d, d >= 0, g)
            d = 128 * t + q - kk
            bs[t, g] = bias(d, d >= 0, g)
            if t < 5:
                bw[t, g] = bias(d, (d >= 0) & (d < 512), g)
    c["n_bc_p"], c["n_bs_p"], c["n_bw_p"] = bc, bs, bw
    cb = np.zeros((16, 128, 32), np.float32)
    ft = np.zeros((16, 128, 32), np.float32)
    blk = np.arange(32)[None, :]
    for t in range(16):
        cur = ((128 * t + np.arange(128)) // 64)[:, None]
        cb[t] = (blk < cur)
        ft[t] = np.where(blk == cur, 1e9, np.where(blk > cur, -1.0, 0.0))
    c["n_cb_p"], c["n_ft_p"] = cb, ft
    c["n_pair_p"] = (np.arange(64)[:, None] // 2 == np.arange(32)[None, :]).astype(np.float32)
    c["n_eexp_p"] = (np.arange(2048)[None, :] // 64 == np.arange(32)[:, None]).astype(np.float32)
    wbm = np.zeros((128, 124), np.float32)
    for r in range(128):
        wbm[r, 60 + r // 32] = 1.0
    c["n_wbm"] = wbm
    c["n_iota"] = np.arange(128, dtype=np.float32).reshape(128, 1)
    tq = np.arange(8)[None, :]
    bcs = np.zeros((4, 64, 32), np.float32)
    bss = np.zeros((17, 4, 128, 32), np.float32)
    bws = np.zeros((5, 4, 128, 32), np.float32)
    for g in range(4):
        d = 2048 + tq - 32 * n - 31
        bcs[g] = bias(d, d >= 0, g)
        for kt in range(16):
            d = 2048 + tq - 128 * kt - kk
            bss[kt, g] = bias(d, d >= 0, g)
        d = tq - kk
        newb = bias(d, (d >= 0) & (kk < 8), g)
        bss[16, g] = newb
        for kt in range(4):
            d = 2048 + tq - (1536 + 128 * kt + kk)
            bws[kt, g] = bias(d, (d >= 0) & (d < 512), g)
        bws[4, g] = newb
    c["n_bc_s"], c["n_bs_s"], c["n_bw_s"] = bcs, bss, bws
    cbs = np.ones((8, 33), np.float32)
    cbs[:, 32] = 0
    fts = np.zeros((8, 33), np.float32)
    fts[:, 32] = 1e9
    c["n_cb_s"], c["n_ft_s"] = cbs, fts
    ps_ = np.zeros((64, 33), np.float32)
    ps_[:, :32] = c["n_pair_p"]
    c["n_pair_s"] = ps_
    ee = np.zeros((33, 17 * 128), np.float32)
    ee[:32, :2048] = c["n_eexp_p"]
    ee[32, 2048:] = 1.0
    c["n_eexp_s"] = ee
    hb = np.zeros((128, 240), np.float32)
    for g in range(4):
        for d in range(1, 16):
            for j in range(4):
                hb[:, g * 60 + (d - 1) * 4 + j] = -sl[4 * g + j] * 128.0 * (d - 1)
    c["n_hb"] = hb
    return c


_NC_CACHE = {}


def kernel(**inputs):
    n = 8
    npool = inputs["cache_cmp_k"].shape[1]
    key = (npool,)
    if key not in _NC_CACHE:
        _NC_CACHE[key] = build(npool=npool, tp=TP)
    nc = _NC_CACHE[key]
    in_maps = [shard_inputs(inputs, c) for c in range(n)]
    res = run_bass_kernel_spmd(nc, in_maps, core_ids=list(range(n)))
    R = res.results
    cat = lambda nm: np.stack([R[c][nm] for c in range(n)], 0)
    y_p = cat("y_p")
    y_s = np.concatenate([R[c]["y_s"].reshape(16, 8, D) for c in range(n)], 0)
    S_p = np.stack([R[c]["S_p"] for c in range(n)], 1)
    S_s = np.concatenate([R[c]["S_s"] for c in range(n)], 1)
    sh_p = np.stack([R[c]["sh_p"] for c in range(n)], 1)
    sh_s = np.concatenate([R[c]["sh_s"] for c in range(n)], 1)
    pl_p = cat("pl_p")[None]
    pl_s = np.concatenate([R[c]["pl_s"] for c in range(n)], 0)[None]
    outs = [y_p, y_s, S_p, S_s, sh_p, sh_s, pl_p, pl_s]
    for nm in ("cmpk", "cmpv", "selk", "selv"):
        outs.append(cat(nm + "_p").reshape(1, n, TP, 4, 64))
        outs.append(np.concatenate([R[c][nm + "_s"].reshape(16, 8, 4, 64) for c in range(n)], 0)[None])
    for nm in ("wink", "winv"):
        outs.append(cat(nm + "_p").reshape(1, n, 512, 4, 64))
        outs.append(np.concatenate([R[c][nm + "_s"].reshape(16, 512, 4, 64) for c in range(n)], 0)[None])
    return tuple(np.ascontiguousarray(o, dtype=np.float32) for o in outs)
```

```python
import contextlib
import numpy as np
import concourse.bass as bass
import concourse.mybir as mybir
from concourse.bass_utils import run_bass_kernel_spmd

F32 = mybir.dt.float32
BF16 = mybir.dt.bfloat16
I32 = mybir.dt.int32
ALU = mybir.AluOpType
AF = mybir.ActivationFunctionType
AX = mybir.AxisListType

import os as _os
STRICT = bool(_os.environ.get("KSTRICT"))
ENGS = ("pe", "dve", "act", "pool", "sp")
NDMA = {"sp": 12, "act": 6, "pool": 6}

D = 1024
TP = 2048
NS = 16
TS = 8
DEPTH = 4
ALPHA = (2.0 * DEPTH) ** 0.25
LN_EPS = 1e-5
A_NC = 4224
GN_EPS = 64e-5
C_NC = 3632


def _key(k):
    if isinstance(k, (str, tuple)):
        return k
    t = getattr(k, "tensor", k)
    return getattr(t, "name", str(t))


class Prog:
    def __init__(self, nc):
        self.nc = nc
        self.q = {e: [] for e in ENGS}
        self.cnt = {e: 0 for e in ENGS}
        self.known = {e: {} for e in ENGS}
        self.lastw = {}
        self.readers = {}
        self.dma_rr = {e: 0 for e in NDMA}
        self.dma_cnt = {}
        self.n_inst = 0

    def _deps(self, reads, writes):
        deps = {}

        def add(ev):
            if ev is None:
                return
            s, v = ev
            if deps.get(s, 0) < v:
                deps[s] = v
        for k in reads:
            add(self.lastw.get(k))
        for k in writes:
            add(self.lastw.get(k))
            for ev in self.readers.get(k, ()):
                add(ev)
        return deps

    def _commit(self, ev, reads, writes):
        for k in reads:
            self.readers.setdefault(k, []).append(ev)
        for k in writes:
            self.lastw[k] = ev
            self.readers[k] = []

    def _waits(self, eng, deps, compute=False):
        waits = []
        kn = self.known[eng]
        for s, v in deps.items():
            if s == "c_pe" and eng == "pe":
                continue
            if compute and not STRICT and s == "c_" + eng and eng in ("dve", "act") and v < self.cnt[eng]:
                continue
            if kn.get(s, 0) >= v:
                continue
            kn[s] = v
            waits.append((s, v))
        return waits

    def op(self, eng, fn, reads=(), writes=()):
        reads = [_key(k) for k in reads]
        writes = [_key(k) for k in writes]
        writes = writes + [r for r in reads if isinstance(r, str) and r.startswith("psb")]
        waits = self._waits(eng, self._deps(reads, writes), compute=True)
        self.cnt[eng] += 1
        ev = ("c_" + eng, self.cnt[eng])
        self.q[eng].append(("op", waits, fn, ev))
        self._commit(ev, reads, writes)
        self.n_inst += 1
        return ev

    def dma(self, eng, out, in_, reads=None, writes=None, fn=None, **kw):
        reads = [_key(k) for k in (reads if reads is not None else [in_])]
        writes = [_key(k) for k in (writes if writes is not None else [out])]
        deps = self._deps(reads, writes)
        i = self.dma_rr[eng]
        self.dma_rr[eng] = (i + 1) % NDMA[eng]
        sname = "d_%s%d" % (eng, i)
        n = self.dma_cnt.get(sname, 0)
        if n > 0 and deps.get(sname, 0) < 16 * n:
            deps[sname] = 16 * n
        waits = self._waits(eng, deps)
        self.dma_cnt[sname] = n + 1
        ev = (sname, 16 * (n + 1))
        self.q[eng].append(("dma", waits, (out, in_, kw, fn), ev))
        self._commit(ev, reads, writes)
        self.n_inst += 1
        return ev

    def barrier(self):
        for eng in ENGS:
            deps = {}
            for f in ENGS:
                if f != "sp" and f != eng and self.cnt[f] > 0:
                    deps["c_" + f] = self.cnt[f]
            for s, n in self.dma_cnt.items():
                deps[s] = 16 * n
            waits = self._waits(eng, deps)
            self.q[eng].append(("wait", waits, None, None))

    def emit(self):
        nc = self.nc
        names = ["c_" + e for e in ENGS if e != "sp"]
        for e, n in NDMA.items():
            names += ["d_%s%d" % (e, i) for i in range(n)]
        with contextlib.ExitStack() as st:
            sems = {nm: st.enter_context(nc.semaphore(nm)) for nm in names}
            block = st.enter_context(nc.Block())

            def run(eng):
                def body(e):
                    for kind, waits, payload, ev in self.q[eng]:
                        for s, v in waits:
                            e.wait_ge(sems[s], v)
                        if kind == "op":
                            payload(e).then_inc(sems[ev[0]], 1)
                        elif kind == "dma":
                            out, in_, kw, fn = payload
                            if fn is not None:
                                fn(e).then_inc(sems[ev[0]], 16)
                            else:
                                e.dma_start(out=out, in_=in_, **kw).then_inc(sems[ev[0]], 16)
                    if eng == "sp":
                        for sname, n in self.dma_cnt.items():
                            e.wait_ge(sems[sname], 16 * n)
                        for en in ENGS:
                            if en != "sp" and self.cnt[en] > 0:
                                e.wait_ge(sems["c_" + en], self.cnt[en])
                return body

            block.sync(run("sp"))
            block.tensor(run("pe"))
            block.vector(run("dve"))
            block.scalar(run("act"))
            block.gpsimd(run("pool"))


def _aps(*xs):
    return [x for x in xs if x is not None and not isinstance(x, (int, float))]


class K:
    def __init__(self, P):
        self.P = P

    def mm(self, out, lhsT, rhs, start=True, stop=True):
        self.P.op("pe", lambda e: e.matmul(out, lhsT=lhsT, rhs=rhs, start=start, stop=stop),
                  reads=[lhsT, rhs], writes=[out])

    def tr(self, out, in_, ident):
        self.P.op("pe", lambda e: e.transpose(out, in_, ident), reads=[in_, ident], writes=[out])

    def tt(self, out, a, b, op, eng="dve"):
        self.P.op(eng, lambda e: e.tensor_tensor(out=out, in0=a, in1=b, op=op), reads=[a, b], writes=[out])

    def ts(self, out, a, s1, op0, s2=None, op1=None, eng="dve"):
        if op1 is None:
            fn = lambda e: e.tensor_scalar(out=out, in0=a, scalar1=s1, scalar2=None, op0=op0)
        else:
            fn = lambda e: e.tensor_scalar(out=out, in0=a, scalar1=s1, scalar2=s2, op0=op0, op1=op1)
        self.P.op(eng, fn, reads=_aps(a, s1, s2), writes=[out])

    def stt(self, out, a, s, b, op0, op1, eng="dve"):
        self.P.op(eng, lambda e: e.scalar_tensor_tensor(out=out, in0=a, scalar=s, in1=b, op0=op0, op1=op1),
                  reads=_aps(a, s, b), writes=[out])

    def red(self, out, in_, op=ALU.add, negate=False, axis=AX.X):
        self.P.op("dve", lambda e: e.tensor_reduce(out=out, in_=in_, axis=axis, op=op, negate=negate),
                  reads=[in_], writes=[out])

    def cp(self, out, in_, eng="dve"):
        if eng == "act":
            self.P.op("act", lambda e: e.copy(out, in_), reads=[in_], writes=[out])
        else:
            self.P.op(eng, lambda e: e.tensor_copy(out, in_), reads=[in_], writes=[out])

    def act(self, out, in_, func, bias=None, scale=None, accum=None):
        kw = {}
        if bias is not None:
            kw["bias"] = bias
        if scale is not None:
            kw["scale"] = scale
        if accum is not None:
            kw["accum_out"] = accum
        self.P.op("act", lambda e: e.activation(out=out, in_=in_, func=func, **kw),
                  reads=_aps(in_, bias, scale), writes=_aps(out, accum))

    def recip(self, out, in_):
        self.P.op("dve", lambda e: e.reciprocal(out, in_), reads=[in_], writes=[out])

    def memset(self, ap, v, eng="pool"):
        self.P.op(eng, lambda e: e.memset(ap, v), writes=[ap])

    def dma(self, out, in_, eng="sp", **kw):
        self.P.dma(eng, out, in_, **kw)


def bc(ap, shape):
    return ap.to_broadcast(shape)


class Ctx:
    pass


def build(npool=2560, tp=TP, layers=(0, 1, 2, 3), dbg=False):
    nc = bass.Bass("TRN2", target_bir_lowering=False)
    C = Ctx()
    C.nc = nc
    C.tp = tp
    P = Prog(nc)
    k = K(P)
    C.P, C.k = P, k

    def din(name, shape, dt=F32):
        return nc.dram_tensor(name, list(shape), dt, kind="ExternalInput").ap()

    def dout(name, shape):
        return nc.dram_tensor(name, list(shape), F32, kind="ExternalOutput").ap()

    def dscr(name, shape, dt=F32):
        return nc.dram_tensor(name, list(shape), dt, kind="Internal").ap()

    I = {}
    for nm, shp in [("xp", (tp, D)), ("xs", (128, D)), ("st_S", (2, NS, 16, 64, 64)), ("st_shift", (2, NS, A_NC)),
                    ("st_pool", (NS, 15, D)), ("cmp_k", (npool * 128, 256)), ("cmp_v", (npool * 128, 256)),
                    ("sel_k", (npool * 128, 256)), ("sel_v", (npool * 128, 256)), ("win_k", (NS, 512, 256)),
                    ("win_v", (NS, 512, 256)), ("ln_g", (4, D)), ("ln_b", (4, D)), ("a_w_in", (2, D, A_NC)),
                    ("a_mu", (2, A_NC)), ("a_w0", (2, D)), ("a_w2", (2, 64, D)), ("a_a0", (2, D)), ("a_a2", (2, 64, D)),
                    ("a_k_k", (2, D)), ("a_k_a", (2, D)), ("a_r_k", (2, D)), ("a_lnx_g", (2, D)), ("a_lnx_b", (2, D)),
                    ("a_w_out", (2, D, D)), ("b_w_in", (D, 2 * D)), ("b_w_grp", (4, 256, 256)), ("b_scale", (1, D)),
                    ("b_w_out", (D, D)), ("c_w_in", (D, C_NC)), ("c_cmp_wk", (1, 32)), ("c_cmp_wv", (1, 32)),
                    ("c_w_out", (D, D)), ("identf", (128, 128)), ("cmask", (128, 2048))]:
        I[nm] = din(nm, shp)
    I["ptab"] = din("ptab", (NS, 16), I32)
    for nm, shp in [("n_bc_p", (16, 4, 64, 512)), ("n_bs_p", (16, 4, 128, 512)), ("n_bw_p", (5, 4, 128, 512)),
                    ("n_cb_p", (16, 128, 32)), ("n_ft_p", (16, 128, 32)), ("n_pair_p", (64, 32)),
                    ("n_wbm", (128, 124)), ("n_iota", (128, 1)), ("n_bc_s", (4, 64, 32)), ("n_bs_s", (17, 4, 128, 32)),
                    ("n_bw_s", (5, 4, 128, 32)), ("n_cb_s", (8, 33)), ("n_ft_s", (8, 33)), ("n_pair_s", (64, 33)),
                    ("n_hb", (128, 240))]:
        I[nm] = din(nm, shp)
    I["n_eexp_p"] = din("n_eexp_p", (32, 2048), BF16)
    I["n_eexp_s"] = din("n_eexp_s", (33, 17 * 128), BF16)
    I["selb"] = din("selb", (128, 64 * 128), BF16)
    O = {}
    for nm, shp in [("y_p", (tp, D)), ("y_s", (128, D)), ("S_p", (2, 16, 64, 64)), ("S_s", (2, NS, 16, 64, 64)),
                    ("sh_p", (2, A_NC)), ("sh_s", (2, NS, A_NC)), ("pl_p", (15, D)), ("pl_s", (NS, 15, D)),
                    ("cmpk_p", (tp, 256)), ("cmpk_s", (128, 256)), ("cmpv_p", (tp, 256)), ("cmpv_s", (128, 256)),
                    ("selk_p", (tp, 256)), ("selk_s", (128, 256)), ("selv_p", (tp, 256)), ("selv_s", (128, 256)),
                    ("wink_p", (512, 256)), ("wink_s", (NS, 512, 256)), ("winv_p", (512, 256)), ("winv_s", (NS, 512, 256))]:
        O[nm] = dout(nm, shp)
    if dbg:
        O["dbg_p"] = dout("dbg_p", (tp, D))
        O["dbg_s"] = dout("dbg_s", (128, D))
    C.I, C.O = I, O
    xa_p, xa_s = dscr("xa_p", (tp, D)), dscr("xa_s", (128, D))
    xb_p, xb_s = dscr("xb_p", (tp, D)), dscr("xb_s", (128, D))
    C.wbf = dscr("wbf", (128, 8, A_NC), BF16)
    C.kvs_scr = dscr("kvs_scr", (128, 1536))

    with contextlib.ExitStack() as gst:
        C.identf = gst.enter_context(nc.sbuf_tensor("identf_sb", [128, 128], F32))
        C.identb = gst.enter_context(nc.sbuf_tensor("identb_sb", [128, 128], BF16))
        C.ps = [gst.enter_context(nc.psum_tensor("psb%d" % i, [128, 512], F32)) for i in range(8)]
        k.dma(C.identf[:], I["identf"])
        k.cp(C.identb[:], C.identf[:])
        chain = [(I["xp"], I["xs"]), (xa_p, xa_s), (xb_p, xb_s), (xa_p, xa_s), (O["y_p"], O["y_s"])]
        for L in range(DEPTH):
            if L not in layers:
                continue
            src, dst = chain[L], chain[L + 1]
            if L == max(layers) and dbg:
                dst = (O["dbg_p"], O["dbg_s"])
            P.barrier()
            with contextlib.ExitStack() as lst:
                if L % 3 == 0:
                    rwkv_layer(C, lst, L // 3, L, src, dst)
                elif L % 3 == 1:
                    pool_layer(C, lst, L, src, dst)
                else:
                    nsa_layer(C, lst, L, src, dst)
                P.barrier()
        P.emit()
    return nc


def ln_tail(C, R, npart, L, dst_rows, T1, crow):
    k, I = C.k, C.I
    st = C.lnst
    k.red(st[0:npart, 0:1], R[0:npart, :])
    k.ts(st[0:npart, 1:2], st[0:npart, 0:1], 1.0 / D, ALU.mult)
    k.ts(R[0:npart, :], R[0:npart, :], st[0:npart, 1:2], ALU.subtract)
    k.tt(T1[0:npart, :], R[0:npart, :], R[0:npart, :], ALU.mult)
    k.red(st[0:npart, 2:3], T1[0:npart, :])
    k.act(st[0:npart, 3:4], st[0:npart, 2:3], AF.Sqrt, bias=C.epsln[0:npart, :], scale=1.0 / D)
    k.recip(st[0:npart, 4:5], st[0:npart, 3:4])
    k.ts(R[0:npart, :], R[0:npart, :], st[0:npart, 4:5], ALU.mult)
    k.dma(crow[0][0:npart, :], I["ln_g"][L:L + 1, :].partition_broadcast(npart), eng="act")
    k.tt(R[0:npart, :], R[0:npart, :], crow[0][0:npart, :], ALU.mult)
    k.dma(crow[1][0:npart, :], I["ln_b"][L:L + 1, :].partition_broadcast(npart), eng="act")
    k.tt(R[0:npart, :], R[0:npart, :], crow[1][0:npart, :], ALU.add)
    k.dma(dst_rows, R[0:npart, :])


def rwkv_layer(C, lst, li, L, src, dst):
    nc, P, k, I, O = C.nc, C.P, C.k, C.I, C.O
    tp = C.tp
    sb = lambda n, s, d=F32: lst.enter_context(nc.sbuf_tensor("a%d_" % L + n, list(s), d))
    ps = C.ps
    Wo = sb("Wo", [128, 8, 1024], BF16)
    Wll = sb("Wll", [128, 8, 128], BF16)
    WG = [sb("WG0", [128, 8, 1024], BF16)]
    W2A2 = sb("W2A2", [128, 1024])
    mucol = sb("mucol", [128, 9])
    SEL = sb("SEL", [128, 64, 128], BF16)
    xin2 = sb("xin2", [128, 1024])
    xin = sb("xin", [64, 1024])
    xTd = sb("xTd", [128, 8, 128], BF16)
    xTsd = sb("xTsd", [128, 8, 128], BF16)
    Pt = sb("Pt", [128, 1024])
    PSt = sb("PSt", [128, 1024])
    PM = {g: sb("PM" + g, [128, 1024]) for g in "rkvz"}
    crow = [sb("crow%d" % i, [128, 1024]) for i in range(2)]
    At = sb("At", [128, 1024])
    KP = sb("KP", [128, 1024])
    T1 = sb("T1", [128, 1024])
    T2 = sb("T2", [128, 1024])
    XRf = sb("XRf", [128, 512])
    XRr = sb("XRr", [128, 512])
    XR = {x: [sb("XR%s%d" % (x, j), [128, 512], BF16) for j in range(2)] for x in ("kk", "w", "ka", "k", "r")}
    va = sb("va", [128, 8, 64])
    vs = sb("vs", [128, 8, 64])
    vT = sb("vT", [128, 8, 64])
    lla = sb("lla", [128, 128])
    llb = sb("llb", [128, 128])
    LLt = sb("LLt", [128, 128])
    YT = sb("YT", [128, 8, 64])
    S = sb("S", [128, 8, 64])
    t1 = sb("t1", [128, 8, 64])
    t2 = sb("t2", [128, 8, 64])
    t3 = sb("t3", [128, 8, 64])
    sa = sb("sa", [128, 8])
    st16 = sb("st16", [128, 5, 16])
    bon = sb("bon", [128, 16])
    G = sb("G", [64, 1024], BF16)
    gT = sb("gT", [128, 8, 64], BF16)
    C.lnst = sb("lnst", [128, 8])
    C.epsln = sb("epsln", [128, 1])
    epsgn = sb("epsgn", [128, 1])
    eps24 = sb("eps24", [128, 1])
    k.memset(C.epsln[:], LN_EPS)
    k.memset(epsgn[:], GN_EPS)
    k.memset(eps24[:], 0.0)

    w_in = I["a_w_in"][li].rearrange("(c p) n -> p c n", p=128)
    for j in range(A_NC // 128):
        stg = Pt[:].rearrange("p (c n) -> p c n", c=8) if j % 2 == 0 else PSt[:].rearrange("p (c n) -> p c n", c=8)
        stgb = (T1 if j % 2 == 0 else T2)[:].bitcast(BF16)[:, 0:1024].rearrange("p (c n) -> p c n", c=8)
        k.dma(stg, w_in[:, :, j * 128:(j + 1) * 128], eng="sp" if j % 2 == 0 else "act")
        k.cp(stgb, stg, eng="pool" if j % 2 == 0 else "act")
        k.dma(C.wbf[:, :, j * 128:(j + 1) * 128], stgb, eng="sp")
    w_out = I["a_w_out"][li].rearrange("(c p) n -> p c n", p=128)
    for j in range(8):
        stg = Pt[:].rearrange("p (c n) -> p c n", c=8) if j % 2 == 0 else PSt[:].rearrange("p (c n) -> p c n", c=8)
        k.dma(stg, w_out[:, :, j * 128:(j + 1) * 128], eng="sp" if j % 2 == 0 else "act")
        k.cp(Wo[:, :, j * 128:(j + 1) * 128], stg, eng="pool" if j % 2 == 0 else "act")
    k.dma(Wll[:], C.wbf[:, :, 4096:4224])
    k.dma(W2A2[0:64, :], I["a_w2"][li])
    k.dma(W2A2[64:128, :], I["a_a2"][li])
    k.dma(mucol[:, 0:8], I["a_mu"][li, 2048:3072].rearrange("(c p) -> p c", p=128), allow_slow_non_contiguous=True)
    k.dma(mucol[:, 8:9], I["a_mu"][li, 4096:4224].rearrange("(c p) -> p c", p=128), allow_slow_non_contiguous=True)
    k.dma(SEL[:], I["selb"].rearrange("p (t m) -> p t m", m=128))
    k.memset(S[:], 0.0)
    EPI = sb("EPI", [64, 1024])
    EPN = sb("EPN", [64, 1024])
    EPX = sb("EPX", [64, 1024])
    FMAR = sb("FMAR", [64, 8, 128])
    FMB = sb("FMB", [64, 8, 64])
    FMK = sb("FMK", [64, 8, 64])
    GB = sb("GB", [64, 8, 128])
    GK = sb("GK", [64, 8, 128])
    PQ = [sb("PQ%d" % i, [64, 8, 64]) for i in range(4)]
    Tm = sb("Tm", [64, 8, 64])
    XT = sb("XT", [64, 8, 64])
    UT = sb("UT", [64, 8, 64])
    ST = sb("ST", [64, 16, 64])
    PCc = sb("PCc", [64, 16])
    MK = sb("MK", [64, 320])
    k.dma(MK[:], I["cmask"][0:64, 512:832])
    MASKAR = MK[:, 0:128]
    MASKNT = MK[:, 128:192]
    TRI = MK[:, 192:256]
    IDN = MK[:, 256:320]
    k.memset(ST[:], 0.0)
    pbi = [0]

    def bank():
        pbi[0] = (pbi[0] + 1) % 8
        return ps[pbi[0]]
    import os
    STOP = int(os.environ.get('STOPAT', '99'))
    if STOP <= 1:
        return

    cri = [0]

    def jrow(src_row, npart=128):
        t = crow[cri[0] % 2]
        cri[0] += 1
        k.dma(t[0:npart, :], src_row.partition_broadcast(npart), eng="act")
        return t

    def h4(t):
        return t[:].rearrange("p (a b j) -> p a b j", a=8, b=2)

    def toxr(X, name):
        X4 = h4(X)
        o3 = XRf[:].rearrange("p (a j) -> p a j", a=8)
        k.cp(o3[0:64], X4[0:64, :, 0, :], eng="act")
        k.cp(o3[64:128], X4[64:128, :, 1, :], eng="act")
        k.cp(XR[name][0][:], XRf[:], eng="pool")
        k.tt(XRr[:], XRf[:], XR[name][0][:], ALU.subtract, eng="pool")
        k.cp(XR[name][1][:], XRr[:], eng="pool")

    ntile_p = tp // 64
    import os
    tiles = [("p", n) for n in range(ntile_p)] + ([("s", 0), ("s", 1)] if not os.environ.get("NOSAMPLE") else [])
    wg_i = [0]
    for kind, n in tiles:
        srcx = src[0] if kind == "p" else src[1]
        dstx = dst[0] if kind == "p" else dst[1]
        r0 = n * 64
        if r0 == 0:
            k.memset(xin2[0:1, :], 0.0)
            k.dma(xin2[1:64, :], srcx[0:63, :])
        else:
            k.dma(xin2[0:64, :], srcx[r0 - 1:r0 + 63, :])
        k.dma(xin[:], srcx[r0:r0 + 64, :], eng="act")
        for b in range(2):
            for c in range(4):
                k.tr(ps[b][:, c * 64:(c + 1) * 64], xin[0:64, (4 * b + c) * 128:(4 * b + c + 1) * 128], C.identf[0:64, 0:64])
            for c in range(4):
                k.tr(ps[b][:, 256 + c * 64:256 + (c + 1) * 64], xin2[0:64, (4 * b + c) * 128:(4 * b + c + 1) * 128], C.identf[0:64, 0:64])
            pv = ps[b][:, 0:256].rearrange("p (c t) -> p c t", c=4)
            pw = ps[b][:, 256:512].rearrange("p (c t) -> p c t", c=4)
            k.cp(xTd[:, 4 * b:4 * b + 4, 0:64], pv, eng="act")
            k.cp(xTd[:, 4 * b:4 * b + 4, 64:128], pv, eng="dve")
            k.cp(xTsd[:, 4 * b:4 * b + 4, 0:64], pw, eng="act")
            k.cp(xTsd[:, 4 * b:4 * b + 4, 64:128], pw, eng="dve")
        if STOP <= 2:
            return
        last_rows = []
        if kind == "p" and n == ntile_p - 1:
            last_rows = [(63, O["sh_p"][li])]
        if kind == "s":
            last_rows = [(sl * 8 + 7, O["sh_s"][li, n * 8 + sl]) for sl in range(8)]
        for gi, g in enumerate("rkvz"):
            wg = WG[0]
            wg_i[0] += 1
            k.dma(wg[:], C.wbf[:, :, gi * 1024:(gi + 1) * 1024], eng="sp")
            for hf in range(2):
                cols = slice(hf * 512, (hf + 1) * 512)
                for c in range(8):
                    k.mm(ps[2][:], xTd[:, c, :], wg[:, c, cols], start=(c == 0), stop=(c == 7))
                for c in range(8):
                    k.mm(ps[3][:], xTsd[:, c, :], wg[:, c, cols], start=(c == 0), stop=(c == 7))
                k.cp(Pt[:, cols], ps[2][:], eng="act")
                k.cp(PSt[:, cols], ps[3][:], eng="act")
            if g == "v" and kind == "s":
                for c in range(8):
                    for dc in range(8):
                        k.mm(ps[2][:, c * 64:(c + 1) * 64], wg[:, dc, c * 128:(c + 1) * 128], xTd[:, dc, 0:64],
                             start=(dc == 0), stop=(dc == 7))
                for c in range(8):
                    for dc in range(8):
                        k.mm(ps[3][:, c * 64:(c + 1) * 64], wg[:, dc, c * 128:(c + 1) * 128], xTsd[:, dc, 0:64],
                             start=(dc == 0), stop=(dc == 7))
                k.cp(va[:].rearrange("p c t -> p (c t)"), ps[2][:], eng="act")
                k.cp(vs[:].rearrange("p c t -> p (c t)"), ps[3][:], eng="act")
                if kind == "s":
                    for sl in range(8):
                        k.dma(vs[:, :, sl * 8], I["st_shift"][li, n * 8 + sl, 2048:3072].rearrange("(c p) -> p c", p=128), eng="act", allow_slow_non_contiguous=True)
                k.tt(vs[:], vs[:], va[:], ALU.subtract)
                k.tt(vs[:], vs[:], mucol[:, 0:8].unsqueeze(2).to_broadcast([128, 8, 64]), ALU.mult)
                k.tt(vT[:], vs[:], va[:], ALU.add)
            if kind == "s":
                for sl in range(8):
                    for hh in range(2):
                        k.dma(PSt[hh * 64 + sl * 8:hh * 64 + sl * 8 + 1, :],
                              I["st_shift"][li, n * 8 + sl:n * 8 + sl + 1, gi * 1024:(gi + 1) * 1024], eng="act")
            for (row, dap) in last_rows:
                k.dma(dap[gi * 1024:(gi + 1) * 1024].unsqueeze(0), Pt[row:row + 1, :], eng="act")
            mur = jrow(I["a_mu"][li:li + 1, gi * 1024:(gi + 1) * 1024])
            k.tt(PSt[:], PSt[:], Pt[:], ALU.subtract)
            k.tt(PSt[:], PSt[:], mur[:], ALU.mult)
            k.tt(PM[g][:], PSt[:], Pt[:], ALU.add)
        if STOP <= 3:
            return
        for c in range(8):
            k.mm(ps[2][:, 0:128], Wll[:, c, :], xTd[:, c, :], start=(c == 0), stop=(c == 7))
        for c in range(8):
            k.mm(ps[3][:, 0:128], Wll[:, c, :], xTsd[:, c, :], start=(c == 0), stop=(c == 7))
        k.cp(lla[:], ps[2][:, 0:128], eng="act")
        k.cp(llb[:], ps[3][:, 0:128], eng="act")
        if kind == "s":
            for sl in range(8):
                for hh in range(2):
                    k.dma(llb[:, hh * 64 + sl * 8:hh * 64 + sl * 8 + 1],
                          I["st_shift"][li, n * 8 + sl, 4096:4224].rearrange("(c p) -> p c", p=128), eng="act", allow_slow_non_contiguous=True)
        for (row, dap) in last_rows:
            k.dma(dap[4096:4224].rearrange("(c p) -> p c", p=128), lla[:, row:row + 1], eng="act", allow_slow_non_contiguous=True)
        k.tt(llb[:], llb[:], lla[:], ALU.subtract)
        k.stt(LLt[:], llb[:], mucol[:, 8:9], lla[:], ALU.mult, ALU.add)
        k.act(LLt[0:64, :], LLt[0:64, :], AF.Tanh)
        if STOP <= 4:
            return
        for hf in range(2):
            cols = slice(hf * 512, (hf + 1) * 512)
            k.mm(ps[2 + hf][:], LLt[0:64, :], W2A2[0:64, cols])
            k.mm(ps[4 + hf][:], LLt[64:128, :], W2A2[64:128, cols])
        w0r = jrow(I["a_w0"][li:li + 1, :])
        for hf in range(2):
            cols = slice(hf * 512, (hf + 1) * 512)
            k.tt(T1[:, cols], ps[2 + hf][:], w0r[:, cols], ALU.add)
        k.act(T1[:], T1[:], AF.Sigmoid)
        CC = float(np.exp(-0.5))
        if kind == "p":
            for hf in range(2):
                cols = slice(hf * 512, (hf + 1) * 512)
                k.mm(ps[6 + hf][0:64, :], TRI, T1[0:64, cols])
                k.act(EPI[:, cols], ps[6 + hf][0:64, :], AF.Exp, scale=-CC)
                k.act(EPN[:, cols], ps[6 + hf][0:64, :], AF.Exp, scale=CC)
                k.tt(EPX[:, cols], ps[6 + hf][0:64, :], T1[0:64, cols], ALU.subtract)
            k.act(EPX[:], EPX[:], AF.Exp, scale=-CC)
            for h in range(16):
                k.mm(ps[2][0:64, h:h + 1], EPI[:, h * 64:(h + 1) * 64], C.identf[0:64, 63:64])
            k.cp(PCc[:], ps[2][0:64, 0:16], eng="act")
        else:
            k.act(T1[:], T1[:], AF.Exp, scale=-CC)
            toxr(T1, "w")
        a0r = jrow(I["a_a0"][li:li + 1, :])
        for hf in range(2):
            cols = slice(hf * 512, (hf + 1) * 512)
            k.tt(At[:, cols], ps[4 + hf][:], a0r[:, cols], ALU.add)
        k.act(At[:], At[:], AF.Sigmoid)
        kkr = jrow(I["a_k_k"][li:li + 1, :])
        k.tt(T1[:], PM["k"][:], kkr[:], ALU.mult)
        k.tt(T2[:], T1[:], T1[:], ALU.mult)
        k.red(st16[:, 0, :], T2[:].rearrange("p (h j) -> p h j", h=16))
        k.ts(st16[:, 0, :], st16[:, 0, :], 1e-24, ALU.max)
        k.act(st16[:, 1, :], st16[:, 0, :], AF.Sqrt)
        k.recip(st16[:, 2, :], st16[:, 1, :])
        k.tt(T1[:].rearrange("p (h j) -> p h j", h=16), T1[:].rearrange("p (h j) -> p h j", h=16),
             st16[:, 2, :].unsqueeze(2).to_broadcast([128, 16, 64]), ALU.mult)
        if kind == "s":
            toxr(T1, "kk")
        k.tt(T2[:], T1[:], At[:], ALU.mult)
        if kind == "s":
            toxr(T2, "ka")
        kar = jrow(I["a_k_a"][li:li + 1, :])
        k.stt(PSt[:], At[:], -1.0, kar[:], ALU.add, ALU.mult)
        k.stt(KP[:], PSt[:], 1.0, PM["k"][:], ALU.add, ALU.mult)
        if kind == "s":
            toxr(KP, "k")
            toxr(PM["r"], "r")
        rkr = jrow(I["a_r_k"][li:li + 1, :])
        k.tt(Pt[:], PM["r"][:], rkr[:], ALU.mult)
        k.tt(Pt[:], Pt[:], KP[:], ALU.mult)
        k.red(bon[:], Pt[:].rearrange("p (h j) -> p h j", h=16))
        if kind == "p":
            k.stt(EPX[:], T1[0:64, :], -1.0, EPX[:], ALU.mult, ALU.mult)
            k.tt(T2[0:64, :], T2[0:64, :], EPN[:], ALU.mult)
            k.tt(EPN[:], KP[0:64, :], EPN[:], ALU.mult)
            k.tt(EPI[:], PM["r"][0:64, :], EPI[:], ALU.mult)

        if STOP <= 5:
            return
        def step(Sx, tl):
            bks = {}
            for bi, x in enumerate(("kk", "w", "ka", "k", "r")):
                bk = ps[3 + bi] if bi < 5 else None
                k.mm(bk[:], SEL[:, tl, :], XR[x][0][:], start=True, stop=False)
                k.mm(bk[:], SEL[:, tl, :], XR[x][1][:], start=False, stop=True)
                bks[x] = bk[:].rearrange("p (a j) -> p a j", a=8)
            k.tt(t1[:], Sx, bks["kk"], ALU.mult)
            k.tt(t3[:], bks["k"], vT[:, :, tl:tl + 1].to_broadcast([128, 8, 64]), ALU.mult)
            k.red(sa[:], t1[:], negate=True)
            k.tt(Sx, Sx, bks["w"], ALU.mult)
            k.tt(t2[:], bks["ka"], sa[:].unsqueeze(2).to_broadcast([128, 8, 64]), ALU.mult)
            k.tt(Sx, Sx, t3[:], ALU.add)
            k.tt(Sx, Sx, t2[:], ALU.add)
            k.tt(t1[:], Sx, bks["r"], ALU.mult)
            k.red(YT[:, :, tl], t1[:])

        if kind == "p":
            i64 = C.identf[0:64, 0:64]
            for hh in range(2):
                H0 = hh * 8
                bA = [bank(), bank()]
                bB, bK = bank(), bank()
                for hl in range(8):
                    hc = slice((H0 + hl) * 64, (H0 + hl + 1) * 64)
                    o = (hl % 4) * 128
                    k.tr(bA[hl // 4][0:64, o:o + 64], EPX[:, hc], i64)
                    k.tr(bA[hl // 4][0:64, o + 64:o + 128], EPI[:, hc], i64)
                    k.tr(bB[0:64, hl * 64:(hl + 1) * 64], T2[0:64, hc], i64)
                    k.tr(bK[0:64, hl * 64:(hl + 1) * 64], EPN[:, hc], i64)
                k.cp(FMAR[:, 0:4, :].rearrange("p a b -> p (a b)"), bA[0][0:64, :], eng="act")
                k.cp(FMAR[:, 4:8, :].rearrange("p a b -> p (a b)"), bA[1][0:64, :], eng="act")
                k.cp(FMB[:].rearrange("p a b -> p (a b)"), bB[0:64, :], eng="dve")
                k.cp(FMK[:].rearrange("p a b -> p (a b)"), bK[0:64, :], eng="dve")
                for (Gd, FMl) in ((GB, FMB), (GK, FMK)):
                    bb = [bank(), bank()]
                    for hl in range(8):
                        o = (hl % 4) * 128
                        k.mm(bb[hl // 4][0:64, o:o + 128], FMl[:, hl, :], FMAR[:, hl, :])
                    for q in range(2):
                        k.tt(Gd[:, 4 * q:4 * q + 4, :], bb[q][0:64, :].rearrange("p (a b) -> p a b", a=4),
                             MASKAR.unsqueeze(1).to_broadcast([64, 4, 128]), ALU.mult)
                bq = bank()
                for hl in range(8):
                    k.mm(bq[0:64, hl * 64:(hl + 1) * 64], FMAR[:, hl, 0:64], FMB[:, hl, :])
                k.tt(PQ[1][:], bq[0:64, :].rearrange("p (a b) -> p a b", a=8), MASKNT.unsqueeze(1).to_broadcast([64, 8, 64]), ALU.mult)
                k.tt(Tm[:], GB[:, :, 0:64], IDN.unsqueeze(1).to_broadcast([64, 8, 64]), ALU.add)
                Pc, Qc = GB[:, :, 0:64], PQ[1]
                for lv in range(5):
                    Pn, Qn = PQ[2 * ((lv + 1) % 2)], PQ[2 * ((lv + 1) % 2) + 1]
                    if lv < 4:
                        bp = bank()
                        for hl in range(8):
                            k.mm(bp[0:64, hl * 64:(hl + 1) * 64], Qc[:, hl, :], Pc[:, hl, :])
                    bq = bank()
                    for hl in range(8):
                        k.mm(bq[0:64, hl * 64:(hl + 1) * 64], Pc[:, hl, :], Qc[:, hl, :])
                    if lv < 4:
                        k.cp(Pn[:].rearrange("p a b -> p (a b)"), bp[0:64, :], eng="act")
                    k.cp(Qn[:].rearrange("p a b -> p (a b)"), bq[0:64, :], eng="act")
                    bt = bank()
                    for hl in range(8):
                        k.mm(bt[0:64, hl * 64:(hl + 1) * 64], Qn[:, hl, :], Tm[:, hl, :])
                    k.tt(Tm[:], Tm[:], bt[0:64, :].rearrange("p (a b) -> p a b", a=8), ALU.add)
                    Pc, Qc = Pn, Qn
                bx = bank()
                for hl in range(8):
                    hc = slice((H0 + hl) * 64, (H0 + hl + 1) * 64)
                    k.mm(bx[0:64, hl * 64:(hl + 1) * 64], FMAR[:, hl, 0:64], ST[:, H0 + hl, :], start=True, stop=False)
                    k.mm(bx[0:64, hl * 64:(hl + 1) * 64], GK[:, hl, 0:64], PM["v"][0:64, hc], start=False, stop=True)
                k.cp(XT[:].rearrange("p a b -> p (a b)"), bx[0:64, :], eng="act")
                bu = bank()
                for hl in range(8):
                    k.mm(bu[0:64, hl * 64:(hl + 1) * 64], Tm[:, hl, :], XT[:, hl, :])
                k.cp(UT[:].rearrange("p a b -> p (a b)"), bu[0:64, :], eng="act")
                by = bank()
                for hl in range(8):
                    hc = slice((H0 + hl) * 64, (H0 + hl + 1) * 64)
                    k.mm(by[0:64, hl * 64:(hl + 1) * 64], FMAR[:, hl, 64:128], ST[:, H0 + hl, :], start=True, stop=False)
                    k.mm(by[0:64, hl * 64:(hl + 1) * 64], GB[:, hl, 64:128], UT[:, hl, :], start=False, stop=False)
                    k.mm(by[0:64, hl * 64:(hl + 1) * 64], GK[:, hl, 64:128], PM["v"][0:64, hc], start=False, stop=True)
                k.cp(KP[0:64, hh * 512:(hh + 1) * 512], by[0:64, :], eng="act")
                bs = bank()
                for hl in range(8):
                    hc = slice((H0 + hl) * 64, (H0 + hl + 1) * 64)
                    k.mm(bs[0:64, hl * 64:(hl + 1) * 64], T2[0:64, hc], UT[:, hl, :], start=True, stop=False)
                    k.mm(bs[0:64, hl * 64:(hl + 1) * 64], EPN[:, hc], PM["v"][0:64, hc], start=False, stop=True)
                k.tt(ST[:, H0:H0 + 8, :], ST[:, H0:H0 + 8, :], bs[0:64, :].rearrange("p (a b) -> p a b", a=8), ALU.add)
                k.tt(ST[:, H0:H0 + 8, :], ST[:, H0:H0 + 8, :], PCc[:, H0:H0 + 8].unsqueeze(2).to_broadcast([64, 8, 64]), ALU.mult)
            if n == ntile_p - 1:
                for q in range(2):
                    bo = bank()
                    for hl in range(8):
                        k.tr(bo[0:64, hl * 64:(hl + 1) * 64], ST[:, q * 8 + hl, :], i64)
                    k.cp(EPI[:, q * 512:(q + 1) * 512], bo[0:64, :], eng="act")
                k.dma(O["S_p"][li].rearrange("h i j -> i h j"), EPI[:].rearrange("p (h j) -> p h j", h=16))
        else:
            for sl in range(8):
                sq = n * 8 + sl
                k.dma(t3[:], I["st_S"][li, sq].rearrange("(a b) i j -> (b i) a j", b=2))
                Sx = KP[:, 0:512].rearrange("p (a j) -> p a j", a=8)
                k.cp(Sx, t3[:], eng="act")
                for t in range(8):
                    step(Sx, sl * 8 + t)
                k.dma(O["S_s"][li, sq].rearrange("(a b) i j -> (b i) a j", b=2), Sx)

        if STOP <= 6:
            return
        Y = T1
        if kind == "p":
            k.cp(Y[0:64, :], KP[0:64, :], eng="pool")
        else:
            for c in range(8):
                k.tr(ps[c // 4][0:64, (c % 4) * 128:(c % 4 + 1) * 128], YT[:, c, :], C.identf[:, :])
            k.cp(Y[0:64, 0:512], ps[0][0:64, :], eng="act")
            k.cp(Y[0:64, 512:1024], ps[1][0:64, :], eng="act")
        Y3 = Y[0:64, :].rearrange("p (h j) -> p h j", h=16)
        k.red(st16[0:64, 0, :], Y3)
        k.ts(st16[0:64, 0, :], st16[0:64, 0, :], 1.0 / 64, ALU.mult)
        k.tt(Y3, Y3, st16[0:64, 0, :].unsqueeze(2).to_broadcast([64, 16, 64]), ALU.subtract)
        k.tt(T2[0:64, :], Y[0:64, :], Y[0:64, :], ALU.mult)
        k.red(st16[0:64, 1, :], T2[0:64, :].rearrange("p (h j) -> p h j", h=16))
        k.act(st16[0:64, 2, :], st16[0:64, 1, :], AF.Sqrt, bias=epsgn[0:64, :], scale=1.0 / 64)
        k.recip(st16[0:64, 3, :], st16[0:64, 2, :])
        k.tt(Y3, Y3, st16[0:64, 3, :].unsqueeze(2).to_broadcast([64, 16, 64]), ALU.mult)
        gr = jrow(I["a_lnx_g"][li:li + 1, :], 64)
        k.tt(Y[0:64, :], Y[0:64, :], gr[0:64, :], ALU.mult)
        br = jrow(I["a_lnx_b"][li:li + 1, :], 64)
        k.tt(Y[0:64, :], Y[0:64, :], br[0:64, :], ALU.add)
        k.tt(T2[0:64, :].rearrange("p (h j) -> p h j", h=16), PM["v"][0:64, :].rearrange("p (h j) -> p h j", h=16),
             bon[0:64, :].unsqueeze(2).to_broadcast([64, 16, 64]), ALU.mult)
        k.tt(Y[0:64, :], Y[0:64, :], T2[0:64, :], ALU.add)
        k.act(T2[0:64, :], PM["z"][0:64, :], AF.Silu)
        k.tt(G[:], Y[0:64, :], T2[0:64, :], ALU.mult)
        psb = ps[2][:].bitcast(BF16)
        for c in range(8):
            k.tr(psb[:, c * 64:(c + 1) * 64], G[:, c * 128:(c + 1) * 128], C.identb[0:64, 0:64])
        k.cp(gT[:].rearrange("p c t -> p (c t)"), psb[:, 0:512], eng="act")
        for hf in range(2):
            cols = slice(hf * 512, (hf + 1) * 512)
            for c in range(8):
                k.mm(ps[hf][0:64, :], gT[:, c, :], Wo[:, c, cols], start=(c == 0), stop=(c == 7))
            k.stt(Y[0:64, cols], xin[:, cols], ALPHA, ps[hf][0:64, :], ALU.mult, ALU.add)
        ln_tail(C, Y, 64, L, dstx[r0:r0 + 64, :], T2, crow)


def pool_layer(C, lst, L, src, dst):
    nc, P, k, I, O = C.nc, C.P, C.k, C.I, C.O
    tp = C.tp
    sb = lambda n, s, d=F32: lst.enter_context(nc.sbuf_tensor("b_" + n, list(s), d))
    ps = C.ps
    Win = sb("Win", [128, 8, 2048], BF16)
    Wg = sb("Wg", [128, 4, 2, 256], BF16)
    Wo = sb("Wo", [128, 8, 1024], BF16)
    scol = sb("scol", [128, 8])
    stg = [sb("stg%d" % i, [128, 8, 128]) for i in range(2)]
    xin = sb("xin", [128, 1024])
    xT = sb("xT", [128, 8, 128], BF16)
    E = sb("E", [128, 8, 16 * 23])
    A = sb("A", [128, 2, 16 * 23])
    B = sb("B", [128, 2, 16 * 23])
    dT = sb("dT", [128, 8, 128], BF16)
    sz = sb("sz", [128, 8, 128])
    gT = sb("gT", [128, 8, 128], BF16)
    R = sb("R", [128, 1024])
    T1 = sb("T1", [128, 1024])
    cinv = sb("cinv", [128, 512])
    crow = [sb("crow%d" % i, [128, 1024]) for i in range(2)]
    C.lnst = sb("lnst", [128, 8])
    C.epsln = sb("epsln", [128, 1])
    k.memset(C.epsln[:], LN_EPS)
    k.dma(cinv[:], I["cmask"][:, 0:512])
    w_in = I["b_w_in"].rearrange("(c p) n -> p c n", p=128)
    for j in range(16):
        k.dma(stg[j % 2][:], w_in[:, :, j * 128:(j + 1) * 128], eng="sp" if j % 2 == 0 else "act")
        k.cp(Win[:, :, j * 128:(j + 1) * 128], stg[j % 2][:], eng="pool" if j % 2 == 0 else "act")
    w_out = I["b_w_out"].rearrange("(c p) n -> p c n", p=128)
    for j in range(8):
        k.dma(stg[j % 2][:], w_out[:, :, j * 128:(j + 1) * 128], eng="sp" if j % 2 == 0 else "act")
        k.cp(Wo[:, :, j * 128:(j + 1) * 128], stg[j % 2][:], eng="pool" if j % 2 == 0 else "act")
    for g in range(4):
        sv = stg[g % 2][:].rearrange("p c n -> p (c n)")[:, 0:512].rearrange("p (c n) -> p c n", c=2)
        k.dma(sv, I["b_w_grp"][g].rearrange("(c p) n -> p c n", p=128), eng="sp")
        k.cp(Wg[:, g, :, :], sv, eng="pool")
    k.dma(scol[:], I["b_scale"][0].rearrange("(c p) -> p c", p=128), allow_slow_non_contiguous=True)
    k.memset(E[:], 0.0)

    ntile_p = tp // 128
    tiles = [("p", n) for n in range(ntile_p)] + [("s", 0)]
    for kind, n in tiles:
        srcx = src[0] if kind == "p" else src[1]
        dstx = dst[0] if kind == "p" else dst[1]
        r0 = n * 128
        nseg, new = (1, 128) if kind == "p" else (16, 8)
        sl = 15 + new
        Ev = E[:, :, 0:nseg * sl].rearrange("p c (s t) -> p c s t", s=nseg)
        k.dma(xin[:], srcx[r0:r0 + 128, :])
        for b in range(2):
            for c in range(4):
                k.tr(ps[b][:, c * 128:(c + 1) * 128], xin[:, (4 * b + c) * 128:(4 * b + c + 1) * 128], C.identf[:, :])
            k.cp(xT[:, 4 * b:4 * b + 4, :].rearrange("p c t -> p (c t)"), ps[b][:], eng="act")
        if kind == "s":
            for hh in range(2):
                k.dma(R[0:120, :], I["st_pool"][hh * 8:(hh + 1) * 8].rearrange("s r d -> (s r) d"))
                for b in range(2):
                    for c in range(4):
                        k.tr(ps[2 + b][:, c * 120:(c + 1) * 120], R[0:120, (4 * b + c) * 128:(4 * b + c + 1) * 128], C.identf[0:120, 0:120])
                    k.cp(Ev[:, 4 * b:4 * b + 4, hh * 8:(hh + 1) * 8, 0:15],
                         ps[2 + b][:, 0:480].rearrange("p (c s r) -> p c s r", c=4, s=8), eng="act")
        for ob in range(4):
            for o4 in range(4):
                oc = ob * 4 + o4
                for dc in range(8):
                    k.mm(ps[4 + ob % 2][:, o4 * 128:(o4 + 1) * 128], Win[:, dc, oc * 128:(oc + 1) * 128], xT[:, dc, :],
                         start=(dc == 0), stop=(dc == 7))
            pv = ps[4 + ob % 2][:].rearrange("p (c s t) -> p c s t", c=4, s=nseg)
            if ob < 2:
                k.cp(Ev[:, ob * 4:ob * 4 + 4, :, 15:sl], pv, eng="act")
            else:
                k.act(sz[:, (ob - 2) * 4:(ob - 2) * 4 + 4, :].rearrange("p c t -> p (c t)"), ps[4 + ob % 2][:], AF.Silu)
        if kind == "p" and n == ntile_p - 1:
            for b in range(2):
                for c in range(4):
                    k.tr(ps[2 + b][0:15, c * 128:(c + 1) * 128], E[:, 4 * b + c, 128:143], C.identf[:, :])
                k.cp(T1[0:15, b * 512:(b + 1) * 512], ps[2 + b][0:15, :], eng="act")
            k.dma(O["pl_p"], T1[0:15, :])
        if kind == "s":
            for hh in range(2):
                for b in range(2):
                    for c in range(4):
                        Ac = A[:, 0, 0:120].rearrange("p (s r) -> p s r", s=8)
                        k.cp(Ac, Ev[:, 4 * b + c, hh * 8:(hh + 1) * 8, 8:23], eng="pool")
                        k.tr(ps[2 + b][0:120, c * 128:(c + 1) * 128], A[:, 0, 0:120], C.identf[:, :])
                    k.cp(T1[0:120, b * 512:(b + 1) * 512], ps[2 + b][0:120, :], eng="act")
                k.dma(O["pl_s"][hh * 8:(hh + 1) * 8].rearrange("s r d -> (s r) d"), T1[0:120, :])
        for g in range(4):
            cur = Ev[:, 2 * g:2 * g + 2]
            bufs = [A[:, :, 0:nseg * sl].rearrange("p c (s t) -> p c s t", s=nseg),
                    B[:, :, 0:nseg * sl].rearrange("p c (s t) -> p c s t", s=nseg)]
            lo = 0
            for si, sh in enumerate((1, 2, 4, 8)[:g + 1]):
                nxt = bufs[si % 2]
                lo2 = lo + sh
                k.tt(nxt[:, :, :, lo2:sl], cur[:, :, :, lo2:sl], cur[:, :, :, lo:sl - sh], ALU.add)
                cur, lo = nxt, lo2
            w = 2 ** (g + 1)
            pooled = bufs[(g + 1) % 2]
            if kind == "p" and n == 0:
                k.tt(pooled[:, :, 0, 15:sl], cur[:, :, 0, 15:sl],
                     cinv[:, g * 128:(g + 1) * 128].unsqueeze(1).to_broadcast([128, 2, 128]), ALU.mult)
                k.tt(dT[:, 2 * g:2 * g + 2, :], pooled[:, :, 0, 15:sl], Ev[:, 2 * g:2 * g + 2, 0, 15:sl], ALU.subtract)
            else:
                k.stt(dT[:, 2 * g:2 * g + 2, :].rearrange("p c (s t) -> p c s t", s=nseg), cur[:, :, :, 15:sl], 1.0 / w,
                      Ev[:, 2 * g:2 * g + 2, :, 15:sl], ALU.mult, ALU.subtract)
        if kind == "p":
            k.cp(A[:, 0, 0:120].rearrange("p (c t) -> p c t", c=8), E[:, :, 128:143], eng="pool")
            k.cp(E[:, :, 0:15], A[:, 0, 0:120].rearrange("p (c t) -> p c t", c=8), eng="pool")
        for jc in range(8):
            g, jl = jc // 2, jc % 2
            for ic in range(2):
                k.mm(ps[6 + jc // 4][:, (jc % 4) * 128:(jc % 4 + 1) * 128], Wg[:, g, ic, jl * 128:(jl + 1) * 128], dT[:, 2 * g + ic, :],
                     start=(ic == 0), stop=(ic == 1))
        for jc in range(8):
            k.stt(gT[:, jc, :], ps[6 + jc // 4][:, (jc % 4) * 128:(jc % 4 + 1) * 128], scol[:, jc:jc + 1], sz[:, jc, :], ALU.mult, ALU.mult)
        for hf in range(2):
            cols = slice(hf * 512, (hf + 1) * 512)
            for c in range(8):
                k.mm(ps[hf][:], gT[:, c, :], Wo[:, c, cols], start=(c == 0), stop=(c == 7))
            k.stt(R[:, cols], xin[:, cols], ALPHA, ps[hf][:], ALU.mult, ALU.add)
        ln_tail(C, R, 128, L, dstx[r0:r0 + 128, :], T1, crow)


NEG = -30000.0
SCL = 0.125


def nsa_layer(C, lst, L, src, dst):
    nc, P, k, I, O = C.nc, C.P, C.k, C.I, C.O
    tp = C.tp
    sb = lambda n, s, d=F32: lst.enter_context(nc.sbuf_tensor("c_" + n, list(s), d))
    ps = C.ps
    Win = sb("Win", [128, 8, C_NC], BF16)
    Wo = sb("Wo", [128, 8, 1024], BF16)
    stg = [sb("stg%d" % i, [128, 8, 128]) for i in range(2)]
    xin = sb("xin", [128, 1024])
    xT = sb("xT", [128, 8, 128], BF16)
    KV = sb("KV", [128, 1536])
    KsT = sb("KsT", [64, 4, 17 * 128], BF16)
    KwT = sb("KwT", [64, 4, 5 * 128], BF16)
    Vs = sb("Vs", [128, 17, 4, 65], BF16)
    Vw = sb("Vw", [128, 5, 4, 65], BF16)
    KcT = sb("KcT", [64, 4, 64], BF16)
    Vc = sb("Vc", [64, 4, 98])
    Wbk = sb("Wbk", [128, 124])
    Wbv = sb("Wbv", [128, 124])
    wcol = sb("wcol", [128, 2])
    QT = sb("QT", [64, 16, 128], BF16)
    GZ = sb("GZ", [128, 1072])
    gates = sb("gates", [128, 48])
    Bt = [sb("Bt%d" % i, [128, 512]) for i in range(4)]
    SBS = sb("SBS", [128, 17 * 128])
    SBW = sb("SBW", [128, 5 * 128])
    SBC = sb("SBC", [64, 128])
    hbias = sb("hbias", [128, 240])
    k.dma(hbias[:], I["n_hb"])
    GBUF = [sb("GBUF%d" % i, [128, 1024]) for i in range(2)]
    tmp = sb("tmp", [128, 512])
    TMP = [tmp, sb("tmp1", [128, 512])]
    ec = sb("ec", [64, 512])
    eb = sb("eb", [128, 512], BF16)
    EB = [eb, sb("eb1", [128, 512], BF16)]
    OB = sb("OB", [128, 4, 98])
    rd = sb("rd", [128, 8])
    imp = sb("imp", [128, 40])
    imp2 = sb("imp2", [128, 40])
    m8 = sb("m8", [128, 16])
    cbt = sb("cbt", [128, 80])
    selT = sb("selT", [40, 128], BF16)
    Eexp = sb("Eexp", [40, 17 * 128], BF16)
    Oacc = sb("Oacc", [128, 1024])
    Gb = sb("Gb", [128, 1024], BF16)
    gT = sb("gT", [128, 8, 128], BF16)
    T1 = sb("T1", [128, 1024])
    idx = sb("idx", [128, 256], I32)
    idf = GBUF[0][:, 0:256]
    crow = [x[:].rearrange("p c n -> p (c n)") for x in stg]
    C.lnst = sb("lnst", [128, 8])
    C.epsln = sb("epsln", [128, 1])
    k.memset(C.epsln[:], LN_EPS)
    w_in = I["c_w_in"].rearrange("(c p) n -> p c n", p=128)
    nj = (C_NC + 127) // 128
    for j in range(nj):
        wd = min(128, C_NC - j * 128)
        k.dma(stg[j % 2][:, :, 0:wd], w_in[:, :, j * 128:j * 128 + wd], eng="sp" if j % 2 == 0 else "act")
        k.cp(Win[:, :, j * 128:j * 128 + wd], stg[j % 2][:, :, 0:wd], eng="pool" if j % 2 == 0 else "act")
    w_out = I["c_w_out"].rearrange("(c p) n -> p c n", p=128)
    for j in range(8):
        k.dma(stg[j % 2][:], w_out[:, :, j * 128:(j + 1) * 128], eng="sp" if j % 2 == 0 else "act")
        k.cp(Wo[:, :, j * 128:(j + 1) * 128], stg[j % 2][:], eng="pool" if j % 2 == 0 else "act")
    for r in range(4):
        k.dma(wcol[r * 32:(r + 1) * 32, 0:1], I["c_cmp_wk"].rearrange("o l -> l o"), allow_slow_non_contiguous=True)
        k.dma(wcol[r * 32:(r + 1) * 32, 1:2], I["c_cmp_wv"].rearrange("o l -> l o"), allow_slow_non_contiguous=True)
    k.dma(Wbk[:], I["n_wbm"])
    k.cp(Wbv[:], Wbk[:], eng="pool")
    k.ts(Wbk[:], Wbk[:], wcol[:, 0:1], ALU.mult)
    k.ts(Wbv[:], Wbv[:], wcol[:, 1:2], ALU.mult)
    k.memset(Vs[:], 0.0)
    k.memset(Vw[:], 0.0)
    k.memset(KsT[:], 0.0)
    k.memset(KwT[:], 0.0)
    k.memset(Vs[:, :, :, 64:65], 1.0)
    k.memset(Vw[:, :, :, 64:65], 1.0)

    def kv_tile(kt, rows_cmp, rows_sel, rows_win, nrows, do_cmp_block=None, win_slot=None):
        if rows_sel is not None:
            ksr, vsr = rows_sel
            for g in range(4):
                k.tr(ps[0][0:64, g * 128:g * 128 + nrows], ksr[:, g * 64:(g + 1) * 64], C.identf[0:nrows, 0:nrows])
            k.cp(KsT[:, :, kt * 128:kt * 128 + nrows], ps[0][0:64, :].rearrange("p (g t) -> p g t", g=4)[:, :, 0:nrows], eng="act")
            k.cp(Vs[0:nrows, kt, :, 0:64], vsr.rearrange("p (g d) -> p g d", g=4), eng="dve")
        if rows_win is not None:
            kwr, vwr = rows_win
            ws = win_slot
            for g in range(4):
                k.tr(ps[1][0:64, g * 128:g * 128 + nrows], kwr[:, g * 64:(g + 1) * 64], C.identf[0:nrows, 0:nrows])
            k.cp(KwT[:, :, ws * 128:ws * 128 + nrows], ps[1][0:64, :].rearrange("p (g t) -> p g t", g=4)[:, :, 0:nrows], eng="act")
            k.cp(Vw[0:nrows, ws, :, 0:64], vwr.rearrange("p (g d) -> p g d", g=4), eng="dve")
        if rows_cmp is not None:
            kcr, vcr = rows_cmp
            t = do_cmp_block
            for g in range(4):
                k.mm(ps[2][0:64, g * 4:(g + 1) * 4], kcr[:, g * 64:(g + 1) * 64], Wbk[:, 60:64])
            k.cp(KcT[:, :, 4 * t:4 * t + 4], ps[2][0:64, 0:16].rearrange("p (g n) -> p g n", g=4), eng="act")
            k.mm(ps[2][0:64, 128:384], Wbv[:, 60 - 4 * t:124 - 4 * t], vcr)
            k.tt(Vc[:, :, 0:64], Vc[:, :, 0:64], ps[2][0:64, 128:384].rearrange("p (g d) -> p g d", g=4), ALU.add)

    bti = [0]

    def attend(nq, nblk, s_tiles, w_tiles, bc_ap, bs_fn, bw_fn, cb_ap, ft_ap, pair_ap, eexp_cols, load_consts=True, resident=False):
        nc4 = 4 * nq
        if load_consts:
            k.dma(cbt[0:nq, 0:nblk], cb_ap)
            k.dma(cbt[0:nq, 40:40 + nblk], ft_ap)
            for g in range(4):
                k.dma(Vc[:, g, 65:65 + nblk], pair_ap)

        def bias_tile(ap, nk):
            if resident:
                return ap
            t = Bt[bti[0] % 4]
            bti[0] += 1
            k.dma(t[0:nk, 0:nc4], ap, eng="sp")
            return t[0:nk, 0:nc4]
        slopes = [2.0 ** (-8.0 * (h + 1) / 16) for h in range(16)]

        for g in range(4):
            Qg = QT[:, 4 * g:4 * g + 4, 0:nq]
            Qg2 = ec
            k.mm(ps[3][0:64, 0:nc4], KcT[:, g, :], QTf[:, g, 0:nc4])
            bt = bias_tile(bc_ap(g), 64)
            k.stt(tmp[0:64, 0:nc4], ps[3][0:64, 0:nc4], SCL, bt, ALU.mult, ALU.add)
            k.act(ec[:, 0:nc4], tmp[0:64, 0:nc4], AF.Exp)
            for j in range(4):
                k.mm(ps[4][0:nq, j * 98:j * 98 + 65 + nblk], ec[:, j * nq:(j + 1) * nq], Vc[:, g, 0:65 + nblk])
            k.cp(OB[0:nq, :, 0:65 + nblk], ps[4][0:nq, 0:392].rearrange("p (j c) -> p j c", j=4)[:, :, 0:65 + nblk], eng="act")
            k.ts(rd[0:nq, 0:4], OB[0:nq, :, 64], 1e-30, ALU.max)
            k.recip(rd[0:nq, 0:4], rd[0:nq, 0:4])
            k.ts(imp[0:nq, 0:nblk], OB[0:nq, 0, 65:65 + nblk], rd[0:nq, 0:1], ALU.mult)
            for j in range(1, 4):
                k.stt(imp[0:nq, 0:nblk], OB[0:nq, j, 65:65 + nblk], rd[0:nq, j:j + 1], imp[0:nq, 0:nblk], ALU.mult, ALU.add)
            k.tt(imp[0:nq, 0:nblk], imp[0:nq, 0:nblk], cbt[0:nq, 0:nblk], ALU.mult)
            k.tt(imp[0:nq, 0:nblk], imp[0:nq, 0:nblk], cbt[0:nq, 40:40 + nblk], ALU.add)
            P.op("dve", lambda e: e.max(out=m8[0:nq, 0:8], in_=imp[0:nq, 0:nblk]), reads=[imp], writes=[m8])
            P.op("dve", lambda e: e.match_replace(out=imp2[0:nq, 0:nblk], in_to_replace=m8[0:nq, 0:8], in_values=imp[0:nq, 0:nblk], imm_value=-2.0),
                 reads=[imp, m8], writes=[imp2])
            P.op("dve", lambda e: e.max(out=m8[0:nq, 8:16], in_=imp2[0:nq, 0:nblk]), reads=[imp2], writes=[m8])
            k.ts(m8[0:nq, 15:16], m8[0:nq, 15:16], 0.0, ALU.max)
            k.ts(imp2[0:nq, 0:nblk], imp[0:nq, 0:nblk], m8[0:nq, 15:16], ALU.is_ge)
            k.tr(ps[5][0:nblk, 0:nq], imp2[0:nq, 0:nblk], C.identf[0:nq, 0:nq])
            k.cp(selT[0:nblk, 0:nq], ps[5][0:nblk, 0:nq], eng="act")
            def accum(first, col, Osrc):
                gsl = gates[0:nq, :].rearrange("p (h c) -> p h c", c=3)[:, 4 * g:4 * g + 4, col]
                k.tt(rd[0:nq, 4:8], rd[0:nq, 0:4], gsl, ALU.mult)
                dstv = Oacc[0:nq, g * 256:(g + 1) * 256].rearrange("p (j d) -> p j d", j=4)
                rb = rd[0:nq, 4:8].unsqueeze(2).to_broadcast([nq, 4, 64])
                if first:
                    k.tt(dstv, Osrc, rb, ALU.mult)
                else:
                    k.tt(OB[0:nq, :, 0:64], Osrc, rb, ALU.mult)
                    k.tt(dstv, dstv, OB[0:nq, :, 0:64], ALU.add)
            accum(True, 0, OB[0:nq, :, 0:64])
            items = []
            for br, tiles, Kt, Vt, bfn in ((1, s_tiles, KsT, Vs, bs_fn), (2, w_tiles, KwT, Vw, bw_fn)):
                for ti, (slot, nk, bidx) in enumerate(tiles):
                    items.append((br, Kt, Vt, bfn, ti, len(tiles), slot, nk, bidx))
            PSC = (ps[3], ps[2])

            def front(i):
                br, Kt, Vt, bfn, ti, nt, slot, nk, bidx = items[i]
                k.mm(PSC[i % 2][0:nk, 0:nc4], Kt[:, g, slot * 128:slot * 128 + nk], QTf[:, g, 0:nc4])
                if br == 1:
                    mo = 128 + (i % 2) * 128
                    k.mm(ps[5][0:nk, mo:mo + nq], Eexp[0:nblk, eexp_cols(slot)], selT[0:nblk, 0:nq])

            def mid_a(i):
                br, Kt, Vt, bfn, ti, nt, slot, nk, bidx = items[i]
                bt = bfn(bidx, g) if resident else bias_tile(bfn(bidx, g), nk)
                k.stt(TMP[i % 2][0:nk, 0:nc4], PSC[i % 2][0:nk, 0:nc4], SCL, bt, ALU.mult, ALU.add)

            def mid_b(i):
                br, Kt, Vt, bfn, ti, nt, slot, nk, bidx = items[i]
                e = EB[i % 2]
                k.act(e[0:nk, 0:nc4], TMP[i % 2][0:nk, 0:nc4], AF.Exp)
                if br == 1:
                    mo = 128 + (i % 2) * 128
                    k.tt(e[0:nk, 0:nc4].rearrange("p (j q) -> p j q", j=4), e[0:nk, 0:nc4].rearrange("p (j q) -> p j q", j=4),
                         ps[5][0:nk, mo:mo + nq].unsqueeze(1).to_broadcast([nk, 4, nq]), ALU.mult)

            def back(i):
                br, Kt, Vt, bfn, ti, nt, slot, nk, bidx = items[i]
                e = EB[i % 2]
                for j in range(4):
                    k.mm(ps[(6, 7, 0, 1)[j]][0:nq, 0:65], e[0:nk, j * nq:(j + 1) * nq], Vt[0:nk, slot, g, :],
                         start=(ti == 0), stop=(ti == nt - 1))
                if ti == nt - 1:
                    for j in range(4):
                        k.cp(OB[0:nq, j, 0:65], ps[(6, 7, 0, 1)[j]][0:nq, 0:65], eng="act")
                    k.ts(rd[0:nq, 0:4], OB[0:nq, :, 64], 1e-30, ALU.max)
                    k.recip(rd[0:nq, 0:4], rd[0:nq, 0:4])
                    accum(False, br, OB[0:nq, :, 0:64])

            front(0)
            mid_a(0)
            for i in range(len(items)):
                if i + 1 < len(items):
                    front(i + 1)
                    mid_a(i + 1)
                mid_b(i)
                back(i)

    QTflat = QT[:].rearrange("p h q -> p (h q)")

    class _QTf:
        nq = 128

        def __getitem__(self, key):
            _, g, _ = key
            n4 = 4 * self.nq
            return QTflat[:, g * n4:(g + 1) * n4]
    QTf = _QTf()

    def qgz(npart, nq_cols):
        for h in range(16):
            for dc in range(8):
                k.mm(ps[7][0:64, (h % 4) * 128:(h % 4) * 128 + nq_cols], Win[:, dc, h * 64:(h + 1) * 64], xT[:, dc, 0:nq_cols],
                     start=(dc == 0), stop=(dc == 7))
            if h % 4 == 3:
                QTf.nq = nq_cols
                qdst = QTflat[:, (h - 3) * nq_cols:(h + 1) * nq_cols].rearrange("p (j q) -> p j q", j=4)
                k.cp(qdst, ps[7][0:64, :].rearrange("p (j q) -> p j q", j=4)[:, :, 0:nq_cols], eng="act")
        for i, (c0, c1) in enumerate(((2560, 3072), (3072, 3584), (3584, 3632))):
            for dc in range(8):
                k.mm(ps[i][0:npart, 0:c1 - c0], xT[:, dc, 0:npart], Win[:, dc, c0:c1], start=(dc == 0), stop=(dc == 7))
            k.cp(GZ[0:npart, c0 - 2560:c1 - 2560], ps[i][0:npart, 0:c1 - c0], eng="act")
        k.act(gates[0:npart, :], GZ[0:npart, 0:48], AF.Sigmoid)

    def finish(npart, dst_rows):
        k.act(T1[0:npart, :], GZ[0:npart, 48:1072], AF.Silu)
        k.tt(Gb[0:npart, :], Oacc[0:npart, :], T1[0:npart, :], ALU.mult)
        psb = ps[2][:].bitcast(BF16)
        for c in range(8):
            k.tr(psb[:, c * 128:c * 128 + npart], Gb[0:npart, c * 128:(c + 1) * 128], C.identb[0:npart, 0:npart])
        k.cp(gT[:, :, 0:npart], psb[:, 0:1024].rearrange("p (c t) -> p c t", c=8)[:, :, 0:npart], eng="act")
        for hf in range(2):
            cols = slice(hf * 512, (hf + 1) * 512)
            for c in range(8):
                k.mm(ps[hf][0:npart, :], gT[:, c, 0:npart], Wo[:, c, cols], start=(c == 0), stop=(c == 7))
            k.stt(T1[0:npart, cols], xin[0:npart, cols], ALPHA, ps[hf][0:npart, :], ALU.mult, ALU.add)
        ln_tail(C, T1, npart, L, dst_rows, Oacc, crow)

    def load_xT(rows_ap, npart):
        k.dma(xin[0:npart, :], rows_ap)
        for b in range(2):
            for c in range(4):
                k.tr(ps[b][:, c * 128:c * 128 + npart], xin[0:npart, (4 * b + c) * 128:(4 * b + c + 1) * 128], C.identf[0:npart, 0:npart])
            k.cp(xT[:, 4 * b:4 * b + 4, 0:npart], ps[b][:].rearrange("p (c t) -> p c t", c=4)[:, :, 0:npart], eng="act")

    def kv_proj(npart):
        for i in range(3):
            for dc in range(8):
                k.mm(ps[3 + i][0:npart, :], xT[:, dc, 0:npart], Win[:, dc, 1024 + i * 512:1024 + (i + 1) * 512], start=(dc == 0), stop=(dc == 7))
            k.cp(KV[0:npart, i * 512:(i + 1) * 512], ps[3 + i][0:npart, :], eng="act")

    k.memset(Vc[:], 0.0)
    k.memset(Vc[:, :, 64:65], 1.0)
    k.memset(KcT[:], 0.0)
    k.dma(Eexp[0:32, 0:2048], I["n_eexp_p"])
    ntile = tp // 128
    for t in range(ntile):
        r0 = t * 128
        load_xT(src[0][r0:r0 + 128, :], 128)
        kv_proj(128)
        for i, nm in enumerate(("cmpk", "cmpv", "selk", "selv")):
            k.dma(O[nm + "_p"][r0:r0 + 128, :], KV[:, i * 256:(i + 1) * 256])
        wr0 = r0 - (tp - min(512, tp))
        if wr0 >= 0:
            k.dma(O["wink_p"][wr0:wr0 + 128, :], KV[:, 1024:1280])
            k.dma(O["winv_p"][wr0:wr0 + 128, :], KV[:, 1280:1536])
        kv_tile(t, (KV[:, 0:256], KV[:, 256:512]), (KV[:, 512:768], KV[:, 768:1024]), (KV[:, 1024:1280], KV[:, 1280:1536]), 128,
                do_cmp_block=t, win_slot=t % 5)
        qgz(128, 128)
        s_tiles = [(kt, 128, t - kt) for kt in range(t + 1)]
        w_tiles = [(kt % 5, 128, t - kt) for kt in range(max(0, t - 4), t + 1)]
        attend(128, 32, s_tiles, w_tiles,
               lambda g: I["n_bc_p"][t, g], lambda d, g: I["n_bs_p"][d, g], lambda d, g: I["n_bw_p"][d, g],
               I["n_cb_p"][t], I["n_ft_p"][t], I["n_pair_p"], lambda slot: slice(slot * 128, (slot + 1) * 128))
        finish(128, dst[0][r0:r0 + 128, :])

    k.dma(idx[:], I["ptab"].rearrange("s n -> (s n)").partition_broadcast(128))
    k.cp(idf, idx[:])
    k.dma(wcol[:, 0:1], I["n_iota"])
    k.ts(idf, idf, 128.0, ALU.mult, wcol[:, 0:1], ALU.add)
    k.cp(idx[:], idf)
    k.dma(Eexp[0:33, 0:17 * 128], I["n_eexp_s"])
    load_xT(src[1][:, :], 128)
    kv_proj(128)
    KVs = C.kvs_scr
    k.dma(KVs, KV[:])
    for i, nm in enumerate(("cmpk", "cmpv", "selk", "selv")):
        k.dma(O[nm + "_s"], KV[:, i * 256:(i + 1) * 256])
    k.dma(SBS[:].rearrange("k (d g c) -> k d g c", d=17, g=4), I["n_bs_s"].rearrange("d g k c -> k d g c"))
    k.dma(SBW[:].rearrange("k (d g c) -> k d g c", d=5, g=4), I["n_bw_s"].rearrange("d g k c -> k d g c"))
    k.dma(SBC[:].rearrange("k (g c) -> k g c", g=4), I["n_bc_s"].rearrange("g k c -> k g c"))
    k.dma(cbt[0:8, 0:33], I["n_cb_s"])
    k.dma(cbt[0:8, 40:73], I["n_ft_s"])
    for g in range(4):
        k.dma(Vc[:, g, 65:98], I["n_pair_s"])
    xTs_all = sb("xTs_all", [128, 8, 128], BF16)
    k.cp(xTs_all[:], xT[:], eng="dve")
    NEW = KV[0:8, :]
    for sq in range(NS):
        k.memset(Vc[:, :, 0:64], 0.0)
        pools = (I["cmp_k"], I["cmp_v"], I["sel_k"], I["sel_v"])

        def gather(pool_ap, slot, dst_tile, sq=sq):
            P.dma("pool", dst_tile, pool_ap, reads=[pool_ap, idx], writes=[dst_tile],
                  fn=lambda e: e.indirect_dma_start(out=dst_tile, out_offset=None, in_=pool_ap,
                                                    in_offset=bass.IndirectOffsetOnAxis(ap=idx[:, sq * 16 + slot:sq * 16 + slot + 1], axis=0)))
        for pg in range(16):
            gb = GBUF[pg % 2]
            for ci in range(4):
                gather(pools[ci], pg, gb[:, ci * 256:(ci + 1) * 256])
            kv_tile(pg, (gb[:, 0:256], gb[:, 256:512]), (gb[:, 512:768], gb[:, 768:1024]), None, 128, do_cmp_block=pg)
        for wt in range(4):
            gb = GBUF[wt % 2]
            k.dma(gb[:, 0:256], I["win_k"][sq, wt * 128:(wt + 1) * 128, :])
            k.dma(gb[:, 256:512], I["win_v"][sq, wt * 128:(wt + 1) * 128, :])
            kv_tile(0, None, None, (gb[:, 0:256], gb[:, 256:512]), 128, win_slot=wt)
        k.dma(NEW, KVs[sq * 8:(sq + 1) * 8, :])
        kv_tile(16, None, (NEW[:, 512:768], NEW[:, 768:1024]), (NEW[:, 1024:1280], NEW[:, 1280:1536]), 8, win_slot=4)
        k.dma(O["wink_s"][sq, 0:504, :], I["win_k"][sq, 8:512, :])
        k.dma(O["winv_s"][sq, 0:504, :], I["win_v"][sq, 8:512, :], eng="act")
        k.dma(O["wink_s"][sq, 504:512, :], NEW[:, 1024:1280])
        k.dma(O["winv_s"][sq, 504:512, :], NEW[:, 1280:1536])
        k.cp(xT[:, :, 0:8], xTs_all[:, :, sq * 8:(sq + 1) * 8], eng="dve")
        k.dma(xin[0:8, :], src[1][sq * 8:(sq + 1) * 8, :])
        qgz(8, 8)
        s_tiles = [(kt, 128, kt) for kt in range(16)] + [(16, 8, 16)]
        w_tiles = [(kt, 128, kt) for kt in range(4)] + [(4, 8, 4)]
        attend(8, 33, s_tiles, w_tiles,
               lambda g: SBC[:, g * 32:(g + 1) * 32],
               lambda d, g: SBS[0:(8 if d == 16 else 128), d * 128 + g * 32:d * 128 + (g + 1) * 32],
               lambda d, g: SBW[0:(8 if d == 4 else 128), d * 128 + g * 32:d * 128 + (g + 1) * 32],
               None, None, None, lambda slot: slice(slot * 128, slot * 128 + (8 if slot == 16 else 128)), load_consts=False, resident=True)
        finish(8, dst[1][sq * 8:(sq + 1) * 8, :])


def _cmask():
    m = np.zeros((128, 2048), np.float32)
    t = np.arange(128)
    for g, w in enumerate((2, 4, 8, 16)):
        m[:, g * 128:(g + 1) * 128] = (1.0 / np.minimum(w, t + 1))[None, :]
    a = np.arange(64)
    su = (a[:, None] < a[None, :]).astype(np.float32)
    ui = (a[:, None] <= a[None, :]).astype(np.float32)
    m[0:64, 512:576] = su
    m[0:64, 576:640] = ui
    m[0:64, 640:704] = su.T
    m[0:64, 704:768] = ui
    m[0:64, 768:832] = np.eye(64, dtype=np.float32)
    return m


def consts():
    import ml_dtypes
    sel = np.zeros((128, 64, 128), np.float32)
    for kk in range(128):
        sel[kk, kk % 64, (kk // 64) * 64:(kk // 64) * 64 + 64] = 1
    return {"identf": np.eye(128, dtype=np.float32), "selb": sel.reshape(128, 64 * 128).astype(ml_dtypes.bfloat16),
            "cmask": _cmask()}


def shard_inputs(inp, c, tp=TP):
    f = lambda a: np.ascontiguousarray(a)
    m = {
        "xp": f(inp["x_prompt"][c, :tp]), "xs": f(inp["x_sample"][16 * c:16 * c + 16].reshape(128, D)),
        "st_S": f(inp["state_rwkv_S"][:, 16 * c:16 * c + 16]), "st_shift": f(inp["state_rwkv_shift"][:, 16 * c:16 * c + 16]),
        "st_pool": f(inp["state_pool"][0, 16 * c:16 * c + 16]),
        "cmp_k": f(inp["cache_cmp_k"][0].reshape(-1, 256)), "cmp_v": f(inp["cache_cmp_v"][0].reshape(-1, 256)),
        "sel_k": f(inp["cache_sel_k"][0].reshape(-1, 256)), "sel_v": f(inp["cache_sel_v"][0].reshape(-1, 256)),
        "win_k": f(inp["state_win_k"][0, 16 * c:16 * c + 16].reshape(16, 512, 256)),
        "win_v": f(inp["state_win_v"][0, 16 * c:16 * c + 16].reshape(16, 512, 256)),
        "ptab": f(inp["page_table"][16 * c:16 * c + 16]).astype(np.int32),
        "a_r_k": f(inp["a_r_k"].reshape(2, D)), "b_w_in": f(inp["b_w_in"][0]), "b_w_grp": f(inp["b_w_grp"][0]),
        "b_scale": f(inp["b_scale"]), "b_w_out": f(inp["b_w_out"][0]), "c_w_in": f(inp["c_w_in"][0]),
        "c_cmp_wk": f(inp["c_cmp_wk"]), "c_cmp_wv": f(inp["c_cmp_wv"]), "c_w_out": f(inp["c_w_out"][0]),
    }
    for nm in ("ln_g", "ln_b", "a_w_in", "a_mu", "a_w0", "a_w2", "a_a0", "a_a2", "a_k_k", "a_k_a", "a_lnx_g", "a_lnx_b", "a_w_out"):
        m[nm] = f(inp[nm])
    m.update(consts())
    m.update(nsa_consts())
    return m


def nsa_consts():
    sl = 2.0 ** (-8.0 * (np.arange(16) + 1) / 16)
    c = {}

    def bias(dist, valid, g):
        K_, nq = dist.shape
        out = np.empty((K_, 4, nq), np.float32)
        for j in range(4):
            out[:, j] = np.where(valid, -sl[4 * g + j] * dist, NEG)
        return out.reshape(K_, 4 * nq)
    q = np.arange(128)[None, :]
    kk = np.arange(128)[:, None]
    n = np.arange(64)[:, None]
    bc = np.zeros((16, 4, 64, 512), np.float32)
    bs = np.zeros((16, 4, 128, 512), np.float32)
    bw = np.zeros((5, 4, 128, 512), np.float32)
    for g in range(4):
        for t in range(16):
            d = 128 * t + q - 32 * n - 31
            bc[t, g] = bias(d, d >= 0, g)
            d = 128 * t + q - kk
            bs[t, g] = bias(d, d >= 0, g)
            if t < 5:
                bw[t, g] = bias(d, (d >= 0) & (d < 512), g)
    c["n_bc_p"], c["n_bs_p"], c["n_bw_p"] = bc, bs, bw
    cb = np.zeros((16, 128, 32), np.float32)
    ft = np.zeros((16, 128, 32), np.float32)
    blk = np.arange(32)[None, :]
    for t in range(16):
        cur = ((128 * t + np.arange(128)) // 64)[:, None]
        cb[t] = (blk < cur)
        ft[t] = np.where(blk == cur, 1e9, np.where(blk > cur, -1.0, 0.0))
    c["n_cb_p"], c["n_ft_p"] = cb, ft
    c["n_pair_p"] = (np.arange(64)[:, None] // 2 == np.arange(32)[None, :]).astype(np.float32)
    c["n_eexp_p"] = (np.arange(2048)[None, :] // 64 == np.arange(32)[:, None]).astype(np.float32)
    wbm = np.zeros((128, 124), np.float32)
    for r in range(128):
        wbm[r, 60 + r // 32] = 1.0
    c["n_wbm"] = wbm
    c["n_iota"] = np.arange(128, dtype=np.float32).reshape(128, 1)
    tq = np.arange(8)[None, :]
    bcs = np.zeros((4, 64, 32), np.float32)
    bss = np.zeros((17, 4, 128, 32), np.float32)
    bws = np.zeros((5, 4, 128, 32), np.float32)
    for g in range(4):
        d = 2048 + tq - 32 * n - 31
        bcs[g] = bias(d, d >= 0, g)
        for kt in range(16):
            d = 2048 + tq - 128 * kt - kk
            bss[kt, g] = bias(d, d >= 0, g)
        d = tq - kk
        newb = bias(d, (d >= 0) & (kk < 8), g)
        bss[16, g] = newb
        for kt in range(4):
            d = 2048 + tq - (1536 + 128 * kt + kk)
            bws[kt, g] = bias(d, (d >= 0) & (d < 512), g)
        bws[4, g] = newb
    c["n_bc_s"], c["n_bs_s"], c["n_bw_s"] = bcs, bss, bws
    cbs = np.ones((8, 33), np.float32)
    cbs[:, 32] = 0
    fts = np.zeros((8, 33), np.float32)
    fts[:, 32] = 1e9
    c["n_cb_s"], c["n_ft_s"] = cbs, fts
    ps_ = np.zeros((64, 33), np.float32)
    ps_[:, :32] = c["n_pair_p"]
    c["n_pair_s"] = ps_
    ee = np.zeros((33, 17 * 128), np.float32)
    ee[:32, :2048] = c["n_eexp_p"]
    ee[32, 2048:] = 1.0
    import ml_dtypes
    c["n_eexp_s"] = ee.astype(ml_dtypes.bfloat16)
    c["n_eexp_p"] = c["n_eexp_p"].astype(ml_dtypes.bfloat16)
    hb = np.zeros((128, 240), np.float32)
    for g in range(4):
        for d in range(1, 16):
            for j in range(4):
                hb[:, g * 60 + (d - 1) * 4 + j] = -sl[4 * g + j] * 128.0 * (d - 1)
    c["n_hb"] = hb
    return c


_NC_CACHE = {}


def kernel(**inputs):
    n = 8
    npool = inputs["cache_cmp_k"].shape[1]
    key = (npool,)
    if key not in _NC_CACHE:
        _NC_CACHE[key] = build(npool=npool, tp=TP)
    nc = _NC_CACHE[key]
    in_maps = [shard_inputs(inputs, c) for c in range(n)]
    res = run_bass_kernel_spmd(nc, in_maps, core_ids=list(range(n)))
    R = res.results
    cat = lambda nm: np.stack([R[c][nm] for c in range(n)], 0)
    y_p = cat("y_p")
    y_s = np.concatenate([R[c]["y_s"].reshape(16, 8, D) for c in range(n)], 0)
    S_p = np.stack([R[c]["S_p"] for c in range(n)], 1)
    S_s = np.concatenate([R[c]["S_s"] for c in range(n)], 1)
    sh_p = np.stack([R[c]["sh_p"] for c in range(n)], 1)
    sh_s = np.concatenate([R[c]["sh_s"] for c in range(n)], 1)
    pl_p = cat("pl_p")[None]
    pl_s = np.concatenate([R[c]["pl_s"] for c in range(n)], 0)[None]
    outs = [y_p, y_s, S_p, S_s, sh_p, sh_s, pl_p, pl_s]
    for nm in ("cmpk", "cmpv", "selk", "selv"):
        outs.append(cat(nm + "_p").reshape(1, n, TP, 4, 64))
        outs.append(np.concatenate([R[c][nm + "_s"].reshape(16, 8, 4, 64) for c in range(n)], 0)[None])
    for nm in ("wink", "winv"):
        outs.append(cat(nm + "_p").reshape(1, n, 512, 4, 64))
        outs.append(np.concatenate([R[c][nm + "_s"].reshape(16, 512, 4, 64) for c in range(n)], 0)[None])
    return tuple(np.ascontiguousarray(o, dtype=np.float32) for o in outs)
```

```python
import contextlib
import numpy as np
import concourse.bass as bass
import concourse.mybir as mybir
from concourse.bass_utils import run_bass_kernel_spmd

F32 = mybir.dt.float32
BF16 = mybir.dt.bfloat16
I32 = mybir.dt.int32
ALU = mybir.AluOpType
AF = mybir.ActivationFunctionType
AX = mybir.AxisListType

import os as _os
STRICT = bool(_os.environ.get("KSTRICT"))
ENGS = ("pe", "dve", "act", "pool", "sp")
NDMA = {"sp": 12, "act": 6, "pool": 6}

D = 1024
TP = 2048
NS = 16
TS = 8
DEPTH = 4
ALPHA = (2.0 * DEPTH) ** 0.25
LN_EPS = 1e-5
A_NC = 4224
GN_EPS = 64e-5
C_NC = 3632


def _key(k):
    if isinstance(k, (str, tuple)):
        return k
    t = getattr(k, "tensor", k)
    return getattr(t, "name", str(t))


class Prog:
    def __init__(self, nc):
        self.nc = nc
        self.q = {e: [] for e in ENGS}
        self.cnt = {e: 0 for e in ENGS}
        self.known = {e: {} for e in ENGS}
        self.lastw = {}
        self.readers = {}
        self.dma_rr = {e: 0 for e in NDMA}
        self.dma_cnt = {}
        self.n_inst = 0

    def _deps(self, reads, writes):
        deps = {}

        def add(ev):
            if ev is None:
                return
            s, v = ev
            if deps.get(s, 0) < v:
                deps[s] = v
        for k in reads:
            add(self.lastw.get(k))
        for k in writes:
            add(self.lastw.get(k))
            for ev in self.readers.get(k, ()):
                add(ev)
        return deps

    def _commit(self, ev, reads, writes):
        for k in reads:
            self.readers.setdefault(k, []).append(ev)
        for k in writes:
            self.lastw[k] = ev
            self.readers[k] = []

    def _waits(self, eng, deps, compute=False):
        waits = []
        kn = self.known[eng]
        for s, v in deps.items():
            if s == "c_pe" and eng == "pe":
                continue
            if compute and not STRICT and s == "c_" + eng and eng in ("dve", "act") and v < self.cnt[eng]:
                continue
            if kn.get(s, 0) >= v:
                continue
            kn[s] = v
            waits.append((s, v))
        return waits

    def op(self, eng, fn, reads=(), writes=()):
        reads = [_key(k) for k in reads]
        writes = [_key(k) for k in writes]
        writes = writes + [r for r in reads if isinstance(r, str) and r.startswith("psb")]
        waits = self._waits(eng, self._deps(reads, writes), compute=True)
        self.cnt[eng] += 1
        ev = ("c_" + eng, self.cnt[eng])
        self.q[eng].append(("op", waits, fn, ev))
        self._commit(ev, reads, writes)
        self.n_inst += 1
        return ev

    def dma(self, eng, out, in_, reads=None, writes=None, fn=None, **kw):
        reads = [_key(k) for k in (reads if reads is not None else [in_])]
        writes = [_key(k) for k in (writes if writes is not None else [out])]
        deps = self._deps(reads, writes)
        i = self.dma_rr[eng]
        self.dma_rr[eng] = (i + 1) % NDMA[eng]
        sname = "d_%s%d" % (eng, i)
        n = self.dma_cnt.get(sname, 0)
        if n > 0 and deps.get(sname, 0) < 16 * n:
            deps[sname] = 16 * n
        waits = self._waits(eng, deps)
        self.dma_cnt[sname] = n + 1
        ev = (sname, 16 * (n + 1))
        self.q[eng].append(("dma", waits, (out, in_, kw, fn), ev))
        self._commit(ev, reads, writes)
        self.n_inst += 1
        return ev

    def barrier(self):
        for eng in ENGS:
            deps = {}
            for f in ENGS:
                if f != "sp" and f != eng and self.cnt[f] > 0:
                    deps["c_" + f] = self.cnt[f]
            for s, n in self.dma_cnt.items():
                deps[s] = 16 * n
            waits = self._waits(eng, deps)
            self.q[eng].append(("wait", waits, None, None))

    def emit(self):
        nc = self.nc
        names = ["c_" + e for e in ENGS if e != "sp"]
        for e, n in NDMA.items():
            names += ["d_%s%d" % (e, i) for i in range(n)]
        with contextlib.ExitStack() as st:
            sems = {nm: st.enter_context(nc.semaphore(nm)) for nm in names}
            block = st.enter_context(nc.Block())

            def run(eng):
                def body(e):
                    for kind, waits, payload, ev in self.q[eng]:
                        for s, v in waits:
                            e.wait_ge(sems[s], v)
                        if kind == "op":
                            payload(e).then_inc(sems[ev[0]], 1)
                        elif kind == "dma":
                            out, in_, kw, fn = payload
                            if fn is not None:
                                fn(e).then_inc(sems[ev[0]], 16)
                            else:
                                e.dma_start(out=out, in_=in_, **kw).then_inc(sems[ev[0]], 16)
                    if eng == "sp":
                        for sname, n in self.dma_cnt.items():
                            e.wait_ge(sems[sname], 16 * n)
                        for en in ENGS:
                            if en != "sp" and self.cnt[en] > 0:
                                e.wait_ge(sems["c_" + en], self.cnt[en])
                return body

            block.sync(run("sp"))
            block.tensor(run("pe"))
            block.vector(run("dve"))
            block.scalar(run("act"))
            block.gpsimd(run("pool"))


def _aps(*xs):
    return [x for x in xs if x is not None and not isinstance(x, (int, float))]


class K:
    def __init__(self, P):
        self.P = P

    def mm(self, out, lhsT, rhs, start=True, stop=True):
        self.P.op("pe", lambda e: e.matmul(out, lhsT=lhsT, rhs=rhs, start=start, stop=stop),
                  reads=[lhsT, rhs], writes=[out])

    def tr(self, out, in_, ident):
        self.P.op("pe", lambda e: e.transpose(out, in_, ident), reads=[in_, ident], writes=[out])

    def tt(self, out, a, b, op, eng="dve"):
        self.P.op(eng, lambda e: e.tensor_tensor(out=out, in0=a, in1=b, op=op), reads=[a, b], writes=[out])

    def ts(self, out, a, s1, op0, s2=None, op1=None, eng="dve"):
        if op1 is None:
            fn = lambda e: e.tensor_scalar(out=out, in0=a, scalar1=s1, scalar2=None, op0=op0)
        else:
            fn = lambda e: e.tensor_scalar(out=out, in0=a, scalar1=s1, scalar2=s2, op0=op0, op1=op1)
        self.P.op(eng, fn, reads=_aps(a, s1, s2), writes=[out])

    def stt(self, out, a, s, b, op0, op1, eng="dve"):
        self.P.op(eng, lambda e: e.scalar_tensor_tensor(out=out, in0=a, scalar=s, in1=b, op0=op0, op1=op1),
                  reads=_aps(a, s, b), writes=[out])

    def red(self, out, in_, op=ALU.add, negate=False, axis=AX.X):
        self.P.op("dve", lambda e: e.tensor_reduce(out=out, in_=in_, axis=axis, op=op, negate=negate),
                  reads=[in_], writes=[out])

    def cp(self, out, in_, eng="dve"):
        if eng == "act":
            self.P.op("act", lambda e: e.copy(out, in_), reads=[in_], writes=[out])
        else:
            self.P.op(eng, lambda e: e.tensor_copy(out, in_), reads=[in_], writes=[out])

    def act(self, out, in_, func, bias=None, scale=None, accum=None):
        kw = {}
        if bias is not None:
            kw["bias"] = bias
        if scale is not None:
            kw["scale"] = scale
        if accum is not None:
            kw["accum_out"] = accum
        self.P.op("act", lambda e: e.activation(out=out, in_=in_, func=func, **kw),
                  reads=_aps(in_, bias, scale), writes=_aps(out, accum))

    def recip(self, out, in_):
        self.P.op("dve", lambda e: e.reciprocal(out, in_), reads=[in_], writes=[out])

    def memset(self, ap, v, eng="pool"):
        self.P.op(eng, lambda e: e.memset(ap, v), writes=[ap])

    def dma(self, out, in_, eng="sp", **kw):
        self.P.dma(eng, out, in_, **kw)


def bc(ap, shape):
    return ap.to_broadcast(shape)


class Ctx:
    pass


def build(npool=2560, tp=TP, layers=(0, 1, 2, 3), dbg=False):
    nc = bass.Bass("TRN2", target_bir_lowering=False)
    C = Ctx()
    C.nc = nc
    C.tp = tp
    P = Prog(nc)
    k = K(P)
    C.P, C.k = P, k

    def din(name, shape, dt=F32):
        return nc.dram_tensor(name, list(shape), dt, kind="ExternalInput").ap()

    def dout(name, shape):
        return nc.dram_tensor(name, list(shape), F32, kind="ExternalOutput").ap()

    def dscr(name, shape, dt=F32):
        return nc.dram_tensor(name, list(shape), dt, kind="Internal").ap()

    I = {}
    for nm, shp in [("xp", (tp, D)), ("xs", (128, D)), ("st_S", (2, NS, 16, 64, 64)), ("st_shift", (2, NS, A_NC)),
                    ("st_pool", (NS, 15, D)), ("cmp_k", (npool * 128, 256)), ("cmp_v", (npool * 128, 256)),
                    ("sel_k", (npool * 128, 256)), ("sel_v", (npool * 128, 256)), ("win_k", (NS, 512, 256)),
                    ("win_v", (NS, 512, 256)), ("ln_g", (4, D)), ("ln_b", (4, D)), ("a_w_in", (2, D, A_NC)),
                    ("a_mu", (2, A_NC)), ("a_w0", (2, D)), ("a_w2", (2, 64, D)), ("a_a0", (2, D)), ("a_a2", (2, 64, D)),
                    ("a_k_k", (2, D)), ("a_k_a", (2, D)), ("a_r_k", (2, D)), ("a_lnx_g", (2, D)), ("a_lnx_b", (2, D)),
                    ("a_w_out", (2, D, D)), ("b_w_in", (D, 2 * D)), ("b_w_grp", (4, 256, 256)), ("b_scale", (1, D)),
                    ("b_w_out", (D, D)), ("c_w_in", (D, C_NC)), ("c_cmp_wk", (1, 32)), ("c_cmp_wv", (1, 32)),
                    ("c_w_out", (D, D)), ("identf", (128, 128)), ("cmask", (128, 2048))]:
        I[nm] = din(nm, shp)
    I["ptab"] = din("ptab", (NS, 16), I32)
    for nm, shp in [("n_bc_p", (16, 4, 64, 512)), ("n_bs_p", (16, 4, 128, 512)), ("n_bw_p", (5, 4, 128, 512)),
                    ("n_cb_p", (16, 128, 32)), ("n_ft_p", (16, 128, 32)), ("n_pair_p", (64, 32)),
                    ("n_wbm", (128, 124)), ("n_iota", (128, 1)), ("n_bc_s", (4, 64, 32)), ("n_bs_s", (17, 4, 128, 32)),
                    ("n_bw_s", (5, 4, 128, 32)), ("n_cb_s", (8, 33)), ("n_ft_s", (8, 33)), ("n_pair_s", (64, 33)),
                    ("n_hb", (128, 240))]:
        I[nm] = din(nm, shp)
    I["n_eexp_p"] = din("n_eexp_p", (32, 2048), BF16)
    I["n_eexp_s"] = din("n_eexp_s", (33, 17 * 128), BF16)
    I["selb"] = din("selb", (128, 64 * 128), BF16)
    O = {}
    for nm, shp in [("y_p", (tp, D)), ("y_s", (128, D)), ("S_p", (2, 16, 64, 64)), ("S_s", (2, NS, 16, 64, 64)),
                    ("sh_p", (2, A_NC)), ("sh_s", (2, NS, A_NC)), ("pl_p", (15, D)), ("pl_s", (NS, 15, D)),
                    ("cmpk_p", (tp, 256)), ("cmpk_s", (128, 256)), ("cmpv_p", (tp, 256)), ("cmpv_s", (128, 256)),
                    ("selk_p", (tp, 256)), ("selk_s", (128, 256)), ("selv_p", (tp, 256)), ("selv_s", (128, 256)),
                    ("wink_p", (512, 256)), ("wink_s", (NS, 512, 256)), ("winv_p", (512, 256)), ("winv_s", (NS, 512, 256))]:
        O[nm] = dout(nm, shp)
    if dbg:
        O["dbg_p"] = dout("dbg_p", (tp, D))
        O["dbg_s"] = dout("dbg_s", (128, D))
    C.I, C.O = I, O
    xa_p, xa_s = dscr("xa_p", (tp, D)), dscr("xa_s", (128, D))
    xb_p, xb_s = dscr("xb_p", (tp, D)), dscr("xb_s", (128, D))
    C.wbf = dscr("wbf", (128, 8, A_NC), BF16)
    C.kvs_scr = dscr("kvs_scr", (128, 1536))

    with contextlib.ExitStack() as gst:
        C.identf = gst.enter_context(nc.sbuf_tensor("identf_sb", [128, 128], F32))
        C.identb = gst.enter_context(nc.sbuf_tensor("identb_sb", [128, 128], BF16))
        C.ps = [gst.enter_context(nc.psum_tensor("psb%d" % i, [128, 512], F32)) for i in range(8)]
        k.dma(C.identf[:], I["identf"])
        k.cp(C.identb[:], C.identf[:])
        chain = [(I["xp"], I["xs"]), (xa_p, xa_s), (xb_p, xb_s), (xa_p, xa_s), (O["y_p"], O["y_s"])]
        for L in range(DEPTH):
            if L not in layers:
                continue
            src, dst = chain[L], chain[L + 1]
            if L == max(layers) and dbg:
                dst = (O["dbg_p"], O["dbg_s"])
            P.barrier()
            with contextlib.ExitStack() as lst:
                if L % 3 == 0:
                    rwkv_layer(C, lst, L // 3, L, src, dst)
                elif L % 3 == 1:
                    pool_layer(C, lst, L, src, dst)
                else:
                    nsa_layer(C, lst, L, src, dst)
                P.barrier()
        P.emit()
    return nc


def ln_tail(C, R, npart, L, dst_rows, T1, crow):
    k, I = C.k, C.I
    st = C.lnst
    k.red(st[0:npart, 0:1], R[0:npart, :])
    k.ts(st[0:npart, 1:2], st[0:npart, 0:1], 1.0 / D, ALU.mult)
    k.ts(R[0:npart, :], R[0:npart, :], st[0:npart, 1:2], ALU.subtract)
    k.tt(T1[0:npart, :], R[0:npart, :], R[0:npart, :], ALU.mult)
    k.red(st[0:npart, 2:3], T1[0:npart, :])
    k.act(st[0:npart, 3:4], st[0:npart, 2:3], AF.Sqrt, bias=C.epsln[0:npart, :], scale=1.0 / D)
    k.recip(st[0:npart, 4:5], st[0:npart, 3:4])
    k.ts(R[0:npart, :], R[0:npart, :], st[0:npart, 4:5], ALU.mult)
    k.dma(crow[0][0:npart, :], I["ln_g"][L:L + 1, :].partition_broadcast(npart), eng="act")
    k.tt(R[0:npart, :], R[0:npart, :], crow[0][0:npart, :], ALU.mult)
    k.dma(crow[1][0:npart, :], I["ln_b"][L:L + 1, :].partition_broadcast(npart), eng="act")
    k.tt(R[0:npart, :], R[0:npart, :], crow[1][0:npart, :], ALU.add)
    k.dma(dst_rows, R[0:npart, :])


def rwkv_layer(C, lst, li, L, src, dst):
    nc, P, k, I, O = C.nc, C.P, C.k, C.I, C.O
    tp = C.tp
    sb = lambda n, s, d=F32: lst.enter_context(nc.sbuf_tensor("a%d_" % L + n, list(s), d))
    ps = C.ps
    Wo = sb("Wo", [128, 8, 1024], BF16)
    Wll = sb("Wll", [128, 8, 128], BF16)
    WG = [sb("WG0", [128, 8, 1024], BF16)]
    W2A2 = sb("W2A2", [128, 1024])
    mucol = sb("mucol", [128, 9])
    SEL = sb("SEL", [128, 64, 128], BF16)
    xin2 = sb("xin2", [128, 1024])
    xin = sb("xin", [64, 1024])
    xTd = sb("xTd", [128, 8, 128], BF16)
    xTsd = sb("xTsd", [128, 8, 128], BF16)
    Pt = sb("Pt", [128, 1024])
    PSt = sb("PSt", [128, 1024])
    PM = {g: sb("PM" + g, [128, 1024]) for g in "rkvz"}
    crow = [sb("crow%d" % i, [128, 1024]) for i in range(2)]
    At = sb("At", [128, 1024])
    KP = sb("KP", [128, 1024])
    T1 = sb("T1", [128, 1024])
    T2 = sb("T2", [128, 1024])
    XRf = sb("XRf", [128, 512])
    XRr = sb("XRr", [128, 512])
    XR = {x: [sb("XR%s%d" % (x, j), [128, 512], BF16) for j in range(2)] for x in ("kk", "w", "ka", "k", "r")}
    va = sb("va", [128, 8, 64])
    vs = sb("vs", [128, 8, 64])
    vT = sb("vT", [128, 8, 64])
    lla = sb("lla", [128, 128])
    llb = sb("llb", [128, 128])
    LLt = sb("LLt", [128, 128])
    YT = sb("YT", [128, 8, 64])
    S = sb("S", [128, 8, 64])
    t1 = sb("t1", [128, 8, 64])
    t2 = sb("t2", [128, 8, 64])
    t3 = sb("t3", [128, 8, 64])
    sa = sb("sa", [128, 8])
    st16 = sb("st16", [128, 5, 16])
    bon = sb("bon", [128, 16])
    G = sb("G", [64, 1024], BF16)
    gT = sb("gT", [128, 8, 64], BF16)
    C.lnst = sb("lnst", [128, 8])
    C.epsln = sb("epsln", [128, 1])
    epsgn = sb("epsgn", [128, 1])
    eps24 = sb("eps24", [128, 1])
    k.memset(C.epsln[:], LN_EPS)
    k.memset(epsgn[:], GN_EPS)
    k.memset(eps24[:], 0.0)

    w_in = I["a_w_in"][li].rearrange("(c p) n -> p c n", p=128)
    for j in range(A_NC // 128):
        stg = Pt[:].rearrange("p (c n) -> p c n", c=8) if j % 2 == 0 else PSt[:].rearrange("p (c n) -> p c n", c=8)
        stgb = (T1 if j % 2 == 0 else T2)[:].bitcast(BF16)[:, 0:1024].rearrange("p (c n) -> p c n", c=8)
        k.dma(stg, w_in[:, :, j * 128:(j + 1) * 128], eng="sp" if j % 2 == 0 else "act")
        k.cp(stgb, stg, eng="pool" if j % 2 == 0 else "act")
        k.dma(C.wbf[:, :, j * 128:(j + 1) * 128], stgb, eng="sp")
    w_out = I["a_w_out"][li].rearrange("(c p) n -> p c n", p=128)
    for j in range(8):
        stg = Pt[:].rearrange("p (c n) -> p c n", c=8) if j % 2 == 0 else PSt[:].rearrange("p (c n) -> p c n", c=8)
        k.dma(stg, w_out[:, :, j * 128:(j + 1) * 128], eng="sp" if j % 2 == 0 else "act")
        k.cp(Wo[:, :, j * 128:(j + 1) * 128], stg, eng="pool" if j % 2 == 0 else "act")
    k.dma(Wll[:], C.wbf[:, :, 4096:4224])
    k.dma(W2A2[0:64, :], I["a_w2"][li])
    k.dma(W2A2[64:128, :], I["a_a2"][li])
    k.dma(mucol[:, 0:8], I["a_mu"][li, 2048:3072].rearrange("(c p) -> p c", p=128), allow_slow_non_contiguous=True)
    k.dma(mucol[:, 8:9], I["a_mu"][li, 4096:4224].rearrange("(c p) -> p c", p=128), allow_slow_non_contiguous=True)
    k.dma(SEL[:], I["selb"].rearrange("p (t m) -> p t m", m=128))
    k.memset(S[:], 0.0)
    EPI = sb("EPI", [64, 1024])
    EPN = sb("EPN", [64, 1024])
    EPX = sb("EPX", [64, 1024])
    FMAR = sb("FMAR", [64, 8, 128])
    FMB = sb("FMB", [64, 8, 64])
    FMK = sb("FMK", [64, 8, 64])
    GB = sb("GB", [64, 8, 128])
    GK = sb("GK", [64, 8, 128])
    PQ = [sb("PQ%d" % i, [64, 8, 64]) for i in range(4)]
    Tm = sb("Tm", [64, 8, 64])
    XT = sb("XT", [64, 8, 64])
    UT = sb("UT", [64, 8, 64])
    ST = sb("ST", [64, 16, 64])
    PCc = sb("PCc", [64, 16])
    MK = sb("MK", [64, 320])
    k.dma(MK[:], I["cmask"][0:64, 512:832])
    MASKAR = MK[:, 0:128]
    MASKNT = MK[:, 128:192]
    TRI = MK[:, 192:256]
    IDN = MK[:, 256:320]
    k.memset(ST[:], 0.0)
    pbi = [0]

    def bank():
        pbi[0] = (pbi[0] + 1) % 8
        return ps[pbi[0]]
    import os
    STOP = int(os.environ.get('STOPAT', '99'))
    if STOP <= 1:
        return

    cri = [0]

    def jrow(src_row, npart=128):
        t = crow[cri[0] % 2]
        cri[0] += 1
        k.dma(t[0:npart, :], src_row.partition_broadcast(npart), eng="act")
        return t

    def h4(t):
        return t[:].rearrange("p (a b j) -> p a b j", a=8, b=2)

    def toxr(X, name):
        X4 = h4(X)
        o3 = XRf[:].rearrange("p (a j) -> p a j", a=8)
        k.cp(o3[0:64], X4[0:64, :, 0, :], eng="act")
        k.cp(o3[64:128], X4[64:128, :, 1, :], eng="act")
        k.cp(XR[name][0][:], XRf[:], eng="pool")
        k.tt(XRr[:], XRf[:], XR[name][0][:], ALU.subtract, eng="pool")
        k.cp(XR[name][1][:], XRr[:], eng="pool")

    ntile_p = tp // 64
    import os
    tiles = [("p", n) for n in range(ntile_p)] + ([("s", 0), ("s", 1)] if not os.environ.get("NOSAMPLE") else [])
    wg_i = [0]
    for kind, n in tiles:
        srcx = src[0] if kind == "p" else src[1]
        dstx = dst[0] if kind == "p" else dst[1]
        r0 = n * 64
        if r0 == 0:
            k.memset(xin2[0:1, :], 0.0)
            k.dma(xin2[1:64, :], srcx[0:63, :])
        else:
            k.dma(xin2[0:64, :], srcx[r0 - 1:r0 + 63, :])
        k.dma(xin[:], srcx[r0:r0 + 64, :], eng="act")
        for b in range(2):
            for c in range(4):
                k.tr(ps[b][:, c * 64:(c + 1) * 64], xin[0:64, (4 * b + c) * 128:(4 * b + c + 1) * 128], C.identf[0:64, 0:64])
            for c in range(4):
                k.tr(ps[b][:, 256 + c * 64:256 + (c + 1) * 64], xin2[0:64, (4 * b + c) * 128:(4 * b + c + 1) * 128], C.identf[0:64, 0:64])
            pv = ps[b][:, 0:256].rearrange("p (c t) -> p c t", c=4)
            pw = ps[b][:, 256:512].rearrange("p (c t) -> p c t", c=4)
            k.cp(xTd[:, 4 * b:4 * b + 4, 0:64], pv, eng="act")
            k.cp(xTd[:, 4 * b:4 * b + 4, 64:128], pv, eng="dve")
            k.cp(xTsd[:, 4 * b:4 * b + 4, 0:64], pw, eng="act")
            k.cp(xTsd[:, 4 * b:4 * b + 4, 64:128], pw, eng="dve")
        if STOP <= 2:
            return
        last_rows = []
        if kind == "p" and n == ntile_p - 1:
            last_rows = [(63, O["sh_p"][li])]
        if kind == "s":
            last_rows = [(sl * 8 + 7, O["sh_s"][li, n * 8 + sl]) for sl in range(8)]
        for gi, g in enumerate("rkvz"):
            wg = WG[0]
            wg_i[0] += 1
            k.dma(wg[:], C.wbf[:, :, gi * 1024:(gi + 1) * 1024], eng="sp")
            for hf in range(2):
                cols = slice(hf * 512, (hf + 1) * 512)
                for c in range(8):
                    k.mm(ps[2][:], xTd[:, c, :], wg[:, c, cols], start=(c == 0), stop=(c == 7))
                for c in range(8):
                    k.mm(ps[3][:], xTsd[:, c, :], wg[:, c, cols], start=(c == 0), stop=(c == 7))
                k.cp(Pt[:, cols], ps[2][:], eng="act")
                k.cp(PSt[:, cols], ps[3][:], eng="act")
            if g == "v" and kind == "s":
                for c in range(8):
                    for dc in range(8):
                        k.mm(ps[2][:, c * 64:(c + 1) * 64], wg[:, dc, c * 128:(c + 1) * 128], xTd[:, dc, 0:64],
                             start=(dc == 0), stop=(dc == 7))
                for c in range(8):
                    for dc in range(8):
                        k.mm(ps[3][:, c * 64:(c + 1) * 64], wg[:, dc, c * 128:(c + 1) * 128], xTsd[:, dc, 0:64],
                             start=(dc == 0), stop=(dc == 7))
                k.cp(va[:].rearrange("p c t -> p (c t)"), ps[2][:], eng="act")
                k.cp(vs[:].rearrange("p c t -> p (c t)"), ps[3][:], eng="act")
                if kind == "s":
                    for sl in range(8):
                        k.dma(vs[:, :, sl * 8], I["st_shift"][li, n * 8 + sl, 2048:3072].rearrange("(c p) -> p c", p=128), eng="act", allow_slow_non_contiguous=True)
                k.tt(vs[:], vs[:], va[:], ALU.subtract)
                k.tt(vs[:], vs[:], mucol[:, 0:8].unsqueeze(2).to_broadcast([128, 8, 64]), ALU.mult)
                k.tt(vT[:], vs[:], va[:], ALU.add)
            if kind == "s":
                for sl in range(8):
                    for hh in range(2):
                        k.dma(PSt[hh * 64 + sl * 8:hh * 64 + sl * 8 + 1, :],
                              I["st_shift"][li, n * 8 + sl:n * 8 + sl + 1, gi * 1024:(gi + 1) * 1024], eng="act")
            for (row, dap) in last_rows:
                k.dma(dap[gi * 1024:(gi + 1) * 1024].unsqueeze(0), Pt[row:row + 1, :], eng="act")
            mur = jrow(I["a_mu"][li:li + 1, gi * 1024:(gi + 1) * 1024])
            k.tt(PSt[:], PSt[:], Pt[:], ALU.subtract)
            k.tt(PSt[:], PSt[:], mur[:], ALU.mult)
            k.tt(PM[g][:], PSt[:], Pt[:], ALU.add)
        if STOP <= 3:
            return
        for c in range(8):
            k.mm(ps[2][:, 0:128], Wll[:, c, :], xTd[:, c, :], start=(c == 0), stop=(c == 7))
        for c in range(8):
            k.mm(ps[3][:, 0:128], Wll[:, c, :], xTsd[:, c, :], start=(c == 0), stop=(c == 7))
        k.cp(lla[:], ps[2][:, 0:128], eng="act")
        k.cp(llb[:], ps[3][:, 0:128], eng="act")
        if kind == "s":
            for sl in range(8):
                for hh in range(2):
                    k.dma(llb[:, hh * 64 + sl * 8:hh * 64 + sl * 8 + 1],
                          I["st_shift"][li, n * 8 + sl, 4096:4224].rearrange("(c p) -> p c", p=128), eng="act", allow_slow_non_contiguous=True)
        for (row, dap) in last_rows:
            k.dma(dap[4096:4224].rearrange("(c p) -> p c", p=128), lla[:, row:row + 1], eng="act", allow_slow_non_contiguous=True)
        k.tt(llb[:], llb[:], lla[:], ALU.subtract)
        k.stt(LLt[:], llb[:], mucol[:, 8:9], lla[:], ALU.mult, ALU.add)
        k.act(LLt[0:64, :], LLt[0:64, :], AF.Tanh)
        if STOP <= 4:
            return
        for hf in range(2):
            cols = slice(hf * 512, (hf + 1) * 512)
            k.mm(ps[2 + hf][:], LLt[0:64, :], W2A2[0:64, cols])
            k.mm(ps[4 + hf][:], LLt[64:128, :], W2A2[64:128, cols])
        w0r = jrow(I["a_w0"][li:li + 1, :])
        for hf in range(2):
            cols = slice(hf * 512, (hf + 1) * 512)
            k.tt(T1[:, cols], ps[2 + hf][:], w0r[:, cols], ALU.add)
        k.act(T1[:], T1[:], AF.Sigmoid)
        CC = float(np.exp(-0.5))
        if kind == "p":
            for hf in range(2):
                cols = slice(hf * 512, (hf + 1) * 512)
                k.mm(ps[6 + hf][0:64, :], TRI, T1[0:64, cols])
                k.act(EPI[:, cols], ps[6 + hf][0:64, :], AF.Exp, scale=-CC)
                k.act(EPN[:, cols], ps[6 + hf][0:64, :], AF.Exp, scale=CC)
                k.tt(EPX[:, cols], ps[6 + hf][0:64, :], T1[0:64, cols], ALU.subtract)
            k.act(EPX[:], EPX[:], AF.Exp, scale=-CC)
            for h in range(16):
                k.mm(ps[2][0:64, h:h + 1], EPI[:, h * 64:(h + 1) * 64], C.identf[0:64, 63:64])
            k.cp(PCc[:], ps[2][0:64, 0:16], eng="act")
        else:
            k.act(T1[:], T1[:], AF.Exp, scale=-CC)
            toxr(T1, "w")
        a0r = jrow(I["a_a0"][li:li + 1, :])
        for hf in range(2):
            cols = slice(hf * 512, (hf + 1) * 512)
            k.tt(At[:, cols], ps[4 + hf][:], a0r[:, cols], ALU.add)
        k.act(At[:], At[:], AF.Sigmoid)
        kkr = jrow(I["a_k_k"][li:li + 1, :])
        k.tt(T1[:], PM["k"][:], kkr[:], ALU.mult)
        k.tt(T2[:], T1[:], T1[:], ALU.mult)
        k.red(st16[:, 0, :], T2[:].rearrange("p (h j) -> p h j", h=16))
        k.ts(st16[:, 0, :], st16[:, 0, :], 1e-24, ALU.max)
        k.act(st16[:, 1, :], st16[:, 0, :], AF.Sqrt)
        k.recip(st16[:, 2, :], st16[:, 1, :])
        k.tt(T1[:].rearrange("p (h j) -> p h j", h=16), T1[:].rearrange("p (h j) -> p h j", h=16),
             st16[:, 2, :].unsqueeze(2).to_broadcast([128, 16, 64]), ALU.mult)
        if kind == "s":
            toxr(T1, "kk")
        k.tt(T2[:], T1[:], At[:], ALU.mult)
        if kind == "s":
            toxr(T2, "ka")
        kar = jrow(I["a_k_a"][li:li + 1, :])
        k.stt(PSt[:], At[:], -1.0, kar[:], ALU.add, ALU.mult)
        k.stt(KP[:], PSt[:], 1.0, PM["k"][:], ALU.add, ALU.mult)
        if kind == "s":
            toxr(KP, "k")
            toxr(PM["r"], "r")
        rkr = jrow(I["a_r_k"][li:li + 1, :])
        k.tt(Pt[:], PM["r"][:], rkr[:], ALU.mult)
        k.tt(Pt[:], Pt[:], KP[:], ALU.mult)
        k.red(bon[:], Pt[:].rearrange("p (h j) -> p h j", h=16))
        if kind == "p":
            k.stt(EPX[:], T1[0:64, :], -1.0, EPX[:], ALU.mult, ALU.mult)
            k.tt(T2[0:64, :], T2[0:64, :], EPN[:], ALU.mult)
            k.tt(EPN[:], KP[0:64, :], EPN[:], ALU.mult)
            k.tt(EPI[:], PM["r"][0:64, :], EPI[:], ALU.mult)

        if STOP <= 5:
            return
        def step(Sx, tl):
            bks = {}
            for bi, x in enumerate(("kk", "w", "ka", "k", "r")):
                bk = ps[3 + bi] if bi < 5 else None
                k.mm(bk[:], SEL[:, tl, :], XR[x][0][:], start=True, stop=False)
                k.mm(bk[:], SEL[:, tl, :], XR[x][1][:], start=False, stop=True)
                bks[x] = bk[:].rearrange("p (a j) -> p a j", a=8)
            k.tt(t1[:], Sx, bks["kk"], ALU.mult)
            k.tt(t3[:], bks["k"], vT[:, :, tl:tl + 1].to_broadcast([128, 8, 64]), ALU.mult)
            k.red(sa[:], t1[:], negate=True)
            k.tt(Sx, Sx, bks["w"], ALU.mult)
            k.tt(t2[:], bks["ka"], sa[:].unsqueeze(2).to_broadcast([128, 8, 64]), ALU.mult)
            k.tt(Sx, Sx, t3[:], ALU.add)
            k.tt(Sx, Sx, t2[:], ALU.add)
            k.tt(t1[:], Sx, bks["r"], ALU.mult)
            k.red(YT[:, :, tl], t1[:])

        if kind == "p":
            i64 = C.identf[0:64, 0:64]
            for hh in range(2):
                H0 = hh * 8
                bA = [bank(), bank()]
                bB, bK = bank(), bank()
                for hl in range(8):
                    hc = slice((H0 + hl) * 64, (H0 + hl + 1) * 64)
                    o = (hl % 4) * 128
                    k.tr(bA[hl // 4][0:64, o:o + 64], EPX[:, hc], i64)
                    k.tr(bA[hl // 4][0:64, o + 64:o + 128], EPI[:, hc], i64)
                    k.tr(bB[0:64, hl * 64:(hl + 1) * 64], T2[0:64, hc], i64)
                    k.tr(bK[0:64, hl * 64:(hl + 1) * 64], EPN[:, hc], i64)
                k.cp(FMAR[:, 0:4, :].rearrange("p a b -> p (a b)"), bA[0][0:64, :], eng="act")
                k.cp(FMAR[:, 4:8, :].rearrange("p a b -> p (a b)"), bA[1][0:64, :], eng="act")
                k.cp(FMB[:].rearrange("p a b -> p (a b)"), bB[0:64, :], eng="dve")
                k.cp(FMK[:].rearrange("p a b -> p (a b)"), bK[0:64, :], eng="dve")
                for (Gd, FMl) in ((GB, FMB), (GK, FMK)):
                    bb = [bank(), bank()]
                    for hl in range(8):
                        o = (hl % 4) * 128
                        k.mm(bb[hl // 4][0:64, o:o + 128], FMl[:, hl, :], FMAR[:, hl, :])
                    for q in range(2):
                        k.tt(Gd[:, 4 * q:4 * q + 4, :], bb[q][0:64, :].rearrange("p (a b) -> p a b", a=4),
                             MASKAR.unsqueeze(1).to_broadcast([64, 4, 128]), ALU.mult)
                bq = bank()
                for hl in range(8):
                    k.mm(bq[0:64, hl * 64:(hl + 1) * 64], FMAR[:, hl, 0:64], FMB[:, hl, :])
                k.tt(PQ[1][:], bq[0:64, :].rearrange("p (a b) -> p a b", a=8), MASKNT.unsqueeze(1).to_broadcast([64, 8, 64]), ALU.mult)
                k.tt(Tm[:], GB[:, :, 0:64], IDN.unsqueeze(1).to_broadcast([64, 8, 64]), ALU.add)
                Pc, Qc = GB[:, :, 0:64], PQ[1]
                for lv in range(5):
                    Pn, Qn = PQ[2 * ((lv + 1) % 2)], PQ[2 * ((lv + 1) % 2) + 1]
                    if lv < 4:
                        bp = bank()
                        for hl in range(8):
                            k.mm(bp[0:64, hl * 64:(hl + 1) * 64], Qc[:, hl, :], Pc[:, hl, :])
                    bq = bank()
                    for hl in range(8):
                        k.mm(bq[0:64, hl * 64:(hl + 1) * 64], Pc[:, hl, :], Qc[:, hl, :])
                    if lv < 4:
                        k.cp(Pn[:].rearrange("p a b -> p (a b)"), bp[0:64, :], eng="act")
                    k.cp(Qn[:].rearrange("p a b -> p (a b)"), bq[0:64, :], eng="act")
                    bt = bank()
                    for hl in range(8):
                        k.mm(bt[0:64, hl * 64:(hl + 1) * 64], Qn[:, hl, :], Tm[:, hl, :])
                    k.tt(Tm[:], Tm[:], bt[0:64, :].rearrange("p (a b) -> p a b", a=8), ALU.add)
                    Pc, Qc = Pn, Qn
                bx = bank()
                for hl in range(8):
                    hc = slice((H0 + hl) * 64, (H0 + hl + 1) * 64)
                    k.mm(bx[0:64, hl * 64:(hl + 1) * 64], FMAR[:, hl, 0:64], ST[:, H0 + hl, :], start=True, stop=False)
                    k.mm(bx[0:64, hl * 64:(hl + 1) * 64], GK[:, hl, 0:64], PM["v"][0:64, hc], start=False, stop=True)
                k.cp(XT[:].rearrange("p a b -> p (a b)"), bx[0:64, :], eng="act")
                bu = bank()
                for hl in range(8):
                    k.mm(bu[0:64, hl * 64:(hl + 1) * 64], Tm[:, hl, :], XT[:, hl, :])
                k.cp(UT[:].rearrange("p a b -> p (a b)"), bu[0:64, :], eng="act")
                by = bank()
                for hl in range(8):
                    hc = slice((H0 + hl) * 64, (H0 + hl + 1) * 64)
                    k.mm(by[0:64, hl * 64:(hl + 1) * 64], FMAR[:, hl, 64:128], ST[:, H0 + hl, :], start=True, stop=False)
                    k.mm(by[0:64, hl * 64:(hl + 1) * 64], GB[:, hl, 64:128], UT[:, hl, :], start=False, stop=False)
                    k.mm(by[0:64, hl * 64:(hl + 1) * 64], GK[:, hl, 64:128], PM["v"][0:64, hc], start=False, stop=True)
                k.cp(KP[0:64, hh * 512:(hh + 1) * 512], by[0:64, :], eng="act")
                bs = bank()
                for hl in range(8):
                    hc = slice((H0 + hl) * 64, (H0 + hl + 1) * 64)
                    k.mm(bs[0:64, hl * 64:(hl + 1) * 64], T2[0:64, hc], UT[:, hl, :], start=True, stop=False)
                    k.mm(bs[0:64, hl * 64:(hl + 1) * 64], EPN[:, hc], PM["v"][0:64, hc], start=False, stop=True)
                k.tt(ST[:, H0:H0 + 8, :], ST[:, H0:H0 + 8, :], bs[0:64, :].rearrange("p (a b) -> p a b", a=8), ALU.add)
                k.tt(ST[:, H0:H0 + 8, :], ST[:, H0:H0 + 8, :], PCc[:, H0:H0 + 8].unsqueeze(2).to_broadcast([64, 8, 64]), ALU.mult)
            if n == ntile_p - 1:
                for q in range(2):
                    bo = bank()
                    for hl in range(8):
                        k.tr(bo[0:64, hl * 64:(hl + 1) * 64], ST[:, q * 8 + hl, :], i64)
                    k.cp(EPI[:, q * 512:(q + 1) * 512], bo[0:64, :], eng="act")
                k.dma(O["S_p"][li].rearrange("h i j -> i h j"), EPI[:].rearrange("p (h j) -> p h j", h=16))
        else:
            for sl in range(8):
                sq = n * 8 + sl
                k.dma(t3[:], I["st_S"][li, sq].rearrange("(a b) i j -> (b i) a j", b=2))
                Sx = KP[:, 0:512].rearrange("p (a j) -> p a j", a=8)
                k.cp(Sx, t3[:], eng="act")
                for t in range(8):
                    step(Sx, sl * 8 + t)
                k.dma(O["S_s"][li, sq].rearrange("(a b) i j -> (b i) a j", b=2), Sx)

        if STOP <= 6:
            return
        Y = T1
        if kind == "p":
            k.cp(Y[0:64, :], KP[0:64, :], eng="pool")
        else:
            for c in range(8):
                k.tr(ps[c // 4][0:64, (c % 4) * 128:(c % 4 + 1) * 128], YT[:, c, :], C.identf[:, :])
            k.cp(Y[0:64, 0:512], ps[0][0:64, :], eng="act")
            k.cp(Y[0:64, 512:1024], ps[1][0:64, :], eng="act")
        Y3 = Y[0:64, :].rearrange("p (h j) -> p h j", h=16)
        k.red(st16[0:64, 0, :], Y3)
        k.ts(st16[0:64, 0, :], st16[0:64, 0, :], 1.0 / 64, ALU.mult)
        k.tt(Y3, Y3, st16[0:64, 0, :].unsqueeze(2).to_broadcast([64, 16, 64]), ALU.subtract)
        k.tt(T2[0:64, :], Y[0:64, :], Y[0:64, :], ALU.mult)
        k.red(st16[0:64, 1, :], T2[0:64, :].rearrange("p (h j) -> p h j", h=16))
        k.act(st16[0:64, 2, :], st16[0:64, 1, :], AF.Sqrt, bias=epsgn[0:64, :], scale=1.0 / 64)
        k.recip(st16[0:64, 3, :], st16[0:64, 2, :])
        k.tt(Y3, Y3, st16[0:64, 3, :].unsqueeze(2).to_broadcast([64, 16, 64]), ALU.mult)
        gr = jrow(I["a_lnx_g"][li:li + 1, :], 64)
        k.tt(Y[0:64, :], Y[0:64, :], gr[0:64, :], ALU.mult)
        br = jrow(I["a_lnx_b"][li:li + 1, :], 64)
        k.tt(Y[0:64, :], Y[0:64, :], br[0:64, :], ALU.add)
        k.tt(T2[0:64, :].rearrange("p (h j) -> p h j", h=16), PM["v"][0:64, :].rearrange("p (h j) -> p h j", h=16),
             bon[0:64, :].unsqueeze(2).to_broadcast([64, 16, 64]), ALU.mult)
        k.tt(Y[0:64, :], Y[0:64, :], T2[0:64, :], ALU.add)
        k.act(T2[0:64, :], PM["z"][0:64, :], AF.Silu)
        k.tt(G[:], Y[0:64, :], T2[0:64, :], ALU.mult)
        psb = ps[2][:].bitcast(BF16)
        for c in range(8):
            k.tr(psb[:, c * 64:(c + 1) * 64], G[:, c * 128:(c + 1) * 128], C.identb[0:64, 0:64])
        k.cp(gT[:].rearrange("p c t -> p (c t)"), psb[:, 0:512], eng="act")
        for hf in range(2):
            cols = slice(hf * 512, (hf + 1) * 512)
            for c in range(8):
                k.mm(ps[hf][0:64, :], gT[:, c, :], Wo[:, c, cols], start=(c == 0), stop=(c == 7))
            k.stt(Y[0:64, cols], xin[:, cols], ALPHA, ps[hf][0:64, :], ALU.mult, ALU.add)
        ln_tail(C, Y, 64, L, dstx[r0:r0 + 64, :], T2, crow)


def pool_layer(C, lst, L, src, dst):
    nc, P, k, I, O = C.nc, C.P, C.k, C.I, C.O
    tp = C.tp
    sb = lambda n, s, d=F32: lst.enter_context(nc.sbuf_tensor("b_" + n, list(s), d))
    ps = C.ps
    Win = sb("Win", [128, 8, 2048], BF16)
    Wg = sb("Wg", [128, 4, 2, 256], BF16)
    Wo = sb("Wo", [128, 8, 1024], BF16)
    scol = sb("scol", [128, 8])
    stg = [sb("stg%d" % i, [128, 8, 128]) for i in range(2)]
    xin = sb("xin", [128, 1024])
    xT = sb("xT", [128, 8, 128], BF16)
    E = sb("E", [128, 8, 16 * 23])
    A = sb("A", [128, 2, 16 * 23])
    B = sb("B", [128, 2, 16 * 23])
    dT = sb("dT", [128, 8, 128], BF16)
    sz = sb("sz", [128, 8, 128])
    gT = sb("gT", [128, 8, 128], BF16)
    R = sb("R", [128, 1024])
    T1 = sb("T1", [128, 1024])
    cinv = sb("cinv", [128, 512])
    crow = [sb("crow%d" % i, [128, 1024]) for i in range(2)]
    C.lnst = sb("lnst", [128, 8])
    C.epsln = sb("epsln", [128, 1])
    k.memset(C.epsln[:], LN_EPS)
    k.dma(cinv[:], I["cmask"][:, 0:512])
    w_in = I["b_w_in"].rearrange("(c p) n -> p c n", p=128)
    for j in range(16):
        k.dma(stg[j % 2][:], w_in[:, :, j * 128:(j + 1) * 128], eng="sp" if j % 2 == 0 else "act")
        k.cp(Win[:, :, j * 128:(j + 1) * 128], stg[j % 2][:], eng="pool" if j % 2 == 0 else "act")
    w_out = I["b_w_out"].rearrange("(c p) n -> p c n", p=128)
    for j in range(8):
        k.dma(stg[j % 2][:], w_out[:, :, j * 128:(j + 1) * 128], eng="sp" if j % 2 == 0 else "act")
        k.cp(Wo[:, :, j * 128:(j + 1) * 128], stg[j % 2][:], eng="pool" if j % 2 == 0 else "act")
    for g in range(4):
        sv = stg[g % 2][:].rearrange("p c n -> p (c n)")[:, 0:512].rearrange("p (c n) -> p c n", c=2)
        k.dma(sv, I["b_w_grp"][g].rearrange("(c p) n -> p c n", p=128), eng="sp")
        k.cp(Wg[:, g, :, :], sv, eng="pool")
    k.dma(scol[:], I["b_scale"][0].rearrange("(c p) -> p c", p=128), allow_slow_non_contiguous=True)
    k.memset(E[:], 0.0)

    ntile_p = tp // 128
    tiles = [("p", n) for n in range(ntile_p)] + [("s", 0)]
    for kind, n in tiles:
        srcx = src[0] if kind == "p" else src[1]
        dstx = dst[0] if kind == "p" else dst[1]
        r0 = n * 128
        nseg, new = (1, 128) if kind == "p" else (16, 8)
        sl = 15 + new
        Ev = E[:, :, 0:nseg * sl].rearrange("p c (s t) -> p c s t", s=nseg)
        k.dma(xin[:], srcx[r0:r0 + 128, :])
        for b in range(2):
            for c in range(4):
                k.tr(ps[b][:, c * 128:(c + 1) * 128], xin[:, (4 * b + c) * 128:(4 * b + c + 1) * 128], C.identf[:, :])
            k.cp(xT[:, 4 * b:4 * b + 4, :].rearrange("p c t -> p (c t)"), ps[b][:], eng="act")
        if kind == "s":
            for hh in range(2):
                k.dma(R[0:120, :], I["st_pool"][hh * 8:(hh + 1) * 8].rearrange("s r d -> (s r) d"))
                for b in range(2):
                    for c in range(4):
                        k.tr(ps[2 + b][:, c * 120:(c + 1) * 120], R[0:120, (4 * b + c) * 128:(4 * b + c + 1) * 128], C.identf[0:120, 0:120])
                    k.cp(Ev[:, 4 * b:4 * b + 4, hh * 8:(hh + 1) * 8, 0:15],
                         ps[2 + b][:, 0:480].rearrange("p (c s r) -> p c s r", c=4, s=8), eng="act")
        for ob in range(4):
            for o4 in range(4):
                oc = ob * 4 + o4
                for dc in range(8):
                    k.mm(ps[4 + ob % 2][:, o4 * 128:(o4 + 1) * 128], Win[:, dc, oc * 128:(oc + 1) * 128], xT[:, dc, :],
                         start=(dc == 0), stop=(dc == 7))
            pv = ps[4 + ob % 2][:].rearrange("p (c s t) -> p c s t", c=4, s=nseg)
            if ob < 2:
                k.cp(Ev[:, ob * 4:ob * 4 + 4, :, 15:sl], pv, eng="act")
            else:
                k.act(sz[:, (ob - 2) * 4:(ob - 2) * 4 + 4, :].rearrange("p c t -> p (c t)"), ps[4 + ob % 2][:], AF.Silu)
        if kind == "p" and n == ntile_p - 1:
            for b in range(2):
                for c in range(4):
                    k.tr(ps[2 + b][0:15, c * 128:(c + 1) * 128], E[:, 4 * b + c, 128:143], C.identf[:, :])
                k.cp(T1[0:15, b * 512:(b + 1) * 512], ps[2 + b][0:15, :], eng="act")
            k.dma(O["pl_p"], T1[0:15, :])
        if kind == "s":
            for hh in range(2):
                for b in range(2):
                    for c in range(4):
                        Ac = A[:, 0, 0:120].rearrange("p (s r) -> p s r", s=8)
                        k.cp(Ac, Ev[:, 4 * b + c, hh * 8:(hh + 1) * 8, 8:23], eng="pool")
                        k.tr(ps[2 + b][0:120, c * 128:(c + 1) * 128], A[:, 0, 0:120], C.identf[:, :])
                    k.cp(T1[0:120, b * 512:(b + 1) * 512], ps[2 + b][0:120, :], eng="act")
                k.dma(O["pl_s"][hh * 8:(hh + 1) * 8].rearrange("s r d -> (s r) d"), T1[0:120, :])
        for g in range(4):
            cur = Ev[:, 2 * g:2 * g + 2]
            bufs = [A[:, :, 0:nseg * sl].rearrange("p c (s t) -> p c s t", s=nseg),
                    B[:, :, 0:nseg * sl].rearrange("p c (s t) -> p c s t", s=nseg)]
            lo = 0
            for si, sh in enumerate((1, 2, 4, 8)[:g + 1]):
                nxt = bufs[si % 2]
                lo2 = lo + sh
                k.tt(nxt[:, :, :, lo2:sl], cur[:, :, :, lo2:sl], cur[:, :, :, lo:sl - sh], ALU.add)
                cur, lo = nxt, lo2
            w = 2 ** (g + 1)
            pooled = bufs[(g + 1) % 2]
            if kind == "p" and n == 0:
                k.tt(pooled[:, :, 0, 15:sl], cur[:, :, 0, 15:sl],
                     cinv[:, g * 128:(g + 1) * 128].unsqueeze(1).to_broadcast([128, 2, 128]), ALU.mult)
                k.tt(dT[:, 2 * g:2 * g + 2, :], pooled[:, :, 0, 15:sl], Ev[:, 2 * g:2 * g + 2, 0, 15:sl], ALU.subtract)
            else:
                k.stt(dT[:, 2 * g:2 * g + 2, :].rearrange("p c (s t) -> p c s t", s=nseg), cur[:, :, :, 15:sl], 1.0 / w,
                      Ev[:, 2 * g:2 * g + 2, :, 15:sl], ALU.mult, ALU.subtract)
        if kind == "p":
            k.cp(A[:, 0, 0:120].rearrange("p (c t) -> p c t", c=8), E[:, :, 128:143], eng="pool")
            k.cp(E[:, :, 0:15], A[:, 0, 0:120].rearrange("p (c t) -> p c t", c=8), eng="pool")
        for jc in range(8):
            g, jl = jc // 2, jc % 2
            for ic in range(2):
                k.mm(ps[6 + jc // 4][:, (jc % 4) * 128:(jc % 4 + 1) * 128], Wg[:, g, ic, jl * 128:(jl + 1) * 128], dT[:, 2 * g + ic, :],
                     start=(ic == 0), stop=(ic == 1))
        for jc in range(8):
            k.stt(gT[:, jc, :], ps[6 + jc // 4][:, (jc % 4) * 128:(jc % 4 + 1) * 128], scol[:, jc:jc + 1], sz[:, jc, :], ALU.mult, ALU.mult)
        for hf in range(2):
            cols = slice(hf * 512, (hf + 1) * 512)
            for c in range(8):
                k.mm(ps[hf][:], gT[:, c, :], Wo[:, c, cols], start=(c == 0), stop=(c == 7))
            k.stt(R[:, cols], xin[:, cols], ALPHA, ps[hf][:], ALU.mult, ALU.add)
        ln_tail(C, R, 128, L, dstx[r0:r0 + 128, :], T1, crow)


NEG = -30000.0
SCL = 0.125


def nsa_layer(C, lst, L, src, dst):
    nc, P, k, I, O = C.nc, C.P, C.k, C.I, C.O
    tp = C.tp
    sb = lambda n, s, d=F32: lst.enter_context(nc.sbuf_tensor("c_" + n, list(s), d))
    ps = C.ps
    Win = sb("Win", [128, 8, C_NC], BF16)
    Wo = sb("Wo", [128, 8, 1024], BF16)
    stg = [sb("stg%d" % i, [128, 8, 128]) for i in range(2)]
    xin = sb("xin", [128, 1024])
    xT = sb("xT", [128, 8, 128], BF16)
    KV = sb("KV", [128, 1536])
    KsT = sb("KsT", [64, 4, 17 * 128], BF16)
    KwT = sb("KwT", [64, 4, 5 * 128], BF16)
    Vs = sb("Vs", [128, 17, 4, 65], BF16)
    Vw = sb("Vw", [128, 5, 4, 65], BF16)
    KcT = sb("KcT", [64, 4, 64], BF16)
    Vc = sb("Vc", [64, 4, 98])
    Wbk = sb("Wbk", [128, 124])
    Wbv = sb("Wbv", [128, 124])
    wcol = sb("wcol", [128, 2])
    QT = sb("QT", [64, 16, 128], BF16)
    GZ = sb("GZ", [128, 1072])
    gates = sb("gates", [128, 48])
    Bt = [sb("Bt%d" % i, [128, 512]) for i in range(4)]
    SBS = sb("SBS", [128, 17 * 128])
    SBW = sb("SBW", [128, 5 * 128])
    SBC = sb("SBC", [64, 128])
    hbias = sb("hbias", [128, 240])
    k.dma(hbias[:], I["n_hb"])
    GBUF = [sb("GBUF%d" % i, [128, 1024]) for i in range(2)]
    tmp = sb("tmp", [128, 512])
    TMP = [tmp, sb("tmp1", [128, 512])]
    ec = sb("ec", [64, 512])
    eb = sb("eb", [128, 512], BF16)
    EB = [eb, sb("eb1", [128, 512], BF16)]
    OB = sb("OB", [128, 4, 98])
    rd = sb("rd", [128, 8])
    imp = sb("imp", [128, 40])
    imp2 = sb("imp2", [128, 40])
    m8 = sb("m8", [128, 16])
    cbt = sb("cbt", [128, 80])
    selT = sb("selT", [40, 128], BF16)
    Eexp = sb("Eexp", [40, 17 * 128], BF16)
    Oacc = sb("Oacc", [128, 1024])
    Gb = sb("Gb", [128, 1024], BF16)
    gT = sb("gT", [128, 8, 128], BF16)
    T1 = sb("T1", [128, 1024])
    idx = sb("idx", [128, 256], I32)
    idf = GBUF[0][:, 0:256]
    crow = [x[:].rearrange("p c n -> p (c n)") for x in stg]
    C.lnst = sb("lnst", [128, 8])
    C.epsln = sb("epsln", [128, 1])
    k.memset(C.epsln[:], LN_EPS)
    w_in = I["c_w_in"].rearrange("(c p) n -> p c n", p=128)
    nj = (C_NC + 127) // 128
    for j in range(nj):
        wd = min(128, C_NC - j * 128)
        k.dma(stg[j % 2][:, :, 0:wd], w_in[:, :, j * 128:j * 128 + wd], eng="sp" if j % 2 == 0 else "act")
        k.cp(Win[:, :, j * 128:j * 128 + wd], stg[j % 2][:, :, 0:wd], eng="pool" if j % 2 == 0 else "act")
    w_out = I["c_w_out"].rearrange("(c p) n -> p c n", p=128)
    for j in range(8):
        k.dma(stg[j % 2][:], w_out[:, :, j * 128:(j + 1) * 128], eng="sp" if j % 2 == 0 else "act")
        k.cp(Wo[:, :, j * 128:(j + 1) * 128], stg[j % 2][:], eng="pool" if j % 2 == 0 else "act")
    for r in range(4):
        k.dma(wcol[r * 32:(r + 1) * 32, 0:1], I["c_cmp_wk"].rearrange("o l -> l o"), allow_slow_non_contiguous=True)
        k.dma(wcol[r * 32:(r + 1) * 32, 1:2], I["c_cmp_wv"].rearrange("o l -> l o"), allow_slow_non_contiguous=True)
    k.dma(Wbk[:], I["n_wbm"])
    k.cp(Wbv[:], Wbk[:], eng="pool")
    k.ts(Wbk[:], Wbk[:], wcol[:, 0:1], ALU.mult)
    k.ts(Wbv[:], Wbv[:], wcol[:, 1:2], ALU.mult)
    k.memset(Vs[:], 0.0)
    k.memset(Vw[:], 0.0)
    k.memset(KsT[:], 0.0)
    k.memset(KwT[:], 0.0)
    k.memset(Vs[:, :, :, 64:65], 1.0)
    k.memset(Vw[:, :, :, 64:65], 1.0)

    def kv_tile(kt, rows_cmp, rows_sel, rows_win, nrows, do_cmp_block=None, win_slot=None):
        if rows_sel is not None:
            ksr, vsr = rows_sel
            for g in range(4):
                k.tr(ps[0][0:64, g * 128:g * 128 + nrows], ksr[:, g * 64:(g + 1) * 64], C.identf[0:nrows, 0:nrows])
            k.cp(KsT[:, :, kt * 128:kt * 128 + nrows], ps[0][0:64, :].rearrange("p (g t) -> p g t", g=4)[:, :, 0:nrows], eng="act")
            k.cp(Vs[0:nrows, kt, :, 0:64], vsr.rearrange("p (g d) -> p g d", g=4), eng="dve")
        if rows_win is not None:
            kwr, vwr = rows_win
            ws = win_slot
            for g in range(4):
                k.tr(ps[1][0:64, g * 128:g * 128 + nrows], kwr[:, g * 64:(g + 1) * 64], C.identf[0:nrows, 0:nrows])
            k.cp(KwT[:, :, ws * 128:ws * 128 + nrows], ps[1][0:64, :].rearrange("p (g t) -> p g t", g=4)[:, :, 0:nrows], eng="act")
            k.cp(Vw[0:nrows, ws, :, 0:64], vwr.rearrange("p (g d) -> p g d", g=4), eng="dve")
        if rows_cmp is not None:
            kcr, vcr = rows_cmp
            t = do_cmp_block
            for g in range(4):
                k.mm(ps[2][0:64, g * 4:(g + 1) * 4], kcr[:, g * 64:(g + 1) * 64], Wbk[:, 60:64])
            k.cp(KcT[:, :, 4 * t:4 * t + 4], ps[2][0:64, 0:16].rearrange("p (g n) -> p g n", g=4), eng="act")
            k.mm(ps[2][0:64, 128:384], Wbv[:, 60 - 4 * t:124 - 4 * t], vcr)
            k.tt(Vc[:, :, 0:64], Vc[:, :, 0:64], ps[2][0:64, 128:384].rearrange("p (g d) -> p g d", g=4), ALU.add)

    bti = [0]

    OS = sb("OS", [128, 260])
    selT4 = sb("selT4", [40, 4, 8], BF16)

    def attend(nq, nblk, s_tiles, w_tiles, bc_ap, bs_fn, bw_fn, cb_ap, ft_ap, pair_ap, eexp_cols, load_consts=True, resident=False, batched=False):
        nc4 = 4 * nq
        if load_consts:
            k.dma(cbt[0:nq, 0:nblk], cb_ap)
            k.dma(cbt[0:nq, 40:40 + nblk], ft_ap)
            for g in range(4):
                k.dma(Vc[:, g, 65:65 + nblk], pair_ap)

        def bias_tile(ap, nk):
            if resident:
                return ap
            t = Bt[bti[0] % 4]
            bti[0] += 1
            k.dma(t[0:nk, 0:nc4], ap, eng="sp")
            return t[0:nk, 0:nc4]
        accs = []

        for g in range(4):
            Qg = QT[:, 4 * g:4 * g + 4, 0:nq]
            Qg2 = ec
            k.mm(ps[3][0:64, 0:nc4], KcT[:, g, :], QTf[:, g, 0:nc4])
            bt = bias_tile(bc_ap(g), 64)
            k.stt(tmp[0:64, 0:nc4], ps[3][0:64, 0:nc4], SCL, bt, ALU.mult, ALU.add)
            k.act(ec[:, 0:nc4], tmp[0:64, 0:nc4], AF.Exp)
            for j in range(4):
                k.mm(ps[4][0:nq, j * 98:j * 98 + 65 + nblk], ec[:, j * nq:(j + 1) * nq], Vc[:, g, 0:65 + nblk])
            k.cp(OB[0:nq, :, 0:65 + nblk], ps[4][0:nq, 0:392].rearrange("p (j c) -> p j c", j=4)[:, :, 0:65 + nblk], eng="act")
            k.ts(rd[0:nq, 0:4], OB[0:nq, :, 64], 1e-30, ALU.max)
            k.recip(rd[0:nq, 0:4], rd[0:nq, 0:4])
            k.ts(imp[0:nq, 0:nblk], OB[0:nq, 0, 65:65 + nblk], rd[0:nq, 0:1], ALU.mult)
            for j in range(1, 4):
                k.stt(imp[0:nq, 0:nblk], OB[0:nq, j, 65:65 + nblk], rd[0:nq, j:j + 1], imp[0:nq, 0:nblk], ALU.mult, ALU.add)
            k.tt(imp[0:nq, 0:nblk], imp[0:nq, 0:nblk], cbt[0:nq, 0:nblk], ALU.mult)
            k.tt(imp[0:nq, 0:nblk], imp[0:nq, 0:nblk], cbt[0:nq, 40:40 + nblk], ALU.add)
            P.op("dve", lambda e: e.max(out=m8[0:nq, 0:8], in_=imp[0:nq, 0:nblk]), reads=[imp], writes=[m8])
            P.op("dve", lambda e: e.match_replace(out=imp2[0:nq, 0:nblk], in_to_replace=m8[0:nq, 0:8], in_values=imp[0:nq, 0:nblk], imm_value=-2.0),
                 reads=[imp, m8], writes=[imp2])
            P.op("dve", lambda e: e.max(out=m8[0:nq, 8:16], in_=imp2[0:nq, 0:nblk]), reads=[imp2], writes=[m8])
            k.ts(m8[0:nq, 15:16], m8[0:nq, 15:16], 0.0, ALU.max)
            k.ts(imp2[0:nq, 0:nblk], imp[0:nq, 0:nblk], m8[0:nq, 15:16], ALU.is_ge)
            k.tr(ps[5][0:nblk, 0:nq], imp2[0:nq, 0:nblk], C.identf[0:nq, 0:nq])
            if batched:
                k.cp(selT4[0:nblk, g, :], ps[5][0:nblk, 0:nq], eng="act")
            else:
                k.cp(selT[0:nblk, 0:nq], ps[5][0:nblk, 0:nq], eng="act")
            def accum(first, col, Osrc, g=g):
                gsl = gates[0:nq, :].rearrange("p (h c) -> p h c", c=3)[:, 4 * g:4 * g + 4, col]
                k.tt(rd[0:nq, 4:8], rd[0:nq, 0:4], gsl, ALU.mult)
                dstv = Oacc[0:nq, g * 256:(g + 1) * 256].rearrange("p (j d) -> p j d", j=4)
                rb = rd[0:nq, 4:8].unsqueeze(2).to_broadcast([nq, 4, 64])
                if first:
                    k.tt(dstv, Osrc, rb, ALU.mult)
                else:
                    k.tt(OB[0:nq, :, 0:64], Osrc, rb, ALU.mult)
                    k.tt(dstv, dstv, OB[0:nq, :, 0:64], ALU.add)
            accum(True, 0, OB[0:nq, :, 0:64])
            if batched:
                accs.append(accum)
                continue
            items = []
            for br, tiles, Kt, Vt, bfn in ((1, s_tiles, KsT, Vs, bs_fn), (2, w_tiles, KwT, Vw, bw_fn)):
                for ti, (slot, nk, bidx) in enumerate(tiles):
                    items.append((br, Kt, Vt, bfn, ti, len(tiles), slot, nk, bidx))
            PSC = (ps[3], ps[2])

            def front(i):
                br, Kt, Vt, bfn, ti, nt, slot, nk, bidx = items[i]
                k.mm(PSC[i % 2][0:nk, 0:nc4], Kt[:, g, slot * 128:slot * 128 + nk], QTf[:, g, 0:nc4])
                if br == 1:
                    mo = 128 + (i % 2) * 128
                    k.mm(ps[5][0:nk, mo:mo + nq], Eexp[0:nblk, eexp_cols(slot)], selT[0:nblk, 0:nq])

            def mid_a(i):
                br, Kt, Vt, bfn, ti, nt, slot, nk, bidx = items[i]
                bt = bfn(bidx, g) if resident else bias_tile(bfn(bidx, g), nk)
                k.stt(TMP[i % 2][0:nk, 0:nc4], PSC[i % 2][0:nk, 0:nc4], SCL, bt, ALU.mult, ALU.add)

            def mid_b(i):
                br, Kt, Vt, bfn, ti, nt, slot, nk, bidx = items[i]
                e = EB[i % 2]
                k.act(e[0:nk, 0:nc4], TMP[i % 2][0:nk, 0:nc4], AF.Exp)
                if br == 1:
                    mo = 128 + (i % 2) * 128
                    k.tt(e[0:nk, 0:nc4].rearrange("p (j q) -> p j q", j=4), e[0:nk, 0:nc4].rearrange("p (j q) -> p j q", j=4),
                         ps[5][0:nk, mo:mo + nq].unsqueeze(1).to_broadcast([nk, 4, nq]), ALU.mult)

            def back(i):
                br, Kt, Vt, bfn, ti, nt, slot, nk, bidx = items[i]
                e = EB[i % 2]
                for j in range(4):
                    k.mm(ps[(6, 7, 0, 1)[j]][0:nq, 0:65], e[0:nk, j * nq:(j + 1) * nq], Vt[0:nk, slot, g, :],
                         start=(ti == 0), stop=(ti == nt - 1))
                if ti == nt - 1:
                    for j in range(4):
                        k.cp(OB[0:nq, j, 0:65], ps[(6, 7, 0, 1)[j]][0:nq, 0:65], eng="act")
                    k.ts(rd[0:nq, 0:4], OB[0:nq, :, 64], 1e-30, ALU.max)
                    k.recip(rd[0:nq, 0:4], rd[0:nq, 0:4])
                    accum(False, br, OB[0:nq, :, 0:64])

            front(0)
            mid_a(0)
            for i in range(len(items)):
                if i + 1 < len(items):
                    front(i + 1)
                    mid_a(i + 1)
                mid_b(i)
                back(i)

        if batched:
            PSC = (ps[3], ps[2])
            for br, tiles, Kt, Vt, SBt in ((1, s_tiles, KsT, Vs, SBS), (2, w_tiles, KwT, Vw, SBW)):
                nt = len(tiles)

                def front(i, br=br, tiles=tiles, Kt=Kt):
                    slot, nk, bidx = tiles[i]
                    for g in range(4):
                        k.mm(PSC[i % 2][0:nk, g * 32:(g + 1) * 32], Kt[:, g, slot * 128:slot * 128 + nk], QTf[:, g, 0:32])
                    if br == 1:
                        mo = 128 + (i % 2) * 128
                        for g in range(4):
                            k.mm(ps[5][0:nk, mo + g * 8:mo + (g + 1) * 8], Eexp[0:nblk, eexp_cols(slot)], selT4[0:nblk, g, :])

                def mid_a(i, tiles=tiles, SBt=SBt):
                    slot, nk, bidx = tiles[i]
                    k.stt(TMP[i % 2][0:nk, 0:128], PSC[i % 2][0:nk, 0:128], SCL, SBt[0:nk, bidx * 128:(bidx + 1) * 128], ALU.mult, ALU.add)

                def mid_b(i, br=br, tiles=tiles):
                    slot, nk, bidx = tiles[i]
                    e = EB[i % 2]
                    k.act(e[0:nk, 0:128], TMP[i % 2][0:nk, 0:128], AF.Exp)
                    if br == 1:
                        mo = 128 + (i % 2) * 128
                        ev = e[0:nk, 0:128].rearrange("p (g j t) -> p g j t", g=4, j=4)
                        k.tt(ev, ev, ps[5][0:nk, mo:mo + 32].rearrange("p (g t) -> p g t", g=4).unsqueeze(2).to_broadcast([nk, 4, 4, 8]), ALU.mult)

                def back(i, tiles=tiles, Vt=Vt, nt=nt):
                    slot, nk, bidx = tiles[i]
                    k.mm(ps[6][:, 0:260], EB[i % 2][0:nk, 0:128], Vt[0:nk, slot, :, :].rearrange("p g c -> p (g c)"),
                         start=(i == 0), stop=(i == nt - 1))

                front(0)
                mid_a(0)
                for i in range(nt):
                    if i + 1 < nt:
                        front(i + 1)
                        mid_a(i + 1)
                    mid_b(i)
                    back(i)
                k.cp(OS[:], ps[6][:, 0:260], eng="act")
                for g in range(4):
                    for j in range(4):
                        h = 4 * g + j
                        k.mm(ps[7][0:8, j * 65:(j + 1) * 65], C.identf[:, h * 8:(h + 1) * 8], OS[:, g * 65:(g + 1) * 65])
                    k.cp(OB[0:8, :, 0:65], ps[7][0:8, 0:260].rearrange("p (j c) -> p j c", j=4), eng="act")
                    k.ts(rd[0:8, 0:4], OB[0:8, :, 64], 1e-30, ALU.max)
                    k.recip(rd[0:8, 0:4], rd[0:8, 0:4])
                    accs[g](False, br, OB[0:8, :, 0:64])

    QTflat = QT[:].rearrange("p h q -> p (h q)")

    class _QTf:
        nq = 128

        def __getitem__(self, key):
            _, g, _ = key
            n4 = 4 * self.nq
            return QTflat[:, g * n4:(g + 1) * n4]
    QTf = _QTf()

    def qgz(npart, nq_cols):
        for h in range(16):
            for dc in range(8):
                k.mm(ps[7][0:64, (h % 4) * 128:(h % 4) * 128 + nq_cols], Win[:, dc, h * 64:(h + 1) * 64], xT[:, dc, 0:nq_cols],
                     start=(dc == 0), stop=(dc == 7))
            if h % 4 == 3:
                QTf.nq = nq_cols
                qdst = QTflat[:, (h - 3) * nq_cols:(h + 1) * nq_cols].rearrange("p (j q) -> p j q", j=4)
                k.cp(qdst, ps[7][0:64, :].rearrange("p (j q) -> p j q", j=4)[:, :, 0:nq_cols], eng="act")
        for i, (c0, c1) in enumerate(((2560, 3072), (3072, 3584), (3584, 3632))):
            for dc in range(8):
                k.mm(ps[i][0:npart, 0:c1 - c0], xT[:, dc, 0:npart], Win[:, dc, c0:c1], start=(dc == 0), stop=(dc == 7))
            k.cp(GZ[0:npart, c0 - 2560:c1 - 2560], ps[i][0:npart, 0:c1 - c0], eng="act")
        k.act(gates[0:npart, :], GZ[0:npart, 0:48], AF.Sigmoid)

    def finish(npart, dst_rows):
        k.act(T1[0:npart, :], GZ[0:npart, 48:1072], AF.Silu)
        k.tt(Gb[0:npart, :], Oacc[0:npart, :], T1[0:npart, :], ALU.mult)
        psb = ps[2][:].bitcast(BF16)
        for c in range(8):
            k.tr(psb[:, c * 128:c * 128 + npart], Gb[0:npart, c * 128:(c + 1) * 128], C.identb[0:npart, 0:npart])
        k.cp(gT[:, :, 0:npart], psb[:, 0:1024].rearrange("p (c t) -> p c t", c=8)[:, :, 0:npart], eng="act")
        for hf in range(2):
            cols = slice(hf * 512, (hf + 1) * 512)
            for c in range(8):
                k.mm(ps[hf][0:npart, :], gT[:, c, 0:npart], Wo[:, c, cols], start=(c == 0), stop=(c == 7))
            k.stt(T1[0:npart, cols], xin[0:npart, cols], ALPHA, ps[hf][0:npart, :], ALU.mult, ALU.add)
        ln_tail(C, T1, npart, L, dst_rows, Oacc, crow)

    def load_xT(rows_ap, npart):
        k.dma(xin[0:npart, :], rows_ap)
        for b in range(2):
            for c in range(4):
                k.tr(ps[b][:, c * 128:c * 128 + npart], xin[0:npart, (4 * b + c) * 128:(4 * b + c + 1) * 128], C.identf[0:npart, 0:npart])
            k.cp(xT[:, 4 * b:4 * b + 4, 0:npart], ps[b][:].rearrange("p (c t) -> p c t", c=4)[:, :, 0:npart], eng="act")

    def kv_proj(npart):
        for i in range(3):
            for dc in range(8):
                k.mm(ps[3 + i][0:npart, :], xT[:, dc, 0:npart], Win[:, dc, 1024 + i * 512:1024 + (i + 1) * 512], start=(dc == 0), stop=(dc == 7))
            k.cp(KV[0:npart, i * 512:(i + 1) * 512], ps[3 + i][0:npart, :], eng="act")

    k.memset(Vc[:], 0.0)
    k.memset(Vc[:, :, 64:65], 1.0)
    k.memset(KcT[:], 0.0)
    k.dma(Eexp[0:32, 0:2048], I["n_eexp_p"])
    ntile = tp // 128
    for t in range(ntile):
        r0 = t * 128
        load_xT(src[0][r0:r0 + 128, :], 128)
        kv_proj(128)
        for i, nm in enumerate(("cmpk", "cmpv", "selk", "selv")):
            k.dma(O[nm + "_p"][r0:r0 + 128, :], KV[:, i * 256:(i + 1) * 256])
        wr0 = r0 - (tp - min(512, tp))
        if wr0 >= 0:
            k.dma(O["wink_p"][wr0:wr0 + 128, :], KV[:, 1024:1280])
            k.dma(O["winv_p"][wr0:wr0 + 128, :], KV[:, 1280:1536])
        kv_tile(t, (KV[:, 0:256], KV[:, 256:512]), (KV[:, 512:768], KV[:, 768:1024]), (KV[:, 1024:1280], KV[:, 1280:1536]), 128,
                do_cmp_block=t, win_slot=t % 5)
        qgz(128, 128)
        s_tiles = [(kt, 128, t - kt) for kt in range(t + 1)]
        w_tiles = [(kt % 5, 128, t - kt) for kt in range(max(0, t - 4), t + 1)]
        attend(128, 32, s_tiles, w_tiles,
               lambda g: I["n_bc_p"][t, g], lambda d, g: I["n_bs_p"][d, g], lambda d, g: I["n_bw_p"][d, g],
               I["n_cb_p"][t], I["n_ft_p"][t], I["n_pair_p"], lambda slot: slice(slot * 128, (slot + 1) * 128))
        finish(128, dst[0][r0:r0 + 128, :])

    k.dma(idx[:], I["ptab"].rearrange("s n -> (s n)").partition_broadcast(128))
    k.cp(idf, idx[:])
    k.dma(wcol[:, 0:1], I["n_iota"])
    k.ts(idf, idf, 128.0, ALU.mult, wcol[:, 0:1], ALU.add)
    k.cp(idx[:], idf)
    k.dma(Eexp[0:33, 0:17 * 128], I["n_eexp_s"])
    load_xT(src[1][:, :], 128)
    kv_proj(128)
    KVs = C.kvs_scr
    k.dma(KVs, KV[:])
    for i, nm in enumerate(("cmpk", "cmpv", "selk", "selv")):
        k.dma(O[nm + "_s"], KV[:, i * 256:(i + 1) * 256])
    k.dma(SBS[:].rearrange("k (d g c) -> k d g c", d=17, g=4), I["n_bs_s"].rearrange("d g k c -> k d g c"))
    k.dma(SBW[:].rearrange("k (d g c) -> k d g c", d=5, g=4), I["n_bw_s"].rearrange("d g k c -> k d g c"))
    k.dma(SBC[:].rearrange("k (g c) -> k g c", g=4), I["n_bc_s"].rearrange("g k c -> k g c"))
    k.dma(cbt[0:8, 0:33], I["n_cb_s"])
    k.dma(cbt[0:8, 40:73], I["n_ft_s"])
    for g in range(4):
        k.dma(Vc[:, g, 65:98], I["n_pair_s"])
    xTs_all = sb("xTs_all", [128, 8, 128], BF16)
    k.cp(xTs_all[:], xT[:], eng="dve")
    NEW = KV[0:8, :]
    for sq in range(NS):
        k.memset(Vc[:, :, 0:64], 0.0)
        pools = (I["cmp_k"], I["cmp_v"], I["sel_k"], I["sel_v"])

        def gather(pool_ap, slot, dst_tile, sq=sq):
            P.dma("pool", dst_tile, pool_ap, reads=[pool_ap, idx], writes=[dst_tile],
                  fn=lambda e: e.indirect_dma_start(out=dst_tile, out_offset=None, in_=pool_ap,
                                                    in_offset=bass.IndirectOffsetOnAxis(ap=idx[:, sq * 16 + slot:sq * 16 + slot + 1], axis=0)))
        for pg in range(16):
            gb = GBUF[pg % 2]
            for ci in range(4):
                gather(pools[ci], pg, gb[:, ci * 256:(ci + 1) * 256])
            kv_tile(pg, (gb[:, 0:256], gb[:, 256:512]), (gb[:, 512:768], gb[:, 768:1024]), None, 128, do_cmp_block=pg)
        for wt in range(4):
            gb = GBUF[wt % 2]
            k.dma(gb[:, 0:256], I["win_k"][sq, wt * 128:(wt + 1) * 128, :])
            k.dma(gb[:, 256:512], I["win_v"][sq, wt * 128:(wt + 1) * 128, :])
            kv_tile(0, None, None, (gb[:, 0:256], gb[:, 256:512]), 128, win_slot=wt)
        k.dma(NEW, KVs[sq * 8:(sq + 1) * 8, :])
        kv_tile(16, None, (NEW[:, 512:768], NEW[:, 768:1024]), (NEW[:, 1024:1280], NEW[:, 1280:1536]), 8, win_slot=4)
        k.dma(O["wink_s"][sq, 0:504, :], I["win_k"][sq, 8:512, :])
        k.dma(O["winv_s"][sq, 0:504, :], I["win_v"][sq, 8:512, :], eng="act")
        k.dma(O["wink_s"][sq, 504:512, :], NEW[:, 1024:1280])
        k.dma(O["winv_s"][sq, 504:512, :], NEW[:, 1280:1536])
        k.cp(xT[:, :, 0:8], xTs_all[:, :, sq * 8:(sq + 1) * 8], eng="dve")
        k.dma(xin[0:8, :], src[1][sq * 8:(sq + 1) * 8, :])
        qgz(8, 8)
        s_tiles = [(kt, 128, kt) for kt in range(16)] + [(16, 8, 16)]
        w_tiles = [(kt, 128, kt) for kt in range(4)] + [(4, 8, 4)]
        attend(8, 33, s_tiles, w_tiles,
               lambda g: SBC[:, g * 32:(g + 1) * 32],
               lambda d, g: SBS[0:(8 if d == 16 else 128), d * 128 + g * 32:d * 128 + (g + 1) * 32],
               lambda d, g: SBW[0:(8 if d == 4 else 128), d * 128 + g * 32:d * 128 + (g + 1) * 32],
               None, None, None, lambda slot: slice(slot * 128, slot * 128 + (8 if slot == 16 else 128)), load_consts=False, resident=True, batched=True)
        finish(8, dst[1][sq * 8:(sq + 1) * 8, :])


def _cmask():
    m = np.zeros((128, 2048), np.float32)
    t = np.arange(128)
    for g, w in enumerate((2, 4, 8, 16)):
        m[:, g * 128:(g + 1) * 128] = (1.0 / np.minimum(w, t + 1))[None, :]
    a = np.arange(64)
    su = (a[:, None] < a[None, :]).astype(np.float32)
    ui = (a[:, None] <= a[None, :]).astype(np.float32)
    m[0:64, 512:576] = su
    m[0:64, 576:640] = ui
    m[0:64, 640:704] = su.T
    m[0:64, 704:768] = ui
    m[0:64, 768:832] = np.eye(64, dtype=np.float32)
    return m


def consts():
    import ml_dtypes
    sel = np.zeros((128, 64, 128), np.float32)
    for kk in range(128):
        sel[kk, kk % 64, (kk // 64) * 64:(kk // 64) * 64 + 64] = 1
    return {"identf": np.eye(128, dtype=np.float32), "selb": sel.reshape(128, 64 * 128).astype(ml_dtypes.bfloat16),
            "cmask": _cmask()}


def shard_inputs(inp, c, tp=TP):
    f = lambda a: np.ascontiguousarray(a)
    m = {
        "xp": f(inp["x_prompt"][c, :tp]), "xs": f(inp["x_sample"][16 * c:16 * c + 16].reshape(128, D)),
        "st_S": f(inp["state_rwkv_S"][:, 16 * c:16 * c + 16]), "st_shift": f(inp["state_rwkv_shift"][:, 16 * c:16 * c + 16]),
        "st_pool": f(inp["state_pool"][0, 16 * c:16 * c + 16]),
        "cmp_k": f(inp["cache_cmp_k"][0].reshape(-1, 256)), "cmp_v": f(inp["cache_cmp_v"][0].reshape(-1, 256)),
        "sel_k": f(inp["cache_sel_k"][0].reshape(-1, 256)), "sel_v": f(inp["cache_sel_v"][0].reshape(-1, 256)),
        "win_k": f(inp["state_win_k"][0, 16 * c:16 * c + 16].reshape(16, 512, 256)),
        "win_v": f(inp["state_win_v"][0, 16 * c:16 * c + 16].reshape(16, 512, 256)),
        "ptab": f(inp["page_table"][16 * c:16 * c + 16]).astype(np.int32),
        "a_r_k": f(inp["a_r_k"].reshape(2, D)), "b_w_in": f(inp["b_w_in"][0]), "b_w_grp": f(inp["b_w_grp"][0]),
        "b_scale": f(inp["b_scale"]), "b_w_out": f(inp["b_w_out"][0]), "c_w_in": f(inp["c_w_in"][0]),
        "c_cmp_wk": f(inp["c_cmp_wk"]), "c_cmp_wv": f(inp["c_cmp_wv"]), "c_w_out": f(inp["c_w_out"][0]),
    }
    for nm in ("ln_g", "ln_b", "a_w_in", "a_mu", "a_w0", "a_w2", "a_a0", "a_a2", "a_k_k", "a_k_a", "a_lnx_g", "a_lnx_b", "a_w_out"):
        m[nm] = f(inp[nm])
    m.update(consts())
    m.update(nsa_consts())
    return m


def nsa_consts():
    sl = 2.0 ** (-8.0 * (np.arange(16) + 1) / 16)
    c = {}

    def bias(dist, valid, g):
        K_, nq = dist.shape
        out = np.empty((K_, 4, nq), np.float32)
        for j in range(4):
            out[:, j] = np.where(valid, -sl[4 * g + j] * dist, NEG)
        return out.reshape(K_, 4 * nq)
    q = np.arange(128)[None, :]
    kk = np.arange(128)[:, None]
    n = np.arange(64)[:, None]
    bc = np.zeros((16, 4, 64, 512), np.float32)
    bs = np.zeros((16, 4, 128, 512), np.float32)
    bw = np.zeros((5, 4, 128, 512), np.float32)
    for g in range(4):
        for t in range(16):
            d = 128 * t + q - 32 * n - 31
            bc[t, g] = bias(d, d >= 0, g)
            d = 128 * t + q - kk
            bs[t, g] = bias(d, d >= 0, g)
            if t < 5:
                bw[t, g] = bias(d, (d >= 0) & (d < 512), g)
    c["n_bc_p"], c["n_bs_p"], c["n_bw_p"] = bc, bs, bw
    cb = np.zeros((16, 128, 32), np.float32)
    ft = np.zeros((16, 128, 32), np.float32)
    blk = np.arange(32)[None, :]
    for t in range(16):
        cur = ((128 * t + np.arange(128)) // 64)[:, None]
        cb[t] = (blk < cur)
        ft[t] = np.where(blk == cur, 1e9, np.where(blk > cur, -1.0, 0.0))
    c["n_cb_p"], c["n_ft_p"] = cb, ft
    c["n_pair_p"] = (np.arange(64)[:, None] // 2 == np.arange(32)[None, :]).astype(np.float32)
    c["n_eexp_p"] = (np.arange(2048)[None, :] // 64 == np.arange(32)[:, None]).astype(np.float32)
    wbm = np.zeros((128, 124), np.float32)
    for r in range(128):
        wbm[r, 60 + r // 32] = 1.0
    c["n_wbm"] = wbm
    c["n_iota"] = np.arange(128, dtype=np.float32).reshape(128, 1)
    tq = np.arange(8)[None, :]
    bcs = np.zeros((4, 64, 32), np.float32)
    bss = np.zeros((17, 4, 128, 32), np.float32)
    bws = np.zeros((5, 4, 128, 32), np.float32)
    for g in range(4):
        d = 2048 + tq - 32 * n - 31
        bcs[g] = bias(d, d >= 0, g)
        for kt in range(16):
            d = 2048 + tq - 128 * kt - kk
            bss[kt, g] = bias(d, d >= 0, g)
        d = tq - kk
        newb = bias(d, (d >= 0) & (kk < 8), g)
        bss[16, g] = newb
        for kt in range(4):
            d = 2048 + tq - (1536 + 128 * kt + kk)
            bws[kt, g] = bias(d, (d >= 0) & (d < 512), g)
        bws[4, g] = newb
    c["n_bc_s"], c["n_bs_s"], c["n_bw_s"] = bcs, bss, bws
    cbs = np.ones((8, 33), np.float32)
    cbs[:, 32] = 0
    fts = np.zeros((8, 33), np.float32)
    fts[:, 32] = 1e9
    c["n_cb_s"], c["n_ft_s"] = cbs, fts
    ps_ = np.zeros((64, 33), np.float32)
    ps_[:, :32] = c["n_pair_p"]
    c["n_pair_s"] = ps_
    ee = np.zeros((33, 17 * 128), np.float32)
    ee[:32, :2048] = c["n_eexp_p"]
    ee[32, 2048:] = 1.0
    import ml_dtypes
    c["n_eexp_s"] = ee.astype(ml_dtypes.bfloat16)
    c["n_eexp_p"] = c["n_eexp_p"].astype(ml_dtypes.bfloat16)
    hb = np.zeros((128, 240), np.float32)
    for g in range(4):
        for d in range(1, 16):
            for j in range(4):
                hb[:, g * 60 + (d - 1) * 4 + j] = -sl[4 * g + j] * 128.0 * (d - 1)
    c["n_hb"] = hb
    return c


_NC_CACHE = {}


def kernel(**inputs):
    n = 8
    npool = inputs["cache_cmp_k"].shape[1]
    key = (npool,)
    if key not in _NC_CACHE:
        _NC_CACHE[key] = build(npool=npool, tp=TP)
    nc = _NC_CACHE[key]
    in_maps = [shard_inputs(inputs, c) for c in range(n)]
    res = run_bass_kernel_spmd(nc, in_maps, core_ids=list(range(n)))
    R = res.results
    cat = lambda nm: np.stack([R[c][nm] for c in range(n)], 0)
    y_p = cat("y_p")
    y_s = np.concatenate([R[c]["y_s"].reshape(16, 8, D) for c in range(n)], 0)
    S_p = np.stack([R[c]["S_p"] for c in range(n)], 1)
    S_s = np.concatenate([R[c]["S_s"] for c in range(n)], 1)
    sh_p = np.stack([R[c]["sh_p"] for c in range(n)], 1)
    sh_s = np.concatenate([R[c]["sh_s"] for c in range(n)], 1)
    pl_p = cat("pl_p")[None]
    pl_s = np.concatenate([R[c]["pl_s"] for c in range(n)], 0)[None]
    outs = [y_p, y_s, S_p, S_s, sh_p, sh_s, pl_p, pl_s]
    for nm in ("cmpk", "cmpv", "selk", "selv"):
        outs.append(cat(nm + "_p").reshape(1, n, TP, 4, 64))
        outs.append(np.concatenate([R[c][nm + "_s"].reshape(16, 8, 4, 64) for c in range(n)], 0)[None])
    for nm in ("wink", "winv"):
        outs.append(cat(nm + "_p").reshape(1, n, 512, 4, 64))
        outs.append(np.concatenate([R[c][nm + "_s"].reshape(16, 512, 4, 64) for c in range(n)], 0)[None])
    return tuple(np.ascontiguousarray(o, dtype=np.float32) for o in outs)
```

```python
import contextlib
import numpy as np
import concourse.bass as bass
import concourse.mybir as mybir
from concourse.bass_utils import run_bass_kernel_spmd

F32 = mybir.dt.float32
BF16 = mybir.dt.bfloat16
I32 = mybir.dt.int32
ALU = mybir.AluOpType
AF = mybir.ActivationFunctionType
AX = mybir.AxisListType

import os as _os
STRICT = bool(_os.environ.get("KSTRICT"))
ENGS = ("pe", "dve", "act", "pool", "sp")
NDMA = {"sp": 12, "act": 6, "pool": 12}

D = 1024
TP = 2048
NS = 16
TS = 8
DEPTH = 4
ALPHA = (2.0 * DEPTH) ** 0.25
LN_EPS = 1e-5
A_NC = 4224
GN_EPS = 64e-5
C_NC = 3632


def _key(k):
    if isinstance(k, (str, tuple)):
        return k
    t = getattr(k, "tensor", k)
    return getattr(t, "name", str(t))


class Prog:
    def __init__(self, nc):
        self.nc = nc
        self.q = {e: [] for e in ENGS}
        self.cnt = {e: 0 for e in ENGS}
        self.known = {e: {} for e in ENGS}
        self.lastw = {}
        self.readers = {}
        self.dma_rr = {e: 0 for e in NDMA}
        self.dma_cnt = {}
        self.n_inst = 0

    def _deps(self, reads, writes):
        deps = {}

        def add(ev):
            if ev is None:
                return
            s, v = ev
            if deps.get(s, 0) < v:
                deps[s] = v
        for k in reads:
            add(self.lastw.get(k))
        for k in writes:
            add(self.lastw.get(k))
            for ev in self.readers.get(k, ()):
                add(ev)
        return deps

    def _commit(self, ev, reads, writes):
        for k in reads:
            self.readers.setdefault(k, []).append(ev)
        for k in writes:
            self.lastw[k] = ev
            self.readers[k] = []

    def _waits(self, eng, deps, compute=False):
        waits = []
        kn = self.known[eng]
        for s, v in deps.items():
            if s == "c_pe" and eng == "pe":
                continue
            if compute and not STRICT and s == "c_" + eng and eng in ("dve", "act") and v < self.cnt[eng]:
                continue
            if kn.get(s, 0) >= v:
                continue
            kn[s] = v
            waits.append((s, v))
        return waits

    def op(self, eng, fn, reads=(), writes=()):
        reads = [_key(k) for k in reads]
        writes = [_key(k) for k in writes]
        writes = writes + [r for r in reads if isinstance(r, str) and r.startswith("psb")]
        waits = self._waits(eng, self._deps(reads, writes), compute=True)
        self.cnt[eng] += 1
        ev = ("c_" + eng, self.cnt[eng])
        self.q[eng].append(("op", waits, fn, ev))
        self._commit(ev, reads, writes)
        self.n_inst += 1
        return ev

    def dma(self, eng, out, in_, reads=None, writes=None, fn=None, **kw):
        reads = [_key(k) for k in (reads if reads is not None else [in_])]
        writes = [_key(k) for k in (writes if writes is not None else [out])]
        deps = self._deps(reads, writes)
        i = self.dma_rr[eng]
        self.dma_rr[eng] = (i + 1) % NDMA[eng]
        sname = "d_%s%d" % (eng, i)
        n = self.dma_cnt.get(sname, 0)
        if n > 0 and deps.get(sname, 0) < 16 * n:
            deps[sname] = 16 * n
        waits = self._waits(eng, deps)
        self.dma_cnt[sname] = n + 1
        ev = (sname, 16 * (n + 1))
        self.q[eng].append(("dma", waits, (out, in_, kw, fn), ev))
        self._commit(ev, reads, writes)
        self.n_inst += 1
        return ev

    def barrier(self):
        for eng in ENGS:
            deps = {}
            for f in ENGS:
                if f != "sp" and f != eng and self.cnt[f] > 0:
                    deps["c_" + f] = self.cnt[f]
            for s, n in self.dma_cnt.items():
                deps[s] = 16 * n
            waits = self._waits(eng, deps)
            self.q[eng].append(("wait", waits, None, None))

    def emit(self):
        nc = self.nc
        names = ["c_" + e for e in ENGS if e != "sp"]
        for e, n in NDMA.items():
            names += ["d_%s%d" % (e, i) for i in range(n)]
        with contextlib.ExitStack() as st:
            sems = {nm: st.enter_context(nc.semaphore(nm)) for nm in names}
            block = st.enter_context(nc.Block())

            def run(eng):
                def body(e):
                    for kind, waits, payload, ev in self.q[eng]:
                        for s, v in waits:
                            e.wait_ge(sems[s], v)
                        if kind == "op":
                            payload(e).then_inc(sems[ev[0]], 1)
                        elif kind == "dma":
                            out, in_, kw, fn = payload
                            if fn is not None:
                                fn(e).then_inc(sems[ev[0]], 16)
                            else:
                                e.dma_start(out=out, in_=in_, **kw).then_inc(sems[ev[0]], 16)
                    if eng == "sp":
                        for sname, n in self.dma_cnt.items():
                            e.wait_ge(sems[sname], 16 * n)
                        for en in ENGS:
                            if en != "sp" and self.cnt[en] > 0:
                                e.wait_ge(sems["c_" + en], self.cnt[en])
                return body

            block.sync(run("sp"))
            block.tensor(run("pe"))
            block.vector(run("dve"))
            block.scalar(run("act"))
            block.gpsimd(run("pool"))


def _aps(*xs):
    return [x for x in xs if x is not None and not isinstance(x, (int, float))]


class K:
    def __init__(self, P):
        self.P = P

    def mm(self, out, lhsT, rhs, start=True, stop=True):
        self.P.op("pe", lambda e: e.matmul(out, lhsT=lhsT, rhs=rhs, start=start, stop=stop),
                  reads=[lhsT, rhs], writes=[out])

    def tr(self, out, in_, ident):
        self.P.op("pe", lambda e: e.transpose(out, in_, ident), reads=[in_, ident], writes=[out])

    def tt(self, out, a, b, op, eng="dve"):
        self.P.op(eng, lambda e: e.tensor_tensor(out=out, in0=a, in1=b, op=op), reads=[a, b], writes=[out])

    def ts(self, out, a, s1, op0, s2=None, op1=None, eng="dve"):
        if op1 is None:
            fn = lambda e: e.tensor_scalar(out=out, in0=a, scalar1=s1, scalar2=None, op0=op0)
        else:
            fn = lambda e: e.tensor_scalar(out=out, in0=a, scalar1=s1, scalar2=s2, op0=op0, op1=op1)
        self.P.op(eng, fn, reads=_aps(a, s1, s2), writes=[out])

    def stt(self, out, a, s, b, op0, op1, eng="dve"):
        self.P.op(eng, lambda e: e.scalar_tensor_tensor(out=out, in0=a, scalar=s, in1=b, op0=op0, op1=op1),
                  reads=_aps(a, s, b), writes=[out])

    def red(self, out, in_, op=ALU.add, negate=False, axis=AX.X):
        self.P.op("dve", lambda e: e.tensor_reduce(out=out, in_=in_, axis=axis, op=op, negate=negate),
                  reads=[in_], writes=[out])

    def cp(self, out, in_, eng="dve"):
        if eng == "act":
            self.P.op("act", lambda e: e.copy(out, in_), reads=[in_], writes=[out])
        else:
            self.P.op(eng, lambda e: e.tensor_copy(out, in_), reads=[in_], writes=[out])

    def act(self, out, in_, func, bias=None, scale=None, accum=None):
        kw = {}
        if bias is not None:
            kw["bias"] = bias
        if scale is not None:
            kw["scale"] = scale
        if accum is not None:
            kw["accum_out"] = accum
        self.P.op("act", lambda e: e.activation(out=out, in_=in_, func=func, **kw),
                  reads=_aps(in_, bias, scale), writes=_aps(out, accum))

    def recip(self, out, in_):
        self.P.op("dve", lambda e: e.reciprocal(out, in_), reads=[in_], writes=[out])

    def memset(self, ap, v, eng="pool"):
        self.P.op(eng, lambda e: e.memset(ap, v), writes=[ap])

    def dma(self, out, in_, eng="sp", **kw):
        self.P.dma(eng, out, in_, **kw)


def bc(ap, shape):
    return ap.to_broadcast(shape)


class Ctx:
    pass


def build(npool=2560, tp=TP, layers=(0, 1, 2, 3), dbg=False):
    nc = bass.Bass("TRN2", target_bir_lowering=False)
    C = Ctx()
    C.nc = nc
    C.tp = tp
    P = Prog(nc)
    k = K(P)
    C.P, C.k = P, k

    def din(name, shape, dt=F32):
        return nc.dram_tensor(name, list(shape), dt, kind="ExternalInput").ap()

    def dout(name, shape):
        return nc.dram_tensor(name, list(shape), F32, kind="ExternalOutput").ap()

    def dscr(name, shape, dt=F32):
        return nc.dram_tensor(name, list(shape), dt, kind="Internal").ap()

    I = {}
    for nm, shp in [("xp", (tp, D)), ("xs", (128, D)), ("st_S", (2, NS, 16, 64, 64)), ("st_shift", (2, NS, A_NC)),
                    ("st_pool", (NS, 15, D)), ("cmp_k", (npool * 128, 256)), ("cmp_v", (npool * 128, 256)),
                    ("sel_k", (npool * 128, 256)), ("sel_v", (npool * 128, 256)), ("win_k", (NS, 512, 256)),
                    ("win_v", (NS, 512, 256)), ("ln_g", (4, D)), ("ln_b", (4, D)), ("a_w_in", (2, D, A_NC)),
                    ("a_mu", (2, A_NC)), ("a_w0", (2, D)), ("a_w2", (2, 64, D)), ("a_a0", (2, D)), ("a_a2", (2, 64, D)),
                    ("a_k_k", (2, D)), ("a_k_a", (2, D)), ("a_r_k", (2, D)), ("a_lnx_g", (2, D)), ("a_lnx_b", (2, D)),
                    ("a_w_out", (2, D, D)), ("b_w_in", (D, 2 * D)), ("b_w_grp", (4, 256, 256)), ("b_scale", (1, D)),
                    ("b_w_out", (D, D)), ("c_w_in", (D, C_NC)), ("c_cmp_wk", (1, 32)), ("c_cmp_wv", (1, 32)),
                    ("c_w_out", (D, D)), ("identf", (128, 128)), ("cmask", (128, 2048))]:
        I[nm] = din(nm, shp)
    I["ptab"] = din("ptab", (NS, 16), I32)
    for nm, shp in [("n_bc_p", (16, 4, 64, 512)), ("n_bs_p", (16, 4, 128, 512)), ("n_bw_p", (5, 4, 128, 512)),
                    ("n_cb_p", (16, 128, 32)), ("n_ft_p", (16, 128, 32)), ("n_pair_p", (64, 32)),
                    ("n_wbm", (128, 124)), ("n_iota", (128, 1)), ("n_bc_s", (4, 64, 32)), ("n_bs_s", (17, 4, 128, 32)),
                    ("n_bw_s", (5, 4, 128, 32)), ("n_cb_s", (8, 33)), ("n_ft_s", (8, 33)), ("n_pair_s", (64, 33)),
                    ("n_hb", (128, 240))]:
        I[nm] = din(nm, shp)
    I["n_eexp_p"] = din("n_eexp_p", (32, 2048), BF16)
    I["n_eexp_s"] = din("n_eexp_s", (33, 17 * 128), BF16)
    I["selb"] = din("selb", (128, 64 * 128), BF16)
    O = {}
    for nm, shp in [("y_p", (tp, D)), ("y_s", (128, D)), ("S_p", (2, 16, 64, 64)), ("S_s", (2, NS, 16, 64, 64)),
                    ("sh_p", (2, A_NC)), ("sh_s", (2, NS, A_NC)), ("pl_p", (15, D)), ("pl_s", (NS, 15, D)),
                    ("cmpk_p", (tp, 256)), ("cmpk_s", (128, 256)), ("cmpv_p", (tp, 256)), ("cmpv_s", (128, 256)),
                    ("selk_p", (tp, 256)), ("selk_s", (128, 256)), ("selv_p", (tp, 256)), ("selv_s", (128, 256)),
                    ("wink_p", (512, 256)), ("wink_s", (NS, 512, 256)), ("winv_p", (512, 256)), ("winv_s", (NS, 512, 256))]:
        O[nm] = dout(nm, shp)
    if dbg:
        O["dbg_p"] = dout("dbg_p", (tp, D))
        O["dbg_s"] = dout("dbg_s", (128, D))
    C.I, C.O = I, O
    xa_p, xa_s = dscr("xa_p", (tp, D)), dscr("xa_s", (128, D))
    xb_p, xb_s = dscr("xb_p", (tp, D)), dscr("xb_s", (128, D))
    C.wbf = dscr("wbf", (128, 8, A_NC), BF16)
    C.kvs_scr = dscr("kvs_scr", (128, 1536))

    with contextlib.ExitStack() as gst:
        C.identf = gst.enter_context(nc.sbuf_tensor("identf_sb", [128, 128], F32))
        C.identb = gst.enter_context(nc.sbuf_tensor("identb_sb", [128, 128], BF16))
        C.ps = [gst.enter_context(nc.psum_tensor("psb%d" % i, [128, 512], F32)) for i in range(8)]
        k.dma(C.identf[:], I["identf"])
        k.cp(C.identb[:], C.identf[:])
        chain = [(I["xp"], I["xs"]), (xa_p, xa_s), (xb_p, xb_s), (xa_p, xa_s), (O["y_p"], O["y_s"])]
        for L in range(DEPTH):
            if L not in layers:
                continue
            src, dst = chain[L], chain[L + 1]
            if L == max(layers) and dbg:
                dst = (O["dbg_p"], O["dbg_s"])
            P.barrier()
            with contextlib.ExitStack() as lst:
                if L % 3 == 0:
                    rwkv_layer(C, lst, L // 3, L, src, dst)
                elif L % 3 == 1:
                    pool_layer(C, lst, L, src, dst)
                else:
                    nsa_layer(C, lst, L, src, dst)
                P.barrier()
        P.emit()
    return nc


def ln_tail(C, R, npart, L, dst_rows, T1, crow):
    k, I = C.k, C.I
    st = C.lnst
    k.red(st[0:npart, 0:1], R[0:npart, :])
    k.ts(st[0:npart, 1:2], st[0:npart, 0:1], 1.0 / D, ALU.mult)
    k.ts(R[0:npart, :], R[0:npart, :], st[0:npart, 1:2], ALU.subtract)
    k.tt(T1[0:npart, :], R[0:npart, :], R[0:npart, :], ALU.mult)
    k.red(st[0:npart, 2:3], T1[0:npart, :])
    k.act(st[0:npart, 3:4], st[0:npart, 2:3], AF.Sqrt, bias=C.epsln[0:npart, :], scale=1.0 / D)
    k.recip(st[0:npart, 4:5], st[0:npart, 3:4])
    k.ts(R[0:npart, :], R[0:npart, :], st[0:npart, 4:5], ALU.mult)
    k.dma(crow[0][0:npart, :], I["ln_g"][L:L + 1, :].partition_broadcast(npart), eng="act")
    k.tt(R[0:npart, :], R[0:npart, :], crow[0][0:npart, :], ALU.mult)
    k.dma(crow[1][0:npart, :], I["ln_b"][L:L + 1, :].partition_broadcast(npart), eng="act")
    k.tt(R[0:npart, :], R[0:npart, :], crow[1][0:npart, :], ALU.add)
    k.dma(dst_rows, R[0:npart, :])


def rwkv_layer(C, lst, li, L, src, dst):
    nc, P, k, I, O = C.nc, C.P, C.k, C.I, C.O
    tp = C.tp
    sb = lambda n, s, d=F32: lst.enter_context(nc.sbuf_tensor("a%d_" % L + n, list(s), d))
    ps = C.ps
    Wo = sb("Wo", [128, 8, 1024], BF16)
    Wll = sb("Wll", [128, 8, 128], BF16)
    WG = [sb("WG0", [128, 8, 1024], BF16)]
    W2A2 = sb("W2A2", [128, 1024])
    mucol = sb("mucol", [128, 9])
    SEL = sb("SEL", [128, 64, 128], BF16)
    xin2 = sb("xin2", [128, 1024])
    xin = sb("xin", [64, 1024])
    xTd = sb("xTd", [128, 8, 128], BF16)
    xTsd = sb("xTsd", [128, 8, 128], BF16)
    Pt = sb("Pt", [128, 1024])
    PSt = sb("PSt", [128, 1024])
    PM = {g: sb("PM" + g, [128, 1024]) for g in "rkvz"}
    crow = [sb("crow%d" % i, [128, 1024]) for i in range(2)]
    At = sb("At", [128, 1024])
    KP = sb("KP", [128, 1024])
    T1 = sb("T1", [128, 1024])
    T2 = sb("T2", [128, 1024])
    XRf = sb("XRf", [128, 512])
    XRr = sb("XRr", [128, 512])
    XR = {x: [sb("XR%s%d" % (x, j), [128, 512], BF16) for j in range(2)] for x in ("kk", "w", "ka", "k", "r")}
    va = sb("va", [128, 8, 64])
    vs = sb("vs", [128, 8, 64])
    vT = sb("vT", [128, 8, 64])
    lla = sb("lla", [128, 128])
    llb = sb("llb", [128, 128])
    LLt = sb("LLt", [128, 128])
    YT = sb("YT", [128, 8, 64])
    S = sb("S", [128, 8, 64])
    t1 = sb("t1", [128, 8, 64])
    t2 = sb("t2", [128, 8, 64])
    t3 = sb("t3", [128, 8, 64])
    sa = sb("sa", [128, 8])
    st16 = sb("st16", [128, 5, 16])
    bon = sb("bon", [128, 16])
    G = sb("G", [64, 1024], BF16)
    gT = sb("gT", [128, 8, 64], BF16)
    C.lnst = sb("lnst", [128, 8])
    C.epsln = sb("epsln", [128, 1])
    epsgn = sb("epsgn", [128, 1])
    eps24 = sb("eps24", [128, 1])
    k.memset(C.epsln[:], LN_EPS)
    k.memset(epsgn[:], GN_EPS)
    k.memset(eps24[:], 0.0)

    w_in = I["a_w_in"][li].rearrange("(c p) n -> p c n", p=128)
    for j in range(A_NC // 128):
        stg = Pt[:].rearrange("p (c n) -> p c n", c=8) if j % 2 == 0 else PSt[:].rearrange("p (c n) -> p c n", c=8)
        stgb = (T1 if j % 2 == 0 else T2)[:].bitcast(BF16)[:, 0:1024].rearrange("p (c n) -> p c n", c=8)
        k.dma(stg, w_in[:, :, j * 128:(j + 1) * 128], eng="sp" if j % 2 == 0 else "act")
        k.cp(stgb, stg, eng="pool" if j % 2 == 0 else "act")
        k.dma(C.wbf[:, :, j * 128:(j + 1) * 128], stgb, eng="sp")
    w_out = I["a_w_out"][li].rearrange("(c p) n -> p c n", p=128)
    for j in range(8):
        stg = Pt[:].rearrange("p (c n) -> p c n", c=8) if j % 2 == 0 else PSt[:].rearrange("p (c n) -> p c n", c=8)
        k.dma(stg, w_out[:, :, j * 128:(j + 1) * 128], eng="sp" if j % 2 == 0 else "act")
        k.cp(Wo[:, :, j * 128:(j + 1) * 128], stg, eng="pool" if j % 2 == 0 else "act")
    k.dma(Wll[:], C.wbf[:, :, 4096:4224])
    k.dma(W2A2[0:64, :], I["a_w2"][li])
    k.dma(W2A2[64:128, :], I["a_a2"][li])
    k.dma(mucol[:, 0:8], I["a_mu"][li, 2048:3072].rearrange("(c p) -> p c", p=128), allow_slow_non_contiguous=True)
    k.dma(mucol[:, 8:9], I["a_mu"][li, 4096:4224].rearrange("(c p) -> p c", p=128), allow_slow_non_contiguous=True)
    k.dma(SEL[:], I["selb"].rearrange("p (t m) -> p t m", m=128))
    k.memset(S[:], 0.0)
    EPI = sb("EPI", [64, 1024])
    EPN = sb("EPN", [64, 1024])
    EPX = sb("EPX", [64, 1024])
    FMAR = sb("FMAR", [64, 8, 128])
    FMB = sb("FMB", [64, 8, 64])
    FMK = sb("FMK", [64, 8, 64])
    GB = sb("GB", [64, 8, 128])
    GK = sb("GK", [64, 8, 128])
    PQ = [sb("PQ%d" % i, [64, 8, 64]) for i in range(4)]
    Tm = sb("Tm", [64, 8, 64])
    XT = sb("XT", [64, 8, 64])
    UT = sb("UT", [64, 8, 64])
    ST = sb("ST", [64, 16, 64])
    PCc = sb("PCc", [64, 16])
    MK = sb("MK", [64, 320])
    k.dma(MK[:], I["cmask"][0:64, 512:832])
    MASKAR = MK[:, 0:128]
    MASKNT = MK[:, 128:192]
    TRI = MK[:, 192:256]
    IDN = MK[:, 256:320]
    k.memset(ST[:], 0.0)
    pbi = [0]

    def bank():
        pbi[0] = (pbi[0] + 1) % 8
        return ps[pbi[0]]
    import os
    STOP = int(os.environ.get('STOPAT', '99'))
    if STOP <= 1:
        return

    cri = [0]

    def jrow(src_row, npart=128):
        t = crow[cri[0] % 2]
        cri[0] += 1
        k.dma(t[0:npart, :], src_row.partition_broadcast(npart), eng="act")
        return t

    def h4(t):
        return t[:].rearrange("p (a b j) -> p a b j", a=8, b=2)

    def toxr(X, name):
        X4 = h4(X)
        o3 = XRf[:].rearrange("p (a j) -> p a j", a=8)
        k.cp(o3[0:64], X4[0:64, :, 0, :], eng="act")
        k.cp(o3[64:128], X4[64:128, :, 1, :], eng="act")
        k.cp(XR[name][0][:], XRf[:], eng="pool")
        k.tt(XRr[:], XRf[:], XR[name][0][:], ALU.subtract, eng="pool")
        k.cp(XR[name][1][:], XRr[:], eng="pool")

    ntile_p = tp // 64
    import os
    tiles = [("p", n) for n in range(ntile_p)] + ([("s", 0), ("s", 1)] if not os.environ.get("NOSAMPLE") else [])
    wg_i = [0]
    for kind, n in tiles:
        srcx = src[0] if kind == "p" else src[1]
        dstx = dst[0] if kind == "p" else dst[1]
        r0 = n * 64
        if r0 == 0:
            k.memset(xin2[0:1, :], 0.0)
            k.dma(xin2[1:64, :], srcx[0:63, :])
        else:
            k.dma(xin2[0:64, :], srcx[r0 - 1:r0 + 63, :])
        k.dma(xin[:], srcx[r0:r0 + 64, :], eng="act")
        for b in range(2):
            for c in range(4):
                k.tr(ps[b][:, c * 64:(c + 1) * 64], xin[0:64, (4 * b + c) * 128:(4 * b + c + 1) * 128], C.identf[0:64, 0:64])
            for c in range(4):
                k.tr(ps[b][:, 256 + c * 64:256 + (c + 1) * 64], xin2[0:64, (4 * b + c) * 128:(4 * b + c + 1) * 128], C.identf[0:64, 0:64])
            pv = ps[b][:, 0:256].rearrange("p (c t) -> p c t", c=4)
            pw = ps[b][:, 256:512].rearrange("p (c t) -> p c t", c=4)
            k.cp(xTd[:, 4 * b:4 * b + 4, 0:64], pv, eng="act")
            k.cp(xTd[:, 4 * b:4 * b + 4, 64:128], pv, eng="dve")
            k.cp(xTsd[:, 4 * b:4 * b + 4, 0:64], pw, eng="act")
            k.cp(xTsd[:, 4 * b:4 * b + 4, 64:128], pw, eng="dve")
        if STOP <= 2:
            return
        last_rows = []
        if kind == "p" and n == ntile_p - 1:
            last_rows = [(63, O["sh_p"][li])]
        if kind == "s":
            last_rows = [(sl * 8 + 7, O["sh_s"][li, n * 8 + sl]) for sl in range(8)]
        for gi, g in enumerate("rkvz"):
            wg = WG[0]
            wg_i[0] += 1
            k.dma(wg[:], C.wbf[:, :, gi * 1024:(gi + 1) * 1024], eng="sp")
            for hf in range(2):
                cols = slice(hf * 512, (hf + 1) * 512)
                for c in range(8):
                    k.mm(ps[2][:], xTd[:, c, :], wg[:, c, cols], start=(c == 0), stop=(c == 7))
                for c in range(8):
                    k.mm(ps[3][:], xTsd[:, c, :], wg[:, c, cols], start=(c == 0), stop=(c == 7))
                k.cp(Pt[:, cols], ps[2][:], eng="act")
                k.cp(PSt[:, cols], ps[3][:], eng="act")
            if g == "v" and kind == "s":
                for c in range(8):
                    for dc in range(8):
                        k.mm(ps[2][:, c * 64:(c + 1) * 64], wg[:, dc, c * 128:(c + 1) * 128], xTd[:, dc, 0:64],
                             start=(dc == 0), stop=(dc == 7))
                for c in range(8):
                    for dc in range(8):
                        k.mm(ps[3][:, c * 64:(c + 1) * 64], wg[:, dc, c * 128:(c + 1) * 128], xTsd[:, dc, 0:64],
                             start=(dc == 0), stop=(dc == 7))
                k.cp(va[:].rearrange("p c t -> p (c t)"), ps[2][:], eng="act")
                k.cp(vs[:].rearrange("p c t -> p (c t)"), ps[3][:], eng="act")
                if kind == "s":
                    for sl in range(8):
                        k.dma(vs[:, :, sl * 8], I["st_shift"][li, n * 8 + sl, 2048:3072].rearrange("(c p) -> p c", p=128), eng="act", allow_slow_non_contiguous=True)
                k.tt(vs[:], vs[:], va[:], ALU.subtract)
                k.tt(vs[:], vs[:], mucol[:, 0:8].unsqueeze(2).to_broadcast([128, 8, 64]), ALU.mult)
                k.tt(vT[:], vs[:], va[:], ALU.add)
            if kind == "s":
                for sl in range(8):
                    for hh in range(2):
                        k.dma(PSt[hh * 64 + sl * 8:hh * 64 + sl * 8 + 1, :],
                              I["st_shift"][li, n * 8 + sl:n * 8 + sl + 1, gi * 1024:(gi + 1) * 1024], eng="act")
            for (row, dap) in last_rows:
                k.dma(dap[gi * 1024:(gi + 1) * 1024].unsqueeze(0), Pt[row:row + 1, :], eng="act")
            mur = jrow(I["a_mu"][li:li + 1, gi * 1024:(gi + 1) * 1024])
            k.tt(PSt[:], PSt[:], Pt[:], ALU.subtract)
            k.tt(PSt[:], PSt[:], mur[:], ALU.mult)
            k.tt(PM[g][:], PSt[:], Pt[:], ALU.add)
        if STOP <= 3:
            return
        for c in range(8):
            k.mm(ps[2][:, 0:128], Wll[:, c, :], xTd[:, c, :], start=(c == 0), stop=(c == 7))
        for c in range(8):
            k.mm(ps[3][:, 0:128], Wll[:, c, :], xTsd[:, c, :], start=(c == 0), stop=(c == 7))
        k.cp(lla[:], ps[2][:, 0:128], eng="act")
        k.cp(llb[:], ps[3][:, 0:128], eng="act")
        if kind == "s":
            for sl in range(8):
                for hh in range(2):
                    k.dma(llb[:, hh * 64 + sl * 8:hh * 64 + sl * 8 + 1],
                          I["st_shift"][li, n * 8 + sl, 4096:4224].rearrange("(c p) -> p c", p=128), eng="act", allow_slow_non_contiguous=True)
        for (row, dap) in last_rows:
            k.dma(dap[4096:4224].rearrange("(c p) -> p c", p=128), lla[:, row:row + 1], eng="act", allow_slow_non_contiguous=True)
        k.tt(llb[:], llb[:], lla[:], ALU.subtract)
        k.stt(LLt[:], llb[:], mucol[:, 8:9], lla[:], ALU.mult, ALU.add)
        k.act(LLt[0:64, :], LLt[0:64, :], AF.Tanh)
        if STOP <= 4:
            return
        for hf in range(2):
            cols = slice(hf * 512, (hf + 1) * 512)
            k.mm(ps[2 + hf][:], LLt[0:64, :], W2A2[0:64, cols])
            k.mm(ps[4 + hf][:], LLt[64:128, :], W2A2[64:128, cols])
        w0r = jrow(I["a_w0"][li:li + 1, :])
        for hf in range(2):
            cols = slice(hf * 512, (hf + 1) * 512)
            k.tt(T1[:, cols], ps[2 + hf][:], w0r[:, cols], ALU.add)
        k.act(T1[:], T1[:], AF.Sigmoid)
        CC = float(np.exp(-0.5))
        if kind == "p":
            for hf in range(2):
                cols = slice(hf * 512, (hf + 1) * 512)
                k.mm(ps[6 + hf][0:64, :], TRI, T1[0:64, cols])
                k.act(EPI[:, cols], ps[6 + hf][0:64, :], AF.Exp, scale=-CC)
                k.act(EPN[:, cols], ps[6 + hf][0:64, :], AF.Exp, scale=CC)
                k.tt(EPX[:, cols], ps[6 + hf][0:64, :], T1[0:64, cols], ALU.subtract)
            k.act(EPX[:], EPX[:], AF.Exp, scale=-CC)
            for h in range(16):
                k.mm(ps[2][0:64, h:h + 1], EPI[:, h * 64:(h + 1) * 64], C.identf[0:64, 63:64])
            k.cp(PCc[:], ps[2][0:64, 0:16], eng="act")
        else:
            k.act(T1[:], T1[:], AF.Exp, scale=-CC)
            toxr(T1, "w")
        a0r = jrow(I["a_a0"][li:li + 1, :])
        for hf in range(2):
            cols = slice(hf * 512, (hf + 1) * 512)
            k.tt(At[:, cols], ps[4 + hf][:], a0r[:, cols], ALU.add)
        k.act(At[:], At[:], AF.Sigmoid)
        kkr = jrow(I["a_k_k"][li:li + 1, :])
        k.tt(T1[:], PM["k"][:], kkr[:], ALU.mult)
        k.tt(T2[:], T1[:], T1[:], ALU.mult)
        k.red(st16[:, 0, :], T2[:].rearrange("p (h j) -> p h j", h=16))
        k.ts(st16[:, 0, :], st16[:, 0, :], 1e-24, ALU.max)
        k.act(st16[:, 1, :], st16[:, 0, :], AF.Sqrt)
        k.recip(st16[:, 2, :], st16[:, 1, :])
        k.tt(T1[:].rearrange("p (h j) -> p h j", h=16), T1[:].rearrange("p (h j) -> p h j", h=16),
             st16[:, 2, :].unsqueeze(2).to_broadcast([128, 16, 64]), ALU.mult)
        if kind == "s":
            toxr(T1, "kk")
        k.tt(T2[:], T1[:], At[:], ALU.mult)
        if kind == "s":
            toxr(T2, "ka")
        kar = jrow(I["a_k_a"][li:li + 1, :])
        k.stt(PSt[:], At[:], -1.0, kar[:], ALU.add, ALU.mult)
        k.stt(KP[:], PSt[:], 1.0, PM["k"][:], ALU.add, ALU.mult)
        if kind == "s":
            toxr(KP, "k")
            toxr(PM["r"], "r")
        rkr = jrow(I["a_r_k"][li:li + 1, :])
        k.tt(Pt[:], PM["r"][:], rkr[:], ALU.mult)
        k.tt(Pt[:], Pt[:], KP[:], ALU.mult)
        k.red(bon[:], Pt[:].rearrange("p (h j) -> p h j", h=16))
        if kind == "p":
            k.stt(EPX[:], T1[0:64, :], -1.0, EPX[:], ALU.mult, ALU.mult)
            k.tt(T2[0:64, :], T2[0:64, :], EPN[:], ALU.mult)
            k.tt(EPN[:], KP[0:64, :], EPN[:], ALU.mult)
            k.tt(EPI[:], PM["r"][0:64, :], EPI[:], ALU.mult)

        if STOP <= 5:
            return
        def step(Sx, tl):
            bks = {}
            for bi, x in enumerate(("kk", "w", "ka", "k", "r")):
                bk = ps[3 + bi] if bi < 5 else None
                k.mm(bk[:], SEL[:, tl, :], XR[x][0][:], start=True, stop=False)
                k.mm(bk[:], SEL[:, tl, :], XR[x][1][:], start=False, stop=True)
                bks[x] = bk[:].rearrange("p (a j) -> p a j", a=8)
            k.tt(t1[:], Sx, bks["kk"], ALU.mult)
            k.tt(t3[:], bks["k"], vT[:, :, tl:tl + 1].to_broadcast([128, 8, 64]), ALU.mult)
            k.red(sa[:], t1[:], negate=True)
            k.tt(Sx, Sx, bks["w"], ALU.mult)
            k.tt(t2[:], bks["ka"], sa[:].unsqueeze(2).to_broadcast([128, 8, 64]), ALU.mult)
            k.tt(Sx, Sx, t3[:], ALU.add)
            k.tt(Sx, Sx, t2[:], ALU.add)
            k.tt(t1[:], Sx, bks["r"], ALU.mult)
            k.red(YT[:, :, tl], t1[:])

        if kind == "p":
            i64 = C.identf[0:64, 0:64]
            for hh in range(2):
                H0 = hh * 8
                bA = [bank(), bank()]
                bB, bK = bank(), bank()
                for hl in range(8):
                    hc = slice((H0 + hl) * 64, (H0 + hl + 1) * 64)
                    o = (hl % 4) * 128
                    k.tr(bA[hl // 4][0:64, o:o + 64], EPX[:, hc], i64)
                    k.tr(bA[hl // 4][0:64, o + 64:o + 128], EPI[:, hc], i64)
                    k.tr(bB[0:64, hl * 64:(hl + 1) * 64], T2[0:64, hc], i64)
                    k.tr(bK[0:64, hl * 64:(hl + 1) * 64], EPN[:, hc], i64)
                k.cp(FMAR[:, 0:4, :].rearrange("p a b -> p (a b)"), bA[0][0:64, :], eng="act")
                k.cp(FMAR[:, 4:8, :].rearrange("p a b -> p (a b)"), bA[1][0:64, :], eng="act")
                k.cp(FMB[:].rearrange("p a b -> p (a b)"), bB[0:64, :], eng="dve")
                k.cp(FMK[:].rearrange("p a b -> p (a b)"), bK[0:64, :], eng="dve")
                for (Gd, FMl) in ((GB, FMB), (GK, FMK)):
                    bb = [bank(), bank()]
                    for hl in range(8):
                        o = (hl % 4) * 128
                        k.mm(bb[hl // 4][0:64, o:o + 128], FMl[:, hl, :], FMAR[:, hl, :])
                    for q in range(2):
                        k.tt(Gd[:, 4 * q:4 * q + 4, :], bb[q][0:64, :].rearrange("p (a b) -> p a b", a=4),
                             MASKAR.unsqueeze(1).to_broadcast([64, 4, 128]), ALU.mult)
                bq = bank()
                for hl in range(8):
                    k.mm(bq[0:64, hl * 64:(hl + 1) * 64], FMAR[:, hl, 0:64], FMB[:, hl, :])
                k.tt(PQ[1][:], bq[0:64, :].rearrange("p (a b) -> p a b", a=8), MASKNT.unsqueeze(1).to_broadcast([64, 8, 64]), ALU.mult)
                k.tt(Tm[:], GB[:, :, 0:64], IDN.unsqueeze(1).to_broadcast([64, 8, 64]), ALU.add)
                Pc, Qc = GB[:, :, 0:64], PQ[1]
                for lv in range(5):
                    Pn, Qn = PQ[2 * ((lv + 1) % 2)], PQ[2 * ((lv + 1) % 2) + 1]
                    if lv < 4:
                        bp = bank()
                        for hl in range(8):
                            k.mm(bp[0:64, hl * 64:(hl + 1) * 64], Qc[:, hl, :], Pc[:, hl, :])
                    bq = bank()
                    for hl in range(8):
                        k.mm(bq[0:64, hl * 64:(hl + 1) * 64], Pc[:, hl, :], Qc[:, hl, :])
                    if lv < 4:
                        k.cp(Pn[:].rearrange("p a b -> p (a b)"), bp[0:64, :], eng="act")
                    k.cp(Qn[:].rearrange("p a b -> p (a b)"), bq[0:64, :], eng="act")
                    bt = bank()
                    for hl in range(8):
                        k.mm(bt[0:64, hl * 64:(hl + 1) * 64], Qn[:, hl, :], Tm[:, hl, :])
                    k.tt(Tm[:], Tm[:], bt[0:64, :].rearrange("p (a b) -> p a b", a=8), ALU.add)
                    Pc, Qc = Pn, Qn
                bx = bank()
                for hl in range(8):
                    hc = slice((H0 + hl) * 64, (H0 + hl + 1) * 64)
                    k.mm(bx[0:64, hl * 64:(hl + 1) * 64], FMAR[:, hl, 0:64], ST[:, H0 + hl, :], start=True, stop=False)
                    k.mm(bx[0:64, hl * 64:(hl + 1) * 64], GK[:, hl, 0:64], PM["v"][0:64, hc], start=False, stop=True)
                k.cp(XT[:].rearrange("p a b -> p (a b)"), bx[0:64, :], eng="act")
                bu = bank()
                for hl in range(8):
                    k.mm(bu[0:64, hl * 64:(hl + 1) * 64], Tm[:, hl, :], XT[:, hl, :])
                k.cp(UT[:].rearrange("p a b -> p (a b)"), bu[0:64, :], eng="act")
                by = bank()
                for hl in range(8):
                    hc = slice((H0 + hl) * 64, (H0 + hl + 1) * 64)
                    k.mm(by[0:64, hl * 64:(hl + 1) * 64], FMAR[:, hl, 64:128], ST[:, H0 + hl, :], start=True, stop=False)
                    k.mm(by[0:64, hl * 64:(hl + 1) * 64], GB[:, hl, 64:128], UT[:, hl, :], start=False, stop=False)
                    k.mm(by[0:64, hl * 64:(hl + 1) * 64], GK[:, hl, 64:128], PM["v"][0:64, hc], start=False, stop=True)
                k.cp(KP[0:64, hh * 512:(hh + 1) * 512], by[0:64, :], eng="act")
                bs = bank()
                for hl in range(8):
                    hc = slice((H0 + hl) * 64, (H0 + hl + 1) * 64)
                    k.mm(bs[0:64, hl * 64:(hl + 1) * 64], T2[0:64, hc], UT[:, hl, :], start=True, stop=False)
                    k.mm(bs[0:64, hl * 64:(hl + 1) * 64], EPN[:, hc], PM["v"][0:64, hc], start=False, stop=True)
                k.tt(ST[:, H0:H0 + 8, :], ST[:, H0:H0 + 8, :], bs[0:64, :].rearrange("p (a b) -> p a b", a=8), ALU.add)
                k.tt(ST[:, H0:H0 + 8, :], ST[:, H0:H0 + 8, :], PCc[:, H0:H0 + 8].unsqueeze(2).to_broadcast([64, 8, 64]), ALU.mult)
            if n == ntile_p - 1:
                for q in range(2):
                    bo = bank()
                    for hl in range(8):
                        k.tr(bo[0:64, hl * 64:(hl + 1) * 64], ST[:, q * 8 + hl, :], i64)
                    k.cp(EPI[:, q * 512:(q + 1) * 512], bo[0:64, :], eng="act")
                k.dma(O["S_p"][li].rearrange("h i j -> i h j"), EPI[:].rearrange("p (h j) -> p h j", h=16))
        else:
            for sl in range(8):
                sq = n * 8 + sl
                k.dma(t3[:], I["st_S"][li, sq].rearrange("(a b) i j -> (b i) a j", b=2))
                Sx = KP[:, 0:512].rearrange("p (a j) -> p a j", a=8)
                k.cp(Sx, t3[:], eng="act")
                for t in range(8):
                    step(Sx, sl * 8 + t)
                k.dma(O["S_s"][li, sq].rearrange("(a b) i j -> (b i) a j", b=2), Sx)

        if STOP <= 6:
            return
        Y = T1
        if kind == "p":
            k.cp(Y[0:64, :], KP[0:64, :], eng="pool")
        else:
            for c in range(8):
                k.tr(ps[c // 4][0:64, (c % 4) * 128:(c % 4 + 1) * 128], YT[:, c, :], C.identf[:, :])
            k.cp(Y[0:64, 0:512], ps[0][0:64, :], eng="act")
            k.cp(Y[0:64, 512:1024], ps[1][0:64, :], eng="act")
        Y3 = Y[0:64, :].rearrange("p (h j) -> p h j", h=16)
        k.red(st16[0:64, 0, :], Y3)
        k.ts(st16[0:64, 0, :], st16[0:64, 0, :], 1.0 / 64, ALU.mult)
        k.tt(Y3, Y3, st16[0:64, 0, :].unsqueeze(2).to_broadcast([64, 16, 64]), ALU.subtract)
        k.tt(T2[0:64, :], Y[0:64, :], Y[0:64, :], ALU.mult)
        k.red(st16[0:64, 1, :], T2[0:64, :].rearrange("p (h j) -> p h j", h=16))
        k.act(st16[0:64, 2, :], st16[0:64, 1, :], AF.Sqrt, bias=epsgn[0:64, :], scale=1.0 / 64)
        k.recip(st16[0:64, 3, :], st16[0:64, 2, :])
        k.tt(Y3, Y3, st16[0:64, 3, :].unsqueeze(2).to_broadcast([64, 16, 64]), ALU.mult)
        gr = jrow(I["a_lnx_g"][li:li + 1, :], 64)
        k.tt(Y[0:64, :], Y[0:64, :], gr[0:64, :], ALU.mult)
        br = jrow(I["a_lnx_b"][li:li + 1, :], 64)
        k.tt(Y[0:64, :], Y[0:64, :], br[0:64, :], ALU.add)
        k.tt(T2[0:64, :].rearrange("p (h j) -> p h j", h=16), PM["v"][0:64, :].rearrange("p (h j) -> p h j", h=16),
             bon[0:64, :].unsqueeze(2).to_broadcast([64, 16, 64]), ALU.mult)
        k.tt(Y[0:64, :], Y[0:64, :], T2[0:64, :], ALU.add)
        k.act(T2[0:64, :], PM["z"][0:64, :], AF.Silu)
        k.tt(G[:], Y[0:64, :], T2[0:64, :], ALU.mult)
        psb = ps[2][:].bitcast(BF16)
        for c in range(8):
            k.tr(psb[:, c * 64:(c + 1) * 64], G[:, c * 128:(c + 1) * 128], C.identb[0:64, 0:64])
        k.cp(gT[:].rearrange("p c t -> p (c t)"), psb[:, 0:512], eng="act")
        for hf in range(2):
            cols = slice(hf * 512, (hf + 1) * 512)
            for c in range(8):
                k.mm(ps[hf][0:64, :], gT[:, c, :], Wo[:, c, cols], start=(c == 0), stop=(c == 7))
            k.stt(Y[0:64, cols], xin[:, cols], ALPHA, ps[hf][0:64, :], ALU.mult, ALU.add)
        ln_tail(C, Y, 64, L, dstx[r0:r0 + 64, :], T2, crow)


def pool_layer(C, lst, L, src, dst):
    nc, P, k, I, O = C.nc, C.P, C.k, C.I, C.O
    tp = C.tp
    sb = lambda n, s, d=F32: lst.enter_context(nc.sbuf_tensor("b_" + n, list(s), d))
    ps = C.ps
    Win = sb("Win", [128, 8, 2048], BF16)
    Wg = sb("Wg", [128, 4, 2, 256], BF16)
    Wo = sb("Wo", [128, 8, 1024], BF16)
    scol = sb("scol", [128, 8])
    stg = [sb("stg%d" % i, [128, 8, 128]) for i in range(2)]
    xin = sb("xin", [128, 1024])
    xT = sb("xT", [128, 8, 128], BF16)
    E = sb("E", [128, 8, 16 * 23])
    A = sb("A", [128, 2, 16 * 23])
    B = sb("B", [128, 2, 16 * 23])
    dT = sb("dT", [128, 8, 128], BF16)
    sz = sb("sz", [128, 8, 128])
    gT = sb("gT", [128, 8, 128], BF16)
    R = sb("R", [128, 1024])
    T1 = sb("T1", [128, 1024])
    cinv = sb("cinv", [128, 512])
    crow = [sb("crow%d" % i, [128, 1024]) for i in range(2)]
    C.lnst = sb("lnst", [128, 8])
    C.epsln = sb("epsln", [128, 1])
    k.memset(C.epsln[:], LN_EPS)
    k.dma(cinv[:], I["cmask"][:, 0:512])
    w_in = I["b_w_in"].rearrange("(c p) n -> p c n", p=128)
    for j in range(16):
        k.dma(stg[j % 2][:], w_in[:, :, j * 128:(j + 1) * 128], eng="sp" if j % 2 == 0 else "act")
        k.cp(Win[:, :, j * 128:(j + 1) * 128], stg[j % 2][:], eng="pool" if j % 2 == 0 else "act")
    w_out = I["b_w_out"].rearrange("(c p) n -> p c n", p=128)
    for j in range(8):
        k.dma(stg[j % 2][:], w_out[:, :, j * 128:(j + 1) * 128], eng="sp" if j % 2 == 0 else "act")
        k.cp(Wo[:, :, j * 128:(j + 1) * 128], stg[j % 2][:], eng="pool" if j % 2 == 0 else "act")
    for g in range(4):
        sv = stg[g % 2][:].rearrange("p c n -> p (c n)")[:, 0:512].rearrange("p (c n) -> p c n", c=2)
        k.dma(sv, I["b_w_grp"][g].rearrange("(c p) n -> p c n", p=128), eng="sp")
        k.cp(Wg[:, g, :, :], sv, eng="pool")
    k.dma(scol[:], I["b_scale"][0].rearrange("(c p) -> p c", p=128), allow_slow_non_contiguous=True)
    k.memset(E[:], 0.0)

    ntile_p = tp // 128
    tiles = [("p", n) for n in range(ntile_p)] + [("s", 0)]
    for kind, n in tiles:
        srcx = src[0] if kind == "p" else src[1]
        dstx = dst[0] if kind == "p" else dst[1]
        r0 = n * 128
        nseg, new = (1, 128) if kind == "p" else (16, 8)
        sl = 15 + new
        Ev = E[:, :, 0:nseg * sl].rearrange("p c (s t) -> p c s t", s=nseg)
        k.dma(xin[:], srcx[r0:r0 + 128, :])
        for b in range(2):
            for c in range(4):
                k.tr(ps[b][:, c * 128:(c + 1) * 128], xin[:, (4 * b + c) * 128:(4 * b + c + 1) * 128], C.identf[:, :])
            k.cp(xT[:, 4 * b:4 * b + 4, :].rearrange("p c t -> p (c t)"), ps[b][:], eng="act")
        if kind == "s":
            for hh in range(2):
                k.dma(R[0:120, :], I["st_pool"][hh * 8:(hh + 1) * 8].rearrange("s r d -> (s r) d"))
                for b in range(2):
                    for c in range(4):
                        k.tr(ps[2 + b][:, c * 120:(c + 1) * 120], R[0:120, (4 * b + c) * 128:(4 * b + c + 1) * 128], C.identf[0:120, 0:120])
                    k.cp(Ev[:, 4 * b:4 * b + 4, hh * 8:(hh + 1) * 8, 0:15],
                         ps[2 + b][:, 0:480].rearrange("p (c s r) -> p c s r", c=4, s=8), eng="act")
        for ob in range(4):
            for o4 in range(4):
                oc = ob * 4 + o4
                for dc in range(8):
                    k.mm(ps[4 + ob % 2][:, o4 * 128:(o4 + 1) * 128], Win[:, dc, oc * 128:(oc + 1) * 128], xT[:, dc, :],
                         start=(dc == 0), stop=(dc == 7))
            pv = ps[4 + ob % 2][:].rearrange("p (c s t) -> p c s t", c=4, s=nseg)
            if ob < 2:
                k.cp(Ev[:, ob * 4:ob * 4 + 4, :, 15:sl], pv, eng="act")
            else:
                k.act(sz[:, (ob - 2) * 4:(ob - 2) * 4 + 4, :].rearrange("p c t -> p (c t)"), ps[4 + ob % 2][:], AF.Silu)
        if kind == "p" and n == ntile_p - 1:
            for b in range(2):
                for c in range(4):
                    k.tr(ps[2 + b][0:15, c * 128:(c + 1) * 128], E[:, 4 * b + c, 128:143], C.identf[:, :])
                k.cp(T1[0:15, b * 512:(b + 1) * 512], ps[2 + b][0:15, :], eng="act")
            k.dma(O["pl_p"], T1[0:15, :])
        if kind == "s":
            for hh in range(2):
                for b in range(2):
                    for c in range(4):
                        Ac = A[:, 0, 0:120].rearrange("p (s r) -> p s r", s=8)
                        k.cp(Ac, Ev[:, 4 * b + c, hh * 8:(hh + 1) * 8, 8:23], eng="pool")
                        k.tr(ps[2 + b][0:120, c * 128:(c + 1) * 128], A[:, 0, 0:120], C.identf[:, :])
                    k.cp(T1[0:120, b * 512:(b + 1) * 512], ps[2 + b][0:120, :], eng="act")
                k.dma(O["pl_s"][hh * 8:(hh + 1) * 8].rearrange("s r d -> (s r) d"), T1[0:120, :])
        for g in range(4):
            cur = Ev[:, 2 * g:2 * g + 2]
            bufs = [A[:, :, 0:nseg * sl].rearrange("p c (s t) -> p c s t", s=nseg),
                    B[:, :, 0:nseg * sl].rearrange("p c (s t) -> p c s t", s=nseg)]
            lo = 0
            for si, sh in enumerate((1, 2, 4, 8)[:g + 1]):
                nxt = bufs[si % 2]
                lo2 = lo + sh
                k.tt(nxt[:, :, :, lo2:sl], cur[:, :, :, lo2:sl], cur[:, :, :, lo:sl - sh], ALU.add)
                cur, lo = nxt, lo2
            w = 2 ** (g + 1)
            pooled = bufs[(g + 1) % 2]
            if kind == "p" and n == 0:
                k.tt(pooled[:, :, 0, 15:sl], cur[:, :, 0, 15:sl],
                     cinv[:, g * 128:(g + 1) * 128].unsqueeze(1).to_broadcast([128, 2, 128]), ALU.mult)
                k.tt(dT[:, 2 * g:2 * g + 2, :], pooled[:, :, 0, 15:sl], Ev[:, 2 * g:2 * g + 2, 0, 15:sl], ALU.subtract)
            else:
                k.stt(dT[:, 2 * g:2 * g + 2, :].rearrange("p c (s t) -> p c s t", s=nseg), cur[:, :, :, 15:sl], 1.0 / w,
                      Ev[:, 2 * g:2 * g + 2, :, 15:sl], ALU.mult, ALU.subtract)
        if kind == "p":
            k.cp(A[:, 0, 0:120].rearrange("p (c t) -> p c t", c=8), E[:, :, 128:143], eng="pool")
            k.cp(E[:, :, 0:15], A[:, 0, 0:120].rearrange("p (c t) -> p c t", c=8), eng="pool")
        for jc in range(8):
            g, jl = jc // 2, jc % 2
            for ic in range(2):
                k.mm(ps[6 + jc // 4][:, (jc % 4) * 128:(jc % 4 + 1) * 128], Wg[:, g, ic, jl * 128:(jl + 1) * 128], dT[:, 2 * g + ic, :],
                     start=(ic == 0), stop=(ic == 1))
        for jc in range(8):
            k.stt(gT[:, jc, :], ps[6 + jc // 4][:, (jc % 4) * 128:(jc % 4 + 1) * 128], scol[:, jc:jc + 1], sz[:, jc, :], ALU.mult, ALU.mult)
        for hf in range(2):
            cols = slice(hf * 512, (hf + 1) * 512)
            for c in range(8):
                k.mm(ps[hf][:], gT[:, c, :], Wo[:, c, cols], start=(c == 0), stop=(c == 7))
            k.stt(R[:, cols], xin[:, cols], ALPHA, ps[hf][:], ALU.mult, ALU.add)
        ln_tail(C, R, 128, L, dstx[r0:r0 + 128, :], T1, crow)


NEG = -30000.0
SCL = 0.125


def nsa_layer(C, lst, L, src, dst):
    nc, P, k, I, O = C.nc, C.P, C.k, C.I, C.O
    tp = C.tp
    sb = lambda n, s, d=F32: lst.enter_context(nc.sbuf_tensor("c_" + n, list(s), d))
    ps = C.ps
    Win = sb("Win", [128, 8, C_NC], BF16)
    Wo = sb("Wo", [128, 8, 1024], BF16)
    stg = [sb("stg%d" % i, [128, 8, 128]) for i in range(2)]
    xin = sb("xin", [128, 1024])
    xT = sb("xT", [128, 8, 128], BF16)
    KV = sb("KV", [128, 1536])
    KsT = sb("KsT", [64, 4, 17 * 128], BF16)
    KwT = sb("KwT", [64, 4, 5 * 128], BF16)
    Vs = sb("Vs", [128, 17, 4, 65], BF16)
    Vw = sb("Vw", [128, 5, 4, 65], BF16)
    KcT = sb("KcT", [64, 4, 64], BF16)
    Vc = sb("Vc", [64, 4, 98])
    Wbk = sb("Wbk", [128, 124])
    Wbv = sb("Wbv", [128, 124])
    wcol = sb("wcol", [128, 2])
    QT = sb("QT", [64, 16, 128], BF16)
    GZ = sb("GZ", [128, 1072])
    gates = sb("gates", [128, 48])
    Bt = [sb("Bt%d" % i, [128, 512]) for i in range(4)]
    SBS = sb("SBS", [128, 17 * 128])
    SBW = sb("SBW", [128, 5 * 128])
    SBC = sb("SBC", [64, 128])
    hbias = sb("hbias", [128, 240])
    k.dma(hbias[:], I["n_hb"])
    GBUF = [sb("GBUF%d" % i, [128, 1024]) for i in range(3)]
    tmp = sb("tmp", [128, 512])
    TMP = [tmp, sb("tmp1", [128, 512])]
    ec = sb("ec", [64, 512])
    eb = sb("eb", [128, 512], BF16)
    EB = [eb, sb("eb1", [128, 512], BF16)]
    OB = sb("OB", [128, 4, 98])
    rd = sb("rd", [128, 8])
    imp = sb("imp", [128, 40])
    imp2 = sb("imp2", [128, 40])
    m8 = sb("m8", [128, 16])
    cbt = sb("cbt", [128, 80])
    selT = sb("selT", [40, 128], BF16)
    Eexp = sb("Eexp", [40, 17 * 128], BF16)
    Oacc = sb("Oacc", [128, 1024])
    Gb = sb("Gb", [128, 1024], BF16)
    gT = sb("gT", [128, 8, 128], BF16)
    T1 = sb("T1", [128, 1024])
    idx = sb("idx", [128, 256], I32)
    idf = GBUF[0][:, 0:256]
    crow = [x[:].rearrange("p c n -> p (c n)") for x in stg]
    C.lnst = sb("lnst", [128, 8])
    C.epsln = sb("epsln", [128, 1])
    k.memset(C.epsln[:], LN_EPS)
    w_in = I["c_w_in"].rearrange("(c p) n -> p c n", p=128)
    nj = (C_NC + 127) // 128
    for j in range(nj):
        wd = min(128, C_NC - j * 128)
        k.dma(stg[j % 2][:, :, 0:wd], w_in[:, :, j * 128:j * 128 + wd], eng="sp" if j % 2 == 0 else "act")
        k.cp(Win[:, :, j * 128:j * 128 + wd], stg[j % 2][:, :, 0:wd], eng="pool" if j % 2 == 0 else "act")
    w_out = I["c_w_out"].rearrange("(c p) n -> p c n", p=128)
    for j in range(8):
        k.dma(stg[j % 2][:], w_out[:, :, j * 128:(j + 1) * 128], eng="sp" if j % 2 == 0 else "act")
        k.cp(Wo[:, :, j * 128:(j + 1) * 128], stg[j % 2][:], eng="pool" if j % 2 == 0 else "act")
    for r in range(4):
        k.dma(wcol[r * 32:(r + 1) * 32, 0:1], I["c_cmp_wk"].rearrange("o l -> l o"), allow_slow_non_contiguous=True)
        k.dma(wcol[r * 32:(r + 1) * 32, 1:2], I["c_cmp_wv"].rearrange("o l -> l o"), allow_slow_non_contiguous=True)
    k.dma(Wbk[:], I["n_wbm"])
    k.cp(Wbv[:], Wbk[:], eng="pool")
    k.ts(Wbk[:], Wbk[:], wcol[:, 0:1], ALU.mult)
    k.ts(Wbv[:], Wbv[:], wcol[:, 1:2], ALU.mult)
    k.memset(Vs[:], 0.0)
    k.memset(Vw[:], 0.0)
    k.memset(KsT[:], 0.0)
    k.memset(KwT[:], 0.0)
    k.memset(Vs[:, :, :, 64:65], 1.0)
    k.memset(Vw[:, :, :, 64:65], 1.0)

    def kv_tile(kt, rows_cmp, rows_sel, rows_win, nrows, do_cmp_block=None, win_slot=None):
        if rows_sel is not None:
            ksr, vsr = rows_sel
            for g in range(4):
                k.tr(ps[0][0:64, g * 128:g * 128 + nrows], ksr[:, g * 64:(g + 1) * 64], C.identf[0:nrows, 0:nrows])
            k.cp(KsT[:, :, kt * 128:kt * 128 + nrows], ps[0][0:64, :].rearrange("p (g t) -> p g t", g=4)[:, :, 0:nrows], eng="act")
            k.cp(Vs[0:nrows, kt, :, 0:64], vsr.rearrange("p (g d) -> p g d", g=4), eng="dve")
        if rows_win is not None:
            kwr, vwr = rows_win
            ws = win_slot
            for g in range(4):
                k.tr(ps[1][0:64, g * 128:g * 128 + nrows], kwr[:, g * 64:(g + 1) * 64], C.identf[0:nrows, 0:nrows])
            k.cp(KwT[:, :, ws * 128:ws * 128 + nrows], ps[1][0:64, :].rearrange("p (g t) -> p g t", g=4)[:, :, 0:nrows], eng="act")
            k.cp(Vw[0:nrows, ws, :, 0:64], vwr.rearrange("p (g d) -> p g d", g=4), eng="dve")
        if rows_cmp is not None:
            kcr, vcr = rows_cmp
            t = do_cmp_block
            for g in range(4):
                k.mm(ps[2][0:64, g * 4:(g + 1) * 4], kcr[:, g * 64:(g + 1) * 64], Wbk[:, 60:64])
            k.cp(KcT[:, :, 4 * t:4 * t + 4], ps[2][0:64, 0:16].rearrange("p (g n) -> p g n", g=4), eng="act")
            k.mm(ps[2][0:64, 128:384], Wbv[:, 60 - 4 * t:124 - 4 * t], vcr)
            k.tt(Vc[:, :, 0:64], Vc[:, :, 0:64], ps[2][0:64, 128:384].rearrange("p (g d) -> p g d", g=4), ALU.add)

    bti = [0]

    OS = sb("OS", [128, 260])
    selT4 = sb("selT4", [40, 4, 8], BF16)

    def attend(nq, nblk, s_tiles, w_tiles, bc_ap, bs_fn, bw_fn, cb_ap, ft_ap, pair_ap, eexp_cols, load_consts=True, resident=False, batched=False):
        nc4 = 4 * nq
        if load_consts:
            k.dma(cbt[0:nq, 0:nblk], cb_ap)
            k.dma(cbt[0:nq, 40:40 + nblk], ft_ap)
            for g in range(4):
                k.dma(Vc[:, g, 65:65 + nblk], pair_ap)

        def bias_tile(ap, nk):
            if resident:
                return ap
            t = Bt[bti[0] % 4]
            bti[0] += 1
            k.dma(t[0:nk, 0:nc4], ap, eng="sp")
            return t[0:nk, 0:nc4]
        accs = []

        for g in range(4):
            Qg = QT[:, 4 * g:4 * g + 4, 0:nq]
            Qg2 = ec
            k.mm(ps[3][0:64, 0:nc4], KcT[:, g, :], QTf[:, g, 0:nc4])
            bt = bias_tile(bc_ap(g), 64)
            k.stt(tmp[0:64, 0:nc4], ps[3][0:64, 0:nc4], SCL, bt, ALU.mult, ALU.add)
            k.act(ec[:, 0:nc4], tmp[0:64, 0:nc4], AF.Exp)
            for j in range(4):
                k.mm(ps[4][0:nq, j * 98:j * 98 + 65 + nblk], ec[:, j * nq:(j + 1) * nq], Vc[:, g, 0:65 + nblk])
            k.cp(OB[0:nq, :, 0:65 + nblk], ps[4][0:nq, 0:392].rearrange("p (j c) -> p j c", j=4)[:, :, 0:65 + nblk], eng="act")
            k.ts(rd[0:nq, 0:4], OB[0:nq, :, 64], 1e-30, ALU.max)
            k.recip(rd[0:nq, 0:4], rd[0:nq, 0:4])
            k.ts(imp[0:nq, 0:nblk], OB[0:nq, 0, 65:65 + nblk], rd[0:nq, 0:1], ALU.mult)
            for j in range(1, 4):
                k.stt(imp[0:nq, 0:nblk], OB[0:nq, j, 65:65 + nblk], rd[0:nq, j:j + 1], imp[0:nq, 0:nblk], ALU.mult, ALU.add)
            k.tt(imp[0:nq, 0:nblk], imp[0:nq, 0:nblk], cbt[0:nq, 0:nblk], ALU.mult)
            k.tt(imp[0:nq, 0:nblk], imp[0:nq, 0:nblk], cbt[0:nq, 40:40 + nblk], ALU.add)
            P.op("dve", lambda e: e.max(out=m8[0:nq, 0:8], in_=imp[0:nq, 0:nblk]), reads=[imp], writes=[m8])
            P.op("dve", lambda e: e.match_replace(out=imp2[0:nq, 0:nblk], in_to_replace=m8[0:nq, 0:8], in_values=imp[0:nq, 0:nblk], imm_value=-2.0),
                 reads=[imp, m8], writes=[imp2])
            P.op("dve", lambda e: e.max(out=m8[0:nq, 8:16], in_=imp2[0:nq, 0:nblk]), reads=[imp2], writes=[m8])
            k.ts(m8[0:nq, 15:16], m8[0:nq, 15:16], 0.0, ALU.max)
            k.ts(imp2[0:nq, 0:nblk], imp[0:nq, 0:nblk], m8[0:nq, 15:16], ALU.is_ge)
            k.tr(ps[5][0:nblk, 0:nq], imp2[0:nq, 0:nblk], C.identf[0:nq, 0:nq])
            if batched:
                k.cp(selT4[0:nblk, g, :], ps[5][0:nblk, 0:nq], eng="act")
            else:
                k.cp(selT[0:nblk, 0:nq], ps[5][0:nblk, 0:nq], eng="act")
            def accum(first, col, Osrc, g=g):
                gsl = gates[0:nq, :].rearrange("p (h c) -> p h c", c=3)[:, 4 * g:4 * g + 4, col]
                k.tt(rd[0:nq, 4:8], rd[0:nq, 0:4], gsl, ALU.mult)
                dstv = Oacc[0:nq, g * 256:(g + 1) * 256].rearrange("p (j d) -> p j d", j=4)
                rb = rd[0:nq, 4:8].unsqueeze(2).to_broadcast([nq, 4, 64])
                if first:
                    k.tt(dstv, Osrc, rb, ALU.mult)
                else:
                    k.tt(OB[0:nq, :, 0:64], Osrc, rb, ALU.mult)
                    k.tt(dstv, dstv, OB[0:nq, :, 0:64], ALU.add)
            accum(True, 0, OB[0:nq, :, 0:64])
            if batched:
                accs.append(accum)
                continue
            items = []
            for br, tiles, Kt, Vt, bfn in ((1, s_tiles, KsT, Vs, bs_fn), (2, w_tiles, KwT, Vw, bw_fn)):
                for ti, (slot, nk, bidx) in enumerate(tiles):
                    items.append((br, Kt, Vt, bfn, ti, len(tiles), slot, nk, bidx))
            PSC = (ps[3], ps[2])

            def front(i):
                br, Kt, Vt, bfn, ti, nt, slot, nk, bidx = items[i]
                k.mm(PSC[i % 2][0:nk, 0:nc4], Kt[:, g, slot * 128:slot * 128 + nk], QTf[:, g, 0:nc4])
                if br == 1:
                    mo = 128 + (i % 2) * 128
                    k.mm(ps[5][0:nk, mo:mo + nq], Eexp[0:nblk, eexp_cols(slot)], selT[0:nblk, 0:nq])

            def mid_a(i):
                br, Kt, Vt, bfn, ti, nt, slot, nk, bidx = items[i]
                bt = bfn(bidx, g) if resident else bias_tile(bfn(bidx, g), nk)
                k.stt(TMP[i % 2][0:nk, 0:nc4], PSC[i % 2][0:nk, 0:nc4], SCL, bt, ALU.mult, ALU.add)

            def mid_b(i):
                br, Kt, Vt, bfn, ti, nt, slot, nk, bidx = items[i]
                e = EB[i % 2]
                k.act(e[0:nk, 0:nc4], TMP[i % 2][0:nk, 0:nc4], AF.Exp)
                if br == 1:
                    mo = 128 + (i % 2) * 128
                    k.tt(e[0:nk, 0:nc4].rearrange("p (j q) -> p j q", j=4), e[0:nk, 0:nc4].rearrange("p (j q) -> p j q", j=4),
                         ps[5][0:nk, mo:mo + nq].unsqueeze(1).to_broadcast([nk, 4, nq]), ALU.mult)

            def back(i):
                br, Kt, Vt, bfn, ti, nt, slot, nk, bidx = items[i]
                e = EB[i % 2]
                for j in range(4):
                    k.mm(ps[(6, 7, 0, 1)[j]][0:nq, 0:65], e[0:nk, j * nq:(j + 1) * nq], Vt[0:nk, slot, g, :],
                         start=(ti == 0), stop=(ti == nt - 1))
                if ti == nt - 1:
                    for j in range(4):
                        k.cp(OB[0:nq, j, 0:65], ps[(6, 7, 0, 1)[j]][0:nq, 0:65], eng="act")
                    k.ts(rd[0:nq, 0:4], OB[0:nq, :, 64], 1e-30, ALU.max)
                    k.recip(rd[0:nq, 0:4], rd[0:nq, 0:4])
                    accum(False, br, OB[0:nq, :, 0:64])

            front(0)
            mid_a(0)
            for i in range(len(items)):
                if i + 1 < len(items):
                    front(i + 1)
                    mid_a(i + 1)
                mid_b(i)
                back(i)

        if batched:
            PSC = (ps[3], ps[2])
            for br, tiles, Kt, Vt, SBt in ((1, s_tiles, KsT, Vs, SBS), (2, w_tiles, KwT, Vw, SBW)):
                nt = len(tiles)

                def front(i, br=br, tiles=tiles, Kt=Kt):
                    slot, nk, bidx = tiles[i]
                    for g in range(4):
                        k.mm(PSC[i % 2][0:nk, g * 32:(g + 1) * 32], Kt[:, g, slot * 128:slot * 128 + nk], QTf[:, g, 0:32])
                    if br == 1:
                        mo = 128 + (i % 2) * 128
                        for g in range(4):
                            k.mm(ps[5][0:nk, mo + g * 8:mo + (g + 1) * 8], Eexp[0:nblk, eexp_cols(slot)], selT4[0:nblk, g, :])

                def mid_a(i, tiles=tiles, SBt=SBt):
                    slot, nk, bidx = tiles[i]
                    k.stt(TMP[i % 2][0:nk, 0:128], PSC[i % 2][0:nk, 0:128], SCL, SBt[0:nk, bidx * 128:(bidx + 1) * 128], ALU.mult, ALU.add)

                def mid_b(i, br=br, tiles=tiles):
                    slot, nk, bidx = tiles[i]
                    e = EB[i % 2]
                    k.act(e[0:nk, 0:128], TMP[i % 2][0:nk, 0:128], AF.Exp)
                    if br == 1:
                        mo = 128 + (i % 2) * 128
                        ev = e[0:nk, 0:128].rearrange("p (g j t) -> p g j t", g=4, j=4)
                        k.tt(ev, ev, ps[5][0:nk, mo:mo + 32].rearrange("p (g t) -> p g t", g=4).unsqueeze(2).to_broadcast([nk, 4, 4, 8]), ALU.mult)

                def back(i, tiles=tiles, Vt=Vt, nt=nt):
                    slot, nk, bidx = tiles[i]
                    k.mm(ps[6][:, 0:260], EB[i % 2][0:nk, 0:128], Vt[0:nk, slot, :, :].rearrange("p g c -> p (g c)"),
                         start=(i == 0), stop=(i == nt - 1))

                front(0)
                mid_a(0)
                for i in range(nt):
                    if i + 1 < nt:
                        front(i + 1)
                        mid_a(i + 1)
                    mid_b(i)
                    back(i)
                k.cp(OS[:], ps[6][:, 0:260], eng="act")
                for g in range(4):
                    for j in range(4):
                        h = 4 * g + j
                        k.mm(ps[7][0:8, j * 65:(j + 1) * 65], C.identf[:, h * 8:(h + 1) * 8], OS[:, g * 65:(g + 1) * 65])
                    k.cp(OB[0:8, :, 0:65], ps[7][0:8, 0:260].rearrange("p (j c) -> p j c", j=4), eng="act")
                    k.ts(rd[0:8, 0:4], OB[0:8, :, 64], 1e-30, ALU.max)
                    k.recip(rd[0:8, 0:4], rd[0:8, 0:4])
                    accs[g](False, br, OB[0:8, :, 0:64])

    QTflat = QT[:].rearrange("p h q -> p (h q)")

    class _QTf:
        nq = 128

        def __getitem__(self, key):
            _, g, _ = key
            n4 = 4 * self.nq
            return QTflat[:, g * n4:(g + 1) * n4]
    QTf = _QTf()

    def qgz(npart, nq_cols):
        for h in range(16):
            for dc in range(8):
                k.mm(ps[7][0:64, (h % 4) * 128:(h % 4) * 128 + nq_cols], Win[:, dc, h * 64:(h + 1) * 64], xT[:, dc, 0:nq_cols],
                     start=(dc == 0), stop=(dc == 7))
            if h % 4 == 3:
                QTf.nq = nq_cols
                qdst = QTflat[:, (h - 3) * nq_cols:(h + 1) * nq_cols].rearrange("p (j q) -> p j q", j=4)
                k.cp(qdst, ps[7][0:64, :].rearrange("p (j q) -> p j q", j=4)[:, :, 0:nq_cols], eng="act")
        for i, (c0, c1) in enumerate(((2560, 3072), (3072, 3584), (3584, 3632))):
            for dc in range(8):
                k.mm(ps[i][0:npart, 0:c1 - c0], xT[:, dc, 0:npart], Win[:, dc, c0:c1], start=(dc == 0), stop=(dc == 7))
            k.cp(GZ[0:npart, c0 - 2560:c1 - 2560], ps[i][0:npart, 0:c1 - c0], eng="act")
        k.act(gates[0:npart, :], GZ[0:npart, 0:48], AF.Sigmoid)

    def finish(npart, dst_rows):
        k.act(T1[0:npart, :], GZ[0:npart, 48:1072], AF.Silu)
        k.tt(Gb[0:npart, :], Oacc[0:npart, :], T1[0:npart, :], ALU.mult)
        psb = ps[2][:].bitcast(BF16)
        for c in range(8):
            k.tr(psb[:, c * 128:c * 128 + npart], Gb[0:npart, c * 128:(c + 1) * 128], C.identb[0:npart, 0:npart])
        k.cp(gT[:, :, 0:npart], psb[:, 0:1024].rearrange("p (c t) -> p c t", c=8)[:, :, 0:npart], eng="act")
        for hf in range(2):
            cols = slice(hf * 512, (hf + 1) * 512)
            for c in range(8):
                k.mm(ps[hf][0:npart, :], gT[:, c, 0:npart], Wo[:, c, cols], start=(c == 0), stop=(c == 7))
            k.stt(T1[0:npart, cols], xin[0:npart, cols], ALPHA, ps[hf][0:npart, :], ALU.mult, ALU.add)
        ln_tail(C, T1, npart, L, dst_rows, Oacc, crow)

    def load_xT(rows_ap, npart):
        k.dma(xin[0:npart, :], rows_ap)
        for b in range(2):
            for c in range(4):
                k.tr(ps[b][:, c * 128:c * 128 + npart], xin[0:npart, (4 * b + c) * 128:(4 * b + c + 1) * 128], C.identf[0:npart, 0:npart])
            k.cp(xT[:, 4 * b:4 * b + 4, 0:npart], ps[b][:].rearrange("p (c t) -> p c t", c=4)[:, :, 0:npart], eng="act")

    def kv_proj(npart):
        for i in range(3):
            for dc in range(8):
                k.mm(ps[3 + i][0:npart, :], xT[:, dc, 0:npart], Win[:, dc, 1024 + i * 512:1024 + (i + 1) * 512], start=(dc == 0), stop=(dc == 7))
            k.cp(KV[0:npart, i * 512:(i + 1) * 512], ps[3 + i][0:npart, :], eng="act")

    k.memset(Vc[:], 0.0)
    k.memset(Vc[:, :, 64:65], 1.0)
    k.memset(KcT[:], 0.0)
    k.dma(Eexp[0:32, 0:2048], I["n_eexp_p"])
    ntile = tp // 128
    for t in range(ntile):
        r0 = t * 128
        load_xT(src[0][r0:r0 + 128, :], 128)
        kv_proj(128)
        for i, nm in enumerate(("cmpk", "cmpv", "selk", "selv")):
            k.dma(O[nm + "_p"][r0:r0 + 128, :], KV[:, i * 256:(i + 1) * 256])
        wr0 = r0 - (tp - min(512, tp))
        if wr0 >= 0:
            k.dma(O["wink_p"][wr0:wr0 + 128, :], KV[:, 1024:1280])
            k.dma(O["winv_p"][wr0:wr0 + 128, :], KV[:, 1280:1536])
        kv_tile(t, (KV[:, 0:256], KV[:, 256:512]), (KV[:, 512:768], KV[:, 768:1024]), (KV[:, 1024:1280], KV[:, 1280:1536]), 128,
                do_cmp_block=t, win_slot=t % 5)
        qgz(128, 128)
        s_tiles = [(kt, 128, t - kt) for kt in range(t + 1)]
        w_tiles = [(kt % 5, 128, t - kt) for kt in range(max(0, t - 4), t + 1)]
        attend(128, 32, s_tiles, w_tiles,
               lambda g: I["n_bc_p"][t, g], lambda d, g: I["n_bs_p"][d, g], lambda d, g: I["n_bw_p"][d, g],
               I["n_cb_p"][t], I["n_ft_p"][t], I["n_pair_p"], lambda slot: slice(slot * 128, (slot + 1) * 128))
        finish(128, dst[0][r0:r0 + 128, :])

    k.dma(idx[:], I["ptab"].rearrange("s n -> (s n)").partition_broadcast(128))
    k.cp(idf, idx[:])
    k.dma(wcol[:, 0:1], I["n_iota"])
    k.ts(idf, idf, 128.0, ALU.mult, wcol[:, 0:1], ALU.add)
    k.cp(idx[:], idf)
    k.dma(Eexp[0:33, 0:17 * 128], I["n_eexp_s"])
    load_xT(src[1][:, :], 128)
    kv_proj(128)
    KVs = C.kvs_scr
    k.dma(KVs, KV[:])
    for i, nm in enumerate(("cmpk", "cmpv", "selk", "selv")):
        k.dma(O[nm + "_s"], KV[:, i * 256:(i + 1) * 256])
    k.dma(SBS[:].rearrange("k (d g c) -> k d g c", d=17, g=4), I["n_bs_s"].rearrange("d g k c -> k d g c"))
    k.dma(SBW[:].rearrange("k (d g c) -> k d g c", d=5, g=4), I["n_bw_s"].rearrange("d g k c -> k d g c"))
    k.dma(SBC[:].rearrange("k (g c) -> k g c", g=4), I["n_bc_s"].rearrange("g k c -> k g c"))
    k.dma(cbt[0:8, 0:33], I["n_cb_s"])
    k.dma(cbt[0:8, 40:73], I["n_ft_s"])
    for g in range(4):
        k.dma(Vc[:, g, 65:98], I["n_pair_s"])
    xTs_all = sb("xTs_all", [128, 8, 128], BF16)
    k.cp(xTs_all[:], xT[:], eng="dve")
    NEW = KV[0:8, :]
    for sq in range(NS):
        k.memset(Vc[:, :, 0:64], 0.0)
        pools = (I["cmp_k"], I["cmp_v"], I["sel_k"], I["sel_v"])

        def gather(pool_ap, slot, dst_tile, sq=sq):
            P.dma("pool", dst_tile, pool_ap, reads=[pool_ap, idx], writes=[dst_tile],
                  fn=lambda e: e.indirect_dma_start(out=dst_tile, out_offset=None, in_=pool_ap,
                                                    in_offset=bass.IndirectOffsetOnAxis(ap=idx[:, sq * 16 + slot:sq * 16 + slot + 1], axis=0)))
        for pg in range(16):
            gb = GBUF[pg % 3]
            for ci in range(4):
                gather(pools[ci], pg, gb[:, ci * 256:(ci + 1) * 256])
            kv_tile(pg, (gb[:, 0:256], gb[:, 256:512]), (gb[:, 512:768], gb[:, 768:1024]), None, 128, do_cmp_block=pg)
        for wt in range(4):
            gb = GBUF[(wt + 1) % 3]
            k.dma(gb[:, 0:256], I["win_k"][sq, wt * 128:(wt + 1) * 128, :])
            k.dma(gb[:, 256:512], I["win_v"][sq, wt * 128:(wt + 1) * 128, :])
            kv_tile(0, None, None, (gb[:, 0:256], gb[:, 256:512]), 128, win_slot=wt)
        k.dma(NEW, KVs[sq * 8:(sq + 1) * 8, :])
        kv_tile(16, None, (NEW[:, 512:768], NEW[:, 768:1024]), (NEW[:, 1024:1280], NEW[:, 1280:1536]), 8, win_slot=4)
        k.dma(O["wink_s"][sq, 0:504, :], I["win_k"][sq, 8:512, :])
        k.dma(O["winv_s"][sq, 0:504, :], I["win_v"][sq, 8:512, :], eng="act")
        k.dma(O["wink_s"][sq, 504:512, :], NEW[:, 1024:1280])
        k.dma(O["winv_s"][sq, 504:512, :], NEW[:, 1280:1536])
        k.cp(xT[:, :, 0:8], xTs_all[:, :, sq * 8:(sq + 1) * 8], eng="dve")
        k.dma(xin[0:8, :], src[1][sq * 8:(sq + 1) * 8, :])
        qgz(8, 8)
        s_tiles = [(kt, 128, kt) for kt in range(16)] + [(16, 8, 16)]
        w_tiles = [(kt, 128, kt) for kt in range(4)] + [(4, 8, 4)]
        attend(8, 33, s_tiles, w_tiles,
               lambda g: SBC[:, g * 32:(g + 1) * 32],
               lambda d, g: SBS[0:(8 if d == 16 else 128), d * 128 + g * 32:d * 128 + (g + 1) * 32],
               lambda d, g: SBW[0:(8 if d == 4 else 128), d * 128 + g * 32:d * 128 + (g + 1) * 32],
               None, None, None, lambda slot: slice(slot * 128, slot * 128 + (8 if slot == 16 else 128)), load_consts=False, resident=True, batched=True)
        finish(8, dst[1][sq * 8:(sq + 1) * 8, :])


def _cmask():
    m = np.zeros((128, 2048), np.float32)
    t = np.arange(128)
    for g, w in enumerate((2, 4, 8, 16)):
        m[:, g * 128:(g + 1) * 128] = (1.0 / np.minimum(w, t + 1))[None, :]
    a = np.arange(64)
    su = (a[:, None] < a[None, :]).astype(np.float32)
    ui = (a[:, None] <= a[None, :]).astype(np.float32)
    m[0:64, 512:576] = su
    m[0:64, 576:640] = ui
    m[0:64, 640:704] = su.T
    m[0:64, 704:768] = ui
    m[0:64, 768:832] = np.eye(64, dtype=np.float32)
    return m


def consts():
    import ml_dtypes
    sel = np.zeros((128, 64, 128), np.float32)
    for kk in range(128):
        sel[kk, kk % 64, (kk // 64) * 64:(kk // 64) * 64 + 64] = 1
    return {"identf": np.eye(128, dtype=np.float32), "selb": sel.reshape(128, 64 * 128).astype(ml_dtypes.bfloat16),
            "cmask": _cmask()}


def shard_inputs(inp, c, tp=TP):
    f = lambda a: np.ascontiguousarray(a)
    m = {
        "xp": f(inp["x_prompt"][c, :tp]), "xs": f(inp["x_sample"][16 * c:16 * c + 16].reshape(128, D)),
        "st_S": f(inp["state_rwkv_S"][:, 16 * c:16 * c + 16]), "st_shift": f(inp["state_rwkv_shift"][:, 16 * c:16 * c + 16]),
        "st_pool": f(inp["state_pool"][0, 16 * c:16 * c + 16]),
        "cmp_k": f(inp["cache_cmp_k"][0].reshape(-1, 256)), "cmp_v": f(inp["cache_cmp_v"][0].reshape(-1, 256)),
        "sel_k": f(inp["cache_sel_k"][0].reshape(-1, 256)), "sel_v": f(inp["cache_sel_v"][0].reshape(-1, 256)),
        "win_k": f(inp["state_win_k"][0, 16 * c:16 * c + 16].reshape(16, 512, 256)),
        "win_v": f(inp["state_win_v"][0, 16 * c:16 * c + 16].reshape(16, 512, 256)),
        "ptab": f(inp["page_table"][16 * c:16 * c + 16]).astype(np.int32),
        "a_r_k": f(inp["a_r_k"].reshape(2, D)), "b_w_in": f(inp["b_w_in"][0]), "b_w_grp": f(inp["b_w_grp"][0]),
        "b_scale": f(inp["b_scale"]), "b_w_out": f(inp["b_w_out"][0]), "c_w_in": f(inp["c_w_in"][0]),
        "c_cmp_wk": f(inp["c_cmp_wk"]), "c_cmp_wv": f(inp["c_cmp_wv"]), "c_w_out": f(inp["c_w_out"][0]),
    }
    for nm in ("ln_g", "ln_b", "a_w_in", "a_mu", "a_w0", "a_w2", "a_a0", "a_a2", "a_k_k", "a_k_a", "a_lnx_g", "a_lnx_b", "a_w_out"):
        m[nm] = f(inp[nm])
    m.update(consts())
    m.update(nsa_consts())
    return m


def nsa_consts():
    sl = 2.0 ** (-8.0 * (np.arange(16) + 1) / 16)
    c = {}

    def bias(dist, valid, g):
        K_, nq = dist.shape
        out = np.empty((K_, 4, nq), np.float32)
        for j in range(4):
            out[:, j] = np.where(valid, -sl[4 * g + j] * dist, NEG)
        return out.reshape(K_, 4 * nq)
    q = np.arange(128)[None, :]
    kk = np.arange(128)[:, None]
    n = np.arange(64)[:, None]
    bc = np.zeros((16, 4, 64, 512), np.float32)
    bs = np.zeros((16, 4, 128, 512), np.float32)
    bw = np.zeros((5, 4, 128, 512), np.float32)
    for g in range(4):
        for t in range(16):
            d = 128 * t + q - 32 * n - 31
            bc[t, g] = bias(d, d >= 0, g)
            d = 128 * t + q - kk
            bs[t, g] = bias(d, d >= 0, g)
            if t < 5:
                bw[t, g] = bias(d, (d >= 0) & (d < 512), g)
    c["n_bc_p"], c["n_bs_p"], c["n_bw_p"] = bc, bs, bw
    cb = np.zeros((16, 128, 32), np.float32)
    ft = np.zeros((16, 128, 32), np.float32)
    blk = np.arange(32)[None, :]
    for t in range(16):
        cur = ((128 * t + np.arange(128)) // 64)[:, None]
        cb[t] = (blk < cur)
        ft[t] = np.where(blk == cur, 1e9, np.where(blk > cur, -1.0, 0.0))
    c["n_cb_p"], c["n_ft_p"] = cb, ft
    c["n_pair_p"] = (np.arange(64)[:, None] // 2 == np.arange(32)[None, :]).astype(np.float32)
    c["n_eexp_p"] = (np.arange(2048)[None, :] // 64 == np.arange(32)[:, None]).astype(np.float32)
    wbm = np.zeros((128, 124), np.float32)
    for r in range(128):
        wbm[r, 60 + r // 32] = 1.0
    c["n_wbm"] = wbm
    c["n_iota"] = np.arange(128, dtype=np.float32).reshape(128, 1)
    tq = np.arange(8)[None, :]
    bcs = np.zeros((4, 64, 32), np.float32)
    bss = np.zeros((17, 4, 128, 32), np.float32)
    bws = np.zeros((5, 4, 128, 32), np.float32)
    for g in range(4):
        d = 2048 + tq - 32 * n - 31
        bcs[g] = bias(d, d >= 0, g)
        for kt in range(16):
            d = 2048 + tq - 128 * kt - kk
            bss[kt, g] = bias(d, d >= 0, g)
        d = tq - kk
        newb = bias(d, (d >= 0) & (kk < 8), g)
        bss[16, g] = newb
        for kt in range(4):
            d = 2048 + tq - (1536 + 128 * kt + kk)
            bws[kt, g] = bias(d, (d >= 0) & (d < 512), g)
        bws[4, g] = newb
    c["n_bc_s"], c["n_bs_s"], c["n_bw_s"] = bcs, bss, bws
    cbs = np.ones((8, 33), np.float32)
    cbs[:, 32] = 0
    fts = np.zeros((8, 33), np.float32)
    fts[:, 32] = 1e9
    c["n_cb_s"], c["n_ft_s"] = cbs, fts
    ps_ = np.zeros((64, 33), np.float32)
    ps_[:, :32] = c["n_pair_p"]
    c["n_pair_s"] = ps_
    ee = np.zeros((33, 17 * 128), np.float32)
    ee[:32, :2048] = c["n_eexp_p"]
    ee[32, 2048:] = 1.0
    import ml_dtypes
    c["n_eexp_s"] = ee.astype(ml_dtypes.bfloat16)
    c["n_eexp_p"] = c["n_eexp_p"].astype(ml_dtypes.bfloat16)
    hb = np.zeros((128, 240), np.float32)
    for g in range(4):
        for d in range(1, 16):
            for j in range(4):
                hb[:, g * 60 + (d - 1) * 4 + j] = -sl[4 * g + j] * 128.0 * (d - 1)
    c["n_hb"] = hb
    return c


_NC_CACHE = {}


def kernel(**inputs):
    n = 8
    npool = inputs["cache_cmp_k"].shape[1]
    key = (npool,)
    if key not in _NC_CACHE:
        _NC_CACHE[key] = build(npool=npool, tp=TP)
    nc = _NC_CACHE[key]
    in_maps = [shard_inputs(inputs, c) for c in range(n)]
    res = run_bass_kernel_spmd(nc, in_maps, core_ids=list(range(n)))
    R = res.results
    cat = lambda nm: np.stack([R[c][nm] for c in range(n)], 0)
    y_p = cat("y_p")
    y_s = np.concatenate([R[c]["y_s"].reshape(16, 8, D) for c in range(n)], 0)
    S_p = np.stack([R[c]["S_p"] for c in range(n)], 1)
    S_s = np.concatenate([R[c]["S_s"] for c in range(n)], 1)
    sh_p = np.stack([R[c]["sh_p"] for c in range(n)], 1)
    sh_s = np.concatenate([R[c]["sh_s"] for c in range(n)], 1)
    pl_p = cat("pl_p")[None]
    pl_s = np.concatenate([R[c]["pl_s"] for c in range(n)], 0)[None]
    outs = [y_p, y_s, S_p, S_s, sh_p, sh_s, pl_p, pl_s]
    for nm in ("cmpk", "cmpv", "selk", "selv"):
        outs.append(cat(nm + "_p").reshape(1, n, TP, 4, 64))
        outs.append(np.concatenate([R[c][nm + "_s"].reshape(16, 8, 4, 64) for c in range(n)], 0)[None])
    for nm in ("wink", "winv"):
        outs.append(cat(nm + "_p").reshape(1, n, 512, 4, 64))
        outs.append(np.concatenate([R[c][nm + "_s"].reshape(16, 512, 4, 64) for c in range(n)], 0)[None])
    return tuple(np.ascontiguousarray(o, dtype=np.float32) for o in outs)
```

```python
import contextlib
import numpy as np
import concourse.bass as bass
import concourse.mybir as mybir
from concourse.bass_utils import run_bass_kernel_spmd

F32 = mybir.dt.float32
BF16 = mybir.dt.bfloat16
I32 = mybir.dt.int32
ALU = mybir.AluOpType
AF = mybir.ActivationFunctionType
AX = mybir.AxisListType

import os as _os
STRICT = bool(_os.environ.get("KSTRICT"))
ENGS = ("pe", "dve", "act", "pool", "sp")
NDMA = {"sp": 12, "act": 6, "pool": 6}

D = 1024
TP = 2048
NS = 16
TS = 8
DEPTH = 4
ALPHA = (2.0 * DEPTH) ** 0.25
LN_EPS = 1e-5
A_NC = 4224
GN_EPS = 64e-5
C_NC = 3632


def _key(k):
    if isinstance(k, (str, tuple)):
        return k
    t = getattr(k, "tensor", k)
    return getattr(t, "name", str(t))


class Prog:
    def __init__(self, nc):
        self.nc = nc
        self.q = {e: [] for e in ENGS}
        self.cnt = {e: 0 for e in ENGS}
        self.known = {e: {} for e in ENGS}
        self.lastw = {}
        self.readers = {}
        self.dma_rr = {e: 0 for e in NDMA}
        self.dma_cnt = {}
        self.n_inst = 0

    def _deps(self, reads, writes):
        deps = {}

        def add(ev):
            if ev is None:
                return
            s, v = ev
            if deps.get(s, 0) < v:
                deps[s] = v
        for k in reads:
            add(self.lastw.get(k))
        for k in writes:
            add(self.lastw.get(k))
            for ev in self.readers.get(k, ()):
                add(ev)
        return deps

    def _commit(self, ev, reads, writes):
        for k in reads:
            self.readers.setdefault(k, []).append(ev)
        for k in writes:
            self.lastw[k] = ev
            self.readers[k] = []

    def _waits(self, eng, deps, compute=False):
        waits = []
        kn = self.known[eng]
        for s, v in deps.items():
            if s == "c_pe" and eng == "pe":
                continue
            if compute and not STRICT and s == "c_" + eng and eng in ("dve", "act") and v < self.cnt[eng]:
                continue
            if kn.get(s, 0) >= v:
                continue
            kn[s] = v
            waits.append((s, v))
        return waits

    def op(self, eng, fn, reads=(), writes=()):
        reads = [_key(k) for k in reads]
        writes = [_key(k) for k in writes]
        writes = writes + [r for r in reads if isinstance(r, str) and r.startswith("psb")]
        waits = self._waits(eng, self._deps(reads, writes), compute=True)
        self.cnt[eng] += 1
        ev = ("c_" + eng, self.cnt[eng])
        self.q[eng].append(("op", waits, fn, ev))
        self._commit(ev, reads, writes)
        self.n_inst += 1
        return ev

    def dma(self, eng, out, in_, reads=None, writes=None, fn=None, **kw):
        reads = [_key(k) for k in (reads if reads is not None else [in_])]
        writes = [_key(k) for k in (writes if writes is not None else [out])]
        deps = self._deps(reads, writes)
        i = self.dma_rr[eng]
        self.dma_rr[eng] = (i + 1) % NDMA[eng]
        sname = "d_%s%d" % (eng, i)
        n = self.dma_cnt.get(sname, 0)
        if n > 0 and deps.get(sname, 0) < 16 * n:
            deps[sname] = 16 * n
        waits = self._waits(eng, deps)
        self.dma_cnt[sname] = n + 1
        ev = (sname, 16 * (n + 1))
        self.q[eng].append(("dma", waits, (out, in_, kw, fn), ev))
        self._commit(ev, reads, writes)
        self.n_inst += 1
        return ev

    def barrier(self):
        for eng in ENGS:
            deps = {}
            for f in ENGS:
                if f != "sp" and f != eng and self.cnt[f] > 0:
                    deps["c_" + f] = self.cnt[f]
            for s, n in self.dma_cnt.items():
                deps[s] = 16 * n
            waits = self._waits(eng, deps)
            self.q[eng].append(("wait", waits, None, None))

    def emit(self):
        nc = self.nc
        names = ["c_" + e for e in ENGS if e != "sp"]
        for e, n in NDMA.items():
            names += ["d_%s%d" % (e, i) for i in range(n)]
        with contextlib.ExitStack() as st:
            sems = {nm: st.enter_context(nc.semaphore(nm)) for nm in names}
            block = st.enter_context(nc.Block())

            def run(eng):
                def body(e):
                    for kind, waits, payload, ev in self.q[eng]:
                        for s, v in waits:
                            e.wait_ge(sems[s], v)
                        if kind == "op":
                            payload(e).then_inc(sems[ev[0]], 1)
                        elif kind == "dma":
                            out, in_, kw, fn = payload
                            if fn is not None:
                                fn(e).then_inc(sems[ev[0]], 16)
                            else:
                                e.dma_start(out=out, in_=in_, **kw).then_inc(sems[ev[0]], 16)
                    if eng == "sp":
                        for sname, n in self.dma_cnt.items():
                            e.wait_ge(sems[sname], 16 * n)
                        for en in ENGS:
                            if en != "sp" and self.cnt[en] > 0:
                                e.wait_ge(sems["c_" + en], self.cnt[en])
                return body

            block.sync(run("sp"))
            block.tensor(run("pe"))
            block.vector(run("dve"))
            block.scalar(run("act"))
            block.gpsimd(run("pool"))


def _aps(*xs):
    return [x for x in xs if x is not None and not isinstance(x, (int, float))]


class K:
    def __init__(self, P):
        self.P = P

    def mm(self, out, lhsT, rhs, start=True, stop=True):
        self.P.op("pe", lambda e: e.matmul(out, lhsT=lhsT, rhs=rhs, start=start, stop=stop),
                  reads=[lhsT, rhs], writes=[out])

    def tr(self, out, in_, ident):
        self.P.op("pe", lambda e: e.transpose(out, in_, ident), reads=[in_, ident], writes=[out])

    def tt(self, out, a, b, op, eng="dve"):
        self.P.op(eng, lambda e: e.tensor_tensor(out=out, in0=a, in1=b, op=op), reads=[a, b], writes=[out])

    def ts(self, out, a, s1, op0, s2=None, op1=None, eng="dve"):
        if op1 is None:
            fn = lambda e: e.tensor_scalar(out=out, in0=a, scalar1=s1, scalar2=None, op0=op0)
        else:
            fn = lambda e: e.tensor_scalar(out=out, in0=a, scalar1=s1, scalar2=s2, op0=op0, op1=op1)
        self.P.op(eng, fn, reads=_aps(a, s1, s2), writes=[out])

    def stt(self, out, a, s, b, op0, op1, eng="dve"):
        self.P.op(eng, lambda e: e.scalar_tensor_tensor(out=out, in0=a, scalar=s, in1=b, op0=op0, op1=op1),
                  reads=_aps(a, s, b), writes=[out])

    def red(self, out, in_, op=ALU.add, negate=False, axis=AX.X):
        self.P.op("dve", lambda e: e.tensor_reduce(out=out, in_=in_, axis=axis, op=op, negate=negate),
                  reads=[in_], writes=[out])

    def cp(self, out, in_, eng="dve"):
        if eng == "act":
            self.P.op("act", lambda e: e.copy(out, in_), reads=[in_], writes=[out])
        else:
            self.P.op(eng, lambda e: e.tensor_copy(out, in_), reads=[in_], writes=[out])

    def act(self, out, in_, func, bias=None, scale=None, accum=None):
        kw = {}
        if bias is not None:
            kw["bias"] = bias
        if scale is not None:
            kw["scale"] = scale
        if accum is not None:
            kw["accum_out"] = accum
        self.P.op("act", lambda e: e.activation(out=out, in_=in_, func=func, **kw),
                  reads=_aps(in_, bias, scale), writes=_aps(out, accum))

    def recip(self, out, in_):
        self.P.op("dve", lambda e: e.reciprocal(out, in_), reads=[in_], writes=[out])

    def memset(self, ap, v, eng="pool"):
        self.P.op(eng, lambda e: e.memset(ap, v), writes=[ap])

    def dma(self, out, in_, eng="sp", **kw):
        self.P.dma(eng, out, in_, **kw)


def bc(ap, shape):
    return ap.to_broadcast(shape)


class Ctx:
    pass


def build(npool=2560, tp=TP, layers=(0, 1, 2, 3), dbg=False):
    nc = bass.Bass("TRN2", target_bir_lowering=False)
    C = Ctx()
    C.nc = nc
    C.tp = tp
    P = Prog(nc)
    k = K(P)
    C.P, C.k = P, k

    def din(name, shape, dt=F32):
        return nc.dram_tensor(name, list(shape), dt, kind="ExternalInput").ap()

    def dout(name, shape):
        return nc.dram_tensor(name, list(shape), F32, kind="ExternalOutput").ap()

    def dscr(name, shape, dt=F32):
        return nc.dram_tensor(name, list(shape), dt, kind="Internal").ap()

    I = {}
    for nm, shp in [("xp", (tp, D)), ("xs", (128, D)), ("st_S", (2, NS, 16, 64, 64)), ("st_shift", (2, NS, A_NC)),
                    ("st_pool", (NS, 15, D)), ("cmp_k", (npool * 128, 256)), ("cmp_v", (npool * 128, 256)),
                    ("sel_k", (npool * 128, 256)), ("sel_v", (npool * 128, 256)), ("win_k", (NS, 512, 256)),
                    ("win_v", (NS, 512, 256)), ("ln_g", (4, D)), ("ln_b", (4, D)), ("a_w_in", (2, D, A_NC)),
                    ("a_mu", (2, A_NC)), ("a_w0", (2, D)), ("a_w2", (2, 64, D)), ("a_a0", (2, D)), ("a_a2", (2, 64, D)),
                    ("a_k_k", (2, D)), ("a_k_a", (2, D)), ("a_r_k", (2, D)), ("a_lnx_g", (2, D)), ("a_lnx_b", (2, D)),
                    ("a_w_out", (2, D, D)), ("b_w_in", (D, 2 * D)), ("b_w_grp", (4, 256, 256)), ("b_scale", (1, D)),
                    ("b_w_out", (D, D)), ("c_w_in", (D, C_NC)), ("c_cmp_wk", (1, 32)), ("c_cmp_wv", (1, 32)),
                    ("c_w_out", (D, D)), ("identf", (128, 128)), ("cmask", (128, 2048))]:
        I[nm] = din(nm, shp)
    I["ptab"] = din("ptab", (NS, 16), I32)
    for nm, shp in [("n_bc_p", (16, 4, 64, 512)), ("n_bs_p", (16, 4, 128, 512)), ("n_bw_p", (5, 4, 128, 512)),
                    ("n_cb_p", (16, 128, 32)), ("n_ft_p", (16, 128, 32)), ("n_pair_p", (64, 32)),
                    ("n_wbm", (128, 124)), ("n_iota", (128, 1)), ("n_bc_s", (4, 64, 32)), ("n_bs_s", (17, 4, 128, 32)),
                    ("n_bw_s", (5, 4, 128, 32)), ("n_cb_s", (8, 33)), ("n_ft_s", (8, 33)), ("n_pair_s", (64, 33)),
                    ("n_hb", (128, 240))]:
        I[nm] = din(nm, shp)
    I["n_eexp_p"] = din("n_eexp_p", (32, 2048), BF16)
    I["n_eexp_s"] = din("n_eexp_s", (33, 17 * 128), BF16)
    I["selb"] = din("selb", (128, 64 * 128), BF16)
    O = {}
    for nm, shp in [("y_p", (tp, D)), ("y_s", (128, D)), ("S_p", (2, 16, 64, 64)), ("S_s", (2, NS, 16, 64, 64)),
                    ("sh_p", (2, A_NC)), ("sh_s", (2, NS, A_NC)), ("pl_p", (15, D)), ("pl_s", (NS, 15, D)),
                    ("cmpk_p", (tp, 256)), ("cmpk_s", (128, 256)), ("cmpv_p", (tp, 256)), ("cmpv_s", (128, 256)),
                    ("selk_p", (tp, 256)), ("selk_s", (128, 256)), ("selv_p", (tp, 256)), ("selv_s", (128, 256)),
                    ("wink_p", (512, 256)), ("wink_s", (NS, 512, 256)), ("winv_p", (512, 256)), ("winv_s", (NS, 512, 256))]:
        O[nm] = dout(nm, shp)
    if dbg:
        O["dbg_p"] = dout("dbg_p", (tp, D))
        O["dbg_s"] = dout("dbg_s", (128, D))
    C.I, C.O = I, O
    xa_p, xa_s = dscr("xa_p", (tp, D)), dscr("xa_s", (128, D))
    xb_p, xb_s = dscr("xb_p", (tp, D)), dscr("xb_s", (128, D))
    C.wbf = dscr("wbf", (128, 8, A_NC), BF16)
    C.kvs_scr = dscr("kvs_scr", (128, 1536))

    with contextlib.ExitStack() as gst:
        C.identf = gst.enter_context(nc.sbuf_tensor("identf_sb", [128, 128], F32))
        C.identb = gst.enter_context(nc.sbuf_tensor("identb_sb", [128, 128], BF16))
        C.ps = [gst.enter_context(nc.psum_tensor("psb%d" % i, [128, 512], F32)) for i in range(8)]
        k.dma(C.identf[:], I["identf"])
        k.cp(C.identb[:], C.identf[:])
        chain = [(I["xp"], I["xs"]), (xa_p, xa_s), (xb_p, xb_s), (xa_p, xa_s), (O["y_p"], O["y_s"])]
        for L in range(DEPTH):
            if L not in layers:
                continue
            src, dst = chain[L], chain[L + 1]
            if L == max(layers) and dbg:
                dst = (O["dbg_p"], O["dbg_s"])
            P.barrier()
            with contextlib.ExitStack() as lst:
                if L % 3 == 0:
                    rwkv_layer(C, lst, L // 3, L, src, dst)
                elif L % 3 == 1:
                    pool_layer(C, lst, L, src, dst)
                else:
                    nsa_layer(C, lst, L, src, dst)
                P.barrier()
        P.emit()
    return nc


def ln_tail(C, R, npart, L, dst_rows, T1, crow):
    k, I = C.k, C.I
    st = C.lnst
    k.red(st[0:npart, 0:1], R[0:npart, :])
    k.ts(st[0:npart, 1:2], st[0:npart, 0:1], 1.0 / D, ALU.mult)
    k.ts(R[0:npart, :], R[0:npart, :], st[0:npart, 1:2], ALU.subtract)
    k.tt(T1[0:npart, :], R[0:npart, :], R[0:npart, :], ALU.mult)
    k.red(st[0:npart, 2:3], T1[0:npart, :])
    k.act(st[0:npart, 3:4], st[0:npart, 2:3], AF.Sqrt, bias=C.epsln[0:npart, :], scale=1.0 / D)
    k.recip(st[0:npart, 4:5], st[0:npart, 3:4])
    k.ts(R[0:npart, :], R[0:npart, :], st[0:npart, 4:5], ALU.mult)
    k.dma(crow[0][0:npart, :], I["ln_g"][L:L + 1, :].partition_broadcast(npart), eng="sp")
    k.tt(R[0:npart, :], R[0:npart, :], crow[0][0:npart, :], ALU.mult)
    k.dma(crow[1][0:npart, :], I["ln_b"][L:L + 1, :].partition_broadcast(npart), eng="sp")
    k.tt(R[0:npart, :], R[0:npart, :], crow[1][0:npart, :], ALU.add)
    k.dma(dst_rows, R[0:npart, :])


def rwkv_layer(C, lst, li, L, src, dst):
    nc, P, k, I, O = C.nc, C.P, C.k, C.I, C.O
    tp = C.tp
    sb = lambda n, s, d=F32: lst.enter_context(nc.sbuf_tensor("a%d_" % L + n, list(s), d))
    ps = C.ps
    Wo = sb("Wo", [128, 8, 1024], BF16)
    Wll = sb("Wll", [128, 8, 128], BF16)
    WG = [sb("WG0", [128, 8, 1024], BF16)]
    W2A2 = sb("W2A2", [128, 1024])
    mucol = sb("mucol", [128, 9])
    SEL = sb("SEL", [128, 64, 128], BF16)
    xin2 = sb("xin2", [128, 1024])
    xin = sb("xin", [64, 1024])
    xTd = sb("xTd", [128, 8, 128], BF16)
    xTsd = sb("xTsd", [128, 8, 128], BF16)
    Pt = sb("Pt", [128, 1024])
    PSt = sb("PSt", [128, 1024])
    PM = {g: sb("PM" + g, [128, 1024]) for g in "rkvz"}
    crow = [sb("crow%d" % i, [128, 1024]) for i in range(2)]
    At = sb("At", [128, 1024])
    KP = sb("KP", [128, 1024])
    T1 = sb("T1", [128, 1024])
    T2 = sb("T2", [128, 1024])
    XRf = sb("XRf", [128, 512])
    XRr = sb("XRr", [128, 512])
    XR = {x: [sb("XR%s%d" % (x, j), [128, 512], BF16) for j in range(2)] for x in ("kk", "w", "ka", "k", "r")}
    va = sb("va", [128, 8, 64])
    vs = sb("vs", [128, 8, 64])
    vT = sb("vT", [128, 8, 64])
    lla = sb("lla", [128, 128])
    llb = sb("llb", [128, 128])
    LLt = sb("LLt", [128, 128])
    YT = sb("YT", [128, 8, 64])
    S = sb("S", [128, 8, 64])
    t1 = sb("t1", [128, 8, 64])
    t2 = sb("t2", [128, 8, 64])
    t3 = sb("t3", [128, 8, 64])
    sa = sb("sa", [128, 8])
    st16 = sb("st16", [128, 5, 16])
    bon = sb("bon", [128, 16])
    G = sb("G", [64, 1024], BF16)
    gT = sb("gT", [128, 8, 64], BF16)
    C.lnst = sb("lnst", [128, 8])
    C.epsln = sb("epsln", [128, 1])
    epsgn = sb("epsgn", [128, 1])
    eps24 = sb("eps24", [128, 1])
    k.memset(C.epsln[:], LN_EPS)
    k.memset(epsgn[:], GN_EPS)
    k.memset(eps24[:], 0.0)

    w_in = I["a_w_in"][li].rearrange("(c p) n -> p c n", p=128)
    for j in range(A_NC // 128):
        stg = Pt[:].rearrange("p (c n) -> p c n", c=8) if j % 2 == 0 else PSt[:].rearrange("p (c n) -> p c n", c=8)
        stgb = (T1 if j % 2 == 0 else T2)[:].bitcast(BF16)[:, 0:1024].rearrange("p (c n) -> p c n", c=8)
        k.dma(stg, w_in[:, :, j * 128:(j + 1) * 128], eng="sp" if j % 2 == 0 else "act")
        k.cp(stgb, stg, eng="pool" if j % 2 == 0 else "act")
        k.dma(C.wbf[:, :, j * 128:(j + 1) * 128], stgb, eng="sp")
    w_out = I["a_w_out"][li].rearrange("(c p) n -> p c n", p=128)
    for j in range(8):
        stg = Pt[:].rearrange("p (c n) -> p c n", c=8) if j % 2 == 0 else PSt[:].rearrange("p (c n) -> p c n", c=8)
        k.dma(stg, w_out[:, :, j * 128:(j + 1) * 128], eng="sp" if j % 2 == 0 else "act")
        k.cp(Wo[:, :, j * 128:(j + 1) * 128], stg, eng="pool" if j % 2 == 0 else "act")
    k.dma(Wll[:], C.wbf[:, :, 4096:4224])
    k.dma(W2A2[0:64, :], I["a_w2"][li])
    k.dma(W2A2[64:128, :], I["a_a2"][li])
    k.dma(mucol[:, 0:8], I["a_mu"][li, 2048:3072].rearrange("(c p) -> p c", p=128), allow_slow_non_contiguous=True)
    k.dma(mucol[:, 8:9], I["a_mu"][li, 4096:4224].rearrange("(c p) -> p c", p=128), allow_slow_non_contiguous=True)
    k.dma(SEL[:], I["selb"].rearrange("p (t m) -> p t m", m=128))
    k.memset(S[:], 0.0)
    EPI = sb("EPI", [64, 1024])
    EPN = sb("EPN", [64, 1024])
    EPX = sb("EPX", [64, 1024])
    FMAR = sb("FMAR", [64, 8, 128])
    FMB = sb("FMB", [64, 8, 64])
    FMK = sb("FMK", [64, 8, 64])
    GB = sb("GB", [64, 8, 128])
    GK = sb("GK", [64, 8, 128])
    PQ = [sb("PQ%d" % i, [64, 8, 64]) for i in range(4)]
    Tm = sb("Tm", [64, 8, 64])
    XT = sb("XT", [64, 8, 64])
    UT = sb("UT", [64, 8, 64])
    ST = sb("ST", [64, 16, 64])
    PCc = sb("PCc", [64, 16])
    MK = sb("MK", [64, 320])
    k.dma(MK[:], I["cmask"][0:64, 512:832])
    MASKAR = MK[:, 0:128]
    MASKNT = MK[:, 128:192]
    TRI = MK[:, 192:256]
    IDN = MK[:, 256:320]
    k.memset(ST[:], 0.0)
    pbi = [0]

    def bank():
        pbi[0] = (pbi[0] + 1) % 8
        return ps[pbi[0]]
    import os
    STOP = int(os.environ.get('STOPAT', '99'))
    if STOP <= 1:
        return

    cri = [0]

    def jrow(src_row, npart=128):
        t = crow[cri[0] % 2]
        cri[0] += 1
        k.dma(t[0:npart, :], src_row.partition_broadcast(npart), eng="sp")
        return t

    def h4(t):
        return t[:].rearrange("p (a b j) -> p a b j", a=8, b=2)

    def toxr(X, name):
        X4 = h4(X)
        o3 = XRf[:].rearrange("p (a j) -> p a j", a=8)
        k.cp(o3[0:64], X4[0:64, :, 0, :], eng="act")
        k.cp(o3[64:128], X4[64:128, :, 1, :], eng="act")
        k.cp(XR[name][0][:], XRf[:], eng="pool")
        k.tt(XRr[:], XRf[:], XR[name][0][:], ALU.subtract, eng="pool")
        k.cp(XR[name][1][:], XRr[:], eng="pool")

    ntile_p = tp // 64
    import os
    tiles = [("p", n) for n in range(ntile_p)] + ([("s", 0), ("s", 1)] if not os.environ.get("NOSAMPLE") else [])
    wg_i = [0]
    for kind, n in tiles:
        srcx = src[0] if kind == "p" else src[1]
        dstx = dst[0] if kind == "p" else dst[1]
        r0 = n * 64
        if r0 == 0:
            k.memset(xin2[0:1, :], 0.0)
            k.dma(xin2[1:64, :], srcx[0:63, :])
        else:
            k.dma(xin2[0:64, :], srcx[r0 - 1:r0 + 63, :])
        k.dma(xin[:], srcx[r0:r0 + 64, :], eng="sp")
        for b in range(2):
            for c in range(4):
                k.tr(ps[b][:, c * 64:(c + 1) * 64], xin[0:64, (4 * b + c) * 128:(4 * b + c + 1) * 128], C.identf[0:64, 0:64])
            for c in range(4):
                k.tr(ps[b][:, 256 + c * 64:256 + (c + 1) * 64], xin2[0:64, (4 * b + c) * 128:(4 * b + c + 1) * 128], C.identf[0:64, 0:64])
            pv = ps[b][:, 0:256].rearrange("p (c t) -> p c t", c=4)
            pw = ps[b][:, 256:512].rearrange("p (c t) -> p c t", c=4)
            k.cp(xTd[:, 4 * b:4 * b + 4, 0:64], pv, eng="act")
            k.cp(xTd[:, 4 * b:4 * b + 4, 64:128], pv, eng="dve")
            k.cp(xTsd[:, 4 * b:4 * b + 4, 0:64], pw, eng="act")
            k.cp(xTsd[:, 4 * b:4 * b + 4, 64:128], pw, eng="dve")
        if STOP <= 2:
            return
        last_rows = []
        if kind == "p" and n == ntile_p - 1:
            last_rows = [(63, O["sh_p"][li])]
        if kind == "s":
            last_rows = [(sl * 8 + 7, O["sh_s"][li, n * 8 + sl]) for sl in range(8)]
        for gi, g in enumerate("rkvz"):
            wg = WG[0]
            wg_i[0] += 1
            k.dma(wg[:], C.wbf[:, :, gi * 1024:(gi + 1) * 1024], eng="sp")
            for hf in range(2):
                cols = slice(hf * 512, (hf + 1) * 512)
                for c in range(8):
                    k.mm(ps[2][:], xTd[:, c, :], wg[:, c, cols], start=(c == 0), stop=(c == 7))
                for c in range(8):
                    k.mm(ps[3][:], xTsd[:, c, :], wg[:, c, cols], start=(c == 0), stop=(c == 7))
                k.cp(Pt[:, cols], ps[2][:], eng="act")
                k.cp(PSt[:, cols], ps[3][:], eng="act")
            if g == "v" and kind == "s":
                for c in range(8):
                    for dc in range(8):
                        k.mm(ps[2][:, c * 64:(c + 1) * 64], wg[:, dc, c * 128:(c + 1) * 128], xTd[:, dc, 0:64],
                             start=(dc == 0), stop=(dc == 7))
                for c in range(8):
                    for dc in range(8):
                        k.mm(ps[3][:, c * 64:(c + 1) * 64], wg[:, dc, c * 128:(c + 1) * 128], xTsd[:, dc, 0:64],
                             start=(dc == 0), stop=(dc == 7))
                k.cp(va[:].rearrange("p c t -> p (c t)"), ps[2][:], eng="act")
                k.cp(vs[:].rearrange("p c t -> p (c t)"), ps[3][:], eng="act")
                if kind == "s":
                    for sl in range(8):
                        k.dma(vs[:, :, sl * 8], I["st_shift"][li, n * 8 + sl, 2048:3072].rearrange("(c p) -> p c", p=128), eng="sp", allow_slow_non_contiguous=True)
                k.tt(vs[:], vs[:], va[:], ALU.subtract)
                k.tt(vs[:], vs[:], mucol[:, 0:8].unsqueeze(2).to_broadcast([128, 8, 64]), ALU.mult)
                k.tt(vT[:], vs[:], va[:], ALU.add)
            if kind == "s":
                for sl in range(8):
                    for hh in range(2):
                        k.dma(PSt[hh * 64 + sl * 8:hh * 64 + sl * 8 + 1, :],
                              I["st_shift"][li, n * 8 + sl:n * 8 + sl + 1, gi * 1024:(gi + 1) * 1024], eng="act")
            for (row, dap) in last_rows:
                k.dma(dap[gi * 1024:(gi + 1) * 1024].unsqueeze(0), Pt[row:row + 1, :], eng="sp")
            mur = jrow(I["a_mu"][li:li + 1, gi * 1024:(gi + 1) * 1024])
            k.tt(PSt[:], PSt[:], Pt[:], ALU.subtract)
            k.tt(PSt[:], PSt[:], mur[:], ALU.mult)
            k.tt(PM[g][:], PSt[:], Pt[:], ALU.add)
        if STOP <= 3:
            return
        for c in range(8):
            k.mm(ps[2][:, 0:128], Wll[:, c, :], xTd[:, c, :], start=(c == 0), stop=(c == 7))
        for c in range(8):
            k.mm(ps[3][:, 0:128], Wll[:, c, :], xTsd[:, c, :], start=(c == 0), stop=(c == 7))
        k.cp(lla[:], ps[2][:, 0:128], eng="act")
        k.cp(llb[:], ps[3][:, 0:128], eng="act")
        if kind == "s":
            for sl in range(8):
                for hh in range(2):
                    k.dma(llb[:, hh * 64 + sl * 8:hh * 64 + sl * 8 + 1],
                          I["st_shift"][li, n * 8 + sl, 4096:4224].rearrange("(c p) -> p c", p=128), eng="act", allow_slow_non_contiguous=True)
        for (row, dap) in last_rows:
            k.dma(dap[4096:4224].rearrange("(c p) -> p c", p=128), lla[:, row:row + 1], eng="sp", allow_slow_non_contiguous=True)
        k.tt(llb[:], llb[:], lla[:], ALU.subtract)
        k.stt(LLt[:], llb[:], mucol[:, 8:9], lla[:], ALU.mult, ALU.add)
        k.act(LLt[0:64, :], LLt[0:64, :], AF.Tanh)
        if STOP <= 4:
            return
        for hf in range(2):
            cols = slice(hf * 512, (hf + 1) * 512)
            k.mm(ps[2 + hf][:], LLt[0:64, :], W2A2[0:64, cols])
            k.mm(ps[4 + hf][:], LLt[64:128, :], W2A2[64:128, cols])
        w0r = jrow(I["a_w0"][li:li + 1, :])
        for hf in range(2):
            cols = slice(hf * 512, (hf + 1) * 512)
            k.tt(T1[:, cols], ps[2 + hf][:], w0r[:, cols], ALU.add)
        k.act(T1[:], T1[:], AF.Sigmoid)
        CC = float(np.exp(-0.5))
        if kind == "p":
            for hf in range(2):
                cols = slice(hf * 512, (hf + 1) * 512)
                k.mm(ps[6 + hf][0:64, :], TRI, T1[0:64, cols])
                k.act(EPI[:, cols], ps[6 + hf][0:64, :], AF.Exp, scale=-CC)
                k.act(EPN[:, cols], ps[6 + hf][0:64, :], AF.Exp, scale=CC)
                k.tt(EPX[:, cols], ps[6 + hf][0:64, :], T1[0:64, cols], ALU.subtract)
            k.act(EPX[:], EPX[:], AF.Exp, scale=-CC)
            for h in range(16):
                k.mm(ps[2][0:64, h:h + 1], EPI[:, h * 64:(h + 1) * 64], C.identf[0:64, 63:64])
            k.cp(PCc[:], ps[2][0:64, 0:16], eng="act")
        else:
            k.act(T1[:], T1[:], AF.Exp, scale=-CC)
            toxr(T1, "w")
        a0r = jrow(I["a_a0"][li:li + 1, :])
        for hf in range(2):
            cols = slice(hf * 512, (hf + 1) * 512)
            k.tt(At[:, cols], ps[4 + hf][:], a0r[:, cols], ALU.add)
        k.act(At[:], At[:], AF.Sigmoid)
        kkr = jrow(I["a_k_k"][li:li + 1, :])
        k.tt(T1[:], PM["k"][:], kkr[:], ALU.mult)
        k.tt(T2[:], T1[:], T1[:], ALU.mult)
        k.red(st16[:, 0, :], T2[:].rearrange("p (h j) -> p h j", h=16))
        k.ts(st16[:, 0, :], st16[:, 0, :], 1e-24, ALU.max)
        k.act(st16[:, 1, :], st16[:, 0, :], AF.Sqrt)
        k.recip(st16[:, 2, :], st16[:, 1, :])
        k.tt(T1[:].rearrange("p (h j) -> p h j", h=16), T1[:].rearrange("p (h j) -> p h j", h=16),
             st16[:, 2, :].unsqueeze(2).to_broadcast([128, 16, 64]), ALU.mult)
        if kind == "s":
            toxr(T1, "kk")
        k.tt(T2[:], T1[:], At[:], ALU.mult)
        if kind == "s":
            toxr(T2, "ka")
        kar = jrow(I["a_k_a"][li:li + 1, :])
        k.stt(PSt[:], At[:], -1.0, kar[:], ALU.add, ALU.mult)
        k.stt(KP[:], PSt[:], 1.0, PM["k"][:], ALU.add, ALU.mult)
        if kind == "s":
            toxr(KP, "k")
            toxr(PM["r"], "r")
        rkr = jrow(I["a_r_k"][li:li + 1, :])
        k.tt(Pt[:], PM["r"][:], rkr[:], ALU.mult)
        k.tt(Pt[:], Pt[:], KP[:], ALU.mult)
        k.red(bon[:], Pt[:].rearrange("p (h j) -> p h j", h=16))
        if kind == "p":
            k.stt(EPX[:], T1[0:64, :], -1.0, EPX[:], ALU.mult, ALU.mult)
            k.tt(T2[0:64, :], T2[0:64, :], EPN[:], ALU.mult)
            k.tt(EPN[:], KP[0:64, :], EPN[:], ALU.mult)
            k.tt(EPI[:], PM["r"][0:64, :], EPI[:], ALU.mult)

        if STOP <= 5:
            return
        def step(Sx, tl):
            bks = {}
            for bi, x in enumerate(("kk", "w", "ka", "k", "r")):
                bk = ps[3 + bi] if bi < 5 else None
                k.mm(bk[:], SEL[:, tl, :], XR[x][0][:], start=True, stop=False)
                k.mm(bk[:], SEL[:, tl, :], XR[x][1][:], start=False, stop=True)
                bks[x] = bk[:].rearrange("p (a j) -> p a j", a=8)
            k.tt(t1[:], Sx, bks["kk"], ALU.mult)
            k.tt(t3[:], bks["k"], vT[:, :, tl:tl + 1].to_broadcast([128, 8, 64]), ALU.mult)
            k.red(sa[:], t1[:], negate=True)
            k.tt(Sx, Sx, bks["w"], ALU.mult)
            k.tt(t2[:], bks["ka"], sa[:].unsqueeze(2).to_broadcast([128, 8, 64]), ALU.mult)
            k.tt(Sx, Sx, t3[:], ALU.add)
            k.tt(Sx, Sx, t2[:], ALU.add)
            k.tt(t1[:], Sx, bks["r"], ALU.mult)
            k.red(YT[:, :, tl], t1[:])

        if kind == "p":
            i64 = C.identf[0:64, 0:64]
            for hh in range(2):
                H0 = hh * 8
                bA = [bank(), bank()]
                bB, bK = bank(), bank()
                for hl in range(8):
                    hc = slice((H0 + hl) * 64, (H0 + hl + 1) * 64)
                    o = (hl % 4) * 128
                    k.tr(bA[hl // 4][0:64, o:o + 64], EPX[:, hc], i64)
                    k.tr(bA[hl // 4][0:64, o + 64:o + 128], EPI[:, hc], i64)
                    k.tr(bB[0:64, hl * 64:(hl + 1) * 64], T2[0:64, hc], i64)
                    k.tr(bK[0:64, hl * 64:(hl + 1) * 64], EPN[:, hc], i64)
                k.cp(FMAR[:, 0:4, :].rearrange("p a b -> p (a b)"), bA[0][0:64, :], eng="act")
                k.cp(FMAR[:, 4:8, :].rearrange("p a b -> p (a b)"), bA[1][0:64, :], eng="act")
                k.cp(FMB[:].rearrange("p a b -> p (a b)"), bB[0:64, :], eng="dve")
                k.cp(FMK[:].rearrange("p a b -> p (a b)"), bK[0:64, :], eng="dve")
                for (Gd, FMl) in ((GB, FMB), (GK, FMK)):
                    bb = [bank(), bank()]
                    for hl in range(8):
                        o = (hl % 4) * 128
                        k.mm(bb[hl // 4][0:64, o:o + 128], FMl[:, hl, :], FMAR[:, hl, :])
                    for q in range(2):
                        k.tt(Gd[:, 4 * q:4 * q + 4, :], bb[q][0:64, :].rearrange("p (a b) -> p a b", a=4),
                             MASKAR.unsqueeze(1).to_broadcast([64, 4, 128]), ALU.mult)
                bq = bank()
                for hl in range(8):
                    k.mm(bq[0:64, hl * 64:(hl + 1) * 64], FMAR[:, hl, 0:64], FMB[:, hl, :])
                k.tt(PQ[1][:], bq[0:64, :].rearrange("p (a b) -> p a b", a=8), MASKNT.unsqueeze(1).to_broadcast([64, 8, 64]), ALU.mult)
                k.tt(Tm[:], GB[:, :, 0:64], IDN.unsqueeze(1).to_broadcast([64, 8, 64]), ALU.add)
                Pc, Qc = GB[:, :, 0:64], PQ[1]
                for lv in range(5):
                    Pn, Qn = PQ[2 * ((lv + 1) % 2)], PQ[2 * ((lv + 1) % 2) + 1]
                    if lv < 4:
                        bp = bank()
                        for hl in range(8):
                            k.mm(bp[0:64, hl * 64:(hl + 1) * 64], Qc[:, hl, :], Pc[:, hl, :])
                    bq = bank()
                    for hl in range(8):
                        k.mm(bq[0:64, hl * 64:(hl + 1) * 64], Pc[:, hl, :], Qc[:, hl, :])
                    if lv < 4:
                        k.cp(Pn[:].rearrange("p a b -> p (a b)"), bp[0:64, :], eng="act")
                    k.cp(Qn[:].rearrange("p a b -> p (a b)"), bq[0:64, :], eng="act")
                    bt = bank()
                    for hl in range(8):
                        k.mm(bt[0:64, hl * 64:(hl + 1) * 64], Qn[:, hl, :], Tm[:, hl, :])
                    k.tt(Tm[:], Tm[:], bt[0:64, :].rearrange("p (a b) -> p a b", a=8), ALU.add)
                    Pc, Qc = Pn, Qn
                bx = bank()
                for hl in range(8):
                    hc = slice((H0 + hl) * 64, (H0 + hl + 1) * 64)
                    k.mm(bx[0:64, hl * 64:(hl + 1) * 64], FMAR[:, hl, 0:64], ST[:, H0 + hl, :], start=True, stop=False)
                    k.mm(bx[0:64, hl * 64:(hl + 1) * 64], GK[:, hl, 0:64], PM["v"][0:64, hc], start=False, stop=True)
                k.cp(XT[:].rearrange("p a b -> p (a b)"), bx[0:64, :], eng="act")
                bu = bank()
                for hl in range(8):
                    k.mm(bu[0:64, hl * 64:(hl + 1) * 64], Tm[:, hl, :], XT[:, hl, :])
                k.cp(UT[:].rearrange("p a b -> p (a b)"), bu[0:64, :], eng="act")
                by = bank()
                for hl in range(8):
                    hc = slice((H0 + hl) * 64, (H0 + hl + 1) * 64)
                    k.mm(by[0:64, hl * 64:(hl + 1) * 64], FMAR[:, hl, 64:128], ST[:, H0 + hl, :], start=True, stop=False)
                    k.mm(by[0:64, hl * 64:(hl + 1) * 64], GB[:, hl, 64:128], UT[:, hl, :], start=False, stop=False)
                    k.mm(by[0:64, hl * 64:(hl + 1) * 64], GK[:, hl, 64:128], PM["v"][0:64, hc], start=False, stop=True)
                k.cp(KP[0:64, hh * 512:(hh + 1) * 512], by[0:64, :], eng="act")
                bs = bank()
                for hl in range(8):
                    hc = slice((H0 + hl) * 64, (H0 + hl + 1) * 64)
                    k.mm(bs[0:64, hl * 64:(hl + 1) * 64], T2[0:64, hc], UT[:, hl, :], start=True, stop=False)
                    k.mm(bs[0:64, hl * 64:(hl + 1) * 64], EPN[:, hc], PM["v"][0:64, hc], start=False, stop=True)
                k.tt(ST[:, H0:H0 + 8, :], ST[:, H0:H0 + 8, :], bs[0:64, :].rearrange("p (a b) -> p a b", a=8), ALU.add)
                k.tt(ST[:, H0:H0 + 8, :], ST[:, H0:H0 + 8, :], PCc[:, H0:H0 + 8].unsqueeze(2).to_broadcast([64, 8, 64]), ALU.mult)
            if n == ntile_p - 1:
                for q in range(2):
                    bo = bank()
                    for hl in range(8):
                        k.tr(bo[0:64, hl * 64:(hl + 1) * 64], ST[:, q * 8 + hl, :], i64)
                    k.cp(EPI[:, q * 512:(q + 1) * 512], bo[0:64, :], eng="act")
                k.dma(O["S_p"][li].rearrange("h i j -> i h j"), EPI[:].rearrange("p (h j) -> p h j", h=16))
        else:
            for sl in range(8):
                sq = n * 8 + sl
                k.dma(t3[:], I["st_S"][li, sq].rearrange("(a b) i j -> (b i) a j", b=2))
                Sx = KP[:, 0:512].rearrange("p (a j) -> p a j", a=8)
                k.cp(Sx, t3[:], eng="act")
                for t in range(8):
                    step(Sx, sl * 8 + t)
                k.dma(O["S_s"][li, sq].rearrange("(a b) i j -> (b i) a j", b=2), Sx)

        if STOP <= 6:
            return
        Y = T1
        if kind == "p":
            k.cp(Y[0:64, :], KP[0:64, :], eng="pool")
        else:
            for c in range(8):
                k.tr(ps[c // 4][0:64, (c % 4) * 128:(c % 4 + 1) * 128], YT[:, c, :], C.identf[:, :])
            k.cp(Y[0:64, 0:512], ps[0][0:64, :], eng="act")
            k.cp(Y[0:64, 512:1024], ps[1][0:64, :], eng="act")
        Y3 = Y[0:64, :].rearrange("p (h j) -> p h j", h=16)
        k.red(st16[0:64, 0, :], Y3)
        k.ts(st16[0:64, 0, :], st16[0:64, 0, :], 1.0 / 64, ALU.mult)
        k.tt(Y3, Y3, st16[0:64, 0, :].unsqueeze(2).to_broadcast([64, 16, 64]), ALU.subtract)
        k.tt(T2[0:64, :], Y[0:64, :], Y[0:64, :], ALU.mult)
        k.red(st16[0:64, 1, :], T2[0:64, :].rearrange("p (h j) -> p h j", h=16))
        k.act(st16[0:64, 2, :], st16[0:64, 1, :], AF.Sqrt, bias=epsgn[0:64, :], scale=1.0 / 64)
        k.recip(st16[0:64, 3, :], st16[0:64, 2, :])
        k.tt(Y3, Y3, st16[0:64, 3, :].unsqueeze(2).to_broadcast([64, 16, 64]), ALU.mult)
        gr = jrow(I["a_lnx_g"][li:li + 1, :], 64)
        k.tt(Y[0:64, :], Y[0:64, :], gr[0:64, :], ALU.mult)
        br = jrow(I["a_lnx_b"][li:li + 1, :], 64)
        k.tt(Y[0:64, :], Y[0:64, :], br[0:64, :], ALU.add)
        k.tt(T2[0:64, :].rearrange("p (h j) -> p h j", h=16), PM["v"][0:64, :].rearrange("p (h j) -> p h j", h=16),
             bon[0:64, :].unsqueeze(2).to_broadcast([64, 16, 64]), ALU.mult)
        k.tt(Y[0:64, :], Y[0:64, :], T2[0:64, :], ALU.add)
        k.act(T2[0:64, :], PM["z"][0:64, :], AF.Silu)
        k.tt(G[:], Y[0:64, :], T2[0:64, :], ALU.mult)
        psb = ps[2][:].bitcast(BF16)
        for c in range(8):
            k.tr(psb[:, c * 64:(c + 1) * 64], G[:, c * 128:(c + 1) * 128], C.identb[0:64, 0:64])
        k.cp(gT[:].rearrange("p c t -> p (c t)"), psb[:, 0:512], eng="act")
        for hf in range(2):
            cols = slice(hf * 512, (hf + 1) * 512)
            for c in range(8):
                k.mm(ps[hf][0:64, :], gT[:, c, :], Wo[:, c, cols], start=(c == 0), stop=(c == 7))
            k.stt(Y[0:64, cols], xin[:, cols], ALPHA, ps[hf][0:64, :], ALU.mult, ALU.add)
        ln_tail(C, Y, 64, L, dstx[r0:r0 + 64, :], T2, crow)


def pool_layer(C, lst, L, src, dst):
    nc, P, k, I, O = C.nc, C.P, C.k, C.I, C.O
    tp = C.tp
    sb = lambda n, s, d=F32: lst.enter_context(nc.sbuf_tensor("b_" + n, list(s), d))
    ps = C.ps
    Win = sb("Win", [128, 8, 2048], BF16)
    Wg = sb("Wg", [128, 4, 2, 256], BF16)
    Wo = sb("Wo", [128, 8, 1024], BF16)
    scol = sb("scol", [128, 8])
    stg = [sb("stg%d" % i, [128, 8, 128]) for i in range(2)]
    xin = sb("xin", [128, 1024])
    xT = sb("xT", [128, 8, 128], BF16)
    E = sb("E", [128, 8, 16 * 23])
    A = sb("A", [128, 2, 16 * 23])
    B = sb("B", [128, 2, 16 * 23])
    dT = sb("dT", [128, 8, 128], BF16)
    sz = sb("sz", [128, 8, 128])
    gT = sb("gT", [128, 8, 128], BF16)
    R = sb("R", [128, 1024])
    T1 = sb("T1", [128, 1024])
    cinv = sb("cinv", [128, 512])
    crow = [sb("crow%d" % i, [128, 1024]) for i in range(2)]
    C.lnst = sb("lnst", [128, 8])
    C.epsln = sb("epsln", [128, 1])
    k.memset(C.epsln[:], LN_EPS)
    k.dma(cinv[:], I["cmask"][:, 0:512])
    w_in = I["b_w_in"].rearrange("(c p) n -> p c n", p=128)
    for j in range(16):
        k.dma(stg[j % 2][:], w_in[:, :, j * 128:(j + 1) * 128], eng="sp" if j % 2 == 0 else "act")
        k.cp(Win[:, :, j * 128:(j + 1) * 128], stg[j % 2][:], eng="pool" if j % 2 == 0 else "act")
    w_out = I["b_w_out"].rearrange("(c p) n -> p c n", p=128)
    for j in range(8):
        k.dma(stg[j % 2][:], w_out[:, :, j * 128:(j + 1) * 128], eng="sp" if j % 2 == 0 else "act")
        k.cp(Wo[:, :, j * 128:(j + 1) * 128], stg[j % 2][:], eng="pool" if j % 2 == 0 else "act")
    for g in range(4):
        sv = stg[g % 2][:].rearrange("p c n -> p (c n)")[:, 0:512].rearrange("p (c n) -> p c n", c=2)
        k.dma(sv, I["b_w_grp"][g].rearrange("(c p) n -> p c n", p=128), eng="sp")
        k.cp(Wg[:, g, :, :], sv, eng="pool")
    k.dma(scol[:], I["b_scale"][0].rearrange("(c p) -> p c", p=128), allow_slow_non_contiguous=True)
    k.memset(E[:], 0.0)

    ntile_p = tp // 128
    tiles = [("p", n) for n in range(ntile_p)] + [("s", 0)]
    for kind, n in tiles:
        srcx = src[0] if kind == "p" else src[1]
        dstx = dst[0] if kind == "p" else dst[1]
        r0 = n * 128
        nseg, new = (1, 128) if kind == "p" else (16, 8)
        sl = 15 + new
        Ev = E[:, :, 0:nseg * sl].rearrange("p c (s t) -> p c s t", s=nseg)
        k.dma(xin[:], srcx[r0:r0 + 128, :])
        for b in range(2):
            for c in range(4):
                k.tr(ps[b][:, c * 128:(c + 1) * 128], xin[:, (4 * b + c) * 128:(4 * b + c + 1) * 128], C.identf[:, :])
            k.cp(xT[:, 4 * b:4 * b + 4, :].rearrange("p c t -> p (c t)"), ps[b][:], eng="act")
        if kind == "s":
            for hh in range(2):
                k.dma(R[0:120, :], I["st_pool"][hh * 8:(hh + 1) * 8].rearrange("s r d -> (s r) d"))
                for b in range(2):
                    for c in range(4):
                        k.tr(ps[2 + b][:, c * 120:(c + 1) * 120], R[0:120, (4 * b + c) * 128:(4 * b + c + 1) * 128], C.identf[0:120, 0:120])
                    k.cp(Ev[:, 4 * b:4 * b + 4, hh * 8:(hh + 1) * 8, 0:15],
                         ps[2 + b][:, 0:480].rearrange("p (c s r) -> p c s r", c=4, s=8), eng="act")
        for ob in range(4):
            for o4 in range(4):
                oc = ob * 4 + o4
                for dc in range(8):
                    k.mm(ps[4 + ob % 2][:, o4 * 128:(o4 + 1) * 128], Win[:, dc, oc * 128:(oc + 1) * 128], xT[:, dc, :],
                         start=(dc == 0), stop=(dc == 7))
            pv = ps[4 + ob % 2][:].rearrange("p (c s t) -> p c s t", c=4, s=nseg)
            if ob < 2:
                k.cp(Ev[:, ob * 4:ob * 4 + 4, :, 15:sl], pv, eng="act")
            else:
                k.act(sz[:, (ob - 2) * 4:(ob - 2) * 4 + 4, :].rearrange("p c t -> p (c t)"), ps[4 + ob % 2][:], AF.Silu)
        if kind == "p" and n == ntile_p - 1:
            for b in range(2):
                for c in range(4):
                    k.tr(ps[2 + b][0:15, c * 128:(c + 1) * 128], E[:, 4 * b + c, 128:143], C.identf[:, :])
                k.cp(T1[0:15, b * 512:(b + 1) * 512], ps[2 + b][0:15, :], eng="act")
            k.dma(O["pl_p"], T1[0:15, :])
        if kind == "s":
            for hh in range(2):
                for b in range(2):
                    for c in range(4):
                        Ac = A[:, 0, 0:120].rearrange("p (s r) -> p s r", s=8)
                        k.cp(Ac, Ev[:, 4 * b + c, hh * 8:(hh + 1) * 8, 8:23], eng="pool")
                        k.tr(ps[2 + b][0:120, c * 128:(c + 1) * 128], A[:, 0, 0:120], C.identf[:, :])
                    k.cp(T1[0:120, b * 512:(b + 1) * 512], ps[2 + b][0:120, :], eng="act")
                k.dma(O["pl_s"][hh * 8:(hh + 1) * 8].rearrange("s r d -> (s r) d"), T1[0:120, :])
        for g in range(4):
            cur = Ev[:, 2 * g:2 * g + 2]
            bufs = [A[:, :, 0:nseg * sl].rearrange("p c (s t) -> p c s t", s=nseg),
                    B[:, :, 0:nseg * sl].rearrange("p c (s t) -> p c s t", s=nseg)]
            lo = 0
            for si, sh in enumerate((1, 2, 4, 8)[:g + 1]):
                nxt = bufs[si % 2]
                lo2 = lo + sh
                k.tt(nxt[:, :, :, lo2:sl], cur[:, :, :, lo2:sl], cur[:, :, :, lo:sl - sh], ALU.add)
                cur, lo = nxt, lo2
            w = 2 ** (g + 1)
            pooled = bufs[(g + 1) % 2]
            if kind == "p" and n == 0:
                k.tt(pooled[:, :, 0, 15:sl], cur[:, :, 0, 15:sl],
                     cinv[:, g * 128:(g + 1) * 128].unsqueeze(1).to_broadcast([128, 2, 128]), ALU.mult)
                k.tt(dT[:, 2 * g:2 * g + 2, :], pooled[:, :, 0, 15:sl], Ev[:, 2 * g:2 * g + 2, 0, 15:sl], ALU.subtract)
            else:
                k.stt(dT[:, 2 * g:2 * g + 2, :].rearrange("p c (s t) -> p c s t", s=nseg), cur[:, :, :, 15:sl], 1.0 / w,
                      Ev[:, 2 * g:2 * g + 2, :, 15:sl], ALU.mult, ALU.subtract)
        if kind == "p":
            k.cp(A[:, 0, 0:120].rearrange("p (c t) -> p c t", c=8), E[:, :, 128:143], eng="pool")
            k.cp(E[:, :, 0:15], A[:, 0, 0:120].rearrange("p (c t) -> p c t", c=8), eng="pool")
        for jc in range(8):
            g, jl = jc // 2, jc % 2
            for ic in range(2):
                k.mm(ps[6 + jc // 4][:, (jc % 4) * 128:(jc % 4 + 1) * 128], Wg[:, g, ic, jl * 128:(jl + 1) * 128], dT[:, 2 * g + ic, :],
                     start=(ic == 0), stop=(ic == 1))
        for jc in range(8):
            k.stt(gT[:, jc, :], ps[6 + jc // 4][:, (jc % 4) * 128:(jc % 4 + 1) * 128], scol[:, jc:jc + 1], sz[:, jc, :], ALU.mult, ALU.mult)
        for hf in range(2):
            cols = slice(hf * 512, (hf + 1) * 512)
            for c in range(8):
                k.mm(ps[hf][:], gT[:, c, :], Wo[:, c, cols], start=(c == 0), stop=(c == 7))
            k.stt(R[:, cols], xin[:, cols], ALPHA, ps[hf][:], ALU.mult, ALU.add)
        ln_tail(C, R, 128, L, dstx[r0:r0 + 128, :], T1, crow)


NEG = -30000.0
SCL = 0.125


def nsa_layer(C, lst, L, src, dst):
    nc, P, k, I, O = C.nc, C.P, C.k, C.I, C.O
    tp = C.tp
    sb = lambda n, s, d=F32: lst.enter_context(nc.sbuf_tensor("c_" + n, list(s), d))
    ps = C.ps
    Win = sb("Win", [128, 8, C_NC], BF16)
    Wo = sb("Wo", [128, 8, 1024], BF16)
    stg = [sb("stg%d" % i, [128, 8, 128]) for i in range(2)]
    xin = sb("xin", [128, 1024])
    xT = sb("xT", [128, 8, 128], BF16)
    KV = sb("KV", [128, 1536])
    KsT = sb("KsT", [64, 4, 17 * 128], BF16)
    KwT = sb("KwT", [64, 4, 5 * 128], BF16)
    Vs = sb("Vs", [128, 17, 4, 65], BF16)
    Vw = sb("Vw", [128, 5, 4, 65], BF16)
    KcT = sb("KcT", [64, 4, 64], BF16)
    Vc = sb("Vc", [64, 4, 98])
    Wbk = sb("Wbk", [128, 124])
    Wbv = sb("Wbv", [128, 124])
    wcol = sb("wcol", [128, 2])
    QT = sb("QT", [64, 16, 128], BF16)
    GZ = sb("GZ", [128, 1072])
    gates = sb("gates", [128, 48])
    Bt = [sb("Bt%d" % i, [128, 512]) for i in range(4)]
    SBS = sb("SBS", [128, 17 * 128])
    SBW = sb("SBW", [128, 5 * 128])
    SBC = sb("SBC", [64, 128])
    hbias = sb("hbias", [128, 240])
    k.dma(hbias[:], I["n_hb"])
    GBUF = [sb("GBUF%d" % i, [128, 1024]) for i in range(2)]
    tmp = sb("tmp", [128, 512])
    TMP = [tmp, sb("tmp1", [128, 512])]
    ec = sb("ec", [64, 512])
    eb = sb("eb", [128, 512], BF16)
    EB = [eb, sb("eb1", [128, 512], BF16)]
    OB = sb("OB", [128, 4, 98])
    rd = sb("rd", [128, 8])
    imp = sb("imp", [128, 40])
    imp2 = sb("imp2", [128, 40])
    m8 = sb("m8", [128, 16])
    cbt = sb("cbt", [128, 80])
    selT = sb("selT", [40, 128], BF16)
    Eexp = sb("Eexp", [40, 17 * 128], BF16)
    Oacc = sb("Oacc", [128, 1024])
    Gb = sb("Gb", [128, 1024], BF16)
    gT = sb("gT", [128, 8, 128], BF16)
    T1 = sb("T1", [128, 1024])
    idx = sb("idx", [128, 256], I32)
    idf = GBUF[0][:, 0:256]
    crow = [x[:].rearrange("p c n -> p (c n)") for x in stg]
    C.lnst = sb("lnst", [128, 8])
    C.epsln = sb("epsln", [128, 1])
    k.memset(C.epsln[:], LN_EPS)
    w_in = I["c_w_in"].rearrange("(c p) n -> p c n", p=128)
    nj = (C_NC + 127) // 128
    for j in range(nj):
        wd = min(128, C_NC - j * 128)
        k.dma(stg[j % 2][:, :, 0:wd], w_in[:, :, j * 128:j * 128 + wd], eng="sp" if j % 2 == 0 else "act")
        k.cp(Win[:, :, j * 128:j * 128 + wd], stg[j % 2][:, :, 0:wd], eng="pool" if j % 2 == 0 else "act")
    w_out = I["c_w_out"].rearrange("(c p) n -> p c n", p=128)
    for j in range(8):
        k.dma(stg[j % 2][:], w_out[:, :, j * 128:(j + 1) * 128], eng="sp" if j % 2 == 0 else "act")
        k.cp(Wo[:, :, j * 128:(j + 1) * 128], stg[j % 2][:], eng="pool" if j % 2 == 0 else "act")
    for r in range(4):
        k.dma(wcol[r * 32:(r + 1) * 32, 0:1], I["c_cmp_wk"].rearrange("o l -> l o"), allow_slow_non_contiguous=True)
        k.dma(wcol[r * 32:(r + 1) * 32, 1:2], I["c_cmp_wv"].rearrange("o l -> l o"), allow_slow_non_contiguous=True)
    k.dma(Wbk[:], I["n_wbm"])
    k.cp(Wbv[:], Wbk[:], eng="pool")
    k.ts(Wbk[:], Wbk[:], wcol[:, 0:1], ALU.mult)
    k.ts(Wbv[:], Wbv[:], wcol[:, 1:2], ALU.mult)
    k.memset(Vs[:], 0.0)
    k.memset(Vw[:], 0.0)
    k.memset(KsT[:], 0.0)
    k.memset(KwT[:], 0.0)
    k.memset(Vs[:, :, :, 64:65], 1.0)
    k.memset(Vw[:, :, :, 64:65], 1.0)

    def kv_tile(kt, rows_cmp, rows_sel, rows_win, nrows, do_cmp_block=None, win_slot=None):
        if rows_sel is not None:
            ksr, vsr = rows_sel
            for g in range(4):
                k.tr(ps[0][0:64, g * 128:g * 128 + nrows], ksr[:, g * 64:(g + 1) * 64], C.identf[0:nrows, 0:nrows])
            k.cp(KsT[:, :, kt * 128:kt * 128 + nrows], ps[0][0:64, :].rearrange("p (g t) -> p g t", g=4)[:, :, 0:nrows], eng="act")
            k.cp(Vs[0:nrows, kt, :, 0:64], vsr.rearrange("p (g d) -> p g d", g=4), eng="dve")
        if rows_win is not None:
            kwr, vwr = rows_win
            ws = win_slot
            for g in range(4):
                k.tr(ps[1][0:64, g * 128:g * 128 + nrows], kwr[:, g * 64:(g + 1) * 64], C.identf[0:nrows, 0:nrows])
            k.cp(KwT[:, :, ws * 128:ws * 128 + nrows], ps[1][0:64, :].rearrange("p (g t) -> p g t", g=4)[:, :, 0:nrows], eng="act")
            k.cp(Vw[0:nrows, ws, :, 0:64], vwr.rearrange("p (g d) -> p g d", g=4), eng="dve")
        if rows_cmp is not None:
            kcr, vcr = rows_cmp
            t = do_cmp_block
            for g in range(4):
                k.mm(ps[2][0:64, g * 4:(g + 1) * 4], kcr[:, g * 64:(g + 1) * 64], Wbk[:, 60:64])
            k.cp(KcT[:, :, 4 * t:4 * t + 4], ps[2][0:64, 0:16].rearrange("p (g n) -> p g n", g=4), eng="act")
            k.mm(ps[2][0:64, 128:384], Wbv[:, 60 - 4 * t:124 - 4 * t], vcr)
            k.tt(Vc[:, :, 0:64], Vc[:, :, 0:64], ps[2][0:64, 128:384].rearrange("p (g d) -> p g d", g=4), ALU.add)

    bti = [0]

    OS = sb("OS", [128, 260])
    selT4 = sb("selT4", [40, 4, 8], BF16)

    def attend(nq, nblk, s_tiles, w_tiles, bc_ap, bs_fn, bw_fn, cb_ap, ft_ap, pair_ap, eexp_cols, load_consts=True, resident=False, batched=False):
        nc4 = 4 * nq
        if load_consts:
            k.dma(cbt[0:nq, 0:nblk], cb_ap)
            k.dma(cbt[0:nq, 40:40 + nblk], ft_ap)
            for g in range(4):
                k.dma(Vc[:, g, 65:65 + nblk], pair_ap)

        def bias_tile(ap, nk):
            if resident:
                return ap
            t = Bt[bti[0] % 4]
            bti[0] += 1
            k.dma(t[0:nk, 0:nc4], ap, eng="sp")
            return t[0:nk, 0:nc4]
        accs = []

        for g in range(4):
            Qg = QT[:, 4 * g:4 * g + 4, 0:nq]
            Qg2 = ec
            k.mm(ps[3][0:64, 0:nc4], KcT[:, g, :], QTf[:, g, 0:nc4])
            bt = bias_tile(bc_ap(g), 64)
            k.stt(tmp[0:64, 0:nc4], ps[3][0:64, 0:nc4], SCL, bt, ALU.mult, ALU.add)
            k.act(ec[:, 0:nc4], tmp[0:64, 0:nc4], AF.Exp)
            for j in range(4):
                k.mm(ps[4][0:nq, j * 98:j * 98 + 65 + nblk], ec[:, j * nq:(j + 1) * nq], Vc[:, g, 0:65 + nblk])
            k.cp(OB[0:nq, :, 0:65 + nblk], ps[4][0:nq, 0:392].rearrange("p (j c) -> p j c", j=4)[:, :, 0:65 + nblk], eng="act")
            k.ts(rd[0:nq, 0:4], OB[0:nq, :, 64], 1e-30, ALU.max)
            k.recip(rd[0:nq, 0:4], rd[0:nq, 0:4])
            k.ts(imp[0:nq, 0:nblk], OB[0:nq, 0, 65:65 + nblk], rd[0:nq, 0:1], ALU.mult)
            for j in range(1, 4):
                k.stt(imp[0:nq, 0:nblk], OB[0:nq, j, 65:65 + nblk], rd[0:nq, j:j + 1], imp[0:nq, 0:nblk], ALU.mult, ALU.add)
            k.tt(imp[0:nq, 0:nblk], imp[0:nq, 0:nblk], cbt[0:nq, 0:nblk], ALU.mult)
            k.tt(imp[0:nq, 0:nblk], imp[0:nq, 0:nblk], cbt[0:nq, 40:40 + nblk], ALU.add)
            P.op("dve", lambda e: e.max(out=m8[0:nq, 0:8], in_=imp[0:nq, 0:nblk]), reads=[imp], writes=[m8])
            P.op("dve", lambda e: e.match_replace(out=imp2[0:nq, 0:nblk], in_to_replace=m8[0:nq, 0:8], in_values=imp[0:nq, 0:nblk], imm_value=-2.0),
                 reads=[imp, m8], writes=[imp2])
            P.op("dve", lambda e: e.max(out=m8[0:nq, 8:16], in_=imp2[0:nq, 0:nblk]), reads=[imp2], writes=[m8])
            k.ts(m8[0:nq, 15:16], m8[0:nq, 15:16], 0.0, ALU.max)
            k.ts(imp2[0:nq, 0:nblk], imp[0:nq, 0:nblk], m8[0:nq, 15:16], ALU.is_ge)
            k.tr(ps[5][0:nblk, 0:nq], imp2[0:nq, 0:nblk], C.identf[0:nq, 0:nq])
            if batched:
                k.cp(selT4[0:nblk, g, :], ps[5][0:nblk, 0:nq], eng="act")
            else:
                k.cp(selT[0:nblk, 0:nq], ps[5][0:nblk, 0:nq], eng="act")
            def accum(first, col, Osrc, g=g):
                gsl = gates[0:nq, :].rearrange("p (h c) -> p h c", c=3)[:, 4 * g:4 * g + 4, col]
                k.tt(rd[0:nq, 4:8], rd[0:nq, 0:4], gsl, ALU.mult)
                dstv = Oacc[0:nq, g * 256:(g + 1) * 256].rearrange("p (j d) -> p j d", j=4)
                rb = rd[0:nq, 4:8].unsqueeze(2).to_broadcast([nq, 4, 64])
                if first:
                    k.tt(dstv, Osrc, rb, ALU.mult)
                else:
                    k.tt(OB[0:nq, :, 0:64], Osrc, rb, ALU.mult)
                    k.tt(dstv, dstv, OB[0:nq, :, 0:64], ALU.add)
            accum(True, 0, OB[0:nq, :, 0:64])
            if batched:
                accs.append(accum)
                continue
            items = []
            for br, tiles, Kt, Vt, bfn in ((1, s_tiles, KsT, Vs, bs_fn), (2, w_tiles, KwT, Vw, bw_fn)):
                for ti, (slot, nk, bidx) in enumerate(tiles):
                    items.append((br, Kt, Vt, bfn, ti, len(tiles), slot, nk, bidx))
            PSC = (ps[3], ps[2])

            def front(i):
                br, Kt, Vt, bfn, ti, nt, slot, nk, bidx = items[i]
                k.mm(PSC[i % 2][0:nk, 0:nc4], Kt[:, g, slot * 128:slot * 128 + nk], QTf[:, g, 0:nc4])
                if br == 1:
                    mo = 128 + (i % 2) * 128
                    k.mm(ps[5][0:nk, mo:mo + nq], Eexp[0:nblk, eexp_cols(slot)], selT[0:nblk, 0:nq])

            def mid_a(i):
                br, Kt, Vt, bfn, ti, nt, slot, nk, bidx = items[i]
                bt = bfn(bidx, g) if resident else bias_tile(bfn(bidx, g), nk)
                k.stt(TMP[i % 2][0:nk, 0:nc4], PSC[i % 2][0:nk, 0:nc4], SCL, bt, ALU.mult, ALU.add)

            def mid_b(i):
                br, Kt, Vt, bfn, ti, nt, slot, nk, bidx = items[i]
                e = EB[i % 2]
                k.act(e[0:nk, 0:nc4], TMP[i % 2][0:nk, 0:nc4], AF.Exp)
                if br == 1:
                    mo = 128 + (i % 2) * 128
                    k.tt(e[0:nk, 0:nc4].rearrange("p (j q) -> p j q", j=4), e[0:nk, 0:nc4].rearrange("p (j q) -> p j q", j=4),
                         ps[5][0:nk, mo:mo + nq].unsqueeze(1).to_broadcast([nk, 4, nq]), ALU.mult)

            def back(i):
                br, Kt, Vt, bfn, ti, nt, slot, nk, bidx = items[i]
                e = EB[i % 2]
                for j in range(4):
                    k.mm(ps[(6, 7, 0, 1)[j]][0:nq, 0:65], e[0:nk, j * nq:(j + 1) * nq], Vt[0:nk, slot, g, :],
                         start=(ti == 0), stop=(ti == nt - 1))
                if ti == nt - 1:
                    for j in range(4):
                        k.cp(OB[0:nq, j, 0:65], ps[(6, 7, 0, 1)[j]][0:nq, 0:65], eng="act")
                    k.ts(rd[0:nq, 0:4], OB[0:nq, :, 64], 1e-30, ALU.max)
                    k.recip(rd[0:nq, 0:4], rd[0:nq, 0:4])
                    accum(False, br, OB[0:nq, :, 0:64])

            front(0)
            mid_a(0)
            for i in range(len(items)):
                if i + 1 < len(items):
                    front(i + 1)
                    mid_a(i + 1)
                mid_b(i)
                back(i)

        if batched:
            PSC = (ps[3], ps[2])
            for br, tiles, Kt, Vt, SBt in ((1, s_tiles, KsT, Vs, SBS), (2, w_tiles, KwT, Vw, SBW)):
                nt = len(tiles)

                def front(i, br=br, tiles=tiles, Kt=Kt):
                    slot, nk, bidx = tiles[i]
                    for g in range(4):
                        k.mm(PSC[i % 2][0:nk, g * 32:(g + 1) * 32], Kt[:, g, slot * 128:slot * 128 + nk], QTf[:, g, 0:32])
                    if br == 1:
                        mo = 128 + (i % 2) * 128
                        for g in range(4):
                            k.mm(ps[5][0:nk, mo + g * 8:mo + (g + 1) * 8], Eexp[0:nblk, eexp_cols(slot)], selT4[0:nblk, g, :])

                def mid_a(i, tiles=tiles, SBt=SBt):
                    slot, nk, bidx = tiles[i]
                    k.stt(TMP[i % 2][0:nk, 0:128], PSC[i % 2][0:nk, 0:128], SCL, SBt[0:nk, bidx * 128:(bidx + 1) * 128], ALU.mult, ALU.add)

                def mid_b(i, br=br, tiles=tiles):
                    slot, nk, bidx = tiles[i]
                    e = EB[i % 2]
                    k.act(e[0:nk, 0:128], TMP[i % 2][0:nk, 0:128], AF.Exp)
                    if br == 1:
                        mo = 128 + (i % 2) * 128
                        ev = e[0:nk, 0:128].rearrange("p (g j t) -> p g j t", g=4, j=4)
                        k.tt(ev, ev, ps[5][0:nk, mo:mo + 32].rearrange("p (g t) -> p g t", g=4).unsqueeze(2).to_broadcast([nk, 4, 4, 8]), ALU.mult)

                def back(i, tiles=tiles, Vt=Vt, nt=nt):
                    slot, nk, bidx = tiles[i]
                    k.mm(ps[6][:, 0:260], EB[i % 2][0:nk, 0:128], Vt[0:nk, slot, :, :].rearrange("p g c -> p (g c)"),
                         start=(i == 0), stop=(i == nt - 1))

                front(0)
                mid_a(0)
                for i in range(nt):
                    if i + 1 < nt:
                        front(i + 1)
                        mid_a(i + 1)
                    mid_b(i)
                    back(i)
                k.cp(OS[:], ps[6][:, 0:260], eng="act")
                for g in range(4):
                    for j in range(4):
                        h = 4 * g + j
                        k.mm(ps[7][0:8, j * 65:(j + 1) * 65], C.identf[:, h * 8:(h + 1) * 8], OS[:, g * 65:(g + 1) * 65])
                    k.cp(OB[0:8, :, 0:65], ps[7][0:8, 0:260].rearrange("p (j c) -> p j c", j=4), eng="act")
                    k.ts(rd[0:8, 0:4], OB[0:8, :, 64], 1e-30, ALU.max)
                    k.recip(rd[0:8, 0:4], rd[0:8, 0:4])
                    accs[g](False, br, OB[0:8, :, 0:64])

    QTflat = QT[:].rearrange("p h q -> p (h q)")

    class _QTf:
        nq = 128

        def __getitem__(self, key):
            _, g, _ = key
            n4 = 4 * self.nq
            return QTflat[:, g * n4:(g + 1) * n4]
    QTf = _QTf()

    def qgz(npart, nq_cols):
        for h in range(16):
            for dc in range(8):
                k.mm(ps[7][0:64, (h % 4) * 128:(h % 4) * 128 + nq_cols], Win[:, dc, h * 64:(h + 1) * 64], xT[:, dc, 0:nq_cols],
                     start=(dc == 0), stop=(dc == 7))
            if h % 4 == 3:
                QTf.nq = nq_cols
                qdst = QTflat[:, (h - 3) * nq_cols:(h + 1) * nq_cols].rearrange("p (j q) -> p j q", j=4)
                k.cp(qdst, ps[7][0:64, :].rearrange("p (j q) -> p j q", j=4)[:, :, 0:nq_cols], eng="act")
        for i, (c0, c1) in enumerate(((2560, 3072), (3072, 3584), (3584, 3632))):
            for dc in range(8):
                k.mm(ps[i][0:npart, 0:c1 - c0], xT[:, dc, 0:npart], Win[:, dc, c0:c1], start=(dc == 0), stop=(dc == 7))
            k.cp(GZ[0:npart, c0 - 2560:c1 - 2560], ps[i][0:npart, 0:c1 - c0], eng="act")
        k.act(gates[0:npart, :], GZ[0:npart, 0:48], AF.Sigmoid)

    def finish(npart, dst_rows):
        k.act(T1[0:npart, :], GZ[0:npart, 48:1072], AF.Silu)
        k.tt(Gb[0:npart, :], Oacc[0:npart, :], T1[0:npart, :], ALU.mult)
        psb = ps[2][:].bitcast(BF16)
        for c in range(8):
            k.tr(psb[:, c * 128:c * 128 + npart], Gb[0:npart, c * 128:(c + 1) * 128], C.identb[0:npart, 0:npart])
        k.cp(gT[:, :, 0:npart], psb[:, 0:1024].rearrange("p (c t) -> p c t", c=8)[:, :, 0:npart], eng="act")
        for hf in range(2):
            cols = slice(hf * 512, (hf + 1) * 512)
            for c in range(8):
                k.mm(ps[hf][0:npart, :], gT[:, c, 0:npart], Wo[:, c, cols], start=(c == 0), stop=(c == 7))
            k.stt(T1[0:npart, cols], xin[0:npart, cols], ALPHA, ps[hf][0:npart, :], ALU.mult, ALU.add)
        ln_tail(C, T1, npart, L, dst_rows, Oacc, crow)

    def load_xT(rows_ap, npart):
        k.dma(xin[0:npart, :], rows_ap)
        for b in range(2):
            for c in range(4):
                k.tr(ps[b][:, c * 128:c * 128 + npart], xin[0:npart, (4 * b + c) * 128:(4 * b + c + 1) * 128], C.identf[0:npart, 0:npart])
            k.cp(xT[:, 4 * b:4 * b + 4, 0:npart], ps[b][:].rearrange("p (c t) -> p c t", c=4)[:, :, 0:npart], eng="act")

    def kv_proj(npart):
        for i in range(3):
            for dc in range(8):
                k.mm(ps[3 + i][0:npart, :], xT[:, dc, 0:npart], Win[:, dc, 1024 + i * 512:1024 + (i + 1) * 512], start=(dc == 0), stop=(dc == 7))
            k.cp(KV[0:npart, i * 512:(i + 1) * 512], ps[3 + i][0:npart, :], eng="act")

    k.memset(Vc[:], 0.0)
    k.memset(Vc[:, :, 64:65], 1.0)
    k.memset(KcT[:], 0.0)
    k.dma(Eexp[0:32, 0:2048], I["n_eexp_p"])
    ntile = tp // 128
    for t in range(ntile):
        r0 = t * 128
        load_xT(src[0][r0:r0 + 128, :], 128)
        kv_proj(128)
        for i, nm in enumerate(("cmpk", "cmpv", "selk", "selv")):
            k.dma(O[nm + "_p"][r0:r0 + 128, :], KV[:, i * 256:(i + 1) * 256])
        wr0 = r0 - (tp - min(512, tp))
        if wr0 >= 0:
            k.dma(O["wink_p"][wr0:wr0 + 128, :], KV[:, 1024:1280])
            k.dma(O["winv_p"][wr0:wr0 + 128, :], KV[:, 1280:1536])
        kv_tile(t, (KV[:, 0:256], KV[:, 256:512]), (KV[:, 512:768], KV[:, 768:1024]), (KV[:, 1024:1280], KV[:, 1280:1536]), 128,
                do_cmp_block=t, win_slot=t % 5)
        qgz(128, 128)
        s_tiles = [(kt, 128, t - kt) for kt in range(t + 1)]
        w_tiles = [(kt % 5, 128, t - kt) for kt in range(max(0, t - 4), t + 1)]
        attend(128, 32, s_tiles, w_tiles,
               lambda g: I["n_bc_p"][t, g], lambda d, g: I["n_bs_p"][d, g], lambda d, g: I["n_bw_p"][d, g],
               I["n_cb_p"][t], I["n_ft_p"][t], I["n_pair_p"], lambda slot: slice(slot * 128, (slot + 1) * 128))
        finish(128, dst[0][r0:r0 + 128, :])

    k.dma(idx[:], I["ptab"].rearrange("s n -> (s n)").partition_broadcast(128))
    k.cp(idf, idx[:])
    k.dma(wcol[:, 0:1], I["n_iota"])
    k.ts(idf, idf, 128.0, ALU.mult, wcol[:, 0:1], ALU.add)
    k.cp(idx[:], idf)
    k.dma(Eexp[0:33, 0:17 * 128], I["n_eexp_s"])
    load_xT(src[1][:, :], 128)
    kv_proj(128)
    KVs = C.kvs_scr
    k.dma(KVs, KV[:])
    for i, nm in enumerate(("cmpk", "cmpv", "selk", "selv")):
        k.dma(O[nm + "_s"], KV[:, i * 256:(i + 1) * 256])
    k.dma(SBS[:].rearrange("k (d g c) -> k d g c", d=17, g=4), I["n_bs_s"].rearrange("d g k c -> k d g c"))
    k.dma(SBW[:].rearrange("k (d g c) -> k d g c", d=5, g=4), I["n_bw_s"].rearrange("d g k c -> k d g c"))
    k.dma(SBC[:].rearrange("k (g c) -> k g c", g=4), I["n_bc_s"].rearrange("g k c -> k g c"))
    k.dma(cbt[0:8, 0:33], I["n_cb_s"])
    k.dma(cbt[0:8, 40:73], I["n_ft_s"])
    for g in range(4):
        k.dma(Vc[:, g, 65:98], I["n_pair_s"])
    xTs_all = sb("xTs_all", [128, 8, 128], BF16)
    k.cp(xTs_all[:], xT[:], eng="dve")
    NEW = KV[0:8, :]
    for sq in range(NS):
        k.memset(Vc[:, :, 0:64], 0.0)
        pools = (I["cmp_k"], I["cmp_v"], I["sel_k"], I["sel_v"])

        def gather(pool_ap, slot, dst_tile, sq=sq):
            P.dma("pool", dst_tile, pool_ap, reads=[pool_ap, idx], writes=[dst_tile],
                  fn=lambda e: e.indirect_dma_start(out=dst_tile, out_offset=None, in_=pool_ap,
                                                    in_offset=bass.IndirectOffsetOnAxis(ap=idx[:, sq * 16 + slot:sq * 16 + slot + 1], axis=0)))
        for pg in range(16):
            gb = GBUF[pg % 2]
            for ci in range(4):
                gather(pools[ci], pg, gb[:, ci * 256:(ci + 1) * 256])
            kv_tile(pg, (gb[:, 0:256], gb[:, 256:512]), (gb[:, 512:768], gb[:, 768:1024]), None, 128, do_cmp_block=pg)
        for wt in range(4):
            gb = GBUF[wt % 2]
            k.dma(gb[:, 0:256], I["win_k"][sq, wt * 128:(wt + 1) * 128, :])
            k.dma(gb[:, 256:512], I["win_v"][sq, wt * 128:(wt + 1) * 128, :])
            kv_tile(0, None, None, (gb[:, 0:256], gb[:, 256:512]), 128, win_slot=wt)
        k.dma(NEW, KVs[sq * 8:(sq + 1) * 8, :])
        kv_tile(16, None, (NEW[:, 512:768], NEW[:, 768:1024]), (NEW[:, 1024:1280], NEW[:, 1280:1536]), 8, win_slot=4)
        k.dma(O["wink_s"][sq, 0:504, :], I["win_k"][sq, 8:512, :])
        k.dma(O["winv_s"][sq, 0:504, :], I["win_v"][sq, 8:512, :], eng="act")
        k.dma(O["wink_s"][sq, 504:512, :], NEW[:, 1024:1280])
        k.dma(O["winv_s"][sq, 504:512, :], NEW[:, 1280:1536])
        k.cp(xT[:, :, 0:8], xTs_all[:, :, sq * 8:(sq + 1) * 8], eng="dve")
        k.dma(xin[0:8, :], src[1][sq * 8:(sq + 1) * 8, :])
        qgz(8, 8)
        s_tiles = [(kt, 128, kt) for kt in range(16)] + [(16, 8, 16)]
        w_tiles = [(kt, 128, kt) for kt in range(4)] + [(4, 8, 4)]
        attend(8, 33, s_tiles, w_tiles,
               lambda g: SBC[:, g * 32:(g + 1) * 32],
               lambda d, g: SBS[0:(8 if d == 16 else 128), d * 128 + g * 32:d * 128 + (g + 1) * 32],
               lambda d, g: SBW[0:(8 if d == 4 else 128), d * 128 + g * 32:d * 128 + (g + 1) * 32],
               None, None, None, lambda slot: slice(slot * 128, slot * 128 + (8 if slot == 16 else 128)), load_consts=False, resident=True, batched=True)
        finish(8, dst[1][sq * 8:(sq + 1) * 8, :])


def _cmask():
    m = np.zeros((128, 2048), np.float32)
    t = np.arange(128)
    for g, w in enumerate((2, 4, 8, 16)):
        m[:, g * 128:(g + 1) * 128] = (1.0 / np.minimum(w, t + 1))[None, :]
    a = np.arange(64)
    su = (a[:, None] < a[None, :]).astype(np.float32)
    ui = (a[:, None] <= a[None, :]).astype(np.float32)
    m[0:64, 512:576] = su
    m[0:64, 576:640] = ui
    m[0:64, 640:704] = su.T
    m[0:64, 704:768] = ui
    m[0:64, 768:832] = np.eye(64, dtype=np.float32)
    return m


def consts():
    import ml_dtypes
    sel = np.zeros((128, 64, 128), np.float32)
    for kk in range(128):
        sel[kk, kk % 64, (kk // 64) * 64:(kk // 64) * 64 + 64] = 1
    return {"identf": np.eye(128, dtype=np.float32), "selb": sel.reshape(128, 64 * 128).astype(ml_dtypes.bfloat16),
            "cmask": _cmask()}


def shard_inputs(inp, c, tp=TP):
    f = lambda a: np.ascontiguousarray(a)
    m = {
        "xp": f(inp["x_prompt"][c, :tp]), "xs": f(inp["x_sample"][16 * c:16 * c + 16].reshape(128, D)),
        "st_S": f(inp["state_rwkv_S"][:, 16 * c:16 * c + 16]), "st_shift": f(inp["state_rwkv_shift"][:, 16 * c:16 * c + 16]),
        "st_pool": f(inp["state_pool"][0, 16 * c:16 * c + 16]),
        "cmp_k": f(inp["cache_cmp_k"][0].reshape(-1, 256)), "cmp_v": f(inp["cache_cmp_v"][0].reshape(-1, 256)),
        "sel_k": f(inp["cache_sel_k"][0].reshape(-1, 256)), "sel_v": f(inp["cache_sel_v"][0].reshape(-1, 256)),
        "win_k": f(inp["state_win_k"][0, 16 * c:16 * c + 16].reshape(16, 512, 256)),
        "win_v": f(inp["state_win_v"][0, 16 * c:16 * c + 16].reshape(16, 512, 256)),
        "ptab": f(inp["page_table"][16 * c:16 * c + 16]).astype(np.int32),
        "a_r_k": f(inp["a_r_k"].reshape(2, D)), "b_w_in": f(inp["b_w_in"][0]), "b_w_grp": f(inp["b_w_grp"][0]),
        "b_scale": f(inp["b_scale"]), "b_w_out": f(inp["b_w_out"][0]), "c_w_in": f(inp["c_w_in"][0]),
        "c_cmp_wk": f(inp["c_cmp_wk"]), "c_cmp_wv": f(inp["c_cmp_wv"]), "c_w_out": f(inp["c_w_out"][0]),
    }
    for nm in ("ln_g", "ln_b", "a_w_in", "a_mu", "a_w0", "a_w2", "a_a0", "a_a2", "a_k_k", "a_k_a", "a_lnx_g", "a_lnx_b", "a_w_out"):
        m[nm] = f(inp[nm])
    m.update(consts())
    m.update(nsa_consts())
    return m


def nsa_consts():
    sl = 2.0 ** (-8.0 * (np.arange(16) + 1) / 16)
    c = {}

    def bias(dist, valid, g):
        K_, nq = dist.shape
        out = np.empty((K_, 4, nq), np.float32)
        for j in range(4):
            out[:, j] = np.where(valid, -sl[4 * g + j] * dist, NEG)
        return out.reshape(K_, 4 * nq)
    q = np.arange(128)[None, :]
    kk = np.arange(128)[:, None]
    n = np.arange(64)[:, None]
    bc = np.zeros((16, 4, 64, 512), np.float32)
    bs = np.zeros((16, 4, 128, 512), np.float32)
    bw = np.zeros((5, 4, 128, 512), np.float32)
    for g in range(4):
        for t in range(16):
            d = 128 * t + q - 32 * n - 31
            bc[t, g] = bias(d, d >= 0, g)
            d = 128 * t + q - kk
            bs[t, g] = bias(d, d >= 0, g)
            if t < 5:
                bw[t, g] = bias(d, (d >= 0) & (d < 512), g)
    c["n_bc_p"], c["n_bs_p"], c["n_bw_p"] = bc, bs, bw
    cb = np.zeros((16, 128, 32), np.float32)
    ft = np.zeros((16, 128, 32), np.float32)
    blk = np.arange(32)[None, :]
    for t in range(16):
        cur = ((128 * t + np.arange(128)) // 64)[:, None]
        cb[t] = (blk < cur)
        ft[t] = np.where(blk == cur, 1e9, np.where(blk > cur, -1.0, 0.0))
    c["n_cb_p"], c["n_ft_p"] = cb, ft
    c["n_pair_p"] = (np.arange(64)[:, None] // 2 == np.arange(32)[None, :]).astype(np.float32)
    c["n_eexp_p"] = (np.arange(2048)[None, :] // 64 == np.arange(32)[:, None]).astype(np.float32)
    wbm = np.zeros((128, 124), np.float32)
    for r in range(128):
        wbm[r, 60 + r // 32] = 1.0
    c["n_wbm"] = wbm
    c["n_iota"] = np.arange(128, dtype=np.float32).reshape(128, 1)
    tq = np.arange(8)[None, :]
    bcs = np.zeros((4, 64, 32), np.float32)
    bss = np.zeros((17, 4, 128, 32), np.float32)
    bws = np.zeros((5, 4, 128, 32), np.float32)
    for g in range(4):
        d = 2048 + tq - 32 * n - 31
        bcs[g] = bias(d, d >= 0, g)
        for kt in range(16):
            d = 2048 + tq - 128 * kt - kk
            bss[kt, g] = bias(d, d >= 0, g)
        d = tq - kk
        newb = bias(d, (d >= 0) & (kk < 8), g)
        bss[16, g] = newb
        for kt in range(4):
            d = 2048 + tq - (1536 + 128 * kt + kk)
            bws[kt, g] = bias(d, (d >= 0) & (d < 512), g)
        bws[4, g] = newb
    c["n_bc_s"], c["n_bs_s"], c["n_bw_s"] = bcs, bss, bws
    cbs = np.ones((8, 33), np.float32)
    cbs[:, 32] = 0
    fts = np.zeros((8, 33), np.float32)
    fts[:, 32] = 1e9
    c["n_cb_s"], c["n_ft_s"] = cbs, fts
    ps_ = np.zeros((64, 33), np.float32)
    ps_[:, :32] = c["n_pair_p"]
    c["n_pair_s"] = ps_
    ee = np.zeros((33, 17 * 128), np.float32)
    ee[:32, :2048] = c["n_eexp_p"]
    ee[32, 2048:] = 1.0
    import ml_dtypes
    c["n_eexp_s"] = ee.astype(ml_dtypes.bfloat16)
    c["n_eexp_p"] = c["n_eexp_p"].astype(ml_dtypes.bfloat16)
    hb = np.zeros((128, 240), np.float32)
    for g in range(4):
        for d in range(1, 16):
            for j in range(4):
                hb[:, g * 60 + (d - 1) * 4 + j] = -sl[4 * g + j] * 128.0 * (d - 1)
    c["n_hb"] = hb
    return c


_NC_CACHE = {}


def kernel(**inputs):
    n = 8
    npool = inputs["cache_cmp_k"].shape[1]
    key = (npool,)
    if key not in _NC_CACHE:
        _NC_CACHE[key] = build(npool=npool, tp=TP)
    nc = _NC_CACHE[key]
    in_maps = [shard_inputs(inputs, c) for c in range(n)]
    res = run_bass_kernel_spmd(nc, in_maps, core_ids=list(range(n)))
    R = res.results
    cat = lambda nm: np.stack([R[c][nm] for c in range(n)], 0)
    y_p = cat("y_p")
    y_s = np.concatenate([R[c]["y_s"].reshape(16, 8, D) for c in range(n)], 0)
    S_p = np.stack([R[c]["S_p"] for c in range(n)], 1)
    S_s = np.concatenate([R[c]["S_s"] for c in range(n)], 1)
    sh_p = np.stack([R[c]["sh_p"] for c in range(n)], 1)
    sh_s = np.concatenate([R[c]["sh_s"] for c in range(n)], 1)
    pl_p = cat("pl_p")[None]
    pl_s = np.concatenate([R[c]["pl_s"] for c in range(n)], 0)[None]
    outs = [y_p, y_s, S_p, S_s, sh_p, sh_s, pl_p, pl_s]
    for nm in ("cmpk", "cmpv", "selk", "selv"):
        outs.append(cat(nm + "_p").reshape(1, n, TP, 4, 64))
        outs.append(np.concatenate([R[c][nm + "_s"].reshape(16, 8, 4, 64) for c in range(n)], 0)[None])
    for nm in ("wink", "winv"):
        outs.append(cat(nm + "_p").reshape(1, n, 512, 4, 64))
        outs.append(np.concatenate([R[c][nm + "_s"].reshape(16, 512, 4, 64) for c in range(n)], 0)[None])
    return tuple(np.ascontiguousarray(o, dtype=np.float32) for o in outs)
```

```python
import contextlib
import numpy as np
import concourse.bass as bass
import concourse.mybir as mybir
from concourse.bass_utils import run_bass_kernel_spmd

F32 = mybir.dt.float32
BF16 = mybir.dt.bfloat16
I32 = mybir.dt.int32
ALU = mybir.AluOpType
AF = mybir.ActivationFunctionType
AX = mybir.AxisListType

import os as _os
STRICT = bool(_os.environ.get("KSTRICT"))
ENGS = ("pe", "dve", "act", "pool", "sp")
NDMA = {"sp": 12, "act": 6, "pool": 6}

D = 1024
TP = 2048
NS = 16
TS = 8
DEPTH = 4
ALPHA = (2.0 * DEPTH) ** 0.25
LN_EPS = 1e-5
A_NC = 4224
GN_EPS = 64e-5
C_NC = 3632


def _key(k):
    if isinstance(k, (str, tuple)):
        return k
    t = getattr(k, "tensor", k)
    return getattr(t, "name", str(t))


class Prog:
    def __init__(self, nc):
        self.nc = nc
        self.q = {e: [] for e in ENGS}
        self.cnt = {e: 0 for e in ENGS}
        self.known = {e: {} for e in ENGS}
        self.lastw = {}
        self.readers = {}
        self.dma_rr = {e: 0 for e in NDMA}
        self.dma_cnt = {}
        self.n_inst = 0

    def _deps(self, reads, writes):
        deps = {}

        def add(ev):
            if ev is None:
                return
            s, v = ev
            if deps.get(s, 0) < v:
                deps[s] = v
        for k in reads:
            add(self.lastw.get(k))
        for k in writes:
            add(self.lastw.get(k))
            for ev in self.readers.get(k, ()):
                add(ev)
        return deps

    def _commit(self, ev, reads, writes):
        for k in reads:
            self.readers.setdefault(k, []).append(ev)
        for k in writes:
            self.lastw[k] = ev
            self.readers[k] = []

    def _waits(self, eng, deps, compute=False):
        waits = []
        kn = self.known[eng]
        for s, v in deps.items():
            if s == "c_pe" and eng == "pe":
                continue
            if compute and not STRICT and s == "c_" + eng and eng in ("dve", "act") and v < self.cnt[eng]:
                continue
            if kn.get(s, 0) >= v:
                continue
            kn[s] = v
            waits.append((s, v))
        return waits

    def op(self, eng, fn, reads=(), writes=()):
        reads = [_key(k) for k in reads]
        writes = [_key(k) for k in writes]
        writes = writes + [r for r in reads if isinstance(r, str) and r.startswith("psb")]
        waits = self._waits(eng, self._deps(reads, writes), compute=True)
        self.cnt[eng] += 1
        ev = ("c_" + eng, self.cnt[eng])
        self.q[eng].append(("op", waits, fn, ev))
        self._commit(ev, reads, writes)
        self.n_inst += 1
        return ev

    def dma(self, eng, out, in_, reads=None, writes=None, fn=None, **kw):
        reads = [_key(k) for k in (reads if reads is not None else [in_])]
        writes = [_key(k) for k in (writes if writes is not None else [out])]
        deps = self._deps(reads, writes)
        i = self.dma_rr[eng]
        self.dma_rr[eng] = (i + 1) % NDMA[eng]
        sname = "d_%s%d" % (eng, i)
        n = self.dma_cnt.get(sname, 0)
        if n > 0 and deps.get(sname, 0) < 16 * n:
            deps[sname] = 16 * n
        waits = self._waits(eng, deps)
        self.dma_cnt[sname] = n + 1
        ev = (sname, 16 * (n + 1))
        self.q[eng].append(("dma", waits, (out, in_, kw, fn), ev))
        self._commit(ev, reads, writes)
        self.n_inst += 1
        return ev

    def barrier(self):
        for eng in ENGS:
            deps = {}
            for f in ENGS:
                if f != "sp" and f != eng and self.cnt[f] > 0:
                    deps["c_" + f] = self.cnt[f]
            for s, n in self.dma_cnt.items():
                deps[s] = 16 * n
            waits = self._waits(eng, deps)
            self.q[eng].append(("wait", waits, None, None))

    def emit(self):
        nc = self.nc
        names = ["c_" + e for e in ENGS if e != "sp"]
        for e, n in NDMA.items():
            names += ["d_%s%d" % (e, i) for i in range(n)]
        with contextlib.ExitStack() as st:
            sems = {nm: st.enter_context(nc.semaphore(nm)) for nm in names}
            block = st.enter_context(nc.Block())

            def run(eng):
                def body(e):
                    for kind, waits, payload, ev in self.q[eng]:
                        for s, v in waits:
                            e.wait_ge(sems[s], v)
                        if kind == "op":
                            payload(e).then_inc(sems[ev[0]], 1)
                        elif kind == "dma":
                            out, in_, kw, fn = payload
                            if fn is not None:
                                fn(e).then_inc(sems[ev[0]], 16)
                            else:
                                e.dma_start(out=out, in_=in_, **kw).then_inc(sems[ev[0]], 16)
                    if eng == "sp":
                        for sname, n in self.dma_cnt.items():
                            e.wait_ge(sems[sname], 16 * n)
                        for en in ENGS:
                            if en != "sp" and self.cnt[en] > 0:
                                e.wait_ge(sems["c_" + en], self.cnt[en])
                return body

            block.sync(run("sp"))
            block.tensor(run("pe"))
            block.vector(run("dve"))
            block.scalar(run("act"))
            block.gpsimd(run("pool"))


def _aps(*xs):
    return [x for x in xs if x is not None and not isinstance(x, (int, float))]


class K:
    def __init__(self, P):
        self.P = P

    def mm(self, out, lhsT, rhs, start=True, stop=True):
        self.P.op("pe", lambda e: e.matmul(out, lhsT=lhsT, rhs=rhs, start=start, stop=stop),
                  reads=[lhsT, rhs], writes=[out])

    def tr(self, out, in_, ident):
        self.P.op("pe", lambda e: e.transpose(out, in_, ident), reads=[in_, ident], writes=[out])

    def tt(self, out, a, b, op, eng="dve"):
        self.P.op(eng, lambda e: e.tensor_tensor(out=out, in0=a, in1=b, op=op), reads=[a, b], writes=[out])

    def ts(self, out, a, s1, op0, s2=None, op1=None, eng="dve"):
        if op1 is None:
            fn = lambda e: e.tensor_scalar(out=out, in0=a, scalar1=s1, scalar2=None, op0=op0)
        else:
            fn = lambda e: e.tensor_scalar(out=out, in0=a, scalar1=s1, scalar2=s2, op0=op0, op1=op1)
        self.P.op(eng, fn, reads=_aps(a, s1, s2), writes=[out])

    def stt(self, out, a, s, b, op0, op1, eng="dve"):
        self.P.op(eng, lambda e: e.scalar_tensor_tensor(out=out, in0=a, scalar=s, in1=b, op0=op0, op1=op1),
                  reads=_aps(a, s, b), writes=[out])

    def red(self, out, in_, op=ALU.add, negate=False, axis=AX.X):
        self.P.op("dve", lambda e: e.tensor_reduce(out=out, in_=in_, axis=axis, op=op, negate=negate),
                  reads=[in_], writes=[out])

    def cp(self, out, in_, eng="dve"):
        if eng == "act":
            self.P.op("act", lambda e: e.copy(out, in_), reads=[in_], writes=[out])
        else:
            self.P.op(eng, lambda e: e.tensor_copy(out, in_), reads=[in_], writes=[out])

    def act(self, out, in_, func, bias=None, scale=None, accum=None):
        kw = {}
        if bias is not None:
            kw["bias"] = bias
        if scale is not None:
            kw["scale"] = scale
        if accum is not None:
            kw["accum_out"] = accum
        self.P.op("act", lambda e: e.activation(out=out, in_=in_, func=func, **kw),
                  reads=_aps(in_, bias, scale), writes=_aps(out, accum))

    def recip(self, out, in_):
        self.P.op("dve", lambda e: e.reciprocal(out, in_), reads=[in_], writes=[out])

    def memset(self, ap, v, eng="pool"):
        self.P.op(eng, lambda e: e.memset(ap, v), writes=[ap])

    def dma(self, out, in_, eng="sp", **kw):
        self.P.dma(eng, out, in_, **kw)


def bc(ap, shape):
    return ap.to_broadcast(shape)


class Ctx:
    pass


def build(npool=2560, tp=TP, layers=(0, 1, 2, 3), dbg=False):
    nc = bass.Bass("TRN2", target_bir_lowering=False)
    C = Ctx()
    C.nc = nc
    C.tp = tp
    P = Prog(nc)
    k = K(P)
    C.P, C.k = P, k

    def din(name, shape, dt=F32):
        return nc.dram_tensor(name, list(shape), dt, kind="ExternalInput").ap()

    def dout(name, shape):
        return nc.dram_tensor(name, list(shape), F32, kind="ExternalOutput").ap()

    def dscr(name, shape, dt=F32):
        return nc.dram_tensor(name, list(shape), dt, kind="Internal").ap()

    I = {}
    for nm, shp in [("xp", (tp, D)), ("xs", (128, D)), ("st_S", (2, NS, 16, 64, 64)), ("st_shift", (2, NS, A_NC)),
                    ("st_pool", (NS, 15, D)), ("cmp_k", (npool * 128, 256)), ("cmp_v", (npool * 128, 256)),
                    ("sel_k", (npool * 128, 256)), ("sel_v", (npool * 128, 256)), ("win_k", (NS, 512, 256)),
                    ("win_v", (NS, 512, 256)), ("ln_g", (4, D)), ("ln_b", (4, D)), ("a_w_in", (2, D, A_NC)),
                    ("a_mu", (2, A_NC)), ("a_w0", (2, D)), ("a_w2", (2, 64, D)), ("a_a0", (2, D)), ("a_a2", (2, 64, D)),
                    ("a_k_k", (2, D)), ("a_k_a", (2, D)), ("a_r_k", (2, D)), ("a_lnx_g", (2, D)), ("a_lnx_b", (2, D)),
                    ("a_w_out", (2, D, D)), ("b_w_in", (D, 2 * D)), ("b_w_grp", (4, 256, 256)), ("b_scale", (1, D)),
                    ("b_w_out", (D, D)), ("c_w_in", (D, C_NC)), ("c_cmp_wk", (1, 32)), ("c_cmp_wv", (1, 32)),
                    ("c_w_out", (D, D)), ("identf", (128, 128)), ("cmask", (128, 2048))]:
        I[nm] = din(nm, shp)
    I["ptab"] = din("ptab", (NS, 16), I32)
    for nm, shp in [("n_bc_p", (16, 4, 64, 512)), ("n_bs_p", (16, 4, 128, 512)), ("n_bw_p", (5, 4, 128, 512)),
                    ("n_cb_p", (16, 128, 32)), ("n_ft_p", (16, 128, 32)), ("n_pair_p", (64, 32)),
                    ("n_wbm", (128, 124)), ("n_iota", (128, 1)), ("n_bc_s", (4, 64, 32)), ("n_bs_s", (17, 4, 128, 32)),
                    ("n_bw_s", (5, 4, 128, 32)), ("n_cb_s", (8, 33)), ("n_ft_s", (8, 33)), ("n_pair_s", (64, 33)),
                    ("n_hb", (128, 240))]:
        I[nm] = din(nm, shp)
    I["n_eexp_p"] = din("n_eexp_p", (32, 2048), BF16)
    I["n_eexp_s"] = din("n_eexp_s", (33, 17 * 128), BF16)
    I["selb"] = din("selb", (128, 64 * 128), BF16)
    O = {}
    for nm, shp in [("y_p", (tp, D)), ("y_s", (128, D)), ("S_p", (2, 16, 64, 64)), ("S_s", (2, NS, 16, 64, 64)),
                    ("sh_p", (2, A_NC)), ("sh_s", (2, NS, A_NC)), ("pl_p", (15, D)), ("pl_s", (NS, 15, D)),
                    ("cmpk_p", (tp, 256)), ("cmpk_s", (128, 256)), ("cmpv_p", (tp, 256)), ("cmpv_s", (128, 256)),
                    ("selk_p", (tp, 256)), ("selk_s", (128, 256)), ("selv_p", (tp, 256)), ("selv_s", (128, 256)),
                    ("wink_p", (512, 256)), ("wink_s", (NS, 512, 256)), ("winv_p", (512, 256)), ("winv_s", (NS, 512, 256))]:
        O[nm] = dout(nm, shp)
    if dbg:
        O["dbg_p"] = dout("dbg_p", (tp, D))
        O["dbg_s"] = dout("dbg_s", (128, D))
    C.I, C.O = I, O
    xa_p, xa_s = dscr("xa_p", (tp, D)), dscr("xa_s", (128, D))
    xb_p, xb_s = dscr("xb_p", (tp, D)), dscr("xb_s", (128, D))
    C.wbf = dscr("wbf", (128, 8, A_NC), BF16)
    C.kvs_scr = dscr("kvs_scr", (128, 1536))

    with contextlib.ExitStack() as gst:
        C.identf = gst.enter_context(nc.sbuf_tensor("identf_sb", [128, 128], F32))
        C.identb = gst.enter_context(nc.sbuf_tensor("identb_sb", [128, 128], BF16))
        C.ps = [gst.enter_context(nc.psum_tensor("psb%d" % i, [128, 512], F32)) for i in range(8)]
        k.dma(C.identf[:], I["identf"])
        k.cp(C.identb[:], C.identf[:])
        chain = [(I["xp"], I["xs"]), (xa_p, xa_s), (xb_p, xb_s), (xa_p, xa_s), (O["y_p"], O["y_s"])]
        for L in range(DEPTH):
            if L not in layers:
                continue
            src, dst = chain[L], chain[L + 1]
            if L == max(layers) and dbg:
                dst = (O["dbg_p"], O["dbg_s"])
            P.barrier()
            with contextlib.ExitStack() as lst:
                if L % 3 == 0:
                    rwkv_layer(C, lst, L // 3, L, src, dst)
                elif L % 3 == 1:
                    pool_layer(C, lst, L, src, dst)
                else:
                    nsa_layer(C, lst, L, src, dst)
                P.barrier()
        P.emit()
    return nc


def ln_tail(C, R, npart, L, dst_rows, T1, crow):
    k, I = C.k, C.I
    st = C.lnst
    k.red(st[0:npart, 0:1], R[0:npart, :])
    k.ts(st[0:npart, 1:2], st[0:npart, 0:1], 1.0 / D, ALU.mult)
    k.ts(R[0:npart, :], R[0:npart, :], st[0:npart, 1:2], ALU.subtract)
    k.tt(T1[0:npart, :], R[0:npart, :], R[0:npart, :], ALU.mult)
    k.red(st[0:npart, 2:3], T1[0:npart, :])
    k.act(st[0:npart, 3:4], st[0:npart, 2:3], AF.Sqrt, bias=C.epsln[0:npart, :], scale=1.0 / D)
    k.recip(st[0:npart, 4:5], st[0:npart, 3:4])
    k.ts(R[0:npart, :], R[0:npart, :], st[0:npart, 4:5], ALU.mult)
    k.dma(crow[0][0:npart, :], I["ln_g"][L:L + 1, :].partition_broadcast(npart), eng="sp")
    k.tt(R[0:npart, :], R[0:npart, :], crow[0][0:npart, :], ALU.mult)
    k.dma(crow[1][0:npart, :], I["ln_b"][L:L + 1, :].partition_broadcast(npart), eng="sp")
    k.tt(R[0:npart, :], R[0:npart, :], crow[1][0:npart, :], ALU.add)
    k.dma(dst_rows, R[0:npart, :], eng="pool")


def rwkv_layer(C, lst, li, L, src, dst):
    nc, P, k, I, O = C.nc, C.P, C.k, C.I, C.O
    tp = C.tp
    sb = lambda n, s, d=F32: lst.enter_context(nc.sbuf_tensor("a%d_" % L + n, list(s), d))
    ps = C.ps
    Wo = sb("Wo", [128, 8, 1024], BF16)
    Wll = sb("Wll", [128, 8, 128], BF16)
    WG = [sb("WG0", [128, 8, 1024], BF16)]
    W2A2 = sb("W2A2", [128, 1024])
    mucol = sb("mucol", [128, 9])
    SEL = sb("SEL", [128, 64, 128], BF16)
    xin2 = sb("xin2", [128, 1024])
    xin = sb("xin", [64, 1024])
    xTd = sb("xTd", [128, 8, 128], BF16)
    xTsd = sb("xTsd", [128, 8, 128], BF16)
    Pt = sb("Pt", [128, 1024])
    PSt = sb("PSt", [128, 1024])
    PM = {g: sb("PM" + g, [128, 1024]) for g in "rkvz"}
    crow = [sb("crow%d" % i, [128, 1024]) for i in range(2)]
    At = sb("At", [128, 1024])
    KP = sb("KP", [128, 1024])
    T1 = sb("T1", [128, 1024])
    T2 = sb("T2", [128, 1024])
    XRf = sb("XRf", [128, 512])
    XRr = sb("XRr", [128, 512])
    XR = {x: [sb("XR%s%d" % (x, j), [128, 512], BF16) for j in range(2)] for x in ("kk", "w", "ka", "k", "r")}
    va = sb("va", [128, 8, 64])
    vs = sb("vs", [128, 8, 64])
    vT = sb("vT", [128, 8, 64])
    lla = sb("lla", [128, 128])
    llb = sb("llb", [128, 128])
    LLt = sb("LLt", [128, 128])
    YT = sb("YT", [128, 8, 64])
    S = sb("S", [128, 8, 64])
    t1 = sb("t1", [128, 8, 64])
    t2 = sb("t2", [128, 8, 64])
    t3 = sb("t3", [128, 8, 64])
    sa = sb("sa", [128, 8])
    st16 = sb("st16", [128, 5, 16])
    bon = sb("bon", [128, 16])
    G = sb("G", [64, 1024], BF16)
    gT = sb("gT", [128, 8, 64], BF16)
    C.lnst = sb("lnst", [128, 8])
    C.epsln = sb("epsln", [128, 1])
    epsgn = sb("epsgn", [128, 1])
    eps24 = sb("eps24", [128, 1])
    k.memset(C.epsln[:], LN_EPS)
    k.memset(epsgn[:], GN_EPS)
    k.memset(eps24[:], 0.0)

    w_in = I["a_w_in"][li].rearrange("(c p) n -> p c n", p=128)
    for j in range(A_NC // 128):
        stg = Pt[:].rearrange("p (c n) -> p c n", c=8) if j % 2 == 0 else PSt[:].rearrange("p (c n) -> p c n", c=8)
        stgb = (T1 if j % 2 == 0 else T2)[:].bitcast(BF16)[:, 0:1024].rearrange("p (c n) -> p c n", c=8)
        k.dma(stg, w_in[:, :, j * 128:(j + 1) * 128], eng="sp" if j % 2 == 0 else "act")
        k.cp(stgb, stg, eng="pool" if j % 2 == 0 else "act")
        k.dma(C.wbf[:, :, j * 128:(j + 1) * 128], stgb, eng="sp")
    w_out = I["a_w_out"][li].rearrange("(c p) n -> p c n", p=128)
    for j in range(8):
        stg = Pt[:].rearrange("p (c n) -> p c n", c=8) if j % 2 == 0 else PSt[:].rearrange("p (c n) -> p c n", c=8)
        k.dma(stg, w_out[:, :, j * 128:(j + 1) * 128], eng="sp" if j % 2 == 0 else "act")
        k.cp(Wo[:, :, j * 128:(j + 1) * 128], stg, eng="pool" if j % 2 == 0 else "act")
    k.dma(Wll[:], C.wbf[:, :, 4096:4224])
    k.dma(W2A2[0:64, :], I["a_w2"][li])
    k.dma(W2A2[64:128, :], I["a_a2"][li])
    k.dma(mucol[:, 0:8], I["a_mu"][li, 2048:3072].rearrange("(c p) -> p c", p=128), allow_slow_non_contiguous=True)
    k.dma(mucol[:, 8:9], I["a_mu"][li, 4096:4224].rearrange("(c p) -> p c", p=128), allow_slow_non_contiguous=True)
    k.dma(SEL[:], I["selb"].rearrange("p (t m) -> p t m", m=128))
    k.memset(S[:], 0.0)
    EPI = sb("EPI", [64, 1024])
    EPN = sb("EPN", [64, 1024])
    EPX = sb("EPX", [64, 1024])
    FMAR = sb("FMAR", [64, 8, 128])
    FMB = sb("FMB", [64, 8, 64])
    FMK = sb("FMK", [64, 8, 64])
    GB = sb("GB", [64, 8, 128])
    GK = sb("GK", [64, 8, 128])
    PQ = [sb("PQ%d" % i, [64, 8, 64]) for i in range(4)]
    Tm = sb("Tm", [64, 8, 64])
    XT = sb("XT", [64, 8, 64])
    UT = sb("UT", [64, 8, 64])
    ST = sb("ST", [64, 16, 64])
    PCc = sb("PCc", [64, 16])
    MK = sb("MK", [64, 320])
    k.dma(MK[:], I["cmask"][0:64, 512:832])
    MASKAR = MK[:, 0:128]
    MASKNT = MK[:, 128:192]
    TRI = MK[:, 192:256]
    IDN = MK[:, 256:320]
    k.memset(ST[:], 0.0)
    pbi = [0]

    def bank():
        pbi[0] = (pbi[0] + 1) % 8
        return ps[pbi[0]]
    import os
    STOP = int(os.environ.get('STOPAT', '99'))
    if STOP <= 1:
        return

    cri = [0]

    def jrow(src_row, npart=128):
        t = crow[cri[0] % 2]
        cri[0] += 1
        k.dma(t[0:npart, :], src_row.partition_broadcast(npart), eng="sp")
        return t

    def h4(t):
        return t[:].rearrange("p (a b j) -> p a b j", a=8, b=2)

    def toxr(X, name):
        X4 = h4(X)
        o3 = XRf[:].rearrange("p (a j) -> p a j", a=8)
        k.cp(o3[0:64], X4[0:64, :, 0, :], eng="act")
        k.cp(o3[64:128], X4[64:128, :, 1, :], eng="act")
        k.cp(XR[name][0][:], XRf[:], eng="pool")
        k.tt(XRr[:], XRf[:], XR[name][0][:], ALU.subtract, eng="pool")
        k.cp(XR[name][1][:], XRr[:], eng="pool")

    ntile_p = tp // 64
    import os
    tiles = [("p", n) for n in range(ntile_p)] + ([("s", 0), ("s", 1)] if not os.environ.get("NOSAMPLE") else [])
    wg_i = [0]
    for kind, n in tiles:
        srcx = src[0] if kind == "p" else src[1]
        dstx = dst[0] if kind == "p" else dst[1]
        r0 = n * 64
        if r0 == 0:
            k.memset(xin2[0:1, :], 0.0)
            k.dma(xin2[1:64, :], srcx[0:63, :])
        else:
            k.dma(xin2[0:64, :], srcx[r0 - 1:r0 + 63, :])
        k.dma(xin[:], srcx[r0:r0 + 64, :], eng="sp")
        for b in range(2):
            for c in range(4):
                k.tr(ps[b][:, c * 64:(c + 1) * 64], xin[0:64, (4 * b + c) * 128:(4 * b + c + 1) * 128], C.identf[0:64, 0:64])
            for c in range(4):
                k.tr(ps[b][:, 256 + c * 64:256 + (c + 1) * 64], xin2[0:64, (4 * b + c) * 128:(4 * b + c + 1) * 128], C.identf[0:64, 0:64])
            pv = ps[b][:, 0:256].rearrange("p (c t) -> p c t", c=4)
            pw = ps[b][:, 256:512].rearrange("p (c t) -> p c t", c=4)
            k.cp(xTd[:, 4 * b:4 * b + 4, 0:64], pv, eng="act")
            k.cp(xTd[:, 4 * b:4 * b + 4, 64:128], pv, eng="dve")
            k.cp(xTsd[:, 4 * b:4 * b + 4, 0:64], pw, eng="act")
            k.cp(xTsd[:, 4 * b:4 * b + 4, 64:128], pw, eng="dve")
        if STOP <= 2:
            return
        last_rows = []
        if kind == "p" and n == ntile_p - 1:
            last_rows = [(63, O["sh_p"][li])]
        if kind == "s":
            last_rows = [(sl * 8 + 7, O["sh_s"][li, n * 8 + sl]) for sl in range(8)]
        for gi, g in enumerate("rkvz"):
            wg = WG[0]
            wg_i[0] += 1
            k.dma(wg[:], C.wbf[:, :, gi * 1024:(gi + 1) * 1024], eng="sp")
            for hf in range(2):
                cols = slice(hf * 512, (hf + 1) * 512)
                for c in range(8):
                    k.mm(ps[2][:], xTd[:, c, :], wg[:, c, cols], start=(c == 0), stop=(c == 7))
                for c in range(8):
                    k.mm(ps[3][:], xTsd[:, c, :], wg[:, c, cols], start=(c == 0), stop=(c == 7))
                k.cp(Pt[:, cols], ps[2][:], eng="act")
                k.cp(PSt[:, cols], ps[3][:], eng="act")
            if g == "v" and kind == "s":
                for c in range(8):
                    for dc in range(8):
                        k.mm(ps[2][:, c * 64:(c + 1) * 64], wg[:, dc, c * 128:(c + 1) * 128], xTd[:, dc, 0:64],
                             start=(dc == 0), stop=(dc == 7))
                for c in range(8):
                    for dc in range(8):
                        k.mm(ps[3][:, c * 64:(c + 1) * 64], wg[:, dc, c * 128:(c + 1) * 128], xTsd[:, dc, 0:64],
                             start=(dc == 0), stop=(dc == 7))
                k.cp(va[:].rearrange("p c t -> p (c t)"), ps[2][:], eng="act")
                k.cp(vs[:].rearrange("p c t -> p (c t)"), ps[3][:], eng="act")
                if kind == "s":
                    for sl in range(8):
                        k.dma(vs[:, :, sl * 8], I["st_shift"][li, n * 8 + sl, 2048:3072].rearrange("(c p) -> p c", p=128), eng="sp", allow_slow_non_contiguous=True)
                k.tt(vs[:], vs[:], va[:], ALU.subtract)
                k.tt(vs[:], vs[:], mucol[:, 0:8].unsqueeze(2).to_broadcast([128, 8, 64]), ALU.mult)
                k.tt(vT[:], vs[:], va[:], ALU.add)
            if kind == "s":
                for sl in range(8):
                    for hh in range(2):
                        k.dma(PSt[hh * 64 + sl * 8:hh * 64 + sl * 8 + 1, :],
                              I["st_shift"][li, n * 8 + sl:n * 8 + sl + 1, gi * 1024:(gi + 1) * 1024], eng="sp")
            for (row, dap) in last_rows:
                k.dma(dap[gi * 1024:(gi + 1) * 1024].unsqueeze(0), Pt[row:row + 1, :], eng="sp")
            mur = jrow(I["a_mu"][li:li + 1, gi * 1024:(gi + 1) * 1024])
            k.tt(PSt[:], PSt[:], Pt[:], ALU.subtract)
            k.tt(PSt[:], PSt[:], mur[:], ALU.mult)
            k.tt(PM[g][:], PSt[:], Pt[:], ALU.add)
        if STOP <= 3:
            return
        for c in range(8):
            k.mm(ps[2][:, 0:128], Wll[:, c, :], xTd[:, c, :], start=(c == 0), stop=(c == 7))
        for c in range(8):
            k.mm(ps[3][:, 0:128], Wll[:, c, :], xTsd[:, c, :], start=(c == 0), stop=(c == 7))
        k.cp(lla[:], ps[2][:, 0:128], eng="act")
        k.cp(llb[:], ps[3][:, 0:128], eng="act")
        if kind == "s":
            for sl in range(8):
                for hh in range(2):
                    k.dma(llb[:, hh * 64 + sl * 8:hh * 64 + sl * 8 + 1],
                          I["st_shift"][li, n * 8 + sl, 4096:4224].rearrange("(c p) -> p c", p=128), eng="sp", allow_slow_non_contiguous=True)
        for (row, dap) in last_rows:
            k.dma(dap[4096:4224].rearrange("(c p) -> p c", p=128), lla[:, row:row + 1], eng="sp", allow_slow_non_contiguous=True)
        k.tt(llb[:], llb[:], lla[:], ALU.subtract)
        k.stt(LLt[:], llb[:], mucol[:, 8:9], lla[:], ALU.mult, ALU.add)
        k.act(LLt[0:64, :], LLt[0:64, :], AF.Tanh)
        if STOP <= 4:
            return
        for hf in range(2):
            cols = slice(hf * 512, (hf + 1) * 512)
            k.mm(ps[2 + hf][:], LLt[0:64, :], W2A2[0:64, cols])
            k.mm(ps[4 + hf][:], LLt[64:128, :], W2A2[64:128, cols])
        w0r = jrow(I["a_w0"][li:li + 1, :])
        for hf in range(2):
            cols = slice(hf * 512, (hf + 1) * 512)
            k.tt(T1[:, cols], ps[2 + hf][:], w0r[:, cols], ALU.add)
        k.act(T1[:], T1[:], AF.Sigmoid)
        CC = float(np.exp(-0.5))
        if kind == "p":
            for hf in range(2):
                cols = slice(hf * 512, (hf + 1) * 512)
                k.mm(ps[6 + hf][0:64, :], TRI, T1[0:64, cols])
                k.act(EPI[:, cols], ps[6 + hf][0:64, :], AF.Exp, scale=-CC)
                k.act(EPN[:, cols], ps[6 + hf][0:64, :], AF.Exp, scale=CC)
                k.tt(EPX[:, cols], ps[6 + hf][0:64, :], T1[0:64, cols], ALU.subtract)
            k.act(EPX[:], EPX[:], AF.Exp, scale=-CC)
            for h in range(16):
                k.mm(ps[2][0:64, h:h + 1], EPI[:, h * 64:(h + 1) * 64], C.identf[0:64, 63:64])
            k.cp(PCc[:], ps[2][0:64, 0:16], eng="act")
        else:
            k.act(T1[:], T1[:], AF.Exp, scale=-CC)
            toxr(T1, "w")
        a0r = jrow(I["a_a0"][li:li + 1, :])
        for hf in range(2):
            cols = slice(hf * 512, (hf + 1) * 512)
            k.tt(At[:, cols], ps[4 + hf][:], a0r[:, cols], ALU.add)
        k.act(At[:], At[:], AF.Sigmoid)
        kkr = jrow(I["a_k_k"][li:li + 1, :])
        k.tt(T1[:], PM["k"][:], kkr[:], ALU.mult)
        k.tt(T2[:], T1[:], T1[:], ALU.mult)
        k.red(st16[:, 0, :], T2[:].rearrange("p (h j) -> p h j", h=16))
        k.ts(st16[:, 0, :], st16[:, 0, :], 1e-24, ALU.max)
        k.act(st16[:, 1, :], st16[:, 0, :], AF.Sqrt)
        k.recip(st16[:, 2, :], st16[:, 1, :])
        k.tt(T1[:].rearrange("p (h j) -> p h j", h=16), T1[:].rearrange("p (h j) -> p h j", h=16),
             st16[:, 2, :].unsqueeze(2).to_broadcast([128, 16, 64]), ALU.mult)
        if kind == "s":
            toxr(T1, "kk")
        k.tt(T2[:], T1[:], At[:], ALU.mult)
        if kind == "s":
            toxr(T2, "ka")
        kar = jrow(I["a_k_a"][li:li + 1, :])
        k.stt(PSt[:], At[:], -1.0, kar[:], ALU.add, ALU.mult)
        k.stt(KP[:], PSt[:], 1.0, PM["k"][:], ALU.add, ALU.mult)
        if kind == "s":
            toxr(KP, "k")
            toxr(PM["r"], "r")
        rkr = jrow(I["a_r_k"][li:li + 1, :])
        k.tt(Pt[:], PM["r"][:], rkr[:], ALU.mult)
        k.tt(Pt[:], Pt[:], KP[:], ALU.mult)
        k.red(bon[:], Pt[:].rearrange("p (h j) -> p h j", h=16))
        if kind == "p":
            k.stt(EPX[:], T1[0:64, :], -1.0, EPX[:], ALU.mult, ALU.mult)
            k.tt(T2[0:64, :], T2[0:64, :], EPN[:], ALU.mult)
            k.tt(EPN[:], KP[0:64, :], EPN[:], ALU.mult)
            k.tt(EPI[:], PM["r"][0:64, :], EPI[:], ALU.mult)

        if STOP <= 5:
            return
        def step(Sx, tl):
            bks = {}
            for bi, x in enumerate(("kk", "w", "ka", "k", "r")):
                bk = ps[3 + bi] if bi < 5 else None
                k.mm(bk[:], SEL[:, tl, :], XR[x][0][:], start=True, stop=False)
                k.mm(bk[:], SEL[:, tl, :], XR[x][1][:], start=False, stop=True)
                bks[x] = bk[:].rearrange("p (a j) -> p a j", a=8)
            k.tt(t1[:], Sx, bks["kk"], ALU.mult)
            k.tt(t3[:], bks["k"], vT[:, :, tl:tl + 1].to_broadcast([128, 8, 64]), ALU.mult)
            k.red(sa[:], t1[:], negate=True)
            k.tt(Sx, Sx, bks["w"], ALU.mult)
            k.tt(t2[:], bks["ka"], sa[:].unsqueeze(2).to_broadcast([128, 8, 64]), ALU.mult)
            k.tt(Sx, Sx, t3[:], ALU.add)
            k.tt(Sx, Sx, t2[:], ALU.add)
            k.tt(t1[:], Sx, bks["r"], ALU.mult)
            k.red(YT[:, :, tl], t1[:])

        if kind == "p":
            i64 = C.identf[0:64, 0:64]
            for hh in range(2):
                H0 = hh * 8
                bA = [bank(), bank()]
                bB, bK = bank(), bank()
                for hl in range(8):
                    hc = slice((H0 + hl) * 64, (H0 + hl + 1) * 64)
                    o = (hl % 4) * 128
                    k.tr(bA[hl // 4][0:64, o:o + 64], EPX[:, hc], i64)
                    k.tr(bA[hl // 4][0:64, o + 64:o + 128], EPI[:, hc], i64)
                    k.tr(bB[0:64, hl * 64:(hl + 1) * 64], T2[0:64, hc], i64)
                    k.tr(bK[0:64, hl * 64:(hl + 1) * 64], EPN[:, hc], i64)
                k.cp(FMAR[:, 0:4, :].rearrange("p a b -> p (a b)"), bA[0][0:64, :], eng="act")
                k.cp(FMAR[:, 4:8, :].rearrange("p a b -> p (a b)"), bA[1][0:64, :], eng="act")
                k.cp(FMB[:].rearrange("p a b -> p (a b)"), bB[0:64, :], eng="dve")
                k.cp(FMK[:].rearrange("p a b -> p (a b)"), bK[0:64, :], eng="dve")
                for (Gd, FMl) in ((GB, FMB), (GK, FMK)):
                    bb = [bank(), bank()]
                    for hl in range(8):
                        o = (hl % 4) * 128
                        k.mm(bb[hl // 4][0:64, o:o + 128], FMl[:, hl, :], FMAR[:, hl, :])
                    for q in range(2):
                        k.tt(Gd[:, 4 * q:4 * q + 4, :], bb[q][0:64, :].rearrange("p (a b) -> p a b", a=4),
                             MASKAR.unsqueeze(1).to_broadcast([64, 4, 128]), ALU.mult)
                bq = bank()
                for hl in range(8):
                    k.mm(bq[0:64, hl * 64:(hl + 1) * 64], FMAR[:, hl, 0:64], FMB[:, hl, :])
                k.tt(PQ[1][:], bq[0:64, :].rearrange("p (a b) -> p a b", a=8), MASKNT.unsqueeze(1).to_broadcast([64, 8, 64]), ALU.mult)
                k.tt(Tm[:], GB[:, :, 0:64], IDN.unsqueeze(1).to_broadcast([64, 8, 64]), ALU.add)
                Pc, Qc = GB[:, :, 0:64], PQ[1]
                for lv in range(5):
                    Pn, Qn = PQ[2 * ((lv + 1) % 2)], PQ[2 * ((lv + 1) % 2) + 1]
                    if lv < 4:
                        bp = bank()
                        for hl in range(8):
                            k.mm(bp[0:64, hl * 64:(hl + 1) * 64], Qc[:, hl, :], Pc[:, hl, :])
                    bq = bank()
                    for hl in range(8):
                        k.mm(bq[0:64, hl * 64:(hl + 1) * 64], Pc[:, hl, :], Qc[:, hl, :])
                    if lv < 4:
                        k.cp(Pn[:].rearrange("p a b -> p (a b)"), bp[0:64, :], eng="act")
                    k.cp(Qn[:].rearrange("p a b -> p (a b)"), bq[0:64, :], eng="act")
                    bt = bank()
                    for hl in range(8):
                        k.mm(bt[0:64, hl * 64:(hl + 1) * 64], Qn[:, hl, :], Tm[:, hl, :])
                    k.tt(Tm[:], Tm[:], bt[0:64, :].rearrange("p (a b) -> p a b", a=8), ALU.add)
                    Pc, Qc = Pn, Qn
                bx = bank()
                for hl in range(8):
                    hc = slice((H0 + hl) * 64, (H0 + hl + 1) * 64)
                    k.mm(bx[0:64, hl * 64:(hl + 1) * 64], FMAR[:, hl, 0:64], ST[:, H0 + hl, :], start=True, stop=False)
                    k.mm(bx[0:64, hl * 64:(hl + 1) * 64], GK[:, hl, 0:64], PM["v"][0:64, hc], start=False, stop=True)
                k.cp(XT[:].rearrange("p a b -> p (a b)"), bx[0:64, :], eng="act")
                bu = bank()
                for hl in range(8):
                    k.mm(bu[0:64, hl * 64:(hl + 1) * 64], Tm[:, hl, :], XT[:, hl, :])
                k.cp(UT[:].rearrange("p a b -> p (a b)"), bu[0:64, :], eng="act")
                by = bank()
                for hl in range(8):
                    hc = slice((H0 + hl) * 64, (H0 + hl + 1) * 64)
                    k.mm(by[0:64, hl * 64:(hl + 1) * 64], FMAR[:, hl, 64:128], ST[:, H0 + hl, :], start=True, stop=False)
                    k.mm(by[0:64, hl * 64:(hl + 1) * 64], GB[:, hl, 64:128], UT[:, hl, :], start=False, stop=False)
                    k.mm(by[0:64, hl * 64:(hl + 1) * 64], GK[:, hl, 64:128], PM["v"][0:64, hc], start=False, stop=True)
                k.cp(KP[0:64, hh * 512:(hh + 1) * 512], by[0:64, :], eng="act")
                bs = bank()
                for hl in range(8):
                    hc = slice((H0 + hl) * 64, (H0 + hl + 1) * 64)
                    k.mm(bs[0:64, hl * 64:(hl + 1) * 64], T2[0:64, hc], UT[:, hl, :], start=True, stop=False)
                    k.mm(bs[0:64, hl * 64:(hl + 1) * 64], EPN[:, hc], PM["v"][0:64, hc], start=False, stop=True)
                k.tt(ST[:, H0:H0 + 8, :], ST[:, H0:H0 + 8, :], bs[0:64, :].rearrange("p (a b) -> p a b", a=8), ALU.add)
                k.tt(ST[:, H0:H0 + 8, :], ST[:, H0:H0 + 8, :], PCc[:, H0:H0 + 8].unsqueeze(2).to_broadcast([64, 8, 64]), ALU.mult)
            if n == ntile_p - 1:
                for q in range(2):
                    bo = bank()
                    for hl in range(8):
                        k.tr(bo[0:64, hl * 64:(hl + 1) * 64], ST[:, q * 8 + hl, :], i64)
                    k.cp(EPI[:, q * 512:(q + 1) * 512], bo[0:64, :], eng="act")
                k.dma(O["S_p"][li].rearrange("h i j -> i h j"), EPI[:].rearrange("p (h j) -> p h j", h=16))
        else:
            for sl in range(8):
                sq = n * 8 + sl
                k.dma(t3[:], I["st_S"][li, sq].rearrange("(a b) i j -> (b i) a j", b=2))
                Sx = KP[:, 0:512].rearrange("p (a j) -> p a j", a=8)
                k.cp(Sx, t3[:], eng="act")
                for t in range(8):
                    step(Sx, sl * 8 + t)
                k.dma(O["S_s"][li, sq].rearrange("(a b) i j -> (b i) a j", b=2), Sx)

        if STOP <= 6:
            return
        Y = T1
        if kind == "p":
            k.cp(Y[0:64, :], KP[0:64, :], eng="pool")
        else:
            for c in range(8):
                k.tr(ps[c // 4][0:64, (c % 4) * 128:(c % 4 + 1) * 128], YT[:, c, :], C.identf[:, :])
            k.cp(Y[0:64, 0:512], ps[0][0:64, :], eng="act")
            k.cp(Y[0:64, 512:1024], ps[1][0:64, :], eng="act")
        Y3 = Y[0:64, :].rearrange("p (h j) -> p h j", h=16)
        k.red(st16[0:64, 0, :], Y3)
        k.ts(st16[0:64, 0, :], st16[0:64, 0, :], 1.0 / 64, ALU.mult)
        k.tt(Y3, Y3, st16[0:64, 0, :].unsqueeze(2).to_broadcast([64, 16, 64]), ALU.subtract)
        k.tt(T2[0:64, :], Y[0:64, :], Y[0:64, :], ALU.mult)
        k.red(st16[0:64, 1, :], T2[0:64, :].rearrange("p (h j) -> p h j", h=16))
        k.act(st16[0:64, 2, :], st16[0:64, 1, :], AF.Sqrt, bias=epsgn[0:64, :], scale=1.0 / 64)
        k.recip(st16[0:64, 3, :], st16[0:64, 2, :])
        k.tt(Y3, Y3, st16[0:64, 3, :].unsqueeze(2).to_broadcast([64, 16, 64]), ALU.mult)
        gr = jrow(I["a_lnx_g"][li:li + 1, :], 64)
        k.tt(Y[0:64, :], Y[0:64, :], gr[0:64, :], ALU.mult)
        br = jrow(I["a_lnx_b"][li:li + 1, :], 64)
        k.tt(Y[0:64, :], Y[0:64, :], br[0:64, :], ALU.add)
        k.tt(T2[0:64, :].rearrange("p (h j) -> p h j", h=16), PM["v"][0:64, :].rearrange("p (h j) -> p h j", h=16),
             bon[0:64, :].unsqueeze(2).to_broadcast([64, 16, 64]), ALU.mult)
        k.tt(Y[0:64, :], Y[0:64, :], T2[0:64, :], ALU.add)
        k.act(T2[0:64, :], PM["z"][0:64, :], AF.Silu)
        k.tt(G[:], Y[0:64, :], T2[0:64, :], ALU.mult)
        psb = ps[2][:].bitcast(BF16)
        for c in range(8):
            k.tr(psb[:, c * 64:(c + 1) * 64], G[:, c * 128:(c + 1) * 128], C.identb[0:64, 0:64])
        k.cp(gT[:].rearrange("p c t -> p (c t)"), psb[:, 0:512], eng="act")
        for hf in range(2):
            cols = slice(hf * 512, (hf + 1) * 512)
            for c in range(8):
                k.mm(ps[hf][0:64, :], gT[:, c, :], Wo[:, c, cols], start=(c == 0), stop=(c == 7))
            k.stt(Y[0:64, cols], xin[:, cols], ALPHA, ps[hf][0:64, :], ALU.mult, ALU.add)
        ln_tail(C, Y, 64, L, dstx[r0:r0 + 64, :], T2, crow)


def pool_layer(C, lst, L, src, dst):
    nc, P, k, I, O = C.nc, C.P, C.k, C.I, C.O
    tp = C.tp
    sb = lambda n, s, d=F32: lst.enter_context(nc.sbuf_tensor("b_" + n, list(s), d))
    ps = C.ps
    Win = sb("Win", [128, 8, 2048], BF16)
    Wg = sb("Wg", [128, 4, 2, 256], BF16)
    Wo = sb("Wo", [128, 8, 1024], BF16)
    scol = sb("scol", [128, 8])
    stg = [sb("stg%d" % i, [128, 8, 128]) for i in range(2)]
    xin = sb("xin", [128, 1024])
    xT = sb("xT", [128, 8, 128], BF16)
    E = sb("E", [128, 8, 16 * 23])
    A = sb("A", [128, 2, 16 * 23])
    B = sb("B", [128, 2, 16 * 23])
    dT = sb("dT", [128, 8, 128], BF16)
    sz = sb("sz", [128, 8, 128])
    gT = sb("gT", [128, 8, 128], BF16)
    R = sb("R", [128, 1024])
    T1 = sb("T1", [128, 1024])
    cinv = sb("cinv", [128, 512])
    crow = [sb("crow%d" % i, [128, 1024]) for i in range(2)]
    C.lnst = sb("lnst", [128, 8])
    C.epsln = sb("epsln", [128, 1])
    k.memset(C.epsln[:], LN_EPS)
    k.dma(cinv[:], I["cmask"][:, 0:512])
    w_in = I["b_w_in"].rearrange("(c p) n -> p c n", p=128)
    for j in range(16):
        k.dma(stg[j % 2][:], w_in[:, :, j * 128:(j + 1) * 128], eng="sp" if j % 2 == 0 else "act")
        k.cp(Win[:, :, j * 128:(j + 1) * 128], stg[j % 2][:], eng="pool" if j % 2 == 0 else "act")
    w_out = I["b_w_out"].rearrange("(c p) n -> p c n", p=128)
    for j in range(8):
        k.dma(stg[j % 2][:], w_out[:, :, j * 128:(j + 1) * 128], eng="sp" if j % 2 == 0 else "act")
        k.cp(Wo[:, :, j * 128:(j + 1) * 128], stg[j % 2][:], eng="pool" if j % 2 == 0 else "act")
    for g in range(4):
        sv = stg[g % 2][:].rearrange("p c n -> p (c n)")[:, 0:512].rearrange("p (c n) -> p c n", c=2)
        k.dma(sv, I["b_w_grp"][g].rearrange("(c p) n -> p c n", p=128), eng="sp")
        k.cp(Wg[:, g, :, :], sv, eng="pool")
    k.dma(scol[:], I["b_scale"][0].rearrange("(c p) -> p c", p=128), allow_slow_non_contiguous=True)
    k.memset(E[:], 0.0)

    ntile_p = tp // 128
    tiles = [("p", n) for n in range(ntile_p)] + [("s", 0)]
    for kind, n in tiles:
        srcx = src[0] if kind == "p" else src[1]
        dstx = dst[0] if kind == "p" else dst[1]
        r0 = n * 128
        nseg, new = (1, 128) if kind == "p" else (16, 8)
        sl = 15 + new
        Ev = E[:, :, 0:nseg * sl].rearrange("p c (s t) -> p c s t", s=nseg)
        k.dma(xin[:], srcx[r0:r0 + 128, :])
        for b in range(2):
            for c in range(4):
                k.tr(ps[b][:, c * 128:(c + 1) * 128], xin[:, (4 * b + c) * 128:(4 * b + c + 1) * 128], C.identf[:, :])
            k.cp(xT[:, 4 * b:4 * b + 4, :].rearrange("p c t -> p (c t)"), ps[b][:], eng="act")
        if kind == "s":
            for hh in range(2):
                k.dma(R[0:120, :], I["st_pool"][hh * 8:(hh + 1) * 8].rearrange("s r d -> (s r) d"))
                for b in range(2):
                    for c in range(4):
                        k.tr(ps[2 + b][:, c * 120:(c + 1) * 120], R[0:120, (4 * b + c) * 128:(4 * b + c + 1) * 128], C.identf[0:120, 0:120])
                    k.cp(Ev[:, 4 * b:4 * b + 4, hh * 8:(hh + 1) * 8, 0:15],
                         ps[2 + b][:, 0:480].rearrange("p (c s r) -> p c s r", c=4, s=8), eng="act")
        for ob in range(4):
            for o4 in range(4):
                oc = ob * 4 + o4
                for dc in range(8):
                    k.mm(ps[4 + ob % 2][:, o4 * 128:(o4 + 1) * 128], Win[:, dc, oc * 128:(oc + 1) * 128], xT[:, dc, :],
                         start=(dc == 0), stop=(dc == 7))
            pv = ps[4 + ob % 2][:].rearrange("p (c s t) -> p c s t", c=4, s=nseg)
            if ob < 2:
                k.cp(Ev[:, ob * 4:ob * 4 + 4, :, 15:sl], pv, eng="act")
            else:
                k.act(sz[:, (ob - 2) * 4:(ob - 2) * 4 + 4, :].rearrange("p c t -> p (c t)"), ps[4 + ob % 2][:], AF.Silu)
        if kind == "p" and n == ntile_p - 1:
            for b in range(2):
                for c in range(4):
                    k.tr(ps[2 + b][0:15, c * 128:(c + 1) * 128], E[:, 4 * b + c, 128:143], C.identf[:, :])
                k.cp(T1[0:15, b * 512:(b + 1) * 512], ps[2 + b][0:15, :], eng="act")
            k.dma(O["pl_p"], T1[0:15, :])
        if kind == "s":
            for hh in range(2):
                for b in range(2):
                    for c in range(4):
                        Ac = A[:, 0, 0:120].rearrange("p (s r) -> p s r", s=8)
                        k.cp(Ac, Ev[:, 4 * b + c, hh * 8:(hh + 1) * 8, 8:23], eng="pool")
                        k.tr(ps[2 + b][0:120, c * 128:(c + 1) * 128], A[:, 0, 0:120], C.identf[:, :])
                    k.cp(T1[0:120, b * 512:(b + 1) * 512], ps[2 + b][0:120, :], eng="act")
                k.dma(O["pl_s"][hh * 8:(hh + 1) * 8].rearrange("s r d -> (s r) d"), T1[0:120, :])
        for g in range(4):
            cur = Ev[:, 2 * g:2 * g + 2]
            bufs = [A[:, :, 0:nseg * sl].rearrange("p c (s t) -> p c s t", s=nseg),
                    B[:, :, 0:nseg * sl].rearrange("p c (s t) -> p c s t", s=nseg)]
            lo = 0
            for si, sh in enumerate((1, 2, 4, 8)[:g + 1]):
                nxt = bufs[si % 2]
                lo2 = lo + sh
                k.tt(nxt[:, :, :, lo2:sl], cur[:, :, :, lo2:sl], cur[:, :, :, lo:sl - sh], ALU.add)
                cur, lo = nxt, lo2
            w = 2 ** (g + 1)
            pooled = bufs[(g + 1) % 2]
            if kind == "p" and n == 0:
                k.tt(pooled[:, :, 0, 15:sl], cur[:, :, 0, 15:sl],
                     cinv[:, g * 128:(g + 1) * 128].unsqueeze(1).to_broadcast([128, 2, 128]), ALU.mult)
                k.tt(dT[:, 2 * g:2 * g + 2, :], pooled[:, :, 0, 15:sl], Ev[:, 2 * g:2 * g + 2, 0, 15:sl], ALU.subtract)
            else:
                k.stt(dT[:, 2 * g:2 * g + 2, :].rearrange("p c (s t) -> p c s t", s=nseg), cur[:, :, :, 15:sl], 1.0 / w,
                      Ev[:, 2 * g:2 * g + 2, :, 15:sl], ALU.mult, ALU.subtract)
        if kind == "p":
            k.cp(A[:, 0, 0:120].rearrange("p (c t) -> p c t", c=8), E[:, :, 128:143], eng="pool")
            k.cp(E[:, :, 0:15], A[:, 0, 0:120].rearrange("p (c t) -> p c t", c=8), eng="pool")
        for jc in range(8):
            g, jl = jc // 2, jc % 2
            for ic in range(2):
                k.mm(ps[6 + jc // 4][:, (jc % 4) * 128:(jc % 4 + 1) * 128], Wg[:, g, ic, jl * 128:(jl + 1) * 128], dT[:, 2 * g + ic, :],
                     start=(ic == 0), stop=(ic == 1))
        for jc in range(8):
            k.stt(gT[:, jc, :], ps[6 + jc // 4][:, (jc % 4) * 128:(jc % 4 + 1) * 128], scol[:, jc:jc + 1], sz[:, jc, :], ALU.mult, ALU.mult)
        for hf in range(2):
            cols = slice(hf * 512, (hf + 1) * 512)
            for c in range(8):
                k.mm(ps[hf][:], gT[:, c, :], Wo[:, c, cols], start=(c == 0), stop=(c == 7))
            k.stt(R[:, cols], xin[:, cols], ALPHA, ps[hf][:], ALU.mult, ALU.add)
        ln_tail(C, R, 128, L, dstx[r0:r0 + 128, :], T1, crow)


NEG = -30000.0
SCL = 0.125


def nsa_layer(C, lst, L, src, dst):
    nc, P, k, I, O = C.nc, C.P, C.k, C.I, C.O
    tp = C.tp
    sb = lambda n, s, d=F32: lst.enter_context(nc.sbuf_tensor("c_" + n, list(s), d))
    ps = C.ps
    Win = sb("Win", [128, 8, C_NC], BF16)
    Wo = sb("Wo", [128, 8, 1024], BF16)
    stg = [sb("stg%d" % i, [128, 8, 128]) for i in range(2)]
    xin = sb("xin", [128, 1024])
    xT = sb("xT", [128, 8, 128], BF16)
    KV = sb("KV", [128, 1536])
    KsT = sb("KsT", [64, 4, 17 * 128], BF16)
    KwT = sb("KwT", [64, 4, 5 * 128], BF16)
    Vs = sb("Vs", [128, 17, 4, 65], BF16)
    Vw = sb("Vw", [128, 5, 4, 65], BF16)
    KcT = sb("KcT", [64, 4, 64], BF16)
    Vc = sb("Vc", [64, 4, 98])
    Wbk = sb("Wbk", [128, 124])
    Wbv = sb("Wbv", [128, 124])
    wcol = sb("wcol", [128, 2])
    QT = sb("QT", [64, 16, 128], BF16)
    GZ = sb("GZ", [128, 1072])
    gates = sb("gates", [128, 48])
    Bt = [sb("Bt%d" % i, [128, 512]) for i in range(4)]
    SBS = sb("SBS", [128, 17 * 128])
    SBW = sb("SBW", [128, 5 * 128])
    SBC = sb("SBC", [64, 128])
    hbias = sb("hbias", [128, 240])
    k.dma(hbias[:], I["n_hb"])
    GBUF = [sb("GBUF%d" % i, [128, 1024]) for i in range(2)]
    tmp = sb("tmp", [128, 512])
    TMP = [tmp, sb("tmp1", [128, 512])]
    ec = sb("ec", [64, 512])
    eb = sb("eb", [128, 512], BF16)
    EB = [eb, sb("eb1", [128, 512], BF16)]
    OB = sb("OB", [128, 4, 98])
    rd = sb("rd", [128, 8])
    imp = sb("imp", [128, 40])
    imp2 = sb("imp2", [128, 40])
    m8 = sb("m8", [128, 16])
    cbt = sb("cbt", [128, 80])
    selT = sb("selT", [40, 128], BF16)
    Eexp = sb("Eexp", [40, 17 * 128], BF16)
    Oacc = sb("Oacc", [128, 1024])
    Gb = sb("Gb", [128, 1024], BF16)
    gT = sb("gT", [128, 8, 128], BF16)
    T1 = sb("T1", [128, 1024])
    idx = sb("idx", [128, 256], I32)
    idf = GBUF[0][:, 0:256]
    crow = [x[:].rearrange("p c n -> p (c n)") for x in stg]
    C.lnst = sb("lnst", [128, 8])
    C.epsln = sb("epsln", [128, 1])
    k.memset(C.epsln[:], LN_EPS)
    w_in = I["c_w_in"].rearrange("(c p) n -> p c n", p=128)
    nj = (C_NC + 127) // 128
    for j in range(nj):
        wd = min(128, C_NC - j * 128)
        k.dma(stg[j % 2][:, :, 0:wd], w_in[:, :, j * 128:j * 128 + wd], eng="sp" if j % 2 == 0 else "act")
        k.cp(Win[:, :, j * 128:j * 128 + wd], stg[j % 2][:, :, 0:wd], eng="pool" if j % 2 == 0 else "act")
    w_out = I["c_w_out"].rearrange("(c p) n -> p c n", p=128)
    for j in range(8):
        k.dma(stg[j % 2][:], w_out[:, :, j * 128:(j + 1) * 128], eng="sp" if j % 2 == 0 else "act")
        k.cp(Wo[:, :, j * 128:(j + 1) * 128], stg[j % 2][:], eng="pool" if j % 2 == 0 else "act")
    for r in range(4):
        k.dma(wcol[r * 32:(r + 1) * 32, 0:1], I["c_cmp_wk"].rearrange("o l -> l o"), allow_slow_non_contiguous=True)
        k.dma(wcol[r * 32:(r + 1) * 32, 1:2], I["c_cmp_wv"].rearrange("o l -> l o"), allow_slow_non_contiguous=True)
    k.dma(Wbk[:], I["n_wbm"])
    k.cp(Wbv[:], Wbk[:], eng="pool")
    k.ts(Wbk[:], Wbk[:], wcol[:, 0:1], ALU.mult)
    k.ts(Wbv[:], Wbv[:], wcol[:, 1:2], ALU.mult)
    k.memset(Vs[:], 0.0)
    k.memset(Vw[:], 0.0)
    k.memset(KsT[:], 0.0)
    k.memset(KwT[:], 0.0)
    k.memset(Vs[:, :, :, 64:65], 1.0)
    k.memset(Vw[:, :, :, 64:65], 1.0)

    def kv_tile(kt, rows_cmp, rows_sel, rows_win, nrows, do_cmp_block=None, win_slot=None):
        if rows_sel is not None:
            ksr, vsr = rows_sel
            for g in range(4):
                k.tr(ps[0][0:64, g * 128:g * 128 + nrows], ksr[:, g * 64:(g + 1) * 64], C.identf[0:nrows, 0:nrows])
            k.cp(KsT[:, :, kt * 128:kt * 128 + nrows], ps[0][0:64, :].rearrange("p (g t) -> p g t", g=4)[:, :, 0:nrows], eng="act")
            k.cp(Vs[0:nrows, kt, :, 0:64], vsr.rearrange("p (g d) -> p g d", g=4), eng="dve")
        if rows_win is not None:
            kwr, vwr = rows_win
            ws = win_slot
            for g in range(4):
                k.tr(ps[1][0:64, g * 128:g * 128 + nrows], kwr[:, g * 64:(g + 1) * 64], C.identf[0:nrows, 0:nrows])
            k.cp(KwT[:, :, ws * 128:ws * 128 + nrows], ps[1][0:64, :].rearrange("p (g t) -> p g t", g=4)[:, :, 0:nrows], eng="act")
            k.cp(Vw[0:nrows, ws, :, 0:64], vwr.rearrange("p (g d) -> p g d", g=4), eng="dve")
        if rows_cmp is not None:
            kcr, vcr = rows_cmp
            t = do_cmp_block
            for g in range(4):
                k.mm(ps[2][0:64, g * 4:(g + 1) * 4], kcr[:, g * 64:(g + 1) * 64], Wbk[:, 60:64])
            k.cp(KcT[:, :, 4 * t:4 * t + 4], ps[2][0:64, 0:16].rearrange("p (g n) -> p g n", g=4), eng="act")
            k.mm(ps[2][0:64, 128:384], Wbv[:, 60 - 4 * t:124 - 4 * t], vcr)
            k.tt(Vc[:, :, 0:64], Vc[:, :, 0:64], ps[2][0:64, 128:384].rearrange("p (g d) -> p g d", g=4), ALU.add)

    bti = [0]

    OS = sb("OS", [128, 260])
    selT4 = sb("selT4", [40, 4, 8], BF16)

    def attend(nq, nblk, s_tiles, w_tiles, bc_ap, bs_fn, bw_fn, cb_ap, ft_ap, pair_ap, eexp_cols, load_consts=True, resident=False, batched=False):
        nc4 = 4 * nq
        if load_consts:
            k.dma(cbt[0:nq, 0:nblk], cb_ap)
            k.dma(cbt[0:nq, 40:40 + nblk], ft_ap)
            for g in range(4):
                k.dma(Vc[:, g, 65:65 + nblk], pair_ap)

        def bias_tile(ap, nk):
            if resident:
                return ap
            t = Bt[bti[0] % 4]
            bti[0] += 1
            k.dma(t[0:nk, 0:nc4], ap, eng="sp")
            return t[0:nk, 0:nc4]
        accs = []

        for g in range(4):
            Qg = QT[:, 4 * g:4 * g + 4, 0:nq]
            Qg2 = ec
            k.mm(ps[3][0:64, 0:nc4], KcT[:, g, :], QTf[:, g, 0:nc4])
            bt = bias_tile(bc_ap(g), 64)
            k.stt(tmp[0:64, 0:nc4], ps[3][0:64, 0:nc4], SCL, bt, ALU.mult, ALU.add)
            k.act(ec[:, 0:nc4], tmp[0:64, 0:nc4], AF.Exp)
            for j in range(4):
                k.mm(ps[4][0:nq, j * 98:j * 98 + 65 + nblk], ec[:, j * nq:(j + 1) * nq], Vc[:, g, 0:65 + nblk])
            k.cp(OB[0:nq, :, 0:65 + nblk], ps[4][0:nq, 0:392].rearrange("p (j c) -> p j c", j=4)[:, :, 0:65 + nblk], eng="act")
            k.ts(rd[0:nq, 0:4], OB[0:nq, :, 64], 1e-30, ALU.max)
            k.recip(rd[0:nq, 0:4], rd[0:nq, 0:4])
            k.ts(imp[0:nq, 0:nblk], OB[0:nq, 0, 65:65 + nblk], rd[0:nq, 0:1], ALU.mult)
            for j in range(1, 4):
                k.stt(imp[0:nq, 0:nblk], OB[0:nq, j, 65:65 + nblk], rd[0:nq, j:j + 1], imp[0:nq, 0:nblk], ALU.mult, ALU.add)
            k.tt(imp[0:nq, 0:nblk], imp[0:nq, 0:nblk], cbt[0:nq, 0:nblk], ALU.mult)
            k.tt(imp[0:nq, 0:nblk], imp[0:nq, 0:nblk], cbt[0:nq, 40:40 + nblk], ALU.add)
            P.op("dve", lambda e: e.max(out=m8[0:nq, 0:8], in_=imp[0:nq, 0:nblk]), reads=[imp], writes=[m8])
            P.op("dve", lambda e: e.match_replace(out=imp2[0:nq, 0:nblk], in_to_replace=m8[0:nq, 0:8], in_values=imp[0:nq, 0:nblk], imm_value=-2.0),
                 reads=[imp, m8], writes=[imp2])
            P.op("dve", lambda e: e.max(out=m8[0:nq, 8:16], in_=imp2[0:nq, 0:nblk]), reads=[imp2], writes=[m8])
            k.ts(m8[0:nq, 15:16], m8[0:nq, 15:16], 0.0, ALU.max)
            k.ts(imp2[0:nq, 0:nblk], imp[0:nq, 0:nblk], m8[0:nq, 15:16], ALU.is_ge)
            k.tr(ps[5][0:nblk, 0:nq], imp2[0:nq, 0:nblk], C.identf[0:nq, 0:nq])
            if batched:
                k.cp(selT4[0:nblk, g, :], ps[5][0:nblk, 0:nq], eng="act")
            else:
                k.cp(selT[0:nblk, 0:nq], ps[5][0:nblk, 0:nq], eng="act")
            def accum(first, col, Osrc, g=g):
                gsl = gates[0:nq, :].rearrange("p (h c) -> p h c", c=3)[:, 4 * g:4 * g + 4, col]
                k.tt(rd[0:nq, 4:8], rd[0:nq, 0:4], gsl, ALU.mult)
                dstv = Oacc[0:nq, g * 256:(g + 1) * 256].rearrange("p (j d) -> p j d", j=4)
                rb = rd[0:nq, 4:8].unsqueeze(2).to_broadcast([nq, 4, 64])
                if first:
                    k.tt(dstv, Osrc, rb, ALU.mult)
                else:
                    k.tt(OB[0:nq, :, 0:64], Osrc, rb, ALU.mult)
                    k.tt(dstv, dstv, OB[0:nq, :, 0:64], ALU.add)
            accum(True, 0, OB[0:nq, :, 0:64])
            if batched:
                accs.append(accum)
                continue
            items = []
            for br, tiles, Kt, Vt, bfn in ((1, s_tiles, KsT, Vs, bs_fn), (2, w_tiles, KwT, Vw, bw_fn)):
                for ti, (slot, nk, bidx) in enumerate(tiles):
                    items.append((br, Kt, Vt, bfn, ti, len(tiles), slot, nk, bidx))
            PSC = (ps[3], ps[2])

            def front(i):
                br, Kt, Vt, bfn, ti, nt, slot, nk, bidx = items[i]
                k.mm(PSC[i % 2][0:nk, 0:nc4], Kt[:, g, slot * 128:slot * 128 + nk], QTf[:, g, 0:nc4])
                if br == 1:
                    mo = 128 + (i % 2) * 128
                    k.mm(ps[5][0:nk, mo:mo + nq], Eexp[0:nblk, eexp_cols(slot)], selT[0:nblk, 0:nq])

            def mid_a(i):
                br, Kt, Vt, bfn, ti, nt, slot, nk, bidx = items[i]
                bt = bfn(bidx, g) if resident else bias_tile(bfn(bidx, g), nk)
                k.stt(TMP[i % 2][0:nk, 0:nc4], PSC[i % 2][0:nk, 0:nc4], SCL, bt, ALU.mult, ALU.add)

            def mid_b(i):
                br, Kt, Vt, bfn, ti, nt, slot, nk, bidx = items[i]
                e = EB[i % 2]
                k.act(e[0:nk, 0:nc4], TMP[i % 2][0:nk, 0:nc4], AF.Exp)
                if br == 1:
                    mo = 128 + (i % 2) * 128
                    k.tt(e[0:nk, 0:nc4].rearrange("p (j q) -> p j q", j=4), e[0:nk, 0:nc4].rearrange("p (j q) -> p j q", j=4),
                         ps[5][0:nk, mo:mo + nq].unsqueeze(1).to_broadcast([nk, 4, nq]), ALU.mult)

            def back(i):
                br, Kt, Vt, bfn, ti, nt, slot, nk, bidx = items[i]
                e = EB[i % 2]
                for j in range(4):
                    k.mm(ps[(6, 7, 0, 1)[j]][0:nq, 0:65], e[0:nk, j * nq:(j + 1) * nq], Vt[0:nk, slot, g, :],
                         start=(ti == 0), stop=(ti == nt - 1))
                if ti == nt - 1:
                    for j in range(4):
                        k.cp(OB[0:nq, j, 0:65], ps[(6, 7, 0, 1)[j]][0:nq, 0:65], eng="act")
                    k.ts(rd[0:nq, 0:4], OB[0:nq, :, 64], 1e-30, ALU.max)
                    k.recip(rd[0:nq, 0:4], rd[0:nq, 0:4])
                    accum(False, br, OB[0:nq, :, 0:64])

            front(0)
            mid_a(0)
            for i in range(len(items)):
                if i + 1 < len(items):
                    front(i + 1)
                    mid_a(i + 1)
                mid_b(i)
                back(i)

        if batched:
            PSC = (ps[3], ps[2])
            for br, tiles, Kt, Vt, SBt in ((1, s_tiles, KsT, Vs, SBS), (2, w_tiles, KwT, Vw, SBW)):
                nt = len(tiles)

                def front(i, br=br, tiles=tiles, Kt=Kt):
                    slot, nk, bidx = tiles[i]
                    for g in range(4):
                        k.mm(PSC[i % 2][0:nk, g * 32:(g + 1) * 32], Kt[:, g, slot * 128:slot * 128 + nk], QTf[:, g, 0:32])
                    if br == 1:
                        mo = 128 + (i % 2) * 128
                        for g in range(4):
                            k.mm(ps[5][0:nk, mo + g * 8:mo + (g + 1) * 8], Eexp[0:nblk, eexp_cols(slot)], selT4[0:nblk, g, :])

                def mid_a(i, tiles=tiles, SBt=SBt):
                    slot, nk, bidx = tiles[i]
                    k.stt(TMP[i % 2][0:nk, 0:128], PSC[i % 2][0:nk, 0:128], SCL, SBt[0:nk, bidx * 128:(bidx + 1) * 128], ALU.mult, ALU.add)

                def mid_b(i, br=br, tiles=tiles):
                    slot, nk, bidx = tiles[i]
                    e = EB[i % 2]
                    k.act(e[0:nk, 0:128], TMP[i % 2][0:nk, 0:128], AF.Exp)
                    if br == 1:
                        mo = 128 + (i % 2) * 128
                        ev = e[0:nk, 0:128].rearrange("p (g j t) -> p g j t", g=4, j=4)
                        k.tt(ev, ev, ps[5][0:nk, mo:mo + 32].rearrange("p (g t) -> p g t", g=4).unsqueeze(2).to_broadcast([nk, 4, 4, 8]), ALU.mult)

                def back(i, tiles=tiles, Vt=Vt, nt=nt):
                    slot, nk, bidx = tiles[i]
                    k.mm(ps[6][:, 0:260], EB[i % 2][0:nk, 0:128], Vt[0:nk, slot, :, :].rearrange("p g c -> p (g c)"),
                         start=(i == 0), stop=(i == nt - 1))

                front(0)
                mid_a(0)
                for i in range(nt):
                    if i + 1 < nt:
                        front(i + 1)
                        mid_a(i + 1)
                    mid_b(i)
                    back(i)
                k.cp(OS[:], ps[6][:, 0:260], eng="act")
                for g in range(4):
                    for j in range(4):
                        h = 4 * g + j
                        k.mm(ps[7][0:8, j * 65:(j + 1) * 65], C.identf[:, h * 8:(h + 1) * 8], OS[:, g * 65:(g + 1) * 65])
                    k.cp(OB[0:8, :, 0:65], ps[7][0:8, 0:260].rearrange("p (j c) -> p j c", j=4), eng="act")
                    k.ts(rd[0:8, 0:4], OB[0:8, :, 64], 1e-30, ALU.max)
                    k.recip(rd[0:8, 0:4], rd[0:8, 0:4])
                    accs[g](False, br, OB[0:8, :, 0:64])

    QTflat = QT[:].rearrange("p h q -> p (h q)")

    class _QTf:
        nq = 128

        def __getitem__(self, key):
            _, g, _ = key
            n4 = 4 * self.nq
            return QTflat[:, g * n4:(g + 1) * n4]
    QTf = _QTf()

    def qgz(npart, nq_cols):
        for h in range(16):
            for dc in range(8):
                k.mm(ps[7][0:64, (h % 4) * 128:(h % 4) * 128 + nq_cols], Win[:, dc, h * 64:(h + 1) * 64], xT[:, dc, 0:nq_cols],
                     start=(dc == 0), stop=(dc == 7))
            if h % 4 == 3:
                QTf.nq = nq_cols
                qdst = QTflat[:, (h - 3) * nq_cols:(h + 1) * nq_cols].rearrange("p (j q) -> p j q", j=4)
                k.cp(qdst, ps[7][0:64, :].rearrange("p (j q) -> p j q", j=4)[:, :, 0:nq_cols], eng="act")
        for i, (c0, c1) in enumerate(((2560, 3072), (3072, 3584), (3584, 3632))):
            for dc in range(8):
                k.mm(ps[i][0:npart, 0:c1 - c0], xT[:, dc, 0:npart], Win[:, dc, c0:c1], start=(dc == 0), stop=(dc == 7))
            k.cp(GZ[0:npart, c0 - 2560:c1 - 2560], ps[i][0:npart, 0:c1 - c0], eng="act")
        k.act(gates[0:npart, :], GZ[0:npart, 0:48], AF.Sigmoid)

    def finish(npart, dst_rows):
        k.act(T1[0:npart, :], GZ[0:npart, 48:1072], AF.Silu)
        k.tt(Gb[0:npart, :], Oacc[0:npart, :], T1[0:npart, :], ALU.mult)
        psb = ps[2][:].bitcast(BF16)
        for c in range(8):
            k.tr(psb[:, c * 128:c * 128 + npart], Gb[0:npart, c * 128:(c + 1) * 128], C.identb[0:npart, 0:npart])
        k.cp(gT[:, :, 0:npart], psb[:, 0:1024].rearrange("p (c t) -> p c t", c=8)[:, :, 0:npart], eng="act")
        for hf in range(2):
            cols = slice(hf * 512, (hf + 1) * 512)
            for c in range(8):
                k.mm(ps[hf][0:npart, :], gT[:, c, 0:npart], Wo[:, c, cols], start=(c == 0), stop=(c == 7))
            k.stt(T1[0:npart, cols], xin[0:npart, cols], ALPHA, ps[hf][0:npart, :], ALU.mult, ALU.add)
        ln_tail(C, T1, npart, L, dst_rows, Oacc, crow)

    def load_xT(rows_ap, npart):
        k.dma(xin[0:npart, :], rows_ap)
        for b in range(2):
            for c in range(4):
                k.tr(ps[b][:, c * 128:c * 128 + npart], xin[0:npart, (4 * b + c) * 128:(4 * b + c + 1) * 128], C.identf[0:npart, 0:npart])
            k.cp(xT[:, 4 * b:4 * b + 4, 0:npart], ps[b][:].rearrange("p (c t) -> p c t", c=4)[:, :, 0:npart], eng="act")

    def kv_proj(npart):
        for i in range(3):
            for dc in range(8):
                k.mm(ps[3 + i][0:npart, :], xT[:, dc, 0:npart], Win[:, dc, 1024 + i * 512:1024 + (i + 1) * 512], start=(dc == 0), stop=(dc == 7))
            k.cp(KV[0:npart, i * 512:(i + 1) * 512], ps[3 + i][0:npart, :], eng="act")

    k.memset(Vc[:], 0.0)
    k.memset(Vc[:, :, 64:65], 1.0)
    k.memset(KcT[:], 0.0)
    k.dma(Eexp[0:32, 0:2048], I["n_eexp_p"])
    ntile = tp // 128
    for t in range(ntile):
        r0 = t * 128
        load_xT(src[0][r0:r0 + 128, :], 128)
        kv_proj(128)
        for i, nm in enumerate(("cmpk", "cmpv", "selk", "selv")):
            k.dma(O[nm + "_p"][r0:r0 + 128, :], KV[:, i * 256:(i + 1) * 256])
        wr0 = r0 - (tp - min(512, tp))
        if wr0 >= 0:
            k.dma(O["wink_p"][wr0:wr0 + 128, :], KV[:, 1024:1280])
            k.dma(O["winv_p"][wr0:wr0 + 128, :], KV[:, 1280:1536])
        kv_tile(t, (KV[:, 0:256], KV[:, 256:512]), (KV[:, 512:768], KV[:, 768:1024]), (KV[:, 1024:1280], KV[:, 1280:1536]), 128,
                do_cmp_block=t, win_slot=t % 5)
        qgz(128, 128)
        s_tiles = [(kt, 128, t - kt) for kt in range(t + 1)]
        w_tiles = [(kt % 5, 128, t - kt) for kt in range(max(0, t - 4), t + 1)]
        attend(128, 32, s_tiles, w_tiles,
               lambda g: I["n_bc_p"][t, g], lambda d, g: I["n_bs_p"][d, g], lambda d, g: I["n_bw_p"][d, g],
               I["n_cb_p"][t], I["n_ft_p"][t], I["n_pair_p"], lambda slot: slice(slot * 128, (slot + 1) * 128))
        finish(128, dst[0][r0:r0 + 128, :])

    k.dma(idx[:], I["ptab"].rearrange("s n -> (s n)").partition_broadcast(128))
    k.cp(idf, idx[:])
    k.dma(wcol[:, 0:1], I["n_iota"])
    k.ts(idf, idf, 128.0, ALU.mult, wcol[:, 0:1], ALU.add)
    k.cp(idx[:], idf)
    k.dma(Eexp[0:33, 0:17 * 128], I["n_eexp_s"])
    load_xT(src[1][:, :], 128)
    kv_proj(128)
    KVs = C.kvs_scr
    k.dma(KVs, KV[:])
    for i, nm in enumerate(("cmpk", "cmpv", "selk", "selv")):
        k.dma(O[nm + "_s"], KV[:, i * 256:(i + 1) * 256])
    k.dma(SBS[:].rearrange("k (d g c) -> k d g c", d=17, g=4), I["n_bs_s"].rearrange("d g k c -> k d g c"))
    k.dma(SBW[:].rearrange("k (d g c) -> k d g c", d=5, g=4), I["n_bw_s"].rearrange("d g k c -> k d g c"))
    k.dma(SBC[:].rearrange("k (g c) -> k g c", g=4), I["n_bc_s"].rearrange("g k c -> k g c"))
    k.dma(cbt[0:8, 0:33], I["n_cb_s"])
    k.dma(cbt[0:8, 40:73], I["n_ft_s"])
    for g in range(4):
        k.dma(Vc[:, g, 65:98], I["n_pair_s"])
    xTs_all = sb("xTs_all", [128, 8, 128], BF16)
    k.cp(xTs_all[:], xT[:], eng="dve")
    NEW = KV[0:8, :]
    for sq in range(NS):
        k.memset(Vc[:, :, 0:64], 0.0)
        pools = (I["cmp_k"], I["cmp_v"], I["sel_k"], I["sel_v"])

        def gather(pool_ap, slot, dst_tile, sq=sq):
            P.dma("pool", dst_tile, pool_ap, reads=[pool_ap, idx], writes=[dst_tile],
                  fn=lambda e: e.indirect_dma_start(out=dst_tile, out_offset=None, in_=pool_ap,
                                                    in_offset=bass.IndirectOffsetOnAxis(ap=idx[:, sq * 16 + slot:sq * 16 + slot + 1], axis=0)))
        for pg in range(16):
            gb = GBUF[pg % 2]
            for ci in range(4):
                gather(pools[ci], pg, gb[:, ci * 256:(ci + 1) * 256])
            kv_tile(pg, (gb[:, 0:256], gb[:, 256:512]), (gb[:, 512:768], gb[:, 768:1024]), None, 128, do_cmp_block=pg)
        for wt in range(4):
            gb = GBUF[wt % 2]
            k.dma(gb[:, 0:256], I["win_k"][sq, wt * 128:(wt + 1) * 128, :])
            k.dma(gb[:, 256:512], I["win_v"][sq, wt * 128:(wt + 1) * 128, :])
            kv_tile(0, None, None, (gb[:, 0:256], gb[:, 256:512]), 128, win_slot=wt)
        k.dma(NEW, KVs[sq * 8:(sq + 1) * 8, :])
        kv_tile(16, None, (NEW[:, 512:768], NEW[:, 768:1024]), (NEW[:, 1024:1280], NEW[:, 1280:1536]), 8, win_slot=4)
        k.dma(O["wink_s"][sq, 0:504, :], I["win_k"][sq, 8:512, :])
        k.dma(O["winv_s"][sq, 0:504, :], I["win_v"][sq, 8:512, :], eng="act")
        k.dma(O["wink_s"][sq, 504:512, :], NEW[:, 1024:1280])
        k.dma(O["winv_s"][sq, 504:512, :], NEW[:, 1280:1536])
        k.cp(xT[:, :, 0:8], xTs_all[:, :, sq * 8:(sq + 1) * 8], eng="dve")
        k.dma(xin[0:8, :], src[1][sq * 8:(sq + 1) * 8, :])
        qgz(8, 8)
        s_tiles = [(kt, 128, kt) for kt in range(16)] + [(16, 8, 16)]
        w_tiles = [(kt, 128, kt) for kt in range(4)] + [(4, 8, 4)]
        attend(8, 33, s_tiles, w_tiles,
               lambda g: SBC[:, g * 32:(g + 1) * 32],
               lambda d, g: SBS[0:(8 if d == 16 else 128), d * 128 + g * 32:d * 128 + (g + 1) * 32],
               lambda d, g: SBW[0:(8 if d == 4 else 128), d * 128 + g * 32:d * 128 + (g + 1) * 32],
               None, None, None, lambda slot: slice(slot * 128, slot * 128 + (8 if slot == 16 else 128)), load_consts=False, resident=True, batched=True)
        finish(8, dst[1][sq * 8:(sq + 1) * 8, :])


def _cmask():
    m = np.zeros((128, 2048), np.float32)
    t = np.arange(128)
    for g, w in enumerate((2, 4, 8, 16)):
        m[:, g * 128:(g + 1) * 128] = (1.0 / np.minimum(w, t + 1))[None, :]
    a = np.arange(64)
    su = (a[:, None] < a[None, :]).astype(np.float32)
    ui = (a[:, None] <= a[None, :]).astype(np.float32)
    m[0:64, 512:576] = su
    m[0:64, 576:640] = ui
    m[0:64, 640:704] = su.T
    m[0:64, 704:768] = ui
    m[0:64, 768:832] = np.eye(64, dtype=np.float32)
    return m


def consts():
    import ml_dtypes
    sel = np.zeros((128, 64, 128), np.float32)
    for kk in range(128):
        sel[kk, kk % 64, (kk // 64) * 64:(kk // 64) * 64 + 64] = 1
    return {"identf": np.eye(128, dtype=np.float32), "selb": sel.reshape(128, 64 * 128).astype(ml_dtypes.bfloat16),
            "cmask": _cmask()}


def shard_inputs(inp, c, tp=TP):
    f = lambda a: np.ascontiguousarray(a)
    m = {
        "xp": f(inp["x_prompt"][c, :tp]), "xs": f(inp["x_sample"][16 * c:16 * c + 16].reshape(128, D)),
        "st_S": f(inp["state_rwkv_S"][:, 16 * c:16 * c + 16]), "st_shift": f(inp["state_rwkv_shift"][:, 16 * c:16 * c + 16]),
        "st_pool": f(inp["state_pool"][0, 16 * c:16 * c + 16]),
        "cmp_k": f(inp["cache_cmp_k"][0].reshape(-1, 256)), "cmp_v": f(inp["cache_cmp_v"][0].reshape(-1, 256)),
        "sel_k": f(inp["cache_sel_k"][0].reshape(-1, 256)), "sel_v": f(inp["cache_sel_v"][0].reshape(-1, 256)),
        "win_k": f(inp["state_win_k"][0, 16 * c:16 * c + 16].reshape(16, 512, 256)),
        "win_v": f(inp["state_win_v"][0, 16 * c:16 * c + 16].reshape(16, 512, 256)),
        "ptab": f(inp["page_table"][16 * c:16 * c + 16]).astype(np.int32),
        "a_r_k": f(inp["a_r_k"].reshape(2, D)), "b_w_in": f(inp["b_w_in"][0]), "b_w_grp": f(inp["b_w_grp"][0]),
        "b_scale": f(inp["b_scale"]), "b_w_out": f(inp["b_w_out"][0]), "c_w_in": f(inp["c_w_in"][0]),
        "c_cmp_wk": f(inp["c_cmp_wk"]), "c_cmp_wv": f(inp["c_cmp_wv"]), "c_w_out": f(inp["c_w_out"][0]),
    }
    for nm in ("ln_g", "ln_b", "a_w_in", "a_mu", "a_w0", "a_w2", "a_a0", "a_a2", "a_k_k", "a_k_a", "a_lnx_g", "a_lnx_b", "a_w_out"):
        m[nm] = f(inp[nm])
    m.update(consts())
    m.update(nsa_consts())
    return m


def nsa_consts():
    sl = 2.0 ** (-8.0 * (np.arange(16) + 1) / 16)
    c = {}

    def bias(dist, valid, g):
        K_, nq = dist.shape
        out = np.empty((K_, 4, nq), np.float32)
        for j in range(4):
            out[:, j] = np.where(valid, -sl[4 * g + j] * dist, NEG)
        return out.reshape(K_, 4 * nq)
    q = np.arange(128)[None, :]
    kk = np.arange(128)[:, None]
    n = np.arange(64)[:, None]
    bc = np.zeros((16, 4, 64, 512), np.float32)
    bs = np.zeros((16, 4, 128, 512), np.float32)
    bw = np.zeros((5, 4, 128, 512), np.float32)
    for g in range(4):
        for t in range(16):
            d = 128 * t + q - 32 * n - 31
            bc[t, g] = bias(d, d >= 0, g)
            d = 128 * t + q - kk
            bs[t, g] = bias(d, d >= 0, g)
            if t < 5:
                bw[t, g] = bias(d, (d >= 0) & (d < 512), g)
    c["n_bc_p"], c["n_bs_p"], c["n_bw_p"] = bc, bs, bw
    cb = np.zeros((16, 128, 32), np.float32)
    ft = np.zeros((16, 128, 32), np.float32)
    blk = np.arange(32)[None, :]
    for t in range(16):
        cur = ((128 * t + np.arange(128)) // 64)[:, None]
        cb[t] = (blk < cur)
        ft[t] = np.where(blk == cur, 1e9, np.where(blk > cur, -1.0, 0.0))
    c["n_cb_p"], c["n_ft_p"] = cb, ft
    c["n_pair_p"] = (np.arange(64)[:, None] // 2 == np.arange(32)[None, :]).astype(np.float32)
    c["n_eexp_p"] = (np.arange(2048)[None, :] // 64 == np.arange(32)[:, None]).astype(np.float32)
    wbm = np.zeros((128, 124), np.float32)
    for r in range(128):
        wbm[r, 60 + r // 32] = 1.0
    c["n_wbm"] = wbm
    c["n_iota"] = np.arange(128, dtype=np.float32).reshape(128, 1)
    tq = np.arange(8)[None, :]
    bcs = np.zeros((4, 64, 32), np.float32)
    bss = np.zeros((17, 4, 128, 32), np.float32)
    bws = np.zeros((5, 4, 128, 32), np.float32)
    for g in range(4):
        d = 2048 + tq - 32 * n - 31
        bcs[g] = bias(d, d >= 0, g)
        for kt in range(16):
            d = 2048 + tq - 128 * kt - kk
            bss[kt, g] = bias(d, d >= 0, g)
        d = tq - kk
        newb = bias(d, (d >= 0) & (kk < 8), g)
        bss[16, g] = newb
        for kt in range(4):
            d = 2048 + tq - (1536 + 128 * kt + kk)
            bws[kt, g] = bias(d, (d >= 0) & (d < 512), g)
        bws[4, g] = newb
    c["n_bc_s"], c["n_bs_s"], c["n_bw_s"] = bcs, bss, bws
    cbs = np.ones((8, 33), np.float32)
    cbs[:, 32] = 0
    fts = np.zeros((8, 33), np.float32)
    fts[:, 32] = 1e9
    c["n_cb_s"], c["n_ft_s"] = cbs, fts
    ps_ = np.zeros((64, 33), np.float32)
    ps_[:, :32] = c["n_pair_p"]
    c["n_pair_s"] = ps_
    ee = np.zeros((33, 17 * 128), np.float32)
    ee[:32, :2048] = c["n_eexp_p"]
    ee[32, 2048:] = 1.0
    import ml_dtypes
    c["n_eexp_s"] = ee.astype(ml_dtypes.bfloat16)
    c["n_eexp_p"] = c["n_eexp_p"].astype(ml_dtypes.bfloat16)
    hb = np.zeros((128, 240), np.float32)
    for g in range(4):
        for d in range(1, 16):
            for j in range(4):
                hb[:, g * 60 + (d - 1) * 4 + j] = -sl[4 * g + j] * 128.0 * (d - 1)
    c["n_hb"] = hb
    return c


_NC_CACHE = {}


def kernel(**inputs):
    n = 8
    npool = inputs["cache_cmp_k"].shape[1]
    key = (npool,)
    if key not in _NC_CACHE:
        _NC_CACHE[key] = build(npool=npool, tp=TP)
    nc = _NC_CACHE[key]
    in_maps = [shard_inputs(inputs, c) for c in range(n)]
    res = run_bass_kernel_spmd(nc, in_maps, core_ids=list(range(n)))
    R = res.results
    cat = lambda nm: np.stack([R[c][nm] for c in range(n)], 0)
    y_p = cat("y_p")
    y_s = np.concatenate([R[c]["y_s"].reshape(16, 8, D) for c in range(n)], 0)
    S_p = np.stack([R[c]["S_p"] for c in range(n)], 1)
    S_s = np.concatenate([R[c]["S_s"] for c in range(n)], 1)
    sh_p = np.stack([R[c]["sh_p"] for c in range(n)], 1)
    sh_s = np.concatenate([R[c]["sh_s"] for c in range(n)], 1)
    pl_p = cat("pl_p")[None]
    pl_s = np.concatenate([R[c]["pl_s"] for c in range(n)], 0)[None]
    outs = [y_p, y_s, S_p, S_s, sh_p, sh_s, pl_p, pl_s]
    for nm in ("cmpk", "cmpv", "selk", "selv"):
        outs.append(cat(nm + "_p").reshape(1, n, TP, 4, 64))
        outs.append(np.concatenate([R[c][nm + "_s"].reshape(16, 8, 4, 64) for c in range(n)], 0)[None])
    for nm in ("wink", "winv"):
        outs.append(cat(nm + "_p").reshape(1, n, 512, 4, 64))
        outs.append(np.concatenate([R[c][nm + "_s"].reshape(16, 512, 4, 64) for c in range(n)], 0)[None])
    return tuple(np.ascontiguousarray(o, dtype=np.float32) for o in outs)
```
